# Optimizing a Trainium2 kernel written in Bass

```python
import math
import jax, jax.numpy as jnp
from jax import lax
import numpy as np

D_MODEL = 1024
BATCH = 8
SEQ = 8192
DEPTH = 2
DEC_BATCH = 16
DEC_SEQ = 2048
PAST_LEN = 128

CONV_WIDTH = D_MODEL
CONV_KERNEL = 31
RET_HEADS = 4
RET_DK = 256
RET_DV = 256
RET_QK = RET_HEADS * RET_DK
RET_V = RET_HEADS * RET_DV
CHUNK = 128
ROPE_BASE = 10000.0
IN_COLS = 2 * CONV_WIDTH + 2 * RET_QK + 2 * RET_V
PEER_HEADS = 8
PEER_NK = 128
PEER_NE = PEER_NK * PEER_NK
PEER_DQ = 256
PEER_TOPK = 16
PEER_BLOCK = 128
EPS = 1e-6

kernel_name = "hybrid_conformer_retention_peer_encoder"


def _rmsnorm(x, g):
    xf = x.astype(jnp.float32)
    y = xf * lax.rsqrt(jnp.mean(xf * xf, axis=-1, keepdims=True) + EPS)
    return (y * g.astype(jnp.float32)).astype(x.dtype)


def _layernorm(x, g, b):
    xf = x.astype(jnp.float32)
    mu = jnp.mean(xf, axis=-1, keepdims=True)
    var = jnp.mean(jnp.square(xf - mu), axis=-1, keepdims=True)
    y = (xf - mu) * lax.rsqrt(var + EPS)
    return (y * g.astype(jnp.float32) + b.astype(jnp.float32)).astype(x.dtype)


def _rope(t):
    L = t.shape[1]
    half = t.shape[-1] // 2
    freqs = 1.0 / (ROPE_BASE ** (jnp.arange(half, dtype=jnp.float32) / half))
    ang = jnp.arange(L, dtype=jnp.float32)[:, None] * freqs[None, :]
    cos = jnp.cos(ang)[None, :, None, :]
    sin = jnp.sin(ang)[None, :, None, :]
    t1, t2 = t[..., :half], t[..., half:]
    return jnp.concatenate([t1 * cos - t2 * sin, t1 * sin + t2 * cos], axis=-1)


def _retention_dir(q, k, v, log_gamma, strict):
    B, L, H, DK = q.shape
    DV = v.shape[-1]
    N = L // CHUNK
    lg = log_gamma.astype(jnp.float32)
    qc = q.reshape(B, N, CHUNK, H, DK)
    kc = k.reshape(B, N, CHUNK, H, DK)
    vc = v.reshape(B, N, CHUNK, H, DV)
    pos = jnp.arange(CHUNK, dtype=jnp.float32)
    diff = pos[:, None] - pos[None, :]
    mask = (diff > 0) if strict else (diff >= 0)
    decay = jnp.where(mask[None], jnp.exp(jnp.where(mask, diff, 0.0)[None] * lg[:, None, None]), 0.0)
    scores = jnp.einsum('bnihd,bnjhd->bnhij', qc, kc) * decay
    intra = jnp.einsum('bnhij,bnjhe->bnihe', scores, vc)
    q_dec = qc * jnp.exp((pos + 1.0)[:, None] * lg[None, :])[..., None]
    k_dec = kc * jnp.exp((CHUNK - 1.0 - pos)[:, None] * lg[None, :])[..., None]
    chunk_decay = jnp.exp(CHUNK * lg)

    def step(S, xs):
        qd, kd, vv = xs
        cross = jnp.einsum('bihd,bhde->bihe', qd, S)
        S = S * chunk_decay[None, :, None, None] + jnp.einsum('bjhd,bjhe->bhde', kd, vv)
        return S, cross

    S0 = jnp.zeros((B, H, DK, DV), jnp.float32)
    _, cross = lax.scan(step, S0, (jnp.moveaxis(q_dec, 1, 0), jnp.moveaxis(k_dec, 1, 0), jnp.moveaxis(vc, 1, 0)))
    cross = jnp.moveaxis(cross, 0, 1)
    return (intra + cross).reshape(B, L, H, DV)


def _mixer(xn, w_in, w_gate, conv_w, conv_b, conv_ln_g, conv_ln_b, w_conv_out,
           log_gamma_fwd, log_gamma_bwd, w_ret_out, w_o):
    B, L, _ = xn.shape
    z = xn @ w_in
    c0 = CONV_WIDTH
    c1 = 2 * CONV_WIDTH
    c2 = c1 + RET_QK
    c3 = c2 + RET_QK
    c4 = c3 + RET_V
    glu_a, glu_b, q, k, v, g = (z[..., :c0], z[..., c0:c1], z[..., c1:c2],
                                z[..., c2:c3], z[..., c3:c4], z[..., c4:])
    u = glu_a * jax.nn.sigmoid(glu_b)
    u = lax.conv_general_dilated(u, conv_w[:, None, :], window_strides=(1,),
                                 padding=[(CONV_KERNEL // 2, CONV_KERNEL // 2)],
                                 dimension_numbers=('NWC', 'WIO', 'NWC'),
                                 feature_group_count=CONV_WIDTH) + conv_b
    u = jax.nn.silu(_layernorm(u, conv_ln_g, conv_ln_b))
    y_conv = u @ w_conv_out
    qh = _rope(q.reshape(B, L, RET_HEADS, RET_DK).astype(jnp.float32))
    kh = _rope(k.reshape(B, L, RET_HEADS, RET_DK).astype(jnp.float32)) * (RET_DK ** -0.5)
    vh = v.reshape(B, L, RET_HEADS, RET_DV).astype(jnp.float32)
    rev = lambda t: jnp.flip(t, axis=1)
    o = (_retention_dir(qh, kh, vh, log_gamma_fwd, False)
         + rev(_retention_dir(rev(qh), rev(kh), rev(vh), log_gamma_bwd, True)))
    o = o * lax.rsqrt(jnp.mean(o * o, axis=-1, keepdims=True) + EPS)
    o = o.reshape(B, L, RET_V).astype(xn.dtype)
    y_ret = (jax.nn.silu(g) * o) @ w_ret_out
    gates = jax.nn.sigmoid(xn @ w_gate)
    merged = gates[..., :D_MODEL] * y_conv + gates[..., D_MODEL:] * y_ret
    return merged @ w_o


def _peer(x, wq, subkeys, u_tab, v_tab):
    B, L, D = x.shape
    xb = x.reshape(-1, PEER_BLOCK, D)

    def blk(xt):
        q = (xt @ wq).reshape(PEER_BLOCK, PEER_HEADS, 2, PEER_DQ // 2)
        s = jnp.einsum('thpd,hpkd->thpk', q, subkeys).astype(jnp.float32)
        sv, si = lax.top_k(s, PEER_TOPK)
        cand = (sv[:, :, 0, :, None] + sv[:, :, 1, None, :]).reshape(PEER_BLOCK, PEER_HEADS, PEER_TOPK * PEER_TOPK)
        sc, ci = lax.top_k(cand, PEER_TOPK)
        i1 = jnp.take_along_axis(si[:, :, 0], ci // PEER_TOPK, axis=-1)
        i2 = jnp.take_along_axis(si[:, :, 1], ci % PEER_TOPK, axis=-1)
        e = i1 * PEER_NK + i2
        gate = jax.nn.softmax(sc, axis=-1).astype(xt.dtype)
        hid = jax.nn.gelu(jnp.einsum('td,thkd->thk', xt, u_tab[e]), approximate=False)
        return jnp.einsum('thk,thkd->td', gate * hid, v_tab[e])

    return lax.map(blk, xb).reshape(B, L, D)


def _trunk(x, norm_mix_g, w_in, w_gate, conv_w, conv_b, conv_ln_g, conv_ln_b, w_conv_out,
           log_gamma_fwd, log_gamma_bwd, w_ret_out, w_o, norm_ffn_g, peer_wq, peer_subkeys,
           peer_u, peer_v, final_norm_g):
    for l in range(DEPTH):
        x = x + _mixer(_rmsnorm(x, norm_mix_g[l]), w_in[l], w_gate[l], conv_w[l], conv_b[l],
                       conv_ln_g[l], conv_ln_b[l], w_conv_out[l], log_gamma_fwd[l],
                       log_gamma_bwd[l], w_ret_out[l], w_o[l])
        x = x + _peer(_rmsnorm(x, norm_ffn_g[l]), peer_wq[l], peer_subkeys[l], peer_u[l], peer_v[l])
    return _rmsnorm(x, final_norm_g)


def setup_inputs(seed: int = 0) -> dict:
    key = jax.random.key(seed)
    ks = jax.random.split(key, 24)
    f32 = jnp.float32
    nrm = lambda k, shape, scale: jax.random.normal(k, shape, f32) * scale
    base_exp = 5.0 + jnp.arange(RET_HEADS, dtype=f32)
    lg_f = jnp.log1p(-jnp.exp2(-(base_exp[None] + 0.1 * jax.random.normal(ks[10], (DEPTH, RET_HEADS), f32))))
    lg_b = jnp.log1p(-jnp.exp2(-(base_exp[None] + 0.1 * jax.random.normal(ks[11], (DEPTH, RET_HEADS), f32))))
    return {
        "x_prompt": nrm(ks[0], (BATCH, SEQ, D_MODEL), 1.0),
        "x_sample": nrm(ks[1], (DEC_BATCH, DEC_SEQ, D_MODEL), 1.0),
        "norm_mix_g": 1.0 + nrm(ks[2], (DEPTH, D_MODEL), 0.02),
        "w_in": nrm(ks[3], (DEPTH, D_MODEL, IN_COLS), D_MODEL ** -0.5),
        "w_gate": nrm(ks[4], (DEPTH, D_MODEL, 2 * D_MODEL), D_MODEL ** -0.5),
        "conv_w": nrm(ks[5], (DEPTH, CONV_KERNEL, CONV_WIDTH), CONV_KERNEL ** -0.5),
        "conv_b": nrm(ks[6], (DEPTH, CONV_WIDTH), 0.02),
        "conv_ln_g": 1.0 + nrm(ks[7], (DEPTH, CONV_WIDTH), 0.02),
        "conv_ln_b": nrm(ks[8], (DEPTH, CONV_WIDTH), 0.02),
        "w_conv_out": nrm(ks[9], (DEPTH, CONV_WIDTH, D_MODEL), CONV_WIDTH ** -0.5),
        "log_gamma_fwd": lg_f,
        "log_gamma_bwd": lg_b,
        "w_ret_out": nrm(ks[12], (DEPTH, RET_V, D_MODEL), RET_V ** -0.5),
        "w_o": nrm(ks[13], (DEPTH, D_MODEL, D_MODEL), D_MODEL ** -0.5),
        "norm_ffn_g": 1.0 + nrm(ks[14], (DEPTH, D_MODEL), 0.02),
        "peer_wq": nrm(ks[15], (DEPTH, D_MODEL, PEER_HEADS * PEER_DQ), D_MODEL ** -0.5),
        "peer_subkeys": nrm(ks[16], (DEPTH, PEER_HEADS, 2, PEER_NK, PEER_DQ // 2), (PEER_DQ // 2) ** -0.5),
        "peer_u": nrm(ks[17], (DEPTH, PEER_NE, D_MODEL), D_MODEL ** -0.5),
        "peer_v": nrm(ks[18], (DEPTH, PEER_NE, D_MODEL), 0.5 * PEER_HEADS ** -0.5),
        "final_norm_g": 1.0 + nrm(ks[19], (D_MODEL,), 0.02),
    }


def reference(x_prompt, x_sample, norm_mix_g, w_in, w_gate, conv_w, conv_b, conv_ln_g, conv_ln_b,
              w_conv_out, log_gamma_fwd, log_gamma_bwd, w_ret_out, w_o, norm_ffn_g, peer_wq,
              peer_subkeys, peer_u, peer_v, final_norm_g):
    y_prompt = _trunk(x_prompt, norm_mix_g, w_in, w_gate, conv_w, conv_b, conv_ln_g, conv_ln_b,
                      w_conv_out, log_gamma_fwd, log_gamma_bwd, w_ret_out, w_o, norm_ffn_g, peer_wq,
                      peer_subkeys, peer_u, peer_v, final_norm_g)
    y_sample = _trunk(x_sample, norm_mix_g, w_in, w_gate, conv_w, conv_b, conv_ln_g, conv_ln_b,
                      w_conv_out, log_gamma_fwd, log_gamma_bwd, w_ret_out, w_o, norm_ffn_g, peer_wq,
                      peer_subkeys, peer_u, peer_v, final_norm_g)
    return (y_prompt, y_sample)
```

```python
from contextlib import ExitStack
import numpy as np
import concourse.bass as bass
import concourse.mybir as mybir
from concourse.bass_utils import run_bass_kernel_spmd

F32 = mybir.dt.float32
BF16 = mybir.dt.bfloat16
U32 = mybir.dt.uint32
AF = mybir.ActivationFunctionType
ALU = mybir.AluOpType
AX = mybir.AxisListType

D = 1024
EPS = 1e-6
NEG = -1.0e30
N_CORES = 8
SEQS_FULL = (8192, 2048, 2048)


class Buf:
    __slots__ = ("name", "lw", "rd")

    def __init__(self, name):
        self.name = name
        self.lw = None
        self.rd = {}


class TT:
    def __init__(self, t, name):
        self.t = t
        self.b = Buf(name)

    def __getitem__(self, k):
        return self.t[k]


def _b(x):
    return x.b if isinstance(x, TT) else x


class Eng:
    def __init__(self, S, name, h, nslots=0, own_skip=False):
        self.S = S
        self.name = name
        self.h = h
        self.waited = {}
        self.own_skip = own_skip
        self.nslots = nslots
        if nslots == 0:
            self.sem = S.new_sem(name)
            self.count = 0
        else:
            self.slots = [[S.new_sem(f"{name}{i}"), 0] for i in range(nslots)]
            self.rr = 0

    def _wait(self, tick):
        sem, val, key = tick
        if self.own_skip and self.nslots == 0 and key == self.sem_key():
            return
        if self.waited.get(key, 0) >= val:
            return
        self.h.wait_ge(sem, val)
        self.waited[key] = val

    def sem_key(self):
        return self.name

    def _deps(self, reads, writes):
        for b in reads:
            b = _b(b)
            if b.lw is not None:
                self._wait(b.lw)
        for b in writes:
            b = _b(b)
            if b.lw is not None:
                self._wait(b.lw)
            for t in b.rd.values():
                self._wait(t)

    def _mark(self, reads, writes, tick):
        for b in reads:
            _b(b).rd[tick[2]] = tick
        for b in writes:
            b = _b(b)
            b.lw = tick
            b.rd = {}

    def op(self, reads, writes, fn):
        if DEAD[0]:
            return
        self._deps(reads, writes)
        inst = fn(self.h)
        self.count += 1
        inst.then_inc(self.sem, 1)
        tick = (self.sem, self.count, self.name)
        self._mark(reads, writes, tick)
        self.S.ninst += 1

    def dma(self, reads, writes, out, in_, **kw):
        if DEAD[0]:
            return
        slot = self.slots[self.rr]
        key = f"{self.name}{self.rr}"
        self.rr = (self.rr + 1) % self.nslots
        if slot[1] > 0:
            self._wait((slot[0], slot[1], key))
        self._deps(reads, writes)
        inst = self.h.dma_start(out=out, in_=in_, **kw)
        slot[1] += 16
        inst.then_inc(slot[0], 16)
        tick = (slot[0], slot[1], key)
        self._mark(reads, writes, tick)
        self.S.ninst += 1

    def last_ticks(self):
        if self.nslots == 0:
            return [(self.sem, self.count, self.name)] if self.count else []
        return [(s[0], s[1], f"{self.name}{i}") for i, s in enumerate(self.slots) if s[1]]


class Sched:
    def __init__(self, nc, stack):
        self.nc = nc
        self.stack = stack
        self.ninst = 0
        self.pe = Eng(self, "pe", nc.tensor, own_skip=True)
        self.act = Eng(self, "act", nc.scalar)
        self.dve = Eng(self, "dve", nc.vector)
        self.pool = Eng(self, "pool", nc.gpsimd)
        self.sp = Eng(self, "sp", nc.sync, nslots=8)
        self.sw = Eng(self, "sw", nc.gpsimd, nslots=6)
        self.all = [self.pe, self.act, self.dve, self.pool, self.sp, self.sw]

    def new_sem(self, name):
        return self.stack.enter_context(self.nc.semaphore("s_" + name))

    def barrier(self):
        ticks = []
        for e in self.all:
            ticks += e.last_ticks()
        for e in (self.pe, self.act, self.dve, self.pool, self.sp):
            for t in ticks:
                if e.nslots == 0 and t[2] == e.name:
                    if e.own_skip:
                        continue
                e._wait(t)


def host_consts(lmax):
    c = np.zeros((128, 1024), np.float32)
    p = np.arange(128, dtype=np.float32)
    c[:, 0:128] = np.eye(128, dtype=np.float32)
    c[:, 128:256] = p[None, :]
    diff = p[None, :] - p[:, None]
    c[:, 256:384] = np.maximum(diff, 0.0)
    c[:, 384:512] = np.maximum(-diff, 0.0)
    c[:, 512:640] = (diff >= 0).astype(np.float32) / 16.0
    c[:, 640:768] = (diff < 0).astype(np.float32) / 16.0
    c[:, 768] = p + 1.0
    c[:, 769] = 128.0 - p
    c[:, 770] = 127.0 - p
    c[:, 771] = p
    c[:, 772] = 128.0
    c[:, 776:792] = np.arange(16, dtype=np.float32)[None, :]
    c[:, 800:928] = 1.0 / 1024.0
    half = 128
    freqs = (1.0 / (10000.0 ** (np.arange(half, dtype=np.float32) / np.float32(half)))).astype(np.float32)
    ang = np.arange(lmax, dtype=np.float32)[:, None] * freqs[None, :]
    cosT = np.ascontiguousarray(np.cos(ang).astype(np.float32).T)
    sinT = np.ascontiguousarray(np.sin(ang).astype(np.float32).T)
    return c, cosT, sinT


class _Stop(Exception):
    pass


SUBSTOP = [0]


DEAD = [False]


def sub(n):
    if SUBSTOP[0] == n:
        DEAD[0] = True


def build(seqs, n_layers=2, do_final=True, stop=0):
    T_tok = sum(seqs)
    NU = T_tok // 256
    NCH = T_tok // 128
    lmax = max(seqs)
    nc = bass.Bass("TRN2", target_bir_lowering=False)
    DEAD[0] = False

    def din(name, shape, dt=F32):
        return nc.dram_tensor(name, list(shape), dt, kind="ExternalInput").ap()

    def dscr(name, shape, dt):
        return nc.dram_tensor(name, list(shape), dt, kind="Internal").ap()

    x_in = din("x", [T_tok, D])
    consts_d = din("consts", [128, 1024])
    cos_d = din("cosT", [128, lmax])
    sin_d = din("sinT", [128, lmax])
    norm_mix_g = din("norm_mix_g", [2, D])
    w_in = din("w_in", [2, D, 6144])
    w_gate = din("w_gate", [2, D, 2048])
    conv_w = din("conv_w", [2, 31, D])
    conv_b = din("conv_b", [2, D])
    conv_ln_g = din("conv_ln_g", [2, D])
    conv_ln_b = din("conv_ln_b", [2, D])
    w_conv_out = din("w_conv_out", [2, D, D])
    lg_f = din("log_gamma_fwd", [2, 4])
    lg_b = din("log_gamma_bwd", [2, 4])
    w_ret_out = din("w_ret_out", [2, D, D])
    w_o = din("w_o", [2, D, D])
    norm_ffn_g = din("norm_ffn_g", [2, D])
    peer_wq = din("peer_wq", [2, D, 2048])
    peer_sk = din("peer_subkeys", [2, 8, 2, 128, 128])
    peer_u = din("peer_u", [2, 16384, D])
    peer_v = din("peer_v", [2, 16384, D])
    final_g = din("final_norm_g", [1, D])
    y_out = nc.dram_tensor("y", [T_tok, D], F32, kind="ExternalOutput").ap()

    X1 = dscr("X1", [T_tok, D], F32)
    XM = dscr("XM", [T_tok, D], F32)
    UT = dscr("UT", [NU, 128, 8 * 256], BF16)
    QT = dscr("QT", [NU, 128, 8 * 256], BF16)
    KT = dscr("KT", [NU, 128, 8 * 256], BF16)
    KK = dscr("KK", [NCH, 128, D], BF16)
    VV = dscr("VV", [NCH, 128, D], BF16)
    SG = dscr("SG", [NCH, 128, D], BF16)
    GT = dscr("GT", [NCH, 128, 2048], BF16)
    YC = dscr("YC", [NCH, 128, D], BF16)
    SB = dscr("SB", [NCH, 128, 8 * 256], BF16)
    UTS = dscr("UTS", [2, 64, 128, 8 * 256], BF16)
    VS = dscr("VS", [2, 64, 128, 2 * 1024], BF16)

    units = []
    u0 = 0
    seq_ranges = []
    for L in seqs:
        nu = L // 256
        seq_ranges.append((u0, u0 + nu))
        for i in range(nu):
            units.append((u0, u0 + nu, i * 256))
        u0 += nu

    dbufs = {}

    def DB(name, idx):
        k = (name, idx)
        if k not in dbufs:
            dbufs[k] = Buf(f"{name}{idx}")
        return dbufs[k]

    with ExitStack() as top:
        S = Sched(nc, top)
        pe, act, dve, pool, sp, sw = S.pe, S.act, S.dve, S.pool, S.sp, S.sw

        uniq = [0]

        def sb(stack, name, shape, dt):
            uniq[0] += 1
            name = f"{name}_{uniq[0]}"
            return TT(stack.enter_context(nc.sbuf_tensor(name, list(shape), dt)), name)

        PB = [TT(top.enter_context(nc.psum_tensor(f"pb{i}", [128, 512], F32)), f"pb{i}") for i in range(8)]

        def pbf(bank):
            return bank.t[:].bitcast(BF16)

        cst = sb(top, "cst", [128, 1024], F32)
        sp.dma([], [cst], cst[:], consts_d[:, :])
        ident_f = cst[:, 0:128]
        identb = sb(top, "identb", [128, 128], BF16)
        iotab = sb(top, "iotab", [128, 128], BF16)
        dve.op([cst], [identb], lambda h: h.tensor_copy(out=identb[:], in_=cst[:, 0:128]))
        dve.op([cst], [iotab], lambda h: h.tensor_copy(out=iotab[:], in_=cst[:, 128:256]))
        iota16 = cst[:, 776:792]

        def mm_group(out_bank, reads, mms, extra_writes=()):
            n = len(mms)

            def fn(h):
                inst = None
                for i, (o, l, r) in enumerate(mms):
                    inst = h.matmul(o, lhsT=l, rhs=r, start=(i == 0), stop=(i == n - 1))
                return inst
            pe.op(reads, [out_bank] + list(extra_writes), fn)

        def transposes(out_banks, reads, trs):
            def fn(h):
                inst = None
                for (o, i_, idn) in trs:
                    inst = h.transpose(o, i_, idn)
                return inst
            pe.op(reads, list(out_banks), fn)

        def norm_T(xin, gtile, xn, xnT, junk, small, bankA, bankB, evac_engs):
            ss, ms, sq, rstd = small
            if isinstance(junk, tuple):
                jap, jb = junk
            else:
                jap, jb = junk[:], junk
            for s in range(2):
                act.op([xin], [jb, ss], lambda h, s=s: h.activation(
                    out=jap, in_=xin[:, s, :], func=AF.Square, accum_out=ss[:, s:s + 1]))
            dve.op([ss], [ms], lambda h: h.tensor_scalar(
                out=ms[:], in0=ss[:], scalar1=1.0 / D, scalar2=EPS, op0=ALU.mult, op1=ALU.add))
            act.op([ms], [sq], lambda h: h.activation(out=sq[:], in_=ms[:], func=AF.Sqrt))
            dve.op([sq], [rstd], lambda h: h.reciprocal(out=rstd[:], in_=sq[:]))
            for s in range(2):
                dve.op([xin, rstd, gtile], [xn], lambda h, s=s: h.scalar_tensor_tensor(
                    out=xn[:, s, :], in0=xin[:, s, :], scalar=rstd[:, s:s + 1], in1=gtile[:],
                    op0=ALU.mult, op1=ALU.mult))
            trs = []
            for kc in range(8):
                bank = bankA if kc < 4 else bankB
                v = pbf(bank)
                for s in range(2):
                    o = v[:, (kc % 4) * 256 + s * 128:(kc % 4) * 256 + (s + 1) * 128]
                    trs.append((o, xn[:, s, kc * 128:(kc + 1) * 128], identb[:]))
            transposes([bankA, bankB], [xn, identb], trs)
            e0, e1 = evac_engs
            e0.op([bankA], [xnT], lambda h: _copy(h, xnT[:, 0:4, :], pbf(bankA).rearrange("p (a t) -> p a t", a=4)))
            e1.op([bankB], [xnT], lambda h: _copy(h, xnT[:, 4:8, :], pbf(bankB).rearrange("p (a t) -> p a t", a=4)))

        def _copy(h, out, in_):
            if h is nc.scalar:
                return h.activation(out=out, in_=in_, func=AF.Copy)
            return h.tensor_copy(out=out, in_=in_)

        def load_weight_bf16(dst, src3, nk, ncols, col0=0):
            for kc in range(nk):
                for c0 in range(0, ncols, 2048):
                    c1 = min(ncols, c0 + 2048)
                    sw.dma([], [dst], dst[:, kc, c0:c1], src3[kc * 128:(kc + 1) * 128, col0 + c0:col0 + c1])

        def bcast_row(dst, row_ap):
            sp.dma([], [dst], dst[:], row_ap.partition_broadcast(128))

        phase_ctr = [0]

        def chk():
            phase_ctr[0] += 1
            if stop and phase_ctr[0] > stop:
                DEAD[0] = True

        with ExitStack() as ph:
            uf = [sb(ph, f"uf{i}", [128, 4, D], F32) for i in range(2)]
            uts = [sb(ph, f"uts{i}", [128, 8, 512], BF16) for i in range(2)]
            vb = [sb(ph, f"vb{i}", [128, 4, D], BF16) for i in range(2)]
            it = 0
            for l in range(n_layers):
                for g in range(32):
                    a = it % 2
                    it += 1
                    rows = peer_u[l, g * 512:(g + 1) * 512, :].rearrange("(c p) d -> p c d", p=128)
                    sp.dma([], [uf[a]], uf[a][:], rows)
                    for kc in range(8):
                        bank = PB[kc]
                        trs = [(bank[:, c * 128:(c + 1) * 128], uf[a][:, c, kc * 128:(kc + 1) * 128], ident_f)
                               for c in range(4)]
                        transposes([bank], [uf[a], cst], trs)
                        e = act if kc % 2 == 0 else dve
                        e.op([bank], [uts[a]], lambda h, kc=kc, bank=bank, a=a: _copy(h, uts[a][:, kc, :], bank[:, :]))
                    for hf in range(2):
                        sp.dma([uts[a]], [DB("UTS", (l, 2 * g + hf))], UTS[l, 2 * g + hf].rearrange("p (a e) -> p a e", a=8),
                               uts[a][:, :, hf * 256:(hf + 1) * 256])
                    vrows = peer_v[l, g * 512:(g + 1) * 512, :].rearrange("(c p) d -> p c d", p=128)
                    sw.dma([], [vb[a]], vb[a][:], vrows)
                    for hf in range(2):
                        sp.dma([vb[a]], [DB("VS", (l, 2 * g + hf))], VS[l, 2 * g + hf].rearrange("p (a e) -> p a e", a=2),
                               vb[a][:, 2 * hf:2 * hf + 2, :])
        S.barrier()

        try:
          for l in range(n_layers):
              chk()
              x_src = x_in if l == 0 else X1
              last = (l == n_layers - 1)
              x_dst = y_out if last else X1

              def xrows(src, u):
                  return src[u * 256:(u + 1) * 256, :].rearrange("(s p) d -> p s d", p=128)

              lay = top.enter_context(ExitStack())
              lgfb = sb(lay, "lgfb", [128, 4], F32)
              lgbb = sb(lay, "lgbb", [128, 4], F32)
              bcast_row(lgfb, lg_f[l:l + 1, :])
              bcast_row(lgbb, lg_b[l:l + 1, :])
              decT = sb(lay, "decT", [128, 4, 128], F32)
              dtmp = sb(lay, "dtmp", [128, 2, 128], F32)
              for hd in range(4):
                  dve.op([cst, lgfb], [dtmp], lambda h, hd=hd: h.tensor_scalar(
                      out=dtmp[:, 0, :], in0=cst[:, 256:384], scalar1=lgfb[:, hd:hd + 1], scalar2=None, op0=ALU.mult))
                  dve.op([cst, lgbb], [dtmp], lambda h, hd=hd: h.tensor_scalar(
                      out=dtmp[:, 1, :], in0=cst[:, 384:512], scalar1=lgbb[:, hd:hd + 1], scalar2=None, op0=ALU.mult))
                  act.op([dtmp], [dtmp], lambda h: h.activation(out=dtmp[:], in_=dtmp[:], func=AF.Exp))
                  dve.op([dtmp, cst], [dtmp], lambda h: h.tensor_tensor(
                      out=dtmp[:, 0, :], in0=dtmp[:, 0, :], in1=cst[:, 512:640], op=ALU.mult))
                  dve.op([dtmp, cst], [dtmp], lambda h: h.tensor_tensor(
                      out=dtmp[:, 1, :], in0=dtmp[:, 1, :], in1=cst[:, 640:768], op=ALU.mult))
                  dve.op([dtmp], [decT], lambda h, hd=hd: h.tensor_tensor(
                      out=decT[:, hd, :], in0=dtmp[:, 0, :], in1=dtmp[:, 1, :], op=ALU.add))
              dsc = sb(lay, "dsc", [128, 6, 4], F32)
              specs = [(lgfb, 768, 0.0), (lgbb, 769, 0.0), (lgfb, 770, -np.log(16.0)), (lgbb, 771, -np.log(16.0)),
                       (lgfb, 772, 0.0), (lgbb, 772, 0.0)]
              for i, (lgt, col, bias) in enumerate(specs):
                  dve.op([cst, lgt], [dsc], lambda h, i=i, lgt=lgt, col=col: h.tensor_scalar(
                      out=dsc[:, i, :], in0=lgt[:], scalar1=cst[:, col:col + 1], scalar2=None, op0=ALU.mult))
              dve.op([dsc], [dsc], lambda h: h.tensor_scalar(
                  out=dsc[:, 2:4, :], in0=dsc[:, 2:4, :], scalar1=float(-np.log(16.0)), scalar2=None, op0=ALU.add))
              act.op([dsc], [dsc], lambda h: h.activation(out=dsc[:], in_=dsc[:], func=AF.Exp))
              A_F, A_B, VD_F, VD_B, CD_F, CD_B = range(6)
              if SUBSTOP[0] == 50:
                  sp.dma([dsc], [DB("Y", -1)], y_out[0:128, 0:24], dsc[:].rearrange("p a b -> p (a b)"))
                  sp.dma([decT], [DB("Y", -2)], y_out[0:128, 32:32 + 512], decT[:].rearrange("p a b -> p (a b)"))
                  sp.dma([lgfb], [DB("Y", -3)], y_out[0:128, 600:604], lgfb[:])
                  sp.dma([lgbb], [DB("Y", -4)], y_out[0:128, 608:612], lgbb[:])
                  sub(50)

              chk()
              with ExitStack() as ph:
                  wsb = sb(ph, "w1a", [128, 8, 4096], BF16)
                  load_weight_bf16(wsb, w_in[l], 8, 4096, 0)
                  gtile = sb(ph, "g1a", [128, D], F32)
                  bcast_row(gtile, norm_mix_g[l:l + 1, :])
                  xin = [sb(ph, f"xin{i}", [128, 2, D], F32) for i in range(2)]
                  cs = [sb(ph, f"cs{i}", [128, 2, 256], F32) for i in range(2)]
                  xn = sb(ph, "xn", [128, 2, D], BF16)
                  xnT = sb(ph, "xnT", [128, 8, 256], BF16)
                  junk = sb(ph, "junk", [128, D], BF16)
                  small = [sb(ph, f"sm{i}", [128, 2], F32) for i in range(4)]
                  uT = [sb(ph, f"uT{i}", [128, 8, 256], BF16) for i in range(2)]
                  qT = [sb(ph, f"qT{i}", [128, 8, 256], BF16) for i in range(2)]
                  kT = [sb(ph, f"kT{i}", [128, 8, 256], BF16) for i in range(2)]
                  ktok = [sb(ph, f"ktok{i}", [128, 2, D], BF16) for i in range(2)]
                  sig = [sb(ph, f"sig{i}", [128, 256], F32) for i in range(2)]
                  rt = [sb(ph, f"rt{i}", [128, 4, 256], F32) for i in range(2)]
                  kraw = [sb(ph, f"kraw{i}", [128, 2, 256], F32) for i in range(2)]

                  def p1a_load(u):
                      a = u % 2
                      sp.dma([DB("X", u)], [xin[a]], xin[a][:], xrows(x_src, u))
                      pos = units[u][2]
                      sp.dma([], [cs[a]], cs[a][:, 0, :], cos_d[:, pos:pos + 256])
                      sp.dma([], [cs[a]], cs[a][:, 1, :], sin_d[:, pos:pos + 256])

                  p1a_load(0)
                  sub(1)
                  pair_i = 0
                  for u in range(NU):
                      a = u % 2
                      if u + 1 < NU:
                          p1a_load(u + 1)
                      sub(20 + u)
                      sub(2)
                      norm_T(xin[a], gtile, xn, xnT, junk, small, PB[6], PB[7], (act, dve))
                      sub(3)
                      cosA = cs[a][:, 0, :]
                      sinA = cs[a][:, 1, :]
                      for pi in range(16):
                          bank = PB[pair_i % 6]
                          pair_i += 1
                          if pi < 8:
                              c1, c2 = pi, 8 + pi
                          elif pi < 12:
                              c1, c2 = 16 + 2 * (pi - 8), 17 + 2 * (pi - 8)
                          else:
                              c1, c2 = 24 + 2 * (pi - 12), 25 + 2 * (pi - 12)
                          for half, cc in enumerate((c1, c2)):
                              mms = [(bank[:, half * 256:(half + 1) * 256], wsb[:, kc, cc * 128:(cc + 1) * 128],
                                      xnT[:, kc, :]) for kc in range(8)]
                              mm_group(bank, [wsb, xnT], mms)
                          p1 = bank[:, 0:256]
                          p2 = bank[:, 256:512]
                          if pi < 8:
                              sg_ = sig[pi % 2]
                              act.op([bank], [sg_], lambda h, sg_=sg_, p2=p2: h.activation(
                                  out=sg_[:], in_=p2, func=AF.Sigmoid))
                              dve.op([bank, sg_], [uT[a]], lambda h, sg_=sg_, p1=p1, pi=pi: h.tensor_tensor(
                                  out=uT[a][:, pi, :], in0=p1, in1=sg_[:], op=ALU.mult))
                          elif pi < 12:
                              hd = pi - 8
                              r = rt[pi % 2]
                              dst = qT[a]
                              dve.op([bank, cs[a]], [r], lambda h, r=r: h.tensor_tensor(out=r[:, 0, :], in0=p1, in1=cosA, op=ALU.mult))
                              dve.op([bank, cs[a]], [r], lambda h, r=r: h.tensor_tensor(out=r[:, 1, :], in0=p2, in1=sinA, op=ALU.mult))
                              dve.op([bank, cs[a]], [r], lambda h, r=r: h.tensor_tensor(out=r[:, 2, :], in0=p1, in1=sinA, op=ALU.mult))
                              dve.op([bank, cs[a]], [r], lambda h, r=r: h.tensor_tensor(out=r[:, 3, :], in0=p2, in1=cosA, op=ALU.mult))
                              pool.op([r], [dst], lambda h, r=r, dst=dst, hd=hd: h.tensor_tensor(
                                  out=dst[:, 2 * hd, :], in0=r[:, 0, :], in1=r[:, 1, :], op=ALU.subtract))
                              pool.op([r], [dst], lambda h, r=r, dst=dst, hd=hd: h.tensor_tensor(
                                  out=dst[:, 2 * hd + 1, :], in0=r[:, 2, :], in1=r[:, 3, :], op=ALU.add))
                          else:
                              hd = pi - 12
                              r = rt[pi % 2]
                              kr = kraw[pi % 2]
                              dst = kT[a]
                              act.op([bank], [kr], lambda h, kr=kr, bank=bank: h.activation(
                                  out=kr[:], in_=bank[:, :].rearrange("p (a t) -> p a t", a=2), func=AF.Copy))
                              pool.op([kr, cs[a]], [r], lambda h, r=r, kr=kr: h.tensor_tensor(out=r[:, 0, :], in0=kr[:, 0, :], in1=cosA, op=ALU.mult))
                              pool.op([kr, cs[a]], [r], lambda h, r=r, kr=kr: h.tensor_tensor(out=r[:, 1, :], in0=kr[:, 1, :], in1=sinA, op=ALU.mult))
                              pool.op([kr, cs[a]], [r], lambda h, r=r, kr=kr: h.tensor_tensor(out=r[:, 2, :], in0=kr[:, 0, :], in1=sinA, op=ALU.mult))
                              pool.op([kr, cs[a]], [r], lambda h, r=r, kr=kr: h.tensor_tensor(out=r[:, 3, :], in0=kr[:, 1, :], in1=cosA, op=ALU.mult))
                              pool.op([r], [dst], lambda h, r=r, dst=dst, hd=hd: h.tensor_tensor(
                                  out=dst[:, 2 * hd, :], in0=r[:, 0, :], in1=r[:, 1, :], op=ALU.subtract))
                              pool.op([r], [dst], lambda h, r=r, dst=dst, hd=hd: h.tensor_tensor(
                                  out=dst[:, 2 * hd + 1, :], in0=r[:, 2, :], in1=r[:, 3, :], op=ALU.add))
                      sub(4)
                      for s in range(2):
                          bank = PB[6 + s]
                          v = pbf(bank)
                          trs = [(v[:, cc * 128:(cc + 1) * 128], kT[a][:, cc, s * 128:(s + 1) * 128], identb[:])
                                 for cc in range(8)]
                          transposes([bank], [kT[a], identb], trs)
                          e = act if s == 0 else dve
                          e.op([bank], [ktok[a]], lambda h, s=s, bank=bank: _copy(h, ktok[a][:, s, :], pbf(bank)))
                      sub(5)
                      flat = lambda t: t[:].rearrange("p a t -> p (a t)")
                      sw.dma([uT[a]], [DB("UT", u)], UT[u], flat(uT[a]))
                      sw.dma([qT[a]], [DB("QT", u)], QT[u], flat(qT[a]))
                      sw.dma([kT[a]], [DB("KT", u)], KT[u], flat(kT[a]))
                      sub(6)
                      for s in range(2):
                          sw.dma([ktok[a]], [DB("KK", 2 * u + s)], KK[2 * u + s], ktok[a][:, s, :])
                      sub(7)
                      sub(10 + u)
              S.barrier()

              chk()
              with ExitStack() as ph:
                  wv = sb(ph, "w1b", [128, 8, 2048], BF16)
                  load_weight_bf16(wv, w_in[l], 8, 2048, 4096)
                  wg = sb(ph, "wg", [128, 8, 2048], BF16)
                  load_weight_bf16(wg, w_gate[l], 8, 2048, 0)
                  gtile = sb(ph, "g1b", [128, D], F32)
                  bcast_row(gtile, norm_mix_g[l:l + 1, :])
                  xin = [sb(ph, f"xin{i}", [128, 2, D], F32) for i in range(2)]
                  kk = [sb(ph, f"kk{i}", [128, 2, D], BF16) for i in range(2)]
                  xn = sb(ph, "xn", [128, 2, D], BF16)
                  xnT = sb(ph, "xnT", [128, 8, 256], BF16)
                  junk = sb(ph, "junk", [128, D], BF16)
                  small = [sb(ph, f"sm{i}", [128, 2], F32) for i in range(4)]
                  vsb = [sb(ph, f"vsb{i}", [128, 2, D], BF16) for i in range(2)]
                  vdb = [sb(ph, f"vdb{i}", [128, 2, D], BF16) for i in range(2)]
                  sgb = [sb(ph, f"sgb{i}", [128, 2, D], BF16) for i in range(2)]
                  gtb = [sb(ph, f"gtb{i}", [128, 2, 2048], BF16) for i in range(2)]
                  Sst = sb(ph, "Sst", [128, 8, 256], F32)
                  Sbf = [sb(ph, f"Sbf{i}", [128, 8, 256], BF16) for i in range(2)]
                  order = []
                  for (a0, a1) in seq_ranges:
                      order += list(range(a1 - 1, a0 - 1, -1))

                  def p1b_load(i):
                      u = order[i]
                      a = i % 2
                      sp.dma([DB("X", u)], [xin[a]], xin[a][:], xrows(x_src, u))
                      for s in range(2):
                          sp.dma([DB("KK", 2 * u + s)], [kk[a]], kk[a][:, s, :], KK[2 * u + s])

                  p1b_load(0)
                  bi = 0
                  sbi = 0
                  for i, u in enumerate(order):
                      a = i % 2
                      if i + 1 < NU:
                          p1b_load(i + 1)
                      sub(30)
                      norm_T(xin[a], gtile, xn, xnT, junk, small, PB[6], PB[7], (act, dve))
                      sub(31)
                      for s in range(2):
                          for blk in range(4):
                              bank = PB[bi % 4]
                              bi += 1
                              mms = [(bank[:, :], xnT[:, kc, s * 128:(s + 1) * 128], wv[:, kc, blk * 512:(blk + 1) * 512])
                                     for kc in range(8)]
                              mm_group(bank, [xnT, wv], mms)
                              sub(40)
                              if blk < 2:
                                  act.op([bank], [vsb[a]], lambda h, s=s, blk=blk, bank=bank: h.activation(
                                      out=vsb[a][:, s, blk * 512:(blk + 1) * 512], in_=bank[:, :], func=AF.Copy))
                                  sub(41)
                                  for hh in range(2):
                                      hd = blk * 2 + hh
                                      act.op([bank, dsc], [vdb[a]], lambda h, s=s, hd=hd, hh=hh, bank=bank: h.activation(
                                          out=vdb[a][:, s, hd * 256:(hd + 1) * 256], in_=bank[:, hh * 256:(hh + 1) * 256],
                                          func=AF.Copy, scale=dsc[:, VD_B, hd:hd + 1]))
                                      sub(42)
                              else:
                                  act.op([bank], [sgb[a]], lambda h, s=s, blk=blk, bank=bank: h.activation(
                                      out=sgb[a][:, s, (blk - 2) * 512:(blk - 1) * 512], in_=bank[:, :], func=AF.Silu))
                          for blk in range(4):
                              bank = PB[bi % 4]
                              bi += 1
                              mms = [(bank[:, :], xnT[:, kc, s * 128:(s + 1) * 128], wg[:, kc, blk * 512:(blk + 1) * 512])
                                     for kc in range(8)]
                              mm_group(bank, [xnT, wg], mms)
                              act.op([bank], [gtb[a]], lambda h, s=s, blk=blk, bank=bank: h.activation(
                                  out=gtb[a][:, s, blk * 512:(blk + 1) * 512], in_=bank[:, :], func=AF.Sigmoid))
                      sub(32)
                      for s in range(2):
                          c = 2 * u + s
                          sw.dma([vsb[a]], [DB("VV", c)], VV[c], vsb[a][:, s, :])
                          sw.dma([sgb[a]], [DB("SG", c)], SG[c], sgb[a][:, s, :])
                          sw.dma([gtb[a]], [DB("GT", c)], GT[c], gtb[a][:, s, :])
                      sub(33)
                      if u == units[u][1] - 1:
                          pool.op([], [Sst], lambda h: h.memset(Sst[:], 0.0))
                          pool.op([], [Sbf[sbi % 2]], lambda h, t=Sbf[sbi % 2]: h.memset(t[:], 0.0))
                      for s in (1, 0):
                          c = 2 * u + s
                          cur = Sbf[sbi % 2]
                          nxt = Sbf[(sbi + 1) % 2]
                          sbi += 1
                          sw.dma([cur], [DB("SB", c)], SB[c], cur[:].rearrange("p a e -> p (a e)"))
                          for hp in range(2):
                              for hh in range(2):
                                  hd = hp * 2 + hh
                                  bank = PB[4 + hh]
                                  for dc in range(2):
                                      mm_group(bank, [kk[a], vdb[a]], [(bank[:, dc * 256:(dc + 1) * 256],
                                               kk[a][:, s, (2 * hd + dc) * 128:(2 * hd + dc + 1) * 128],
                                               vdb[a][:, s, hd * 256:(hd + 1) * 256])])
                                  dve.op([bank, dsc, Sst], [Sst], lambda h, hd=hd, bank=bank: h.scalar_tensor_tensor(
                                      out=Sst[:, 2 * hd:2 * hd + 2, :], in0=Sst[:, 2 * hd:2 * hd + 2, :],
                                      scalar=dsc[:, CD_B, hd:hd + 1],
                                      in1=bank[:, :].rearrange("p (a e) -> p a e", a=2), op0=ALU.mult, op1=ALU.add))
                          pool.op([Sst], [nxt], lambda h, nxt=nxt: h.tensor_copy(out=nxt[:], in_=Sst[:]))
                          sub(34)
              S.barrier()

              chk()
              with ExitStack() as ph:
                  wco = sb(ph, "wco", [128, 8, D], BF16)
                  load_weight_bf16(wco, w_conv_out[l], 8, D, 0)
                  praw = sb(ph, "praw", [34, D], F32)
                  sp.dma([], [praw], praw[0:31, :], conv_w[l])
                  sp.dma([], [praw], praw[31:32, :], conv_b[l:l + 1, :])
                  sp.dma([], [praw], praw[32:33, :], conv_ln_g[l:l + 1, :])
                  sp.dma([], [praw], praw[33:34, :], conv_ln_b[l:l + 1, :])
                  cpar = sb(ph, "cpar", [128, 8, 34], F32)
                  for cc in range(8):
                      bank = PB[cc]
                      transposes([bank], [praw, cst], [(bank[:, 0:34], praw[:, cc * 128:(cc + 1) * 128], cst[0:34, 0:34])])
                      dve.op([bank], [cpar], lambda h, cc=cc, bank=bank: h.tensor_copy(out=cpar[:, cc, :], in_=bank[:, 0:34]))
                  diag = sb(ph, "diag", [128, 8, 31, 128], BF16)
                  for cc in range(8):
                      for k in range(31):
                          e = dve if (k % 2 == 0) else pool
                          e.op([cst, cpar], [diag], lambda h, cc=cc, k=k: h.tensor_scalar(
                              out=diag[:, cc, k, :], in0=cst[:, 0:128], scalar1=cpar[:, cc, k:k + 1], scalar2=None,
                              op0=ALU.mult))
                  uh = [sb(ph, f"uh{i}", [128, 8, 286], BF16) for i in range(2)]
                  uhl = [Buf(f"uhl{i}") for i in range(2)]
                  uhr = [Buf(f"uhr{i}") for i in range(2)]
                  cv = sb(ph, "cv", [128, 8, 256], F32)
                  sq2 = sb(ph, "sq2", [128, 8, 256], F32)
                  mean = sb(ph, "mean", [128, 256], F32)
                  m2 = sb(ph, "m2", [128, 256], F32)
                  rstd = sb(ph, "rstdc", [128, 256], F32)
                  uln = [sb(ph, f"uln{i}", [128, 8, 256], BF16) for i in range(2)]
                  ycb = [sb(ph, f"ycb{i}", [128, 2, D], BF16) for i in range(2)]
                  onesm = cst[:, 800:928]

                  def p2c_load(u):
                      a = u % 2
                      u_lo, u_hi, _ = units[u]
                      UTu = lambda uu: UT[uu].rearrange("p (a t) -> p a t", a=8)
                      sp.dma([DB("UT", u)], [uh[a]], uh[a][:, :, 15:271], UTu(u))
                      if u > u_lo:
                          sp.dma([DB("UT", u - 1)], [uhl[a]], uh[a][:, :, 0:15], UTu(u - 1)[:, :, 241:256])
                      else:
                          pool.op([], [uhl[a]], lambda h: h.memset(uh[a][:, :, 0:15], 0.0))
                      if u + 1 < u_hi:
                          sp.dma([DB("UT", u + 1)], [uhr[a]], uh[a][:, :, 271:286], UTu(u + 1)[:, :, 0:15])
                      else:
                          pool.op([], [uhr[a]], lambda h: h.memset(uh[a][:, :, 271:286], 0.0))

                  p2c_load(0)
                  for u in range(NU):
                      a = u % 2
                      if u + 1 < NU:
                          p2c_load(u + 1)
                      for cc in range(8):
                          bank = PB[cc // 2]
                          half = cc % 2
                          mms = [(bank[:, half * 256:(half + 1) * 256], diag[:, cc, k, :], uh[a][:, cc, k:k + 256])
                                 for k in range(31)]
                          mm_group(bank, [diag, uh[a], uhl[a], uhr[a]], mms)
                          act.op([bank, cpar], [cv], lambda h, cc=cc, bank=bank, half=half: h.activation(
                              out=cv[:, cc, :], in_=bank[:, half * 256:(half + 1) * 256], func=AF.Identity,
                              bias=cpar[:, cc, 31:32]))
                          act.op([cv], [sq2], lambda h, cc=cc: h.activation(out=sq2[:, cc, :], in_=cv[:, cc, :], func=AF.Square))
                      bm = PB[4]
                      mm_group(bm, [cst, cv], [(bm[:, 0:256], onesm, cv[:, cc, :]) for cc in range(8)])
                      mm_group(bm, [cst, sq2], [(bm[:, 256:512], onesm, sq2[:, cc, :]) for cc in range(8)])
                      act.op([bm], [mean], lambda h: h.activation(out=mean[:], in_=bm[:, 0:256], func=AF.Copy))
                      act.op([bm], [m2], lambda h: h.activation(out=m2[:], in_=bm[:, 0:256], func=AF.Square))
                      dve.op([bm, m2], [m2], lambda h: h.tensor_tensor(out=m2[:], in0=bm[:, 256:512], in1=m2[:], op=ALU.subtract))
                      dve.op([m2], [m2], lambda h: h.tensor_scalar(out=m2[:], in0=m2[:], scalar1=EPS, scalar2=None, op0=ALU.add))
                      act.op([m2], [m2], lambda h: h.activation(out=m2[:], in_=m2[:], func=AF.Sqrt))
                      dve.op([m2], [rstd], lambda h: h.reciprocal(out=rstd[:], in_=m2[:]))
                      bc = lambda t: t[:].unsqueeze(1).to_broadcast([128, 8, 256])
                      pool.op([cv, mean], [cv], lambda h: h.tensor_tensor(out=cv[:], in0=cv[:], in1=bc(mean), op=ALU.subtract))
                      dve.op([cv, rstd], [cv], lambda h: h.tensor_tensor(out=cv[:], in0=cv[:], in1=bc(rstd), op=ALU.mult))
                      for cc in range(8):
                          act.op([cv, cpar], [uln[a]], lambda h, cc=cc: h.activation(
                              out=uln[a][:, cc, :], in_=cv[:, cc, :], func=AF.Silu,
                              scale=cpar[:, cc, 32:33], bias=cpar[:, cc, 33:34]))
                      for s in range(2):
                          for blk in range(2):
                              bank = PB[5 + (s * 2 + blk) % 3]
                              mms = [(bank[:, :], uln[a][:, cc, s * 128:(s + 1) * 128], wco[:, cc, blk * 512:(blk + 1) * 512])
                                     for cc in range(8)]
                              mm_group(bank, [uln[a], wco], mms)
                              dve.op([bank], [ycb[a]], lambda h, s=s, blk=blk, bank=bank: h.tensor_copy(
                                  out=ycb[a][:, s, blk * 512:(blk + 1) * 512], in_=bank[:, :]))
                          sw.dma([ycb[a]], [DB("YC", 2 * u + s)], YC[2 * u + s], ycb[a][:, s, :])
              S.barrier()

              chk()
              with ExitStack() as ph:
                  wro = sb(ph, "wro", [128, 8, D], BF16)
                  load_weight_bf16(wro, w_ret_out[l], 8, D, 0)
                  wo = sb(ph, "wo", [128, 8, D], BF16)
                  load_weight_bf16(wo, w_o[l], 8, D, 0)
                  xin = [sb(ph, f"xin{i}", [128, 2, D], F32) for i in range(2)]
                  qTl = [sb(ph, f"qTl{i}", [128, 8, 256], BF16) for i in range(2)]
                  kTl = [sb(ph, f"kTl{i}", [128, 8, 256], BF16) for i in range(2)]
                  kk = [sb(ph, f"kk{i}", [128, 2, D], BF16) for i in range(2)]
                  vv = [sb(ph, f"vv{i}", [128, 2, D], BF16) for i in range(2)]
                  sgl = [sb(ph, f"sgl{i}", [128, 2, D], BF16) for i in range(2)]
                  gtl = [sb(ph, f"gtl{i}", [128, 2, 2048], BF16) for i in range(2)]
                  ycl = [sb(ph, f"ycl{i}", [128, 2, D], BF16) for i in range(2)]
                  sbw = [sb(ph, f"sbw{i}", [128, 2, 8 * 256], BF16) for i in range(2)]
                  Sst = sb(ph, "Sstf", [128, 8, 256], F32)
                  Sbf = sb(ph, "Sbff", [128, 8, 256], BF16)
                  PT = [sb(ph, f"PT{i}", [128, 128], BF16) for i in range(2)]
                  tf = [sb(ph, f"tf{i}", [128, 256], F32) for i in range(2)]
                  o_sb = sb(ph, "o_sb", [128, D], F32)
                  vdf = sb(ph, "vdf", [128, D], BF16)
                  gn = [sb(ph, f"gn{i}", [128, 4], F32) for i in range(4)]
                  junk = sb(ph, "junk", [128, 256], BF16)
                  om = sb(ph, "om", [128, D], BF16)
                  omT = sb(ph, "omT", [128, 8, 256], BF16)
                  m1 = sb(ph, "m1", [128, D], F32)
                  mg = sb(ph, "mg", [128, D], BF16)
                  mgT = sb(ph, "mgT", [128, 8, 256], BF16)
                  xo = [sb(ph, f"xo{i}", [128, 2, D], F32) for i in range(2)]

                  def p3a_load(u):
                      a = u % 2
                      sp.dma([DB("X", u)], [xin[a]], xin[a][:], xrows(x_src, u))
                      sp.dma([DB("QT", u)], [qTl[a]], qTl[a][:].rearrange("p a t -> p (a t)"), QT[u])
                      sp.dma([DB("KT", u)], [kTl[a]], kTl[a][:].rearrange("p a t -> p (a t)"), KT[u])
                      for s in range(2):
                          c = 2 * u + s
                          sp.dma([DB("KK", c)], [kk[a]], kk[a][:, s, :], KK[c])
                          sp.dma([DB("VV", c)], [vv[a]], vv[a][:, s, :], VV[c])
                          sp.dma([DB("SG", c)], [sgl[a]], sgl[a][:, s, :], SG[c])
                          sp.dma([DB("GT", c)], [gtl[a]], gtl[a][:, s, :], GT[c])
                          sp.dma([DB("YC", c)], [ycl[a]], ycl[a][:, s, :], YC[c])
                          sp.dma([DB("SB", c)], [sbw[a]], sbw[a][:, s, :], SB[c])

                  p3a_load(0)
                  hi = 0
                  for u in range(NU):
                      a = u % 2
                      if u + 1 < NU:
                          p3a_load(u + 1)
                      if u == units[u][0]:
                          pool.op([], [Sst], lambda h: h.memset(Sst[:], 0.0))
                          pool.op([], [Sbf], lambda h: h.memset(Sbf[:], 0.0))
                      for s in range(2):
                          ts = slice(s * 128, (s + 1) * 128)
                          sbv = sbw[a][:, s, :].rearrange("p (a e) -> p a e", a=8)
                          for hd in range(4):
                              bA = PB[hi % 2]
                              bO = PB[2 + hi % 2]
                              bC = PB[4 + hi % 2]
                              pt = PT[hi % 2]
                              t_f = tf[hi % 2]
                              hi += 1
                              mm_group(bA, [kTl[a], qTl[a]], [(bA[:, 0:128], kTl[a][:, 2 * hd + dc, ts], qTl[a][:, 2 * hd + dc, ts])
                                                             for dc in range(2)])
                              dve.op([bA, decT], [pt], lambda h, hd=hd, bA=bA, pt=pt: h.tensor_tensor(
                                  out=pt[:], in0=bA[:, 0:128], in1=decT[:, hd, :], op=ALU.mult))
                              mm_group(bO, [pt, vv[a]], [(bO[:, 0:256], pt[:], vv[a][:, s, hd * 256:(hd + 1) * 256])])
                              mm_group(bO, [qTl[a], Sbf], [(bO[:, 256:512], qTl[a][:, 2 * hd + dc, ts], Sbf[:, 2 * hd + dc, :])
                                                          for dc in range(2)])
                              mm_group(bC, [qTl[a], sbw[a]], [(bC[:, 0:256], qTl[a][:, 2 * hd + dc, ts], sbv[:, 2 * hd + dc, :])
                                                             for dc in range(2)])
                              act.op([bO, dsc], [t_f], lambda h, hd=hd, bO=bO, t_f=t_f: h.activation(
                                  out=t_f[:], in_=bO[:, 256:512], func=AF.Copy, scale=dsc[:, A_F, hd:hd + 1]))
                              dve.op([bC, dsc, t_f], [t_f], lambda h, hd=hd, bC=bC, t_f=t_f: h.scalar_tensor_tensor(
                                  out=t_f[:], in0=bC[:, 0:256], scalar=dsc[:, A_B, hd:hd + 1], in1=t_f[:],
                                  op0=ALU.mult, op1=ALU.add))
                              dve.op([bO, t_f], [o_sb], lambda h, hd=hd, bO=bO, t_f=t_f: h.tensor_tensor(
                                  out=o_sb[:, hd * 256:(hd + 1) * 256], in0=bO[:, 0:256], in1=t_f[:], op=ALU.add))
                          for hd in range(4):
                              pool.op([vv[a], dsc], [vdf], lambda h, hd=hd, s=s: h.tensor_scalar(
                                  out=vdf[:, hd * 256:(hd + 1) * 256], in0=vv[a][:, s, hd * 256:(hd + 1) * 256],
                                  scalar1=dsc[:, VD_F, hd:hd + 1], scalar2=None, op0=ALU.mult))
                          for hd in range(4):
                              bank = PB[6 + hd % 2]
                              for dc in range(2):
                                  mm_group(bank, [kk[a], vdf], [(bank[:, dc * 256:(dc + 1) * 256],
                                           kk[a][:, s, (2 * hd + dc) * 128:(2 * hd + dc + 1) * 128],
                                           vdf[:, hd * 256:(hd + 1) * 256])])
                              dve.op([bank, dsc, Sst], [Sst], lambda h, hd=hd, bank=bank: h.scalar_tensor_tensor(
                                  out=Sst[:, 2 * hd:2 * hd + 2, :], in0=Sst[:, 2 * hd:2 * hd + 2, :],
                                  scalar=dsc[:, CD_F, hd:hd + 1],
                                  in1=bank[:, :].rearrange("p (a e) -> p a e", a=2), op0=ALU.mult, op1=ALU.add))
                          pool.op([Sst], [Sbf], lambda h: h.tensor_copy(out=Sbf[:], in_=Sst[:]))
                          for hd in range(4):
                              act.op([o_sb], [junk, gn[0]], lambda h, hd=hd: h.activation(
                                  out=junk[:], in_=o_sb[:, hd * 256:(hd + 1) * 256], func=AF.Square,
                                  accum_out=gn[0][:, hd:hd + 1]))
                          dve.op([gn[0]], [gn[1]], lambda h: h.tensor_scalar(
                              out=gn[1][:], in0=gn[0][:], scalar1=1.0 / 256.0, scalar2=EPS, op0=ALU.mult, op1=ALU.add))
                          act.op([gn[1]], [gn[2]], lambda h: h.activation(out=gn[2][:], in_=gn[1][:], func=AF.Sqrt))
                          dve.op([gn[2]], [gn[3]], lambda h: h.reciprocal(out=gn[3][:], in_=gn[2][:]))
                          for hd in range(4):
                              dve.op([o_sb, gn[3], sgl[a]], [om], lambda h, hd=hd, s=s: h.scalar_tensor_tensor(
                                  out=om[:, hd * 256:(hd + 1) * 256], in0=o_sb[:, hd * 256:(hd + 1) * 256],
                                  scalar=gn[3][:, hd:hd + 1], in1=sgl[a][:, s, hd * 256:(hd + 1) * 256],
                                  op0=ALU.mult, op1=ALU.mult))
                          bT = PB[6 + s]
                          v = pbf(bT)
                          transposes([bT], [om, identb], [(v[:, cc * 128:(cc + 1) * 128], om[:, cc * 128:(cc + 1) * 128], identb[:])
                                                          for cc in range(8)])
                          act.op([bT], [omT], lambda h, s=s, bT=bT: h.activation(
                              out=omT[:, :, s * 128:(s + 1) * 128], in_=pbf(bT).rearrange("p (a t) -> p a t", a=8), func=AF.Copy))
                          for blk in range(2):
                              bank = PB[blk]
                              mm_group(bank, [omT, wro], [(bank[:, :], omT[:, cc, ts], wro[:, cc, blk * 512:(blk + 1) * 512])
                                                         for cc in range(8)])
                              cs_ = slice(blk * 512, (blk + 1) * 512)
                              pool.op([ycl[a], gtl[a]], [m1], lambda h, s=s, cs_=cs_: h.tensor_tensor(
                                  out=m1[:, cs_], in0=ycl[a][:, s, cs_], in1=gtl[a][:, s, cs_], op=ALU.mult))
                              dve.op([bank, gtl[a], m1], [mg], lambda h, s=s, blk=blk, bank=bank, cs_=cs_: h.tensor_tensor(
                                  out=mg[:, cs_], in0=bank[:, :], in1=gtl[a][:, s, 1024 + blk * 512:1024 + (blk + 1) * 512],
                                  op=ALU.mult))
                              pool.op([m1, mg], [mg], lambda h, cs_=cs_: h.tensor_tensor(
                                  out=mg[:, cs_], in0=mg[:, cs_], in1=m1[:, cs_], op=ALU.add))
                          bT2 = PB[4 + s]
                          v2 = pbf(bT2)
                          transposes([bT2], [mg, identb], [(v2[:, cc * 128:(cc + 1) * 128], mg[:, cc * 128:(cc + 1) * 128], identb[:])
                                                           for cc in range(8)])
                          act.op([bT2], [mgT], lambda h, s=s, bT2=bT2: h.activation(
                              out=mgT[:, :, s * 128:(s + 1) * 128], in_=pbf(bT2).rearrange("p (a t) -> p a t", a=8), func=AF.Copy))
                          for blk in range(2):
                              bank = PB[2 + blk]
                              mm_group(bank, [mgT, wo], [(bank[:, :], mgT[:, cc, ts], wo[:, cc, blk * 512:(blk + 1) * 512])
                                                        for cc in range(8)])
                              dve.op([bank, xin[a]], [xo[a]], lambda h, s=s, blk=blk, bank=bank: h.tensor_tensor(
                                  out=xo[a][:, s, blk * 512:(blk + 1) * 512], in0=bank[:, :],
                                  in1=xin[a][:, s, blk * 512:(blk + 1) * 512], op=ALU.add))
                      sw.dma([xo[a]], [DB("XM", u)], xrows(XM, u), xo[a][:])
              S.barrier()

              chk()
              with ExitStack() as ph:
                  wq = sb(ph, "wq", [128, 8, 2048], BF16)
                  load_weight_bf16(wq, peer_wq[l], 8, 2048, 0)
                  gtile = sb(ph, "g3b", [128, D], F32)
                  bcast_row(gtile, norm_ffn_g[l:l + 1, :])
                  if last and do_final:
                      gfin = sb(ph, "gfin", [128, D], F32)
                      bcast_row(gfin, final_g[0:1, :])
                  skT = sb(ph, "skT", [128, 16, 128], BF16)
                  with ExitStack() as ph2:
                      skraw = sb(ph2, "skraw", [128, 16, 128], F32)
                      sp.dma([], [skraw], skraw[:], peer_sk[l].rearrange("h p k d -> k (h p) d"))
                      for cq in range(16):
                          bank = PB[cq % 8]
                          transposes([bank], [skraw, cst], [(bank[:, 0:128], skraw[:, cq, :], ident_f)])
                          dve.op([bank], [skT], lambda h, cq=cq, bank=bank: h.tensor_copy(out=skT[:, cq, :], in_=bank[:, 0:128]))
                      S.barrier()
                  xw = sb(ph, "xw", [128, 2, D], F32)
                  xn = sb(ph, "xn", [128, 2, D], BF16)
                  xnT = [sb(ph, f"xnT{i}", [128, 8, 256], BF16) for i in range(2)]
                  junk = (xn[:, 0, :], xn)
                  small = [sb(ph, f"sm{i}", [128, 2], F32) for i in range(4)]
                  qTp = sb(ph, "qTp", [128, 16, 256], BF16)
                  sc = sb(ph, "sc", [128, 16, 128], F32)
                  sc2 = sb(ph, "sc2", [128, 16, 128], F32)
                  sv = sb(ph, "sv", [128, 16, 16], F32)
                  si = sb(ph, "si", [128, 16, 16], U32)
                  sif = sb(ph, "sif", [128, 16, 16], F32)
                  cand = TT(sc.t[:].rearrange("p (h two) k -> p h (two k)", two=2), "cand_alias")
                  cand.b = sc.b
                  cand2 = TT(sc2.t[:].rearrange("p (h two) k -> p h (two k)", two=2), "cand2_alias")
                  cand2.b = sc2.b
                  s16 = sb(ph, "s16", [128, 8, 16], F32)
                  ci = sb(ph, "ci", [128, 8, 16], U32)
                  cia = sb(ph, "cia", [128, 8, 16], U32)
                  cib = sb(ph, "cib", [128, 8, 16], U32)
                  caf = sb(ph, "caf", [128, 8, 16], F32)
                  cbf = sb(ph, "cbf", [128, 8, 16], F32)
                  eq = TT(sc2.t[:].rearrange("p (h two) (a b) -> p h (two a) b", two=2, b=16), "eq_alias")
                  eq.b = sc2.b
                  rowi = [sb(ph, f"rowi{i}", [128, 128], F32) for i in range(3)]
                  gsum = sb(ph, "gsum", [128, 8], F32)
                  colT = [sb(ph, f"colT{i}", [128, 256], F32) for i in range(3)]
                  TBK = 8
                  P1 = [sb(ph, f"P1_{i}", [128, TBK, 128], BF16) for i in range(2)]
                  P2 = [sb(ph, f"P2_{i}", [128, TBK, 128], BF16) for i in range(2)]
                  GTs = sb(ph, "GTs", [128, 128, 256], BF16)
                  hT = [sb(ph, f"hT{i}", [128, 256], BF16) for i in range(2)]
                  gh = [sb(ph, f"gh{i}", [128, 256], BF16) for i in range(3)]
                  NSB = 3
                  ub = [sb(ph, f"ub{i}", [128, 8, 256], BF16) for i in range(NSB)]
                  vbuf = [sb(ph, f"vbuf{i}", [128, 2, D], BF16) for i in range(NSB)]

                  def route_A(u):
                      a = u % 2
                      sp.dma([DB("XM", u)], [xw], xw[:], xrows(XM, u))
                      norm_T(xw, gtile, xn, xnT[a], junk, small, PB[6], PB[7], (act, dve))
                      for cq in range(16):
                          bank = PB[4 + cq % 2]
                          mm_group(bank, [wq, xnT[a]], [(bank[:, 0:256], wq[:, kc, cq * 128:(cq + 1) * 128], xnT[a][:, kc, :])
                                                       for kc in range(8)])
                          act.op([bank], [qTp], lambda h, cq=cq, bank=bank: h.activation(
                              out=qTp[:, cq, :], in_=bank[:, 0:256], func=AF.Copy))
                      for s in range(2):
                          ts = slice(s * 128, (s + 1) * 128)
                          for q4 in range(4):
                              bank = PB[4 + q4 % 2]
                              for j in range(4):
                                  cq = q4 * 4 + j
                                  mm_group(bank, [qTp, skT], [(bank[:, j * 128:(j + 1) * 128], qTp[:, cq, ts], skT[:, cq, :])])
                              act.op([bank], [sc], lambda h, q4=q4, bank=bank: h.activation(
                                  out=sc[:, q4 * 4:(q4 + 1) * 4, :], in_=bank[:, :].rearrange("p (a k) -> p a k", a=4),
                                  func=AF.Copy))
                          for cq in range(16):
                              dve.op([sc], [sv], lambda h, cq=cq: h.max(out=sv[:, cq, 0:8], in_=sc[:, cq, :]))
                              dve.op([sc, sv], [si], lambda h, cq=cq: h.max_index(out=si[:, cq, 0:8], in_max=sv[:, cq, 0:8], in_values=sc[:, cq, :]))
                              dve.op([sc, sv], [sc2], lambda h, cq=cq: h.match_replace(
                                  out=sc2[:, cq, :], in_to_replace=sv[:, cq, 0:8], in_values=sc[:, cq, :], imm_value=NEG))
                              dve.op([sc2], [sv], lambda h, cq=cq: h.max(out=sv[:, cq, 8:16], in_=sc2[:, cq, :]))
                              dve.op([sc2, sv], [si], lambda h, cq=cq: h.max_index(out=si[:, cq, 8:16], in_max=sv[:, cq, 8:16], in_values=sc2[:, cq, :]))
                          dve.op([si], [sif], lambda h: h.tensor_copy(out=sif[:], in_=si[:]))
                          sv4 = sv[:].rearrange("p (h two) k -> p h two k", two=2)
                          sif4 = sif[:].rearrange("p (h two) k -> p h two k", two=2)
                          cand4 = cand[:].rearrange("p h (a b) -> p h a b", a=16)
                          dve.op([sv], [cand], lambda h: h.tensor_tensor(
                              out=cand4, in0=sv4[:, :, 0, :].unsqueeze(3).to_broadcast([128, 8, 16, 16]),
                              in1=sv4[:, :, 1, :].unsqueeze(2).to_broadcast([128, 8, 16, 16]), op=ALU.add))
                          for hd in range(8):
                              dve.op([cand], [s16], lambda h, hd=hd: h.max(out=s16[:, hd, 0:8], in_=cand[:, hd, :]))
                              dve.op([cand, s16], [ci], lambda h, hd=hd: h.max_index(out=ci[:, hd, 0:8], in_max=s16[:, hd, 0:8], in_values=cand[:, hd, :]))
                              dve.op([cand, s16], [cand2], lambda h, hd=hd: h.match_replace(
                                  out=cand2[:, hd, :], in_to_replace=s16[:, hd, 0:8], in_values=cand[:, hd, :], imm_value=NEG))
                              dve.op([cand2], [s16], lambda h, hd=hd: h.max(out=s16[:, hd, 8:16], in_=cand2[:, hd, :]))
                              dve.op([cand2, s16], [ci], lambda h, hd=hd: h.max_index(out=ci[:, hd, 8:16], in_max=s16[:, hd, 8:16], in_values=cand2[:, hd, :]))
                          g3 = rowi[2][:].rearrange("p (h k) -> p h k", h=8)
                          dve.op([s16], [rowi[2]], lambda h: h.tensor_tensor(
                              out=g3, in0=s16[:], in1=s16[:, :, 0:1].to_broadcast([128, 8, 16]), op=ALU.subtract))
                          act.op([rowi[2]], [rowi[2]], lambda h: h.activation(out=rowi[2][:], in_=rowi[2][:], func=AF.Exp))
                          dve.op([rowi[2]], [gsum], lambda h: h.tensor_reduce(out=gsum[:], in_=g3, axis=AX.X, op=ALU.add))
                          dve.op([gsum], [gsum], lambda h: h.reciprocal(out=gsum[:], in_=gsum[:]))
                          dve.op([rowi[2], gsum], [rowi[2]], lambda h: h.tensor_tensor(
                              out=g3, in0=g3, in1=gsum[:].unsqueeze(2).to_broadcast([128, 8, 16]), op=ALU.mult))
                          dve.op([ci], [cia], lambda h: h.tensor_single_scalar(out=cia[:], in_=ci[:], scalar=4, op=ALU.logical_shift_right))
                          dve.op([ci], [cib], lambda h: h.tensor_single_scalar(out=cib[:], in_=ci[:], scalar=15, op=ALU.bitwise_and))
                          dve.op([cia], [caf], lambda h: h.tensor_copy(out=caf[:], in_=cia[:]))
                          dve.op([cib], [cbf], lambda h: h.tensor_copy(out=cbf[:], in_=cib[:]))
                          io4 = iota16.unsqueeze(1).unsqueeze(1).to_broadcast([128, 8, 16, 16])
                          for which, (cf, half) in enumerate(((caf, 0), (cbf, 1))):
                              dst3 = rowi[which][:].rearrange("p (h k) -> p h k", h=8)
                              dve.op([cf, cst], [eq], lambda h, cf=cf: h.tensor_tensor(
                                  out=eq[:], in0=cf[:].unsqueeze(3).to_broadcast([128, 8, 16, 16]), in1=io4, op=ALU.is_equal))
                              dve.op([eq, sif], [eq], lambda h, half=half: h.tensor_tensor(
                                  out=eq[:], in0=eq[:], in1=sif4[:, :, half, :].unsqueeze(2).to_broadcast([128, 8, 16, 16]),
                                  op=ALU.mult))
                              dve.op([eq], [rowi[which]], lambda h, dst3=dst3: h.tensor_reduce(
                                  out=dst3, in_=eq[:], axis=AX.X, op=ALU.add))
                          for w3 in range(3):
                              bank = PB[6 + w3 % 2]
                              transposes([bank], [rowi[w3], cst], [(bank[:, 0:128], rowi[w3][:], ident_f)])
                              act.op([bank], [colT[w3]], lambda h, w3=w3, bank=bank, ts=ts: h.activation(
                                  out=colT[w3][:, ts], in_=bank[:, 0:128], func=AF.Copy))

                  def route_B(u):
                      bi = 0
                      for t0 in range(0, 256, TBK):
                          p1 = P1[(t0 // TBK) % 2]
                          p2 = P2[(t0 // TBK) % 2]
                          for tt in range(TBK):
                              t = t0 + tt
                              dve.op([iotab, colT[0], colT[2]], [p1], lambda h, t=t, tt=tt, p1=p1: h.tensor_scalar(
                                  out=p1[:, tt, :], in0=iotab[:], scalar1=colT[0][:, t:t + 1], scalar2=colT[2][:, t:t + 1],
                                  op0=ALU.is_equal, op1=ALU.mult))
                              dve.op([iotab, colT[1]], [p2], lambda h, t=t, tt=tt, p2=p2: h.tensor_scalar(
                                  out=p2[:, tt, :], in0=iotab[:], scalar1=colT[1][:, t:t + 1], scalar2=None,
                                  op0=ALU.is_equal))
                          for q in range(TBK // 4):
                              bank = PB[4 + bi % 4]
                              bi += 1
                              for j in range(4):
                                  tt = q * 4 + j
                                  mm_group(bank, [p1, p2], [(bank[:, j * 128:(j + 1) * 128], p2[:, tt, :], p1[:, tt, :])])
                              tq = t0 + q * 4
                              act.op([bank], [GTs], lambda h, bank=bank, tq=tq: h.activation(
                                  out=GTs[:, :, tq:tq + 4].transpose([0, 2, 1]),
                                  in_=bank[:, :].rearrange("p (t i) -> p t i", t=4), func=AF.Copy))

                  def dense(u):
                      a = u % 2
                      steps = []
                      for g in range(64):
                          for c in range(2):
                              steps.append((g, c))

                      def emit_load(g):
                          sp.dma([DB("UTS", (l, g))], [ub[g % NSB]], ub[g % NSB][:].rearrange("p a e -> p (a e)"), UTS[l, g])
                          sp.dma([DB("VS", (l, g))], [vbuf[g % NSB]], vbuf[g % NSB][:].rearrange("p a e -> p (a e)"), VS[l, g])

                      def emit_H(i):
                          g, c = steps[i]
                          bank = PB[4 + i % 2]
                          mm_group(bank, [ub[g % NSB], xnT[a]], [(bank[:, 0:256], ub[g % NSB][:, kc, c * 128:(c + 1) * 128], xnT[a][:, kc, :])
                                                                for kc in range(8)])
                          h_ = hT[i % 2]
                          g_ = gh[i % 3]
                          act.op([bank], [h_], lambda h, bank=bank, h_=h_: h.activation(out=h_[:], in_=bank[:, 0:256], func=AF.Gelu))
                          pool.op([h_, GTs], [g_], lambda h, h_=h_, g_=g_, i1=2 * g + c: h.tensor_tensor(
                              out=g_[:], in0=h_[:], in1=GTs[:, i1, :], op=ALU.mult))

                      def emit_V(i):
                          g, c = steps[i]
                          g_ = gh[i % 3]
                          for s in range(2):
                              for blk in range(2):
                                  bank = PB[s * 2 + blk]
                                  pe.op([g_, vbuf[g % NSB]], [bank], lambda h, s=s, blk=blk, bank=bank, g_=g_, g=g, c=c, i=i: h.matmul(
                                      bank[:, :], lhsT=g_[:, s * 128:(s + 1) * 128], rhs=vbuf[g % NSB][:, c, blk * 512:(blk + 1) * 512],
                                      start=(i == 0), stop=(i == len(steps) - 1)))

                      emit_load(0)
                      emit_load(1)
                      emit_H(0)
                      for i in range(len(steps)):
                          g, c = steps[i]
                          if c == 0 and g + 2 < 64:
                              emit_load(g + 2)
                          if i + 1 < len(steps):
                              emit_H(i + 1)
                          emit_V(i)

                  def finalize(u):
                      xo = xw
                      yo = xw
                      sp.dma([DB("XM", u)], [xw], xw[:], xrows(XM, u))
                      for s in range(2):
                          for blk in range(2):
                              bank = PB[s * 2 + blk]
                              dve.op([bank, xw], [xw], lambda h, s=s, blk=blk, bank=bank: h.tensor_tensor(
                                  out=xw[:, s, blk * 512:(blk + 1) * 512], in0=bank[:, :],
                                  in1=xw[:, s, blk * 512:(blk + 1) * 512], op=ALU.add))
                      if last and do_final:
                          ss, ms, sq, rstd = small
                          for s in range(2):
                              act.op([xo], [xn, ss], lambda h, s=s: h.activation(
                                  out=xn[:, 0, :], in_=xo[:, s, :], func=AF.Square, accum_out=ss[:, s:s + 1]))
                          dve.op([ss], [ms], lambda h: h.tensor_scalar(
                              out=ms[:], in0=ss[:], scalar1=1.0 / D, scalar2=EPS, op0=ALU.mult, op1=ALU.add))
                          act.op([ms], [sq], lambda h: h.activation(out=sq[:], in_=ms[:], func=AF.Sqrt))
                          dve.op([sq], [rstd], lambda h: h.reciprocal(out=rstd[:], in_=sq[:]))
                          for s in range(2):
                              dve.op([xo, rstd, gfin], [yo], lambda h, s=s: h.scalar_tensor_tensor(
                                  out=yo[:, s, :], in0=xo[:, s, :], scalar=rstd[:, s:s + 1], in1=gfin[:],
                                  op0=ALU.mult, op1=ALU.mult))
                          sw.dma([yo], [DB("Y", u)], xrows(x_dst, u), yo[:])
                      else:
                          sw.dma([xo], [DB("X", u)], xrows(x_dst, u), xo[:])

                  route_A(0)
                  route_B(0)
                  for u in range(NU):
                      if u + 1 < NU:
                          route_A(u + 1)
                      dense(u)
                      finalize(u)
                      if u + 1 < NU:
                          route_B(u + 1)
              S.barrier()
              lay.close()
        except _Stop:
            pass
        S.barrier()
    return nc, S.ninst


_WEIGHT_KEYS = ["norm_mix_g", "w_in", "w_gate", "conv_w", "conv_b", "conv_ln_g", "conv_ln_b", "w_conv_out",
                "log_gamma_fwd", "log_gamma_bwd", "w_ret_out", "w_o", "norm_ffn_g", "peer_wq", "peer_subkeys",
                "peer_u", "peer_v"]


def run_cores(x_per_core, weights, seqs, n_layers=2, do_final=True, stop=0):
    nc, ninst = build(seqs, n_layers=n_layers, do_final=do_final, stop=stop)
    consts, cosT, sinT = host_consts(max(seqs))
    base = {k: np.ascontiguousarray(np.asarray(weights[k], dtype=np.float32)) for k in _WEIGHT_KEYS}
    base["final_norm_g"] = np.ascontiguousarray(np.asarray(weights["final_norm_g"], np.float32).reshape(1, D))
    base["consts"] = consts
    base["cosT"] = cosT
    base["sinT"] = sinT
    in_maps = []
    for xc in x_per_core:
        m = dict(base)
        m["x"] = np.ascontiguousarray(xc, dtype=np.float32)
        in_maps.append(m)
    res = run_bass_kernel_spmd(nc, in_maps, core_ids=list(range(len(x_per_core))))
    return [r["y"] for r in res.results]


def kernel(x_prompt, x_sample, **weights):
    x_prompt = np.asarray(x_prompt, dtype=np.float32)
    x_sample = np.asarray(x_sample, dtype=np.float32)
    xs = []
    for c in range(N_CORES):
        xs.append(np.concatenate([x_prompt[c], x_sample[2 * c], x_sample[2 * c + 1]], axis=0))
    ys = run_cores(xs, weights, SEQS_FULL)
    y_prompt = np.stack([y[0:8192] for y in ys], axis=0)
    y_sample = np.stack([ys[c // 2][8192 + 2048 * (c % 2):8192 + 2048 * (c % 2 + 1)] for c in range(16)], axis=0)
    return (y_prompt.astype(np.float32), y_sample.astype(np.float32))
```

```python
from contextlib import ExitStack
import numpy as np
import concourse.bass as bass
import concourse.mybir as mybir
from concourse.bass_utils import run_bass_kernel_spmd

F32 = mybir.dt.float32
BF16 = mybir.dt.bfloat16
U32 = mybir.dt.uint32
AF = mybir.ActivationFunctionType
ALU = mybir.AluOpType
AX = mybir.AxisListType

D = 1024
EPS = 1e-6
NEG = -1.0e30
N_CORES = 8
SEQS_FULL = (8192, 2048, 2048)


class Buf:
    __slots__ = ("name", "lw", "rd")

    def __init__(self, name):
        self.name = name
        self.lw = None
        self.rd = {}


class TT:
    def __init__(self, t, name):
        self.t = t
        self.b = Buf(name)

    def __getitem__(self, k):
        return self.t[k]


def _b(x):
    return x.b if isinstance(x, TT) else x


class Eng:
    def __init__(self, S, name, h, nslots=0, own_skip=False):
        self.S = S
        self.name = name
        self.h = h
        self.waited = {}
        self.own_skip = own_skip
        self.nslots = nslots
        if nslots == 0:
            self.sem = S.new_sem(name)
            self.count = 0
        else:
            self.slots = [[S.new_sem(f"{name}{i}"), 0] for i in range(nslots)]
            self.rr = 0

    def _wait(self, tick):
        sem, val, key = tick
        if self.own_skip and self.nslots == 0 and key == self.sem_key():
            return
        if self.waited.get(key, 0) >= val:
            return
        self.h.wait_ge(sem, val)
        self.waited[key] = val

    def sem_key(self):
        return self.name

    def _deps(self, reads, writes):
        own = self.name if self.nslots == 0 else None
        for b in reads:
            b = _b(b)
            if b.lw is not None:
                self._wait(b.lw)
        for b in writes:
            b = _b(b)
            if b.lw is not None and b.lw[2] != own:
                self._wait(b.lw)
            for t in b.rd.values():
                if t[2] != own:
                    self._wait(t)

    def _mark(self, reads, writes, tick):
        for b in reads:
            _b(b).rd[tick[2]] = tick
        for b in writes:
            b = _b(b)
            b.lw = tick
            b.rd = {}

    def op(self, reads, writes, fn):
        if DEAD[0]:
            return
        self._deps(reads, writes)
        inst = fn(self.h)
        self.count += 1
        inst.then_inc(self.sem, 1)
        tick = (self.sem, self.count, self.name)
        self._mark(reads, writes, tick)
        self.S.ninst += 1

    def dma(self, reads, writes, out, in_, **kw):
        if DEAD[0]:
            return
        slot = self.slots[self.rr]
        key = f"{self.name}{self.rr}"
        self.rr = (self.rr + 1) % self.nslots
        if slot[1] > 0:
            self._wait((slot[0], slot[1], key))
        self._deps(reads, writes)
        inst = self.h.dma_start(out=out, in_=in_, **kw)
        slot[1] += 16
        inst.then_inc(slot[0], 16)
        tick = (slot[0], slot[1], key)
        self._mark(reads, writes, tick)
        self.S.ninst += 1

    def last_ticks(self):
        if self.nslots == 0:
            return [(self.sem, self.count, self.name)] if self.count else []
        return [(s[0], s[1], f"{self.name}{i}") for i, s in enumerate(self.slots) if s[1]]


class Sched:
    def __init__(self, nc, stack):
        self.nc = nc
        self.stack = stack
        self.ninst = 0
        self.pe = Eng(self, "pe", nc.tensor, own_skip=True)
        self.act = Eng(self, "act", nc.scalar)
        self.dve = Eng(self, "dve", nc.vector)
        self.pool = Eng(self, "pool", nc.gpsimd)
        self.sp = Eng(self, "sp", nc.sync, nslots=8)
        self.sw = Eng(self, "sw", nc.gpsimd, nslots=6)
        self.all = [self.pe, self.act, self.dve, self.pool, self.sp, self.sw]

    def new_sem(self, name):
        return self.stack.enter_context(self.nc.semaphore("s_" + name))

    def barrier(self):
        ticks = []
        for e in self.all:
            ticks += e.last_ticks()
        for e in (self.pe, self.act, self.dve, self.pool, self.sp):
            for t in ticks:
                if e.nslots == 0 and t[2] == e.name:
                    if e.own_skip:
                        continue
                e._wait(t)


def host_consts(lmax):
    c = np.zeros((128, 1024), np.float32)
    p = np.arange(128, dtype=np.float32)
    c[:, 0:128] = np.eye(128, dtype=np.float32)
    c[:, 128:256] = p[None, :]
    diff = p[None, :] - p[:, None]
    c[:, 256:384] = np.maximum(diff, 0.0)
    c[:, 384:512] = np.maximum(-diff, 0.0)
    c[:, 512:640] = (diff >= 0).astype(np.float32) / 16.0
    c[:, 640:768] = (diff < 0).astype(np.float32) / 16.0
    c[:, 768] = p + 1.0
    c[:, 769] = 128.0 - p
    c[:, 770] = 127.0 - p
    c[:, 771] = p
    c[:, 772] = 128.0
    c[:, 776:792] = np.arange(16, dtype=np.float32)[None, :]
    c[:, 800:928] = 1.0 / 1024.0
    half = 128
    freqs = (1.0 / (10000.0 ** (np.arange(half, dtype=np.float32) / np.float32(half)))).astype(np.float32)
    ang = np.arange(lmax, dtype=np.float32)[:, None] * freqs[None, :]
    cosT = np.ascontiguousarray(np.cos(ang).astype(np.float32).T)
    sinT = np.ascontiguousarray(np.sin(ang).astype(np.float32).T)
    return c, cosT, sinT


class _Stop(Exception):
    pass


SUBSTOP = [0]
SKIP = set()
CFG = {"NSB": 3, "LOOK": 2}


DEAD = [False]


def sub(n):
    if SUBSTOP[0] == n:
        DEAD[0] = True


def build(seqs, n_layers=2, do_final=True, stop=0):
    T_tok = sum(seqs)
    NU = T_tok // 256
    NCH = T_tok // 128
    lmax = max(seqs)
    nc = bass.Bass("TRN2", target_bir_lowering=False)
    DEAD[0] = False

    def din(name, shape, dt=F32):
        return nc.dram_tensor(name, list(shape), dt, kind="ExternalInput").ap()

    def dscr(name, shape, dt):
        return nc.dram_tensor(name, list(shape), dt, kind="Internal").ap()

    x_in = din("x", [T_tok, D])
    consts_d = din("consts", [128, 1024])
    cos_d = din("cosT", [128, lmax])
    sin_d = din("sinT", [128, lmax])
    norm_mix_g = din("norm_mix_g", [2, D])
    w_in = din("w_in", [2, D, 6144])
    w_gate = din("w_gate", [2, D, 2048])
    conv_w = din("conv_w", [2, 31, D])
    conv_b = din("conv_b", [2, D])
    conv_ln_g = din("conv_ln_g", [2, D])
    conv_ln_b = din("conv_ln_b", [2, D])
    w_conv_out = din("w_conv_out", [2, D, D])
    lg_f = din("log_gamma_fwd", [2, 4])
    lg_b = din("log_gamma_bwd", [2, 4])
    w_ret_out = din("w_ret_out", [2, D, D])
    w_o = din("w_o", [2, D, D])
    norm_ffn_g = din("norm_ffn_g", [2, D])
    peer_wq = din("peer_wq", [2, D, 2048])
    peer_sk = din("peer_subkeys", [2, 8, 2, 128, 128])
    peer_u = din("peer_u", [2, 16384, D])
    peer_v = din("peer_v", [2, 16384, D])
    final_g = din("final_norm_g", [1, D])
    y_out = nc.dram_tensor("y", [T_tok, D], F32, kind="ExternalOutput").ap()

    X1 = dscr("X1", [T_tok, D], F32)
    XM = dscr("XM", [T_tok, D], F32)
    UT = dscr("UT", [NU, 128, 8 * 256], BF16)
    QT = dscr("QT", [NU, 128, 8 * 256], BF16)
    KT = dscr("KT", [NU, 128, 8 * 256], BF16)
    KK = dscr("KK", [NCH, 128, D], BF16)
    VV = dscr("VV", [NCH, 128, D], BF16)
    SG = dscr("SG", [NCH, 128, D], BF16)
    GT = dscr("GT", [NCH, 128, 2048], BF16)
    YC = dscr("YC", [NCH, 128, D], BF16)
    SB = dscr("SB", [NCH, 128, 8 * 256], BF16)
    UTS = dscr("UTS", [2, 64, 128, 8 * 256], BF16)
    VS = dscr("VS", [2, 64, 128, 2 * 1024], BF16)

    units = []
    u0 = 0
    seq_ranges = []
    for L in seqs:
        nu = L // 256
        seq_ranges.append((u0, u0 + nu))
        for i in range(nu):
            units.append((u0, u0 + nu, i * 256))
        u0 += nu

    dbufs = {}

    def DB(name, idx):
        k = (name, idx)
        if k not in dbufs:
            dbufs[k] = Buf(f"{name}{idx}")
        return dbufs[k]

    with ExitStack() as top:
        S = Sched(nc, top)
        pe, act, dve, pool, sp, sw = S.pe, S.act, S.dve, S.pool, S.sp, S.sw

        uniq = [0]

        def sb(stack, name, shape, dt):
            uniq[0] += 1
            name = f"{name}_{uniq[0]}"
            return TT(stack.enter_context(nc.sbuf_tensor(name, list(shape), dt)), name)

        PB = [TT(top.enter_context(nc.psum_tensor(f"pb{i}", [128, 512], F32)), f"pb{i}") for i in range(8)]

        def pbf(bank):
            return bank.t[:].bitcast(BF16)

        cst = sb(top, "cst", [128, 1024], F32)
        sp.dma([], [cst], cst[:], consts_d[:, :])
        ident_f = cst[:, 0:128]
        identb = sb(top, "identb", [128, 128], BF16)
        iotab = sb(top, "iotab", [128, 128], BF16)
        dve.op([cst], [identb], lambda h: h.tensor_copy(out=identb[:], in_=cst[:, 0:128]))
        dve.op([cst], [iotab], lambda h: h.tensor_copy(out=iotab[:], in_=cst[:, 128:256]))
        iota16 = cst[:, 776:792]

        def mm_group(out_bank, reads, mms, extra_writes=()):
            n = len(mms)

            def fn(h):
                inst = None
                for i, (o, l, r) in enumerate(mms):
                    inst = h.matmul(o, lhsT=l, rhs=r, start=(i == 0), stop=(i == n - 1))
                return inst
            pe.op(reads, [out_bank] + list(extra_writes), fn)

        def transposes(out_banks, reads, trs):
            def fn(h):
                inst = None
                for (o, i_, idn) in trs:
                    inst = h.transpose(o, i_, idn)
                return inst
            pe.op(reads, list(out_banks), fn)

        def norm_T(xin, gtile, xn, xnT, junk, small, bankA, bankB, evac_engs):
            ss, ms, sq, rstd = small
            if isinstance(junk, tuple):
                jap, jb = junk
            else:
                jap, jb = junk[:], junk
            for s in range(2):
                act.op([xin], [jb, ss], lambda h, s=s: h.activation(
                    out=jap, in_=xin[:, s, :], func=AF.Square, accum_out=ss[:, s:s + 1]))
            dve.op([ss], [ms], lambda h: h.tensor_scalar(
                out=ms[:], in0=ss[:], scalar1=1.0 / D, scalar2=EPS, op0=ALU.mult, op1=ALU.add))
            act.op([ms], [sq], lambda h: h.activation(out=sq[:], in_=ms[:], func=AF.Sqrt))
            dve.op([sq], [rstd], lambda h: h.reciprocal(out=rstd[:], in_=sq[:]))
            for s in range(2):
                dve.op([xin, rstd, gtile], [xn], lambda h, s=s: h.scalar_tensor_tensor(
                    out=xn[:, s, :], in0=xin[:, s, :], scalar=rstd[:, s:s + 1], in1=gtile[:],
                    op0=ALU.mult, op1=ALU.mult))
            trs = []
            for kc in range(8):
                bank = bankA if kc < 4 else bankB
                v = pbf(bank)
                for s in range(2):
                    o = v[:, (kc % 4) * 256 + s * 128:(kc % 4) * 256 + (s + 1) * 128]
                    trs.append((o, xn[:, s, kc * 128:(kc + 1) * 128], identb[:]))
            transposes([bankA, bankB], [xn, identb], trs)
            e0, e1 = evac_engs
            e0.op([bankA], [xnT], lambda h: _copy(h, xnT[:, 0:4, :], pbf(bankA).rearrange("p (a t) -> p a t", a=4)))
            e1.op([bankB], [xnT], lambda h: _copy(h, xnT[:, 4:8, :], pbf(bankB).rearrange("p (a t) -> p a t", a=4)))

        def _copy(h, out, in_):
            if h is nc.scalar:
                return h.activation(out=out, in_=in_, func=AF.Copy)
            return h.tensor_copy(out=out, in_=in_)

        def load_weight_bf16(dst, src3, nk, ncols, col0=0):
            for kc in range(nk):
                for c0 in range(0, ncols, 2048):
                    c1 = min(ncols, c0 + 2048)
                    sw.dma([], [dst], dst[:, kc, c0:c1], src3[kc * 128:(kc + 1) * 128, col0 + c0:col0 + c1])

        def bcast_row(dst, row_ap):
            sp.dma([], [dst], dst[:], row_ap.partition_broadcast(128))

        phase_ctr = [0]

        def chk():
            phase_ctr[0] += 1
            if stop and phase_ctr[0] > stop:
                DEAD[0] = True

        with ExitStack() as ph:
            uf = [sb(ph, f"uf{i}", [128, 4, D], F32) for i in range(2)]
            uts = [sb(ph, f"uts{i}", [128, 8, 512], BF16) for i in range(2)]
            vb = [sb(ph, f"vb{i}", [128, 4, D], BF16) for i in range(2)]
            it = 0
            for l in range(n_layers):
                for g in range(32):
                    a = it % 2
                    it += 1
                    rows = peer_u[l, g * 512:(g + 1) * 512, :].rearrange("(c p) d -> p c d", p=128)
                    sp.dma([], [uf[a]], uf[a][:], rows)
                    for kc in range(8):
                        bank = PB[kc]
                        trs = [(bank[:, c * 128:(c + 1) * 128], uf[a][:, c, kc * 128:(kc + 1) * 128], ident_f)
                               for c in range(4)]
                        transposes([bank], [uf[a], cst], trs)
                        e = act if kc % 2 == 0 else dve
                        e.op([bank], [uts[a]], lambda h, kc=kc, bank=bank, a=a: _copy(h, uts[a][:, kc, :], bank[:, :]))
                    for hf in range(2):
                        sp.dma([uts[a]], [DB("UTS", (l, 2 * g + hf))], UTS[l, 2 * g + hf].rearrange("p (a e) -> p a e", a=8),
                               uts[a][:, :, hf * 256:(hf + 1) * 256])
                    vrows = peer_v[l, g * 512:(g + 1) * 512, :].rearrange("(c p) d -> p c d", p=128)
                    sw.dma([], [vb[a]], vb[a][:], vrows)
                    for hf in range(2):
                        sp.dma([vb[a]], [DB("VS", (l, 2 * g + hf))], VS[l, 2 * g + hf].rearrange("p (a e) -> p a e", a=2),
                               vb[a][:, 2 * hf:2 * hf + 2, :])
        S.barrier()

        try:
          for l in range(n_layers):
              chk()
              x_src = x_in if l == 0 else X1
              last = (l == n_layers - 1)
              x_dst = y_out if last else X1

              def xrows(src, u):
                  return src[u * 256:(u + 1) * 256, :].rearrange("(s p) d -> p s d", p=128)

              lay = top.enter_context(ExitStack())
              lgfb = sb(lay, "lgfb", [128, 4], F32)
              lgbb = sb(lay, "lgbb", [128, 4], F32)
              bcast_row(lgfb, lg_f[l:l + 1, :])
              bcast_row(lgbb, lg_b[l:l + 1, :])
              dsc = sb(lay, "dsc", [128, 6, 4], F32)
              specs = [(lgfb, 768, 0.0), (lgbb, 769, 0.0), (lgfb, 770, -np.log(16.0)), (lgbb, 771, -np.log(16.0)),
                       (lgfb, 772, 0.0), (lgbb, 772, 0.0)]
              for i, (lgt, col, bias) in enumerate(specs):
                  dve.op([cst, lgt], [dsc], lambda h, i=i, lgt=lgt, col=col: h.tensor_scalar(
                      out=dsc[:, i, :], in0=lgt[:], scalar1=cst[:, col:col + 1], scalar2=None, op0=ALU.mult))
              dve.op([dsc], [dsc], lambda h: h.tensor_scalar(
                  out=dsc[:, 2:4, :], in0=dsc[:, 2:4, :], scalar1=float(-np.log(16.0)), scalar2=None, op0=ALU.add))
              act.op([dsc], [dsc], lambda h: h.activation(out=dsc[:], in_=dsc[:], func=AF.Exp))
              A_F, A_B, VD_F, VD_B, CD_F, CD_B = range(6)
              if SUBSTOP[0] == 50:
                  sp.dma([dsc], [DB("Y", -1)], y_out[0:128, 0:24], dsc[:].rearrange("p a b -> p (a b)"))
                  sp.dma([decT], [DB("Y", -2)], y_out[0:128, 32:32 + 512], decT[:].rearrange("p a b -> p (a b)"))
                  sp.dma([lgfb], [DB("Y", -3)], y_out[0:128, 600:604], lgfb[:])
                  sp.dma([lgbb], [DB("Y", -4)], y_out[0:128, 608:612], lgbb[:])
                  sub(50)

              chk()
              with ExitStack() as ph:
                  wsb = sb(ph, "w1a", [128, 8, 4096], BF16)
                  load_weight_bf16(wsb, w_in[l], 8, 4096, 0)
                  gtile = sb(ph, "g1a", [128, D], F32)
                  bcast_row(gtile, norm_mix_g[l:l + 1, :])
                  xin = [sb(ph, f"xin{i}", [128, 2, D], F32) for i in range(2)]
                  cs = [sb(ph, f"cs{i}", [128, 2, 256], F32) for i in range(2)]
                  xn = sb(ph, "xn", [128, 2, D], BF16)
                  xnT = sb(ph, "xnT", [128, 8, 256], BF16)
                  junk = sb(ph, "junk", [128, D], BF16)
                  small = [sb(ph, f"sm{i}", [128, 2], F32) for i in range(4)]
                  uT = [sb(ph, f"uT{i}", [128, 8, 256], BF16) for i in range(2)]
                  qT = [sb(ph, f"qT{i}", [128, 8, 256], BF16) for i in range(2)]
                  kT = [sb(ph, f"kT{i}", [128, 8, 256], BF16) for i in range(2)]
                  ktok = [sb(ph, f"ktok{i}", [128, 2, D], BF16) for i in range(2)]
                  sig = [sb(ph, f"sig{i}", [128, 256], F32) for i in range(2)]
                  rt = [sb(ph, f"rt{i}", [128, 4, 256], F32) for i in range(2)]
                  kraw = [sb(ph, f"kraw{i}", [128, 2, 256], F32) for i in range(2)]

                  def p1a_load(u):
                      a = u % 2
                      sp.dma([DB("X", u)], [xin[a]], xin[a][:], xrows(x_src, u))
                      pos = units[u][2]
                      sp.dma([], [cs[a]], cs[a][:, 0, :], cos_d[:, pos:pos + 256])
                      sp.dma([], [cs[a]], cs[a][:, 1, :], sin_d[:, pos:pos + 256])

                  p1a_load(0)
                  sub(1)
                  pair_i = 0
                  for u in range(NU):
                      a = u % 2
                      if u + 1 < NU:
                          p1a_load(u + 1)
                      sub(20 + u)
                      sub(2)
                      norm_T(xin[a], gtile, xn, xnT, junk, small, PB[6], PB[7], (act, dve))
                      sub(3)
                      cosA = cs[a][:, 0, :]
                      sinA = cs[a][:, 1, :]
                      for pi in range(16):
                          bank = PB[pair_i % 6]
                          pair_i += 1
                          if pi < 8:
                              c1, c2 = pi, 8 + pi
                          elif pi < 12:
                              c1, c2 = 16 + 2 * (pi - 8), 17 + 2 * (pi - 8)
                          else:
                              c1, c2 = 24 + 2 * (pi - 12), 25 + 2 * (pi - 12)
                          for half, cc in enumerate((c1, c2)):
                              mms = [(bank[:, half * 256:(half + 1) * 256], wsb[:, kc, cc * 128:(cc + 1) * 128],
                                      xnT[:, kc, :]) for kc in range(8)]
                              mm_group(bank, [wsb, xnT], mms)
                          p1 = bank[:, 0:256]
                          p2 = bank[:, 256:512]
                          if pi < 8:
                              sg_ = sig[pi % 2]
                              act.op([bank], [sg_], lambda h, sg_=sg_, p2=p2: h.activation(
                                  out=sg_[:], in_=p2, func=AF.Sigmoid))
                              dve.op([bank, sg_], [uT[a]], lambda h, sg_=sg_, p1=p1, pi=pi: h.tensor_tensor(
                                  out=uT[a][:, pi, :], in0=p1, in1=sg_[:], op=ALU.mult))
                          elif pi < 12:
                              hd = pi - 8
                              r = rt[pi % 2]
                              dst = qT[a]
                              dve.op([bank, cs[a]], [r], lambda h, r=r: h.tensor_tensor(out=r[:, 0, :], in0=p1, in1=cosA, op=ALU.mult))
                              dve.op([bank, cs[a]], [r], lambda h, r=r: h.tensor_tensor(out=r[:, 1, :], in0=p2, in1=sinA, op=ALU.mult))
                              dve.op([bank, cs[a]], [r], lambda h, r=r: h.tensor_tensor(out=r[:, 2, :], in0=p1, in1=sinA, op=ALU.mult))
                              dve.op([bank, cs[a]], [r], lambda h, r=r: h.tensor_tensor(out=r[:, 3, :], in0=p2, in1=cosA, op=ALU.mult))
                              pool.op([r], [dst], lambda h, r=r, dst=dst, hd=hd: h.tensor_tensor(
                                  out=dst[:, 2 * hd, :], in0=r[:, 0, :], in1=r[:, 1, :], op=ALU.subtract))
                              pool.op([r], [dst], lambda h, r=r, dst=dst, hd=hd: h.tensor_tensor(
                                  out=dst[:, 2 * hd + 1, :], in0=r[:, 2, :], in1=r[:, 3, :], op=ALU.add))
                          else:
                              hd = pi - 12
                              r = rt[pi % 2]
                              kr = kraw[pi % 2]
                              dst = kT[a]
                              act.op([bank], [kr], lambda h, kr=kr, bank=bank: h.activation(
                                  out=kr[:], in_=bank[:, :].rearrange("p (a t) -> p a t", a=2), func=AF.Copy))
                              pool.op([kr, cs[a]], [r], lambda h, r=r, kr=kr: h.tensor_tensor(out=r[:, 0, :], in0=kr[:, 0, :], in1=cosA, op=ALU.mult))
                              pool.op([kr, cs[a]], [r], lambda h, r=r, kr=kr: h.tensor_tensor(out=r[:, 1, :], in0=kr[:, 1, :], in1=sinA, op=ALU.mult))
                              pool.op([kr, cs[a]], [r], lambda h, r=r, kr=kr: h.tensor_tensor(out=r[:, 2, :], in0=kr[:, 0, :], in1=sinA, op=ALU.mult))
                              pool.op([kr, cs[a]], [r], lambda h, r=r, kr=kr: h.tensor_tensor(out=r[:, 3, :], in0=kr[:, 1, :], in1=cosA, op=ALU.mult))
                              pool.op([r], [dst], lambda h, r=r, dst=dst, hd=hd: h.tensor_tensor(
                                  out=dst[:, 2 * hd, :], in0=r[:, 0, :], in1=r[:, 1, :], op=ALU.subtract))
                              pool.op([r], [dst], lambda h, r=r, dst=dst, hd=hd: h.tensor_tensor(
                                  out=dst[:, 2 * hd + 1, :], in0=r[:, 2, :], in1=r[:, 3, :], op=ALU.add))
                      sub(4)
                      for s in range(2):
                          bank = PB[6 + s]
                          v = pbf(bank)
                          trs = [(v[:, cc * 128:(cc + 1) * 128], kT[a][:, cc, s * 128:(s + 1) * 128], identb[:])
                                 for cc in range(8)]
                          transposes([bank], [kT[a], identb], trs)
                          e = act if s == 0 else dve
                          e.op([bank], [ktok[a]], lambda h, s=s, bank=bank: _copy(h, ktok[a][:, s, :], pbf(bank)))
                      sub(5)
                      flat = lambda t: t[:].rearrange("p a t -> p (a t)")
                      sw.dma([uT[a]], [DB("UT", u)], UT[u], flat(uT[a]))
                      sw.dma([qT[a]], [DB("QT", u)], QT[u], flat(qT[a]))
                      sw.dma([kT[a]], [DB("KT", u)], KT[u], flat(kT[a]))
                      sub(6)
                      for s in range(2):
                          sw.dma([ktok[a]], [DB("KK", 2 * u + s)], KK[2 * u + s], ktok[a][:, s, :])
                      sub(7)
                      sub(10 + u)
              S.barrier()

              chk()
              with ExitStack() as ph:
                  wv = sb(ph, "w1b", [128, 8, 2048], BF16)
                  load_weight_bf16(wv, w_in[l], 8, 2048, 4096)
                  wg = sb(ph, "wg", [128, 8, 2048], BF16)
                  load_weight_bf16(wg, w_gate[l], 8, 2048, 0)
                  gtile = sb(ph, "g1b", [128, D], F32)
                  bcast_row(gtile, norm_mix_g[l:l + 1, :])
                  xin = [sb(ph, f"xin{i}", [128, 2, D], F32) for i in range(2)]
                  kk = [sb(ph, f"kk{i}", [128, 2, D], BF16) for i in range(2)]
                  xn = sb(ph, "xn", [128, 2, D], BF16)
                  xnT = sb(ph, "xnT", [128, 8, 256], BF16)
                  junk = sb(ph, "junk", [128, D], BF16)
                  small = [sb(ph, f"sm{i}", [128, 2], F32) for i in range(4)]
                  vsb = [sb(ph, f"vsb{i}", [128, 2, D], BF16) for i in range(2)]
                  vdb = [sb(ph, f"vdb{i}", [128, 2, D], BF16) for i in range(2)]
                  sgb = [sb(ph, f"sgb{i}", [128, 2, D], BF16) for i in range(2)]
                  gtb = [sb(ph, f"gtb{i}", [128, 2, 2048], BF16) for i in range(2)]
                  Sst = sb(ph, "Sst", [128, 8, 256], F32)
                  Sbf = [sb(ph, f"Sbf{i}", [128, 8, 256], BF16) for i in range(2)]
                  order = []
                  for (a0, a1) in seq_ranges:
                      order += list(range(a1 - 1, a0 - 1, -1))

                  def p1b_load(i):
                      u = order[i]
                      a = i % 2
                      sp.dma([DB("X", u)], [xin[a]], xin[a][:], xrows(x_src, u))
                      for s in range(2):
                          sp.dma([DB("KK", 2 * u + s)], [kk[a]], kk[a][:, s, :], KK[2 * u + s])

                  p1b_load(0)
                  bi = 0
                  sbi = 0
                  for i, u in enumerate(order):
                      a = i % 2
                      if i + 1 < NU:
                          p1b_load(i + 1)
                      sub(30)
                      norm_T(xin[a], gtile, xn, xnT, junk, small, PB[6], PB[7], (act, dve))
                      sub(31)
                      for s in range(2):
                          for blk in range(4):
                              bank = PB[bi % 4]
                              bi += 1
                              mms = [(bank[:, :], xnT[:, kc, s * 128:(s + 1) * 128], wv[:, kc, blk * 512:(blk + 1) * 512])
                                     for kc in range(8)]
                              mm_group(bank, [xnT, wv], mms)
                              sub(40)
                              if blk < 2:
                                  act.op([bank], [vsb[a]], lambda h, s=s, blk=blk, bank=bank: h.activation(
                                      out=vsb[a][:, s, blk * 512:(blk + 1) * 512], in_=bank[:, :], func=AF.Copy))
                                  sub(41)
                                  for hh in range(2):
                                      hd = blk * 2 + hh
                                      act.op([bank, dsc], [vdb[a]], lambda h, s=s, hd=hd, hh=hh, bank=bank: h.activation(
                                          out=vdb[a][:, s, hd * 256:(hd + 1) * 256], in_=bank[:, hh * 256:(hh + 1) * 256],
                                          func=AF.Copy, scale=dsc[:, VD_B, hd:hd + 1]))
                                      sub(42)
                              else:
                                  act.op([bank], [sgb[a]], lambda h, s=s, blk=blk, bank=bank: h.activation(
                                      out=sgb[a][:, s, (blk - 2) * 512:(blk - 1) * 512], in_=bank[:, :], func=AF.Silu))
                          for blk in range(4):
                              bank = PB[bi % 4]
                              bi += 1
                              mms = [(bank[:, :], xnT[:, kc, s * 128:(s + 1) * 128], wg[:, kc, blk * 512:(blk + 1) * 512])
                                     for kc in range(8)]
                              mm_group(bank, [xnT, wg], mms)
                              act.op([bank], [gtb[a]], lambda h, s=s, blk=blk, bank=bank: h.activation(
                                  out=gtb[a][:, s, blk * 512:(blk + 1) * 512], in_=bank[:, :], func=AF.Sigmoid))
                      sub(32)
                      for s in range(2):
                          c = 2 * u + s
                          sw.dma([vsb[a]], [DB("VV", c)], VV[c], vsb[a][:, s, :])
                          sw.dma([sgb[a]], [DB("SG", c)], SG[c], sgb[a][:, s, :])
                          sw.dma([gtb[a]], [DB("GT", c)], GT[c], gtb[a][:, s, :])
                      sub(33)
                      if u == units[u][1] - 1:
                          pool.op([], [Sst], lambda h: h.memset(Sst[:], 0.0))
                          pool.op([], [Sbf[sbi % 2]], lambda h, t=Sbf[sbi % 2]: h.memset(t[:], 0.0))
                      for s in (1, 0):
                          c = 2 * u + s
                          cur = Sbf[sbi % 2]
                          nxt = Sbf[(sbi + 1) % 2]
                          sbi += 1
                          sw.dma([cur], [DB("SB", c)], SB[c], cur[:].rearrange("p a e -> p (a e)"))
                          for hp in range(2):
                              for hh in range(2):
                                  hd = hp * 2 + hh
                                  bank = PB[4 + hh]
                                  for dc in range(2):
                                      mm_group(bank, [kk[a], vdb[a]], [(bank[:, dc * 256:(dc + 1) * 256],
                                               kk[a][:, s, (2 * hd + dc) * 128:(2 * hd + dc + 1) * 128],
                                               vdb[a][:, s, hd * 256:(hd + 1) * 256])])
                                  dve.op([bank, dsc, Sst], [Sst], lambda h, hd=hd, bank=bank: h.scalar_tensor_tensor(
                                      out=Sst[:, 2 * hd:2 * hd + 2, :], in0=Sst[:, 2 * hd:2 * hd + 2, :],
                                      scalar=dsc[:, CD_B, hd:hd + 1],
                                      in1=bank[:, :].rearrange("p (a e) -> p a e", a=2), op0=ALU.mult, op1=ALU.add))
                          pool.op([Sst], [nxt], lambda h, nxt=nxt: h.tensor_copy(out=nxt[:], in_=Sst[:]))
                          sub(34)
              S.barrier()

              chk()
              with ExitStack() as ph:
                  wco = sb(ph, "wco", [128, 8, D], BF16)
                  load_weight_bf16(wco, w_conv_out[l], 8, D, 0)
                  praw = sb(ph, "praw", [34, D], F32)
                  sp.dma([], [praw], praw[0:31, :], conv_w[l])
                  sp.dma([], [praw], praw[31:32, :], conv_b[l:l + 1, :])
                  sp.dma([], [praw], praw[32:33, :], conv_ln_g[l:l + 1, :])
                  sp.dma([], [praw], praw[33:34, :], conv_ln_b[l:l + 1, :])
                  cpar = sb(ph, "cpar", [128, 8, 34], F32)
                  for cc in range(8):
                      bank = PB[cc]
                      transposes([bank], [praw, cst], [(bank[:, 0:34], praw[:, cc * 128:(cc + 1) * 128], cst[0:34, 0:34])])
                      dve.op([bank], [cpar], lambda h, cc=cc, bank=bank: h.tensor_copy(out=cpar[:, cc, :], in_=bank[:, 0:34]))
                  diag = sb(ph, "diag", [128, 8, 31, 128], BF16)
                  for cc in range(8):
                      for k in range(31):
                          e = dve if (k % 2 == 0) else pool
                          e.op([cst, cpar], [diag], lambda h, cc=cc, k=k: h.tensor_scalar(
                              out=diag[:, cc, k, :], in0=cst[:, 0:128], scalar1=cpar[:, cc, k:k + 1], scalar2=None,
                              op0=ALU.mult))
                  uh = [sb(ph, f"uh{i}", [128, 8, 286], BF16) for i in range(2)]
                  uhl = [Buf(f"uhl{i}") for i in range(2)]
                  uhr = [Buf(f"uhr{i}") for i in range(2)]
                  cv = sb(ph, "cv", [128, 8, 256], F32)
                  sq2 = sb(ph, "sq2", [128, 8, 256], F32)
                  mean = sb(ph, "mean", [128, 256], F32)
                  m2 = sb(ph, "m2", [128, 256], F32)
                  rstd = sb(ph, "rstdc", [128, 256], F32)
                  uln = [sb(ph, f"uln{i}", [128, 8, 256], BF16) for i in range(2)]
                  ycb = [sb(ph, f"ycb{i}", [128, 2, D], BF16) for i in range(2)]
                  onesm = cst[:, 800:928]

                  def p2c_load(u):
                      a = u % 2
                      u_lo, u_hi, _ = units[u]
                      UTu = lambda uu: UT[uu].rearrange("p (a t) -> p a t", a=8)
                      sp.dma([DB("UT", u)], [uh[a]], uh[a][:, :, 15:271], UTu(u))
                      if u > u_lo:
                          sp.dma([DB("UT", u - 1)], [uhl[a]], uh[a][:, :, 0:15], UTu(u - 1)[:, :, 241:256])
                      else:
                          pool.op([], [uhl[a]], lambda h: h.memset(uh[a][:, :, 0:15], 0.0))
                      if u + 1 < u_hi:
                          sp.dma([DB("UT", u + 1)], [uhr[a]], uh[a][:, :, 271:286], UTu(u + 1)[:, :, 0:15])
                      else:
                          pool.op([], [uhr[a]], lambda h: h.memset(uh[a][:, :, 271:286], 0.0))

                  p2c_load(0)
                  for u in range(NU):
                      a = u % 2
                      if u + 1 < NU:
                          p2c_load(u + 1)
                      for cc in range(8):
                          bank = PB[cc // 2]
                          half = cc % 2
                          mms = [(bank[:, half * 256:(half + 1) * 256], diag[:, cc, k, :], uh[a][:, cc, k:k + 256])
                                 for k in range(31)]
                          mm_group(bank, [diag, uh[a], uhl[a], uhr[a]], mms)
                          act.op([bank, cpar], [cv], lambda h, cc=cc, bank=bank, half=half: h.activation(
                              out=cv[:, cc, :], in_=bank[:, half * 256:(half + 1) * 256], func=AF.Identity,
                              bias=cpar[:, cc, 31:32]))
                          act.op([cv], [sq2], lambda h, cc=cc: h.activation(out=sq2[:, cc, :], in_=cv[:, cc, :], func=AF.Square))
                      bm = PB[4]
                      mm_group(bm, [cst, cv], [(bm[:, 0:256], onesm, cv[:, cc, :]) for cc in range(8)])
                      mm_group(bm, [cst, sq2], [(bm[:, 256:512], onesm, sq2[:, cc, :]) for cc in range(8)])
                      act.op([bm], [mean], lambda h: h.activation(out=mean[:], in_=bm[:, 0:256], func=AF.Copy))
                      act.op([bm], [m2], lambda h: h.activation(out=m2[:], in_=bm[:, 0:256], func=AF.Square))
                      dve.op([bm, m2], [m2], lambda h: h.tensor_tensor(out=m2[:], in0=bm[:, 256:512], in1=m2[:], op=ALU.subtract))
                      dve.op([m2], [m2], lambda h: h.tensor_scalar(out=m2[:], in0=m2[:], scalar1=EPS, scalar2=None, op0=ALU.add))
                      act.op([m2], [m2], lambda h: h.activation(out=m2[:], in_=m2[:], func=AF.Sqrt))
                      dve.op([m2], [rstd], lambda h: h.reciprocal(out=rstd[:], in_=m2[:]))
                      bc = lambda t: t[:].unsqueeze(1).to_broadcast([128, 8, 256])
                      pool.op([cv, mean], [cv], lambda h: h.tensor_tensor(out=cv[:], in0=cv[:], in1=bc(mean), op=ALU.subtract))
                      dve.op([cv, rstd], [cv], lambda h: h.tensor_tensor(out=cv[:], in0=cv[:], in1=bc(rstd), op=ALU.mult))
                      for cc in range(8):
                          act.op([cv, cpar], [uln[a]], lambda h, cc=cc: h.activation(
                              out=uln[a][:, cc, :], in_=cv[:, cc, :], func=AF.Silu,
                              scale=cpar[:, cc, 32:33], bias=cpar[:, cc, 33:34]))
                      for s in range(2):
                          for blk in range(2):
                              bank = PB[5 + (s * 2 + blk) % 3]
                              mms = [(bank[:, :], uln[a][:, cc, s * 128:(s + 1) * 128], wco[:, cc, blk * 512:(blk + 1) * 512])
                                     for cc in range(8)]
                              mm_group(bank, [uln[a], wco], mms)
                              dve.op([bank], [ycb[a]], lambda h, s=s, blk=blk, bank=bank: h.tensor_copy(
                                  out=ycb[a][:, s, blk * 512:(blk + 1) * 512], in_=bank[:, :]))
                          sw.dma([ycb[a]], [DB("YC", 2 * u + s)], YC[2 * u + s], ycb[a][:, s, :])
              S.barrier()

              chk()
              with ExitStack() as ph:
                  decT = sb(ph, "decT", [128, 4, 128], F32)
                  dtmp = sb(ph, "dtmp", [128, 2, 128], F32)
                  for hd in range(4):
                      dve.op([cst, lgfb], [dtmp], lambda h, hd=hd: h.tensor_scalar(
                          out=dtmp[:, 0, :], in0=cst[:, 256:384], scalar1=lgfb[:, hd:hd + 1], scalar2=None, op0=ALU.mult))
                      dve.op([cst, lgbb], [dtmp], lambda h, hd=hd: h.tensor_scalar(
                          out=dtmp[:, 1, :], in0=cst[:, 384:512], scalar1=lgbb[:, hd:hd + 1], scalar2=None, op0=ALU.mult))
                      act.op([dtmp], [dtmp], lambda h: h.activation(out=dtmp[:], in_=dtmp[:], func=AF.Exp))
                      dve.op([dtmp, cst], [dtmp], lambda h: h.tensor_tensor(
                          out=dtmp[:, 0, :], in0=dtmp[:, 0, :], in1=cst[:, 512:640], op=ALU.mult))
                      dve.op([dtmp, cst], [dtmp], lambda h: h.tensor_tensor(
                          out=dtmp[:, 1, :], in0=dtmp[:, 1, :], in1=cst[:, 640:768], op=ALU.mult))
                      dve.op([dtmp], [decT], lambda h, hd=hd: h.tensor_tensor(
                          out=decT[:, hd, :], in0=dtmp[:, 0, :], in1=dtmp[:, 1, :], op=ALU.add))
                  wro = sb(ph, "wro", [128, 8, D], BF16)
                  load_weight_bf16(wro, w_ret_out[l], 8, D, 0)
                  wo = sb(ph, "wo", [128, 8, D], BF16)
                  load_weight_bf16(wo, w_o[l], 8, D, 0)
                  xin = [sb(ph, f"xin{i}", [128, 2, D], F32) for i in range(2)]
                  qTl = [sb(ph, f"qTl{i}", [128, 8, 256], BF16) for i in range(2)]
                  kTl = [sb(ph, f"kTl{i}", [128, 8, 256], BF16) for i in range(2)]
                  kk = [sb(ph, f"kk{i}", [128, 2, D], BF16) for i in range(2)]
                  vv = [sb(ph, f"vv{i}", [128, 2, D], BF16) for i in range(2)]
                  sgl = [sb(ph, f"sgl{i}", [128, 2, D], BF16) for i in range(2)]
                  gtl = [sb(ph, f"gtl{i}", [128, 2, 2048], BF16) for i in range(2)]
                  ycl = [sb(ph, f"ycl{i}", [128, 2, D], BF16) for i in range(2)]
                  sbw = [sb(ph, f"sbw{i}", [128, 2, 8 * 256], BF16) for i in range(2)]
                  Sst = sb(ph, "Sstf", [128, 8, 256], F32)
                  Sbf = sb(ph, "Sbff", [128, 8, 256], BF16)
                  PT = [sb(ph, f"PT{i}", [128, 128], BF16) for i in range(2)]
                  tf = [sb(ph, f"tf{i}", [128, 256], F32) for i in range(2)]
                  o_sb = sb(ph, "o_sb", [128, D], F32)
                  vdf = sb(ph, "vdf", [128, D], BF16)
                  gn = [sb(ph, f"gn{i}", [128, 4], F32) for i in range(4)]
                  junk = sb(ph, "junk", [128, 256], BF16)
                  om = sb(ph, "om", [128, D], BF16)
                  omT = sb(ph, "omT", [128, 8, 256], BF16)
                  m1 = sb(ph, "m1", [128, D], F32)
                  mg = sb(ph, "mg", [128, D], BF16)
                  mgT = sb(ph, "mgT", [128, 8, 256], BF16)
                  xo = [sb(ph, f"xo{i}", [128, 2, D], F32) for i in range(2)]

                  def p3a_load(u):
                      a = u % 2
                      sp.dma([DB("X", u)], [xin[a]], xin[a][:], xrows(x_src, u))
                      sp.dma([DB("QT", u)], [qTl[a]], qTl[a][:].rearrange("p a t -> p (a t)"), QT[u])
                      sp.dma([DB("KT", u)], [kTl[a]], kTl[a][:].rearrange("p a t -> p (a t)"), KT[u])
                      for s in range(2):
                          c = 2 * u + s
                          sp.dma([DB("KK", c)], [kk[a]], kk[a][:, s, :], KK[c])
                          sp.dma([DB("VV", c)], [vv[a]], vv[a][:, s, :], VV[c])
                          sp.dma([DB("SG", c)], [sgl[a]], sgl[a][:, s, :], SG[c])
                          sp.dma([DB("GT", c)], [gtl[a]], gtl[a][:, s, :], GT[c])
                          sp.dma([DB("YC", c)], [ycl[a]], ycl[a][:, s, :], YC[c])
                          sp.dma([DB("SB", c)], [sbw[a]], sbw[a][:, s, :], SB[c])

                  p3a_load(0)
                  hi = 0
                  for u in range(NU):
                      a = u % 2
                      if u + 1 < NU:
                          p3a_load(u + 1)
                      if u == units[u][0]:
                          pool.op([], [Sst], lambda h: h.memset(Sst[:], 0.0))
                          pool.op([], [Sbf], lambda h: h.memset(Sbf[:], 0.0))
                      for s in range(2):
                          ts = slice(s * 128, (s + 1) * 128)
                          sbv = sbw[a][:, s, :].rearrange("p (a e) -> p a e", a=8)
                          for hd in range(4):
                              bA = PB[hi % 2]
                              bO = PB[2 + hi % 2]
                              bC = PB[4 + hi % 2]
                              pt = PT[hi % 2]
                              t_f = tf[hi % 2]
                              hi += 1
                              mm_group(bA, [kTl[a], qTl[a]], [(bA[:, 0:128], kTl[a][:, 2 * hd + dc, ts], qTl[a][:, 2 * hd + dc, ts])
                                                             for dc in range(2)])
                              dve.op([bA, decT], [pt], lambda h, hd=hd, bA=bA, pt=pt: h.tensor_tensor(
                                  out=pt[:], in0=bA[:, 0:128], in1=decT[:, hd, :], op=ALU.mult))
                              mm_group(bO, [pt, vv[a]], [(bO[:, 0:256], pt[:], vv[a][:, s, hd * 256:(hd + 1) * 256])])
                              mm_group(bO, [qTl[a], Sbf], [(bO[:, 256:512], qTl[a][:, 2 * hd + dc, ts], Sbf[:, 2 * hd + dc, :])
                                                          for dc in range(2)])
                              mm_group(bC, [qTl[a], sbw[a]], [(bC[:, 0:256], qTl[a][:, 2 * hd + dc, ts], sbv[:, 2 * hd + dc, :])
                                                             for dc in range(2)])
                              act.op([bO, dsc], [t_f], lambda h, hd=hd, bO=bO, t_f=t_f: h.activation(
                                  out=t_f[:], in_=bO[:, 256:512], func=AF.Copy, scale=dsc[:, A_F, hd:hd + 1]))
                              dve.op([bC, dsc, t_f], [t_f], lambda h, hd=hd, bC=bC, t_f=t_f: h.scalar_tensor_tensor(
                                  out=t_f[:], in0=bC[:, 0:256], scalar=dsc[:, A_B, hd:hd + 1], in1=t_f[:],
                                  op0=ALU.mult, op1=ALU.add))
                              dve.op([bO, t_f], [o_sb], lambda h, hd=hd, bO=bO, t_f=t_f: h.tensor_tensor(
                                  out=o_sb[:, hd * 256:(hd + 1) * 256], in0=bO[:, 0:256], in1=t_f[:], op=ALU.add))
                          for hd in range(4):
                              pool.op([vv[a], dsc], [vdf], lambda h, hd=hd, s=s: h.tensor_scalar(
                                  out=vdf[:, hd * 256:(hd + 1) * 256], in0=vv[a][:, s, hd * 256:(hd + 1) * 256],
                                  scalar1=dsc[:, VD_F, hd:hd + 1], scalar2=None, op0=ALU.mult))
                          for hd in range(4):
                              bank = PB[6 + hd % 2]
                              for dc in range(2):
                                  mm_group(bank, [kk[a], vdf], [(bank[:, dc * 256:(dc + 1) * 256],
                                           kk[a][:, s, (2 * hd + dc) * 128:(2 * hd + dc + 1) * 128],
                                           vdf[:, hd * 256:(hd + 1) * 256])])
                              dve.op([bank, dsc, Sst], [Sst], lambda h, hd=hd, bank=bank: h.scalar_tensor_tensor(
                                  out=Sst[:, 2 * hd:2 * hd + 2, :], in0=Sst[:, 2 * hd:2 * hd + 2, :],
                                  scalar=dsc[:, CD_F, hd:hd + 1],
                                  in1=bank[:, :].rearrange("p (a e) -> p a e", a=2), op0=ALU.mult, op1=ALU.add))
                          pool.op([Sst], [Sbf], lambda h: h.tensor_copy(out=Sbf[:], in_=Sst[:]))
                          for hd in range(4):
                              act.op([o_sb], [junk, gn[0]], lambda h, hd=hd: h.activation(
                                  out=junk[:], in_=o_sb[:, hd * 256:(hd + 1) * 256], func=AF.Square,
                                  accum_out=gn[0][:, hd:hd + 1]))
                          dve.op([gn[0]], [gn[1]], lambda h: h.tensor_scalar(
                              out=gn[1][:], in0=gn[0][:], scalar1=1.0 / 256.0, scalar2=EPS, op0=ALU.mult, op1=ALU.add))
                          act.op([gn[1]], [gn[2]], lambda h: h.activation(out=gn[2][:], in_=gn[1][:], func=AF.Sqrt))
                          dve.op([gn[2]], [gn[3]], lambda h: h.reciprocal(out=gn[3][:], in_=gn[2][:]))
                          for hd in range(4):
                              dve.op([o_sb, gn[3], sgl[a]], [om], lambda h, hd=hd, s=s: h.scalar_tensor_tensor(
                                  out=om[:, hd * 256:(hd + 1) * 256], in0=o_sb[:, hd * 256:(hd + 1) * 256],
                                  scalar=gn[3][:, hd:hd + 1], in1=sgl[a][:, s, hd * 256:(hd + 1) * 256],
                                  op0=ALU.mult, op1=ALU.mult))
                          bT = PB[6 + s]
                          v = pbf(bT)
                          transposes([bT], [om, identb], [(v[:, cc * 128:(cc + 1) * 128], om[:, cc * 128:(cc + 1) * 128], identb[:])
                                                          for cc in range(8)])
                          act.op([bT], [omT], lambda h, s=s, bT=bT: h.activation(
                              out=omT[:, :, s * 128:(s + 1) * 128], in_=pbf(bT).rearrange("p (a t) -> p a t", a=8), func=AF.Copy))
                          for blk in range(2):
                              bank = PB[blk]
                              mm_group(bank, [omT, wro], [(bank[:, :], omT[:, cc, ts], wro[:, cc, blk * 512:(blk + 1) * 512])
                                                         for cc in range(8)])
                              cs_ = slice(blk * 512, (blk + 1) * 512)
                              pool.op([ycl[a], gtl[a]], [m1], lambda h, s=s, cs_=cs_: h.tensor_tensor(
                                  out=m1[:, cs_], in0=ycl[a][:, s, cs_], in1=gtl[a][:, s, cs_], op=ALU.mult))
                              dve.op([bank, gtl[a], m1], [mg], lambda h, s=s, blk=blk, bank=bank, cs_=cs_: h.tensor_tensor(
                                  out=mg[:, cs_], in0=bank[:, :], in1=gtl[a][:, s, 1024 + blk * 512:1024 + (blk + 1) * 512],
                                  op=ALU.mult))
                              pool.op([m1, mg], [mg], lambda h, cs_=cs_: h.tensor_tensor(
                                  out=mg[:, cs_], in0=mg[:, cs_], in1=m1[:, cs_], op=ALU.add))
                          bT2 = PB[4 + s]
                          v2 = pbf(bT2)
                          transposes([bT2], [mg, identb], [(v2[:, cc * 128:(cc + 1) * 128], mg[:, cc * 128:(cc + 1) * 128], identb[:])
                                                           for cc in range(8)])
                          act.op([bT2], [mgT], lambda h, s=s, bT2=bT2: h.activation(
                              out=mgT[:, :, s * 128:(s + 1) * 128], in_=pbf(bT2).rearrange("p (a t) -> p a t", a=8), func=AF.Copy))
                          for blk in range(2):
                              bank = PB[2 + blk]
                              mm_group(bank, [mgT, wo], [(bank[:, :], mgT[:, cc, ts], wo[:, cc, blk * 512:(blk + 1) * 512])
                                                        for cc in range(8)])
                              dve.op([bank, xin[a]], [xo[a]], lambda h, s=s, blk=blk, bank=bank: h.tensor_tensor(
                                  out=xo[a][:, s, blk * 512:(blk + 1) * 512], in0=bank[:, :],
                                  in1=xin[a][:, s, blk * 512:(blk + 1) * 512], op=ALU.add))
                      sw.dma([xo[a]], [DB("XM", u)], xrows(XM, u), xo[a][:])
              S.barrier()

              chk()
              with ExitStack() as ph:
                  wq = sb(ph, "wq", [128, 8, 2048], BF16)
                  load_weight_bf16(wq, peer_wq[l], 8, 2048, 0)
                  gtile = sb(ph, "g3b", [128, D], F32)
                  bcast_row(gtile, norm_ffn_g[l:l + 1, :])
                  if last and do_final:
                      gfin = sb(ph, "gfin", [128, D], F32)
                      bcast_row(gfin, final_g[0:1, :])
                  skT = sb(ph, "skT", [128, 16, 128], BF16)
                  with ExitStack() as ph2:
                      skraw = sb(ph2, "skraw", [128, 16, 128], F32)
                      sp.dma([], [skraw], skraw[:], peer_sk[l].rearrange("h p k d -> k (h p) d"))
                      for cq in range(16):
                          bank = PB[cq % 8]
                          transposes([bank], [skraw, cst], [(bank[:, 0:128], skraw[:, cq, :], ident_f)])
                          dve.op([bank], [skT], lambda h, cq=cq, bank=bank: h.tensor_copy(out=skT[:, cq, :], in_=bank[:, 0:128]))
                      S.barrier()
                  xw = sb(ph, "xw", [128, 2, D], F32)
                  xnT = [sb(ph, f"xnT{i}", [128, 8, 256], BF16) for i in range(2)]
                  small = [sb(ph, f"sm{i}", [128, 2], F32) for i in range(4)]
                  qTp = sb(ph, "qTp", [128, 16, 256], BF16)
                  sc_ = [sb(ph, f"sc{i}", [128, 16, 128], F32) for i in range(2)]
                  sc2 = sb(ph, "sc2", [128, 16, 128], F32)
                  xn = TT(sc2.t[:].rearrange("p a k -> p (a k)").bitcast(BF16)[:, 0:2 * D].rearrange("p (s d) -> p s d", s=2),
                          "xn_alias")
                  xn.b = sc2.b
                  junk = (xn[:, 0, :], xn)
                  sv = sb(ph, "sv", [128, 16, 16], F32)
                  si = sb(ph, "si", [128, 16, 16], U32)
                  sif = sb(ph, "sif", [128, 16, 16], F32)
                  cand_ = []
                  for i_ in range(2):
                      c_ = TT(sc_[i_].t[:].rearrange("p (h two) k -> p h (two k)", two=2), f"cand_alias{i_}")
                      c_.b = sc_[i_].b
                      cand_.append(c_)
                  cand2 = TT(sc2.t[:].rearrange("p (h two) k -> p h (two k)", two=2), "cand2_alias")
                  cand2.b = sc2.b
                  s16_ = [sb(ph, f"s16_{i}", [128, 8, 16], F32) for i in range(2)]
                  ci = sb(ph, "ci", [128, 8, 16], U32)
                  cia = sb(ph, "cia", [128, 8, 16], U32)
                  cib = sb(ph, "cib", [128, 8, 16], U32)
                  caf = sb(ph, "caf", [128, 8, 16], F32)
                  cbf = sb(ph, "cbf", [128, 8, 16], F32)
                  eq = TT(sc2.t[:].rearrange("p (h two) (a b) -> p h (two a) b", two=2, b=16), "eq_alias")
                  eq.b = sc2.b
                  rowi_ = [[sb(ph, f"rowi{j}_{i}", [128, 128], F32) for i in range(3)] for j in range(2)]
                  gsum = sb(ph, "gsum", [128, 8], F32)
                  colT = [sb(ph, f"colT{i}", [128, 256], F32) for i in range(3)]
                  TBK = 4
                  P1 = [sb(ph, f"P1_{i}", [128, TBK, 128], BF16) for i in range(2)]
                  P2 = [sb(ph, f"P2_{i}", [128, TBK, 128], BF16) for i in range(2)]
                  GTs = sb(ph, "GTs", [128, 128, 256], BF16)
                  hT = [sb(ph, f"hT{i}", [128, 256], BF16) for i in range(3)]
                  gh = [sb(ph, f"gh{i}", [128, 256], BF16) for i in range(4)]
                  P2ENG = pool if CFG.get("P2POOL") else dve
                  NSB = CFG["NSB"]
                  LOOK = CFG["LOOK"]
                  ub = [sb(ph, f"ub{i}", [128, 8, 256], BF16) for i in range(NSB)]
                  vbuf = [sb(ph, f"vbuf{i}", [128, 2, D], BF16) for i in range(NSB)]

                  def route_A(u):
                      a = u % 2
                      sp.dma([DB("XM", u)], [xw], xw[:], xrows(XM, u))
                      norm_T(xw, gtile, xn, xnT[a], junk, small, PB[6], PB[7], (act, dve))
                      for cq in range(16):
                          bank = PB[4 + cq % 2]
                          mm_group(bank, [wq, xnT[a]], [(bank[:, 0:256], wq[:, kc, cq * 128:(cq + 1) * 128], xnT[a][:, kc, :])
                                                       for kc in range(8)])
                          act.op([bank], [qTp], lambda h, cq=cq, bank=bank: h.activation(
                              out=qTp[:, cq, :], in_=bank[:, 0:256], func=AF.Copy))
                      for s in range(2):
                          ts = slice(s * 128, (s + 1) * 128)
                          sc = sc_[s]
                          for q4 in range(4):
                              bank = PB[4 + q4 % 2]
                              for j in range(4):
                                  cq = q4 * 4 + j
                                  mm_group(bank, [qTp, skT], [(bank[:, j * 128:(j + 1) * 128], qTp[:, cq, ts], skT[:, cq, :])])
                              act.op([bank], [sc], lambda h, q4=q4, bank=bank, sc=sc: h.activation(
                                  out=sc[:, q4 * 4:(q4 + 1) * 4, :], in_=bank[:, :].rearrange("p (a k) -> p a k", a=4),
                                  func=AF.Copy))
                      for s in range(2):
                          if "A2" in SKIP:
                              break
                          ts = slice(s * 128, (s + 1) * 128)
                          sc = sc_[s]
                          cand = cand_[s]
                          s16 = s16_[s]
                          rowi = rowi_[s]
                          for cq in range(16):
                              dve.op([sc], [sv], lambda h, cq=cq: h.max(out=sv[:, cq, 0:8], in_=sc[:, cq, :]))
                              dve.op([sc, sv], [si], lambda h, cq=cq: h.max_index(out=si[:, cq, 0:8], in_max=sv[:, cq, 0:8], in_values=sc[:, cq, :]))
                              dve.op([sc, sv], [sc2], lambda h, cq=cq: h.match_replace(
                                  out=sc2[:, cq, :], in_to_replace=sv[:, cq, 0:8], in_values=sc[:, cq, :], imm_value=NEG))
                              dve.op([sc2], [sv], lambda h, cq=cq: h.max(out=sv[:, cq, 8:16], in_=sc2[:, cq, :]))
                              dve.op([sc2, sv], [si], lambda h, cq=cq: h.max_index(out=si[:, cq, 8:16], in_max=sv[:, cq, 8:16], in_values=sc2[:, cq, :]))
                          dve.op([si], [sif], lambda h: h.tensor_copy(out=sif[:], in_=si[:]))
                          sv4 = sv[:].rearrange("p (h two) k -> p h two k", two=2)
                          sif4 = sif[:].rearrange("p (h two) k -> p h two k", two=2)
                          cand4 = cand[:].rearrange("p h (a b) -> p h a b", a=16)
                          dve.op([sv], [cand], lambda h: h.tensor_tensor(
                              out=cand4, in0=sv4[:, :, 0, :].unsqueeze(3).to_broadcast([128, 8, 16, 16]),
                              in1=sv4[:, :, 1, :].unsqueeze(2).to_broadcast([128, 8, 16, 16]), op=ALU.add))
                          for hd in range(8):
                              dve.op([cand], [s16], lambda h, hd=hd: h.max(out=s16[:, hd, 0:8], in_=cand[:, hd, :]))
                              dve.op([cand, s16], [ci], lambda h, hd=hd: h.max_index(out=ci[:, hd, 0:8], in_max=s16[:, hd, 0:8], in_values=cand[:, hd, :]))
                              dve.op([cand, s16], [cand2], lambda h, hd=hd: h.match_replace(
                                  out=cand2[:, hd, :], in_to_replace=s16[:, hd, 0:8], in_values=cand[:, hd, :], imm_value=NEG))
                              dve.op([cand2], [s16], lambda h, hd=hd: h.max(out=s16[:, hd, 8:16], in_=cand2[:, hd, :]))
                              dve.op([cand2, s16], [ci], lambda h, hd=hd: h.max_index(out=ci[:, hd, 8:16], in_max=s16[:, hd, 8:16], in_values=cand2[:, hd, :]))
                          dve.op([ci], [cia], lambda h: h.tensor_single_scalar(out=cia[:], in_=ci[:], scalar=4, op=ALU.logical_shift_right))
                          dve.op([ci], [cib], lambda h: h.tensor_single_scalar(out=cib[:], in_=ci[:], scalar=15, op=ALU.bitwise_and))
                          dve.op([cia], [caf], lambda h: h.tensor_copy(out=caf[:], in_=cia[:]))
                          dve.op([cib], [cbf], lambda h: h.tensor_copy(out=cbf[:], in_=cib[:]))
                          io4 = iota16.unsqueeze(1).unsqueeze(1).to_broadcast([128, 8, 16, 16])
                          for which, (cf, half) in enumerate(((caf, 0), (cbf, 1))):
                              dst3 = rowi[which][:].rearrange("p (h k) -> p h k", h=8)
                              dve.op([cf, cst], [eq], lambda h, cf=cf: h.tensor_tensor(
                                  out=eq[:], in0=cf[:].unsqueeze(3).to_broadcast([128, 8, 16, 16]), in1=io4, op=ALU.is_equal))
                              dve.op([eq, sif], [eq], lambda h, half=half: h.tensor_tensor(
                                  out=eq[:], in0=eq[:], in1=sif4[:, :, half, :].unsqueeze(2).to_broadcast([128, 8, 16, 16]),
                                  op=ALU.mult))
                              dve.op([eq], [rowi[which]], lambda h, dst3=dst3: h.tensor_reduce(
                                  out=dst3, in_=eq[:], axis=AX.X, op=ALU.add))

                  def route_B(u):
                      for s in range(2):
                          ts = slice(s * 128, (s + 1) * 128)
                          s16 = s16_[s]
                          rowi = rowi_[s]
                          g3 = rowi[2][:].rearrange("p (h k) -> p h k", h=8)
                          dve.op([s16], [rowi[2]], lambda h: h.tensor_tensor(
                              out=g3, in0=s16[:], in1=s16[:, :, 0:1].to_broadcast([128, 8, 16]), op=ALU.subtract))
                          act.op([rowi[2]], [rowi[2]], lambda h: h.activation(out=rowi[2][:], in_=rowi[2][:], func=AF.Exp))
                          dve.op([rowi[2]], [gsum], lambda h: h.tensor_reduce(out=gsum[:], in_=g3, axis=AX.X, op=ALU.add))
                          dve.op([gsum], [gsum], lambda h: h.reciprocal(out=gsum[:], in_=gsum[:]))
                          dve.op([rowi[2], gsum], [rowi[2]], lambda h: h.tensor_tensor(
                              out=g3, in0=g3, in1=gsum[:].unsqueeze(2).to_broadcast([128, 8, 16]), op=ALU.mult))
                          for w3 in range(3):
                              bank = PB[6 + w3 % 2]
                              transposes([bank], [rowi[w3], cst], [(bank[:, 0:128], rowi[w3][:], ident_f)])
                              act.op([bank], [colT[w3]], lambda h, w3=w3, bank=bank, ts=ts: h.activation(
                                  out=colT[w3][:, ts], in_=bank[:, 0:128], func=AF.Copy))
                      bi = 0
                      for t0 in range(0, 256, TBK):
                          p1 = P1[(t0 // TBK) % 2]
                          p2 = P2[(t0 // TBK) % 2]
                          for tt in range(TBK):
                              t = t0 + tt
                              dve.op([iotab, colT[0], colT[2]], [p1], lambda h, t=t, tt=tt, p1=p1: h.tensor_scalar(
                                  out=p1[:, tt, :], in0=iotab[:], scalar1=colT[0][:, t:t + 1], scalar2=colT[2][:, t:t + 1],
                                  op0=ALU.is_equal, op1=ALU.mult))
                              P2ENG.op([iotab, colT[1]], [p2], lambda h, t=t, tt=tt, p2=p2: h.tensor_scalar(
                                  out=p2[:, tt, :], in0=iotab[:], scalar1=colT[1][:, t:t + 1], scalar2=None,
                                  op0=ALU.is_equal))
                          for q in range(TBK // 4):
                              bank = PB[4 + bi % 4]
                              bi += 1
                              for j in range(4):
                                  tt = q * 4 + j
                                  mm_group(bank, [p1, p2], [(bank[:, j * 128:(j + 1) * 128], p2[:, tt, :], p1[:, tt, :])])
                              tq = t0 + q * 4
                              act.op([bank], [GTs], lambda h, bank=bank, tq=tq: h.activation(
                                  out=GTs[:, :, tq:tq + 4].transpose([0, 2, 1]),
                                  in_=bank[:, :].rearrange("p (t i) -> p t i", t=4), func=AF.Copy))

                  def dense(u):
                      a = u % 2
                      steps = []
                      for g in range(64):
                          for c in range(2):
                              steps.append((g, c))

                      def emit_load(g):
                          if "dload" in SKIP and g >= NSB:
                              return
                          sp.dma([DB("UTS", (l, g))], [ub[g % NSB]], ub[g % NSB][:].rearrange("p a e -> p (a e)"), UTS[l, g])
                          sp.dma([DB("VS", (l, g))], [vbuf[g % NSB]], vbuf[g % NSB][:].rearrange("p a e -> p (a e)"), VS[l, g])

                      def emit_H(i):
                          g, c = steps[i]
                          bank = PB[4 + i % (LOOK + 1)]
                          mm_group(bank, [ub[g % NSB], xnT[a]], [(bank[:, 0:256], ub[g % NSB][:, kc, c * 128:(c + 1) * 128], xnT[a][:, kc, :])
                                                                for kc in range(8)])
                          h_ = hT[i % 3]
                          g_ = gh[i % 4]
                          act.op([bank], [h_], lambda h, bank=bank, h_=h_: h.activation(out=h_[:], in_=bank[:, 0:256], func=AF.Gelu))
                          (dve if CFG.get("GHENG") else pool).op([h_, GTs], [g_], lambda h, h_=h_, g_=g_, i1=2 * g + c: h.tensor_tensor(
                              out=g_[:], in0=h_[:], in1=GTs[:, i1, :], op=ALU.mult))

                      def emit_V(i):
                          g, c = steps[i]
                          g_ = gh[i % 4]
                          for s in range(2):
                              for blk in range(2):
                                  bank = PB[s * 2 + blk]
                                  pe.op([g_, vbuf[g % NSB]], [bank], lambda h, s=s, blk=blk, bank=bank, g_=g_, g=g, c=c, i=i: h.matmul(
                                      bank[:, :], lhsT=g_[:, s * 128:(s + 1) * 128], rhs=vbuf[g % NSB][:, c, blk * 512:(blk + 1) * 512],
                                      start=(i == 0), stop=(i == len(steps) - 1)))

                      for g0 in range(NSB - 1):
                          emit_load(g0)
                      for i0 in range(LOOK):
                          emit_H(i0)
                      for i in range(len(steps)):
                          g, c = steps[i]
                          if c == 0 and g + NSB - 1 < 64:
                              emit_load(g + NSB - 1)
                          if i + LOOK < len(steps):
                              emit_H(i + LOOK)
                          emit_V(i)

                  def finalize(u):
                      xo = xw
                      yo = xw
                      sp.dma([DB("XM", u)], [xw], xw[:], xrows(XM, u))
                      for s in range(2):
                          for blk in range(2):
                              bank = PB[s * 2 + blk]
                              dve.op([bank, xw], [xw], lambda h, s=s, blk=blk, bank=bank: h.tensor_tensor(
                                  out=xw[:, s, blk * 512:(blk + 1) * 512], in0=bank[:, :],
                                  in1=xw[:, s, blk * 512:(blk + 1) * 512], op=ALU.add))
                      if last and do_final:
                          ss, ms, sq, rstd = small
                          for s in range(2):
                              act.op([xo], [xn, ss], lambda h, s=s: h.activation(
                                  out=xn[:, 0, :], in_=xo[:, s, :], func=AF.Square, accum_out=ss[:, s:s + 1]))
                          dve.op([ss], [ms], lambda h: h.tensor_scalar(
                              out=ms[:], in0=ss[:], scalar1=1.0 / D, scalar2=EPS, op0=ALU.mult, op1=ALU.add))
                          act.op([ms], [sq], lambda h: h.activation(out=sq[:], in_=ms[:], func=AF.Sqrt))
                          dve.op([sq], [rstd], lambda h: h.reciprocal(out=rstd[:], in_=sq[:]))
                          for s in range(2):
                              dve.op([xo, rstd, gfin], [yo], lambda h, s=s: h.scalar_tensor_tensor(
                                  out=yo[:, s, :], in0=xo[:, s, :], scalar=rstd[:, s:s + 1], in1=gfin[:],
                                  op0=ALU.mult, op1=ALU.mult))
                          sw.dma([yo], [DB("Y", u)], xrows(x_dst, u), yo[:])
                      else:
                          sw.dma([xo], [DB("X", u)], xrows(x_dst, u), xo[:])

                  route_A(0)
                  if "routeB" not in SKIP:
                      route_B(0)
                  for u in range(NU):
                      if u + 1 < NU:
                          route_A(u + 1)
                      if "dense" not in SKIP:
                          dense(u)
                      finalize(u)
                      if u + 1 < NU and "routeB" not in SKIP:
                          route_B(u + 1)
              S.barrier()
              lay.close()
        except _Stop:
            pass
        S.barrier()
    return nc, S.ninst


_WEIGHT_KEYS = ["norm_mix_g", "w_in", "w_gate", "conv_w", "conv_b", "conv_ln_g", "conv_ln_b", "w_conv_out",
                "log_gamma_fwd", "log_gamma_bwd", "w_ret_out", "w_o", "norm_ffn_g", "peer_wq", "peer_subkeys",
                "peer_u", "peer_v"]


def run_cores(x_per_core, weights, seqs, n_layers=2, do_final=True, stop=0):
    nc, ninst = build(seqs, n_layers=n_layers, do_final=do_final, stop=stop)
    consts, cosT, sinT = host_consts(max(seqs))
    base = {k: np.ascontiguousarray(np.asarray(weights[k], dtype=np.float32)) for k in _WEIGHT_KEYS}
    base["final_norm_g"] = np.ascontiguousarray(np.asarray(weights["final_norm_g"], np.float32).reshape(1, D))
    base["consts"] = consts
    base["cosT"] = cosT
    base["sinT"] = sinT
    in_maps = []
    for xc in x_per_core:
        m = dict(base)
        m["x"] = np.ascontiguousarray(xc, dtype=np.float32)
        in_maps.append(m)
    res = run_bass_kernel_spmd(nc, in_maps, core_ids=list(range(len(x_per_core))))
    return [r["y"] for r in res.results]


def kernel(x_prompt, x_sample, **weights):
    x_prompt = np.asarray(x_prompt, dtype=np.float32)
    x_sample = np.asarray(x_sample, dtype=np.float32)
    xs = []
    for c in range(N_CORES):
        xs.append(np.concatenate([x_prompt[c], x_sample[2 * c], x_sample[2 * c + 1]], axis=0))
    ys = run_cores(xs, weights, SEQS_FULL)
    y_prompt = np.stack([y[0:8192] for y in ys], axis=0)
    y_sample = np.stack([ys[c // 2][8192 + 2048 * (c % 2):8192 + 2048 * (c % 2 + 1)] for c in range(16)], axis=0)
    return (y_prompt.astype(np.float32), y_sample.astype(np.float32))
```

```python
from contextlib import ExitStack
import numpy as np
import concourse.bass as bass
import concourse.mybir as mybir
from concourse.bass_utils import run_bass_kernel_spmd

F32 = mybir.dt.float32
BF16 = mybir.dt.bfloat16
U32 = mybir.dt.uint32
AF = mybir.ActivationFunctionType
ALU = mybir.AluOpType
AX = mybir.AxisListType

D = 1024
EPS = 1e-6
NEG = -1.0e30
N_CORES = 8
SEQS_FULL = (8192, 2048, 2048)


class Buf:
    __slots__ = ("name", "lw", "rd")

    def __init__(self, name):
        self.name = name
        self.lw = None
        self.rd = {}


class TT:
    def __init__(self, t, name):
        self.t = t
        self.b = Buf(name)

    def __getitem__(self, k):
        return self.t[k]


def _b(x):
    return x.b if isinstance(x, TT) else x


class Eng:
    def __init__(self, S, name, h, nslots=0, own_skip=False):
        self.S = S
        self.name = name
        self.h = h
        self.waited = {}
        self.own_skip = own_skip
        self.nslots = nslots
        if nslots == 0:
            self.sem = S.new_sem(name)
            self.count = 0
        else:
            self.slots = [[S.new_sem(f"{name}{i}"), 0] for i in range(nslots)]
            self.rr = 0

    def _wait(self, tick):
        sem, val, key = tick
        if self.own_skip and self.nslots == 0 and key == self.sem_key():
            return
        if self.waited.get(key, 0) >= val:
            return
        self.h.wait_ge(sem, val)
        self.waited[key] = val

    def sem_key(self):
        return self.name

    def _deps(self, reads, writes):
        own = self.name if self.nslots == 0 else None
        for b in reads:
            b = _b(b)
            if b.lw is not None:
                self._wait(b.lw)
        for b in writes:
            b = _b(b)
            if b.lw is not None and b.lw[2] != own:
                self._wait(b.lw)
            for t in b.rd.values():
                if t[2] != own:
                    self._wait(t)

    def _mark(self, reads, writes, tick):
        for b in reads:
            _b(b).rd[tick[2]] = tick
        for b in writes:
            b = _b(b)
            b.lw = tick
            b.rd = {}

    def op(self, reads, writes, fn):
        if DEAD[0]:
            return
        self._deps(reads, writes)
        inst = fn(self.h)
        self.count += 1
        inst.then_inc(self.sem, 1)
        tick = (self.sem, self.count, self.name)
        self._mark(reads, writes, tick)
        self.S.ninst += 1

    def dma(self, reads, writes, out, in_, **kw):
        if DEAD[0]:
            return
        slot = self.slots[self.rr]
        key = f"{self.name}{self.rr}"
        self.rr = (self.rr + 1) % self.nslots
        if slot[1] > 0:
            self._wait((slot[0], slot[1], key))
        self._deps(reads, writes)
        inst = self.h.dma_start(out=out, in_=in_, **kw)
        slot[1] += 16
        inst.then_inc(slot[0], 16)
        tick = (slot[0], slot[1], key)
        self._mark(reads, writes, tick)
        self.S.ninst += 1

    def last_ticks(self):
        if self.nslots == 0:
            return [(self.sem, self.count, self.name)] if self.count else []
        return [(s[0], s[1], f"{self.name}{i}") for i, s in enumerate(self.slots) if s[1]]


class Sched:
    def __init__(self, nc, stack):
        self.nc = nc
        self.stack = stack
        self.ninst = 0
        self.pe = Eng(self, "pe", nc.tensor, own_skip=True)
        self.act = Eng(self, "act", nc.scalar)
        self.dve = Eng(self, "dve", nc.vector)
        self.pool = Eng(self, "pool", nc.gpsimd)
        self.sp = Eng(self, "sp", nc.sync, nslots=8)
        self.sw = Eng(self, "sw", nc.gpsimd, nslots=6)
        self.all = [self.pe, self.act, self.dve, self.pool, self.sp, self.sw]

    def new_sem(self, name):
        return self.stack.enter_context(self.nc.semaphore("s_" + name))

    def barrier(self):
        ticks = []
        for e in self.all:
            ticks += e.last_ticks()
        for e in (self.pe, self.act, self.dve, self.pool, self.sp):
            for t in ticks:
                if e.nslots == 0 and t[2] == e.name:
                    if e.own_skip:
                        continue
                e._wait(t)


def host_consts(lmax):
    c = np.zeros((128, 1024), np.float32)
    p = np.arange(128, dtype=np.float32)
    c[:, 0:128] = np.eye(128, dtype=np.float32)
    c[:, 128:256] = p[None, :]
    diff = p[None, :] - p[:, None]
    c[:, 256:384] = np.maximum(diff, 0.0)
    c[:, 384:512] = np.maximum(-diff, 0.0)
    c[:, 512:640] = (diff >= 0).astype(np.float32) / 16.0
    c[:, 640:768] = (diff < 0).astype(np.float32) / 16.0
    c[:, 768] = p + 1.0
    c[:, 769] = 128.0 - p
    c[:, 770] = 127.0 - p
    c[:, 771] = p
    c[:, 772] = 128.0
    c[:, 776:792] = np.arange(16, dtype=np.float32)[None, :]
    c[:, 800:928] = 1.0 / 1024.0
    half = 128
    freqs = (1.0 / (10000.0 ** (np.arange(half, dtype=np.float32) / np.float32(half)))).astype(np.float32)
    ang = np.arange(lmax, dtype=np.float32)[:, None] * freqs[None, :]
    cosT = np.ascontiguousarray(np.cos(ang).astype(np.float32).T)
    sinT = np.ascontiguousarray(np.sin(ang).astype(np.float32).T)
    return c, cosT, sinT


class _Stop(Exception):
    pass


SUBSTOP = [0]
SKIP = set()
CFG = {"NSB": 3, "LOOK": 2}


DEAD = [False]


def sub(n):
    if SUBSTOP[0] == n:
        DEAD[0] = True


def build(seqs, n_layers=2, do_final=True, stop=0):
    T_tok = sum(seqs)
    NU = T_tok // 256
    NCH = T_tok // 128
    lmax = max(seqs)
    nc = bass.Bass("TRN2", target_bir_lowering=False)
    DEAD[0] = False

    def din(name, shape, dt=F32):
        return nc.dram_tensor(name, list(shape), dt, kind="ExternalInput").ap()

    def dscr(name, shape, dt):
        return nc.dram_tensor(name, list(shape), dt, kind="Internal").ap()

    x_in = din("x", [T_tok, D])
    consts_d = din("consts", [128, 1024])
    cos_d = din("cosT", [128, lmax])
    sin_d = din("sinT", [128, lmax])
    norm_mix_g = din("norm_mix_g", [2, D])
    w_in = din("w_in", [2, D, 6144])
    w_gate = din("w_gate", [2, D, 2048])
    conv_w = din("conv_w", [2, 31, D])
    conv_b = din("conv_b", [2, D])
    conv_ln_g = din("conv_ln_g", [2, D])
    conv_ln_b = din("conv_ln_b", [2, D])
    w_conv_out = din("w_conv_out", [2, D, D])
    lg_f = din("log_gamma_fwd", [2, 4])
    lg_b = din("log_gamma_bwd", [2, 4])
    w_ret_out = din("w_ret_out", [2, D, D])
    w_o = din("w_o", [2, D, D])
    norm_ffn_g = din("norm_ffn_g", [2, D])
    peer_wq = din("peer_wq", [2, D, 2048])
    peer_sk = din("peer_subkeys", [2, 8, 2, 128, 128])
    peer_u = din("peer_u", [2, 16384, D])
    peer_v = din("peer_v", [2, 16384, D])
    final_g = din("final_norm_g", [1, D])
    y_out = nc.dram_tensor("y", [T_tok, D], F32, kind="ExternalOutput").ap()

    X1 = dscr("X1", [T_tok, D], F32)
    XM = dscr("XM", [T_tok, D], F32)
    UT = dscr("UT", [NU, 128, 8 * 256], BF16)
    QT = dscr("QT", [NU, 128, 8 * 256], BF16)
    KT = dscr("KT", [NU, 128, 8 * 256], BF16)
    KK = dscr("KK", [NCH, 128, D], BF16)
    VV = dscr("VV", [NCH, 128, D], BF16)
    SG = dscr("SG", [NCH, 128, D], BF16)
    GT = dscr("GT", [NCH, 128, 2048], BF16)
    YC = dscr("YC", [NCH, 128, D], BF16)
    SB = dscr("SB", [NCH, 128, 8 * 256], BF16)
    UTS = dscr("UTS", [2, 64, 128, 8 * 256], BF16)
    VS = dscr("VS", [2, 64, 128, 2 * 1024], BF16)

    units = []
    u0 = 0
    seq_ranges = []
    for L in seqs:
        nu = L // 256
        seq_ranges.append((u0, u0 + nu))
        for i in range(nu):
            units.append((u0, u0 + nu, i * 256))
        u0 += nu

    dbufs = {}

    def DB(name, idx):
        k = (name, idx)
        if k not in dbufs:
            dbufs[k] = Buf(f"{name}{idx}")
        return dbufs[k]

    with ExitStack() as top:
        S = Sched(nc, top)
        pe, act, dve, pool, sp, sw = S.pe, S.act, S.dve, S.pool, S.sp, S.sw

        uniq = [0]

        def sb(stack, name, shape, dt):
            uniq[0] += 1
            name = f"{name}_{uniq[0]}"
            return TT(stack.enter_context(nc.sbuf_tensor(name, list(shape), dt)), name)

        PB = [TT(top.enter_context(nc.psum_tensor(f"pb{i}", [128, 512], F32)), f"pb{i}") for i in range(8)]

        def pbf(bank):
            return bank.t[:].bitcast(BF16)

        cst = sb(top, "cst", [128, 1024], F32)
        sp.dma([], [cst], cst[:], consts_d[:, :])
        ident_f = cst[:, 0:128]
        identb = sb(top, "identb", [128, 128], BF16)
        iotab = sb(top, "iotab", [128, 128], BF16)
        dve.op([cst], [identb], lambda h: h.tensor_copy(out=identb[:], in_=cst[:, 0:128]))
        dve.op([cst], [iotab], lambda h: h.tensor_copy(out=iotab[:], in_=cst[:, 128:256]))
        iota16 = cst[:, 776:792]

        def mm_group(out_bank, reads, mms, extra_writes=()):
            n = len(mms)

            def fn(h):
                inst = None
                for i, (o, l, r) in enumerate(mms):
                    inst = h.matmul(o, lhsT=l, rhs=r, start=(i == 0), stop=(i == n - 1))
                return inst
            pe.op(reads, [out_bank] + list(extra_writes), fn)

        def transposes(out_banks, reads, trs):
            def fn(h):
                inst = None
                for (o, i_, idn) in trs:
                    inst = h.transpose(o, i_, idn)
                return inst
            pe.op(reads, list(out_banks), fn)

        def norm_T(xin, gtile, xn, xnT, junk, small, bankA, bankB, evac_engs):
            ss, ms, sq, rstd = small
            if isinstance(junk, tuple):
                jap, jb = junk
            else:
                jap, jb = junk[:], junk
            for s in range(2):
                act.op([xin], [jb, ss], lambda h, s=s: h.activation(
                    out=jap, in_=xin[:, s, :], func=AF.Square, accum_out=ss[:, s:s + 1]))
            dve.op([ss], [ms], lambda h: h.tensor_scalar(
                out=ms[:], in0=ss[:], scalar1=1.0 / D, scalar2=EPS, op0=ALU.mult, op1=ALU.add))
            act.op([ms], [sq], lambda h: h.activation(out=sq[:], in_=ms[:], func=AF.Sqrt))
            dve.op([sq], [rstd], lambda h: h.reciprocal(out=rstd[:], in_=sq[:]))
            for s in range(2):
                dve.op([xin, rstd, gtile], [xn], lambda h, s=s: h.scalar_tensor_tensor(
                    out=xn[:, s, :], in0=xin[:, s, :], scalar=rstd[:, s:s + 1], in1=gtile[:],
                    op0=ALU.mult, op1=ALU.mult))
            trs = []
            for kc in range(8):
                bank = bankA if kc < 4 else bankB
                v = pbf(bank)
                for s in range(2):
                    o = v[:, (kc % 4) * 256 + s * 128:(kc % 4) * 256 + (s + 1) * 128]
                    trs.append((o, xn[:, s, kc * 128:(kc + 1) * 128], identb[:]))
            transposes([bankA, bankB], [xn, identb], trs)
            e0, e1 = evac_engs
            e0.op([bankA], [xnT], lambda h: _copy(h, xnT[:, 0:4, :], pbf(bankA).rearrange("p (a t) -> p a t", a=4)))
            e1.op([bankB], [xnT], lambda h: _copy(h, xnT[:, 4:8, :], pbf(bankB).rearrange("p (a t) -> p a t", a=4)))

        def _copy(h, out, in_):
            if h is nc.scalar:
                return h.activation(out=out, in_=in_, func=AF.Copy)
            return h.tensor_copy(out=out, in_=in_)

        def load_weight_bf16(dst, src3, nk, ncols, col0=0):
            for kc in range(nk):
                for c0 in range(0, ncols, 2048):
                    c1 = min(ncols, c0 + 2048)
                    sw.dma([], [dst], dst[:, kc, c0:c1], src3[kc * 128:(kc + 1) * 128, col0 + c0:col0 + c1])

        def bcast_row(dst, row_ap):
            sp.dma([], [dst], dst[:], row_ap.partition_broadcast(128))

        phase_ctr = [0]

        def chk():
            phase_ctr[0] += 1
            if stop and phase_ctr[0] > stop:
                DEAD[0] = True

        with ExitStack() as ph:
            uf = [sb(ph, f"uf{i}", [128, 4, D], F32) for i in range(2)]
            uts = [sb(ph, f"uts{i}", [128, 8, 512], BF16) for i in range(2)]
            vb = [sb(ph, f"vb{i}", [128, 4, D], BF16) for i in range(2)]
            it = 0
            for l in range(n_layers):
                for g in range(32):
                    a = it % 2
                    it += 1
                    rows = peer_u[l, g * 512:(g + 1) * 512, :].rearrange("(c p) d -> p c d", p=128)
                    sp.dma([], [uf[a]], uf[a][:], rows)
                    for kc in range(8):
                        bank = PB[kc]
                        trs = [(bank[:, c * 128:(c + 1) * 128], uf[a][:, c, kc * 128:(kc + 1) * 128], ident_f)
                               for c in range(4)]
                        transposes([bank], [uf[a], cst], trs)
                        e = act if kc % 2 == 0 else dve
                        e.op([bank], [uts[a]], lambda h, kc=kc, bank=bank, a=a: _copy(h, uts[a][:, kc, :], bank[:, :]))
                    for hf in range(2):
                        sp.dma([uts[a]], [DB("UTS", (l, 2 * g + hf))], UTS[l, 2 * g + hf].rearrange("p (a e) -> p a e", a=8),
                               uts[a][:, :, hf * 256:(hf + 1) * 256])
                    vrows = peer_v[l, g * 512:(g + 1) * 512, :].rearrange("(c p) d -> p c d", p=128)
                    sw.dma([], [vb[a]], vb[a][:], vrows)
                    for hf in range(2):
                        sp.dma([vb[a]], [DB("VS", (l, 2 * g + hf))], VS[l, 2 * g + hf].rearrange("p (a e) -> p a e", a=2),
                               vb[a][:, 2 * hf:2 * hf + 2, :])
        S.barrier()

        try:
          for l in range(n_layers):
              chk()
              x_src = x_in if l == 0 else X1
              last = (l == n_layers - 1)
              x_dst = y_out if last else X1

              def xrows(src, u):
                  return src[u * 256:(u + 1) * 256, :].rearrange("(s p) d -> p s d", p=128)

              lay = top.enter_context(ExitStack())
              lgfb = sb(lay, "lgfb", [128, 4], F32)
              lgbb = sb(lay, "lgbb", [128, 4], F32)
              bcast_row(lgfb, lg_f[l:l + 1, :])
              bcast_row(lgbb, lg_b[l:l + 1, :])
              dsc = sb(lay, "dsc", [128, 6, 4], F32)
              specs = [(lgfb, 768, 0.0), (lgbb, 769, 0.0), (lgfb, 770, -np.log(16.0)), (lgbb, 771, -np.log(16.0)),
                       (lgfb, 772, 0.0), (lgbb, 772, 0.0)]
              for i, (lgt, col, bias) in enumerate(specs):
                  dve.op([cst, lgt], [dsc], lambda h, i=i, lgt=lgt, col=col: h.tensor_scalar(
                      out=dsc[:, i, :], in0=lgt[:], scalar1=cst[:, col:col + 1], scalar2=None, op0=ALU.mult))
              dve.op([dsc], [dsc], lambda h: h.tensor_scalar(
                  out=dsc[:, 2:4, :], in0=dsc[:, 2:4, :], scalar1=float(-np.log(16.0)), scalar2=None, op0=ALU.add))
              act.op([dsc], [dsc], lambda h: h.activation(out=dsc[:], in_=dsc[:], func=AF.Exp))
              A_F, A_B, VD_F, VD_B, CD_F, CD_B = range(6)
              if SUBSTOP[0] == 50:
                  sp.dma([dsc], [DB("Y", -1)], y_out[0:128, 0:24], dsc[:].rearrange("p a b -> p (a b)"))
                  sp.dma([decT], [DB("Y", -2)], y_out[0:128, 32:32 + 512], decT[:].rearrange("p a b -> p (a b)"))
                  sp.dma([lgfb], [DB("Y", -3)], y_out[0:128, 600:604], lgfb[:])
                  sp.dma([lgbb], [DB("Y", -4)], y_out[0:128, 608:612], lgbb[:])
                  sub(50)

              chk()
              with ExitStack() as ph:
                  wsb = sb(ph, "w1a", [128, 8, 4096], BF16)
                  load_weight_bf16(wsb, w_in[l], 8, 4096, 0)
                  gtile = sb(ph, "g1a", [128, D], F32)
                  bcast_row(gtile, norm_mix_g[l:l + 1, :])
                  xin = [sb(ph, f"xin{i}", [128, 2, D], F32) for i in range(2)]
                  cs = [sb(ph, f"cs{i}", [128, 2, 256], F32) for i in range(2)]
                  xn = sb(ph, "xn", [128, 2, D], BF16)
                  xnT = sb(ph, "xnT", [128, 8, 256], BF16)
                  junk = sb(ph, "junk", [128, D], BF16)
                  small = [sb(ph, f"sm{i}", [128, 2], F32) for i in range(4)]
                  uT = [sb(ph, f"uT{i}", [128, 8, 256], BF16) for i in range(2)]
                  qT = [sb(ph, f"qT{i}", [128, 8, 256], BF16) for i in range(2)]
                  kT = [sb(ph, f"kT{i}", [128, 8, 256], BF16) for i in range(2)]
                  ktok = [sb(ph, f"ktok{i}", [128, 2, D], BF16) for i in range(2)]
                  sig = [sb(ph, f"sig{i}", [128, 256], F32) for i in range(2)]
                  rt = [sb(ph, f"rt{i}", [128, 4, 256], F32) for i in range(4)]
                  kraw = [sb(ph, f"kraw{i}", [128, 2, 256], F32) for i in range(2)]

                  def p1a_load(u):
                      a = u % 2
                      sp.dma([DB("X", u)], [xin[a]], xin[a][:], xrows(x_src, u))
                      pos = units[u][2]
                      sp.dma([], [cs[a]], cs[a][:, 0, :], cos_d[:, pos:pos + 256])
                      sp.dma([], [cs[a]], cs[a][:, 1, :], sin_d[:, pos:pos + 256])

                  p1a_load(0)
                  sub(1)
                  pair_i = 0
                  for u in range(NU):
                      a = u % 2
                      if u + 1 < NU:
                          p1a_load(u + 1)
                      sub(20 + u)
                      sub(2)
                      norm_T(xin[a], gtile, xn, xnT, junk, small, PB[6], PB[7], (act, dve))
                      sub(3)
                      cosA = cs[a][:, 0, :]
                      sinA = cs[a][:, 1, :]
                      for pi in range(16):
                          bank = PB[pair_i % 6]
                          pair_i += 1
                          if pi < 8:
                              c1, c2 = pi, 8 + pi
                          elif pi < 12:
                              c1, c2 = 16 + 2 * (pi - 8), 17 + 2 * (pi - 8)
                          else:
                              c1, c2 = 24 + 2 * (pi - 12), 25 + 2 * (pi - 12)
                          for half, cc in enumerate((c1, c2)):
                              mms = [(bank[:, half * 256:(half + 1) * 256], wsb[:, kc, cc * 128:(cc + 1) * 128],
                                      xnT[:, kc, :]) for kc in range(8)]
                              mm_group(bank, [wsb, xnT], mms)
                          p1 = bank[:, 0:256]
                          p2 = bank[:, 256:512]
                          if pi < 8:
                              sg_ = sig[pi % 2]
                              act.op([bank], [sg_], lambda h, sg_=sg_, p2=p2: h.activation(
                                  out=sg_[:], in_=p2, func=AF.Sigmoid))
                              dve.op([bank, sg_], [uT[a]], lambda h, sg_=sg_, p1=p1, pi=pi: h.tensor_tensor(
                                  out=uT[a][:, pi, :], in0=p1, in1=sg_[:], op=ALU.mult))
                          elif pi < 12:
                              hd = pi - 8
                              r = rt[pi % 4]
                              dst = qT[a]
                              dve.op([bank, cs[a]], [r], lambda h, r=r: h.tensor_tensor(out=r[:, 0, :], in0=p1, in1=cosA, op=ALU.mult))
                              dve.op([bank, cs[a]], [r], lambda h, r=r: h.tensor_tensor(out=r[:, 1, :], in0=p2, in1=sinA, op=ALU.mult))
                              dve.op([bank, cs[a]], [r], lambda h, r=r: h.tensor_tensor(out=r[:, 2, :], in0=p1, in1=sinA, op=ALU.mult))
                              dve.op([bank, cs[a]], [r], lambda h, r=r: h.tensor_tensor(out=r[:, 3, :], in0=p2, in1=cosA, op=ALU.mult))
                              pool.op([r], [dst], lambda h, r=r, dst=dst, hd=hd: h.tensor_tensor(
                                  out=dst[:, 2 * hd, :], in0=r[:, 0, :], in1=r[:, 1, :], op=ALU.subtract))
                              pool.op([r], [dst], lambda h, r=r, dst=dst, hd=hd: h.tensor_tensor(
                                  out=dst[:, 2 * hd + 1, :], in0=r[:, 2, :], in1=r[:, 3, :], op=ALU.add))
                          else:
                              hd = pi - 12
                              r = rt[pi % 4]
                              dst = kT[a]
                              dve.op([bank, cs[a]], [r], lambda h, r=r: h.tensor_tensor(out=r[:, 0, :], in0=p1, in1=cosA, op=ALU.mult))
                              dve.op([bank, cs[a]], [r], lambda h, r=r: h.tensor_tensor(out=r[:, 1, :], in0=p2, in1=sinA, op=ALU.mult))
                              dve.op([bank, cs[a]], [r], lambda h, r=r: h.tensor_tensor(out=r[:, 2, :], in0=p1, in1=sinA, op=ALU.mult))
                              dve.op([bank, cs[a]], [r], lambda h, r=r: h.tensor_tensor(out=r[:, 3, :], in0=p2, in1=cosA, op=ALU.mult))
                              pool.op([r], [dst], lambda h, r=r, dst=dst, hd=hd: h.tensor_tensor(
                                  out=dst[:, 2 * hd, :], in0=r[:, 0, :], in1=r[:, 1, :], op=ALU.subtract))
                              pool.op([r], [dst], lambda h, r=r, dst=dst, hd=hd: h.tensor_tensor(
                                  out=dst[:, 2 * hd + 1, :], in0=r[:, 2, :], in1=r[:, 3, :], op=ALU.add))
                      sub(4)
                      for s in range(2):
                          bank = PB[6 + s]
                          v = pbf(bank)
                          trs = [(v[:, cc * 128:(cc + 1) * 128], kT[a][:, cc, s * 128:(s + 1) * 128], identb[:])
                                 for cc in range(8)]
                          transposes([bank], [kT[a], identb], trs)
                          e = act if s == 0 else dve
                          e.op([bank], [ktok[a]], lambda h, s=s, bank=bank: _copy(h, ktok[a][:, s, :], pbf(bank)))
                      sub(5)
                      flat = lambda t: t[:].rearrange("p a t -> p (a t)")
                      sw.dma([uT[a]], [DB("UT", u)], UT[u], flat(uT[a]))
                      sw.dma([qT[a]], [DB("QT", u)], QT[u], flat(qT[a]))
                      sw.dma([kT[a]], [DB("KT", u)], KT[u], flat(kT[a]))
                      sub(6)
                      for s in range(2):
                          sw.dma([ktok[a]], [DB("KK", 2 * u + s)], KK[2 * u + s], ktok[a][:, s, :])
                      sub(7)
                      sub(10 + u)
              S.barrier()

              chk()
              with ExitStack() as ph:
                  wv = sb(ph, "w1b", [128, 8, 2048], BF16)
                  load_weight_bf16(wv, w_in[l], 8, 2048, 4096)
                  wg = sb(ph, "wg", [128, 8, 2048], BF16)
                  load_weight_bf16(wg, w_gate[l], 8, 2048, 0)
                  gtile = sb(ph, "g1b", [128, D], F32)
                  bcast_row(gtile, norm_mix_g[l:l + 1, :])
                  xin = [sb(ph, f"xin{i}", [128, 2, D], F32) for i in range(2)]
                  kk = [sb(ph, f"kk{i}", [128, 2, D], BF16) for i in range(2)]
                  xn = sb(ph, "xn", [128, 2, D], BF16)
                  xnT = sb(ph, "xnT", [128, 8, 256], BF16)
                  junk = sb(ph, "junk", [128, D], BF16)
                  small = [sb(ph, f"sm{i}", [128, 2], F32) for i in range(4)]
                  vsb = [sb(ph, f"vsb{i}", [128, 2, D], BF16) for i in range(2)]
                  vdb = [sb(ph, f"vdb{i}", [128, 2, D], BF16) for i in range(2)]
                  sgb = [sb(ph, f"sgb{i}", [128, 2, D], BF16) for i in range(2)]
                  gtb = [sb(ph, f"gtb{i}", [128, 2, 2048], BF16) for i in range(2)]
                  Sst = sb(ph, "Sst", [128, 8, 256], F32)
                  Sbf = [sb(ph, f"Sbf{i}", [128, 8, 256], BF16) for i in range(2)]
                  order = []
                  for (a0, a1) in seq_ranges:
                      order += list(range(a1 - 1, a0 - 1, -1))

                  def p1b_load(i):
                      u = order[i]
                      a = i % 2
                      sp.dma([DB("X", u)], [xin[a]], xin[a][:], xrows(x_src, u))
                      for s in range(2):
                          sp.dma([DB("KK", 2 * u + s)], [kk[a]], kk[a][:, s, :], KK[2 * u + s])

                  p1b_load(0)
                  bi = 0
                  sbi = 0
                  for i, u in enumerate(order):
                      a = i % 2
                      if i + 1 < NU:
                          p1b_load(i + 1)
                      sub(30)
                      norm_T(xin[a], gtile, xn, xnT, junk, small, PB[6], PB[7], (act, dve))
                      sub(31)
                      for s in range(2):
                          for blk in range(4):
                              bank = PB[bi % 4]
                              bi += 1
                              mms = [(bank[:, :], xnT[:, kc, s * 128:(s + 1) * 128], wv[:, kc, blk * 512:(blk + 1) * 512])
                                     for kc in range(8)]
                              mm_group(bank, [xnT, wv], mms)
                              sub(40)
                              if blk < 2:
                                  act.op([bank], [vsb[a]], lambda h, s=s, blk=blk, bank=bank: h.activation(
                                      out=vsb[a][:, s, blk * 512:(blk + 1) * 512], in_=bank[:, :], func=AF.Copy))
                                  sub(41)
                                  for hh in range(2):
                                      hd = blk * 2 + hh
                                      act.op([bank, dsc], [vdb[a]], lambda h, s=s, hd=hd, hh=hh, bank=bank: h.activation(
                                          out=vdb[a][:, s, hd * 256:(hd + 1) * 256], in_=bank[:, hh * 256:(hh + 1) * 256],
                                          func=AF.Copy, scale=dsc[:, VD_B, hd:hd + 1]))
                                      sub(42)
                              else:
                                  act.op([bank], [sgb[a]], lambda h, s=s, blk=blk, bank=bank: h.activation(
                                      out=sgb[a][:, s, (blk - 2) * 512:(blk - 1) * 512], in_=bank[:, :], func=AF.Silu))
                          for blk in range(4):
                              bank = PB[bi % 4]
                              bi += 1
                              mms = [(bank[:, :], xnT[:, kc, s * 128:(s + 1) * 128], wg[:, kc, blk * 512:(blk + 1) * 512])
                                     for kc in range(8)]
                              mm_group(bank, [xnT, wg], mms)
                              act.op([bank], [gtb[a]], lambda h, s=s, blk=blk, bank=bank: h.activation(
                                  out=gtb[a][:, s, blk * 512:(blk + 1) * 512], in_=bank[:, :], func=AF.Sigmoid))
                      sub(32)
                      for s in range(2):
                          c = 2 * u + s
                          sw.dma([vsb[a]], [DB("VV", c)], VV[c], vsb[a][:, s, :])
                          sw.dma([sgb[a]], [DB("SG", c)], SG[c], sgb[a][:, s, :])
                          sw.dma([gtb[a]], [DB("GT", c)], GT[c], gtb[a][:, s, :])
                      sub(33)
                      if u == units[u][1] - 1:
                          pool.op([], [Sst], lambda h: h.memset(Sst[:], 0.0))
                          pool.op([], [Sbf[sbi % 2]], lambda h, t=Sbf[sbi % 2]: h.memset(t[:], 0.0))
                      for s in (1, 0):
                          c = 2 * u + s
                          cur = Sbf[sbi % 2]
                          nxt = Sbf[(sbi + 1) % 2]
                          sbi += 1
                          sw.dma([cur], [DB("SB", c)], SB[c], cur[:].rearrange("p a e -> p (a e)"))
                          for hp in range(2):
                              for hh in range(2):
                                  hd = hp * 2 + hh
                                  bank = PB[4 + hh]
                                  for dc in range(2):
                                      mm_group(bank, [kk[a], vdb[a]], [(bank[:, dc * 256:(dc + 1) * 256],
                                               kk[a][:, s, (2 * hd + dc) * 128:(2 * hd + dc + 1) * 128],
                                               vdb[a][:, s, hd * 256:(hd + 1) * 256])])
                                  dve.op([bank, dsc, Sst], [Sst], lambda h, hd=hd, bank=bank: h.scalar_tensor_tensor(
                                      out=Sst[:, 2 * hd:2 * hd + 2, :], in0=Sst[:, 2 * hd:2 * hd + 2, :],
                                      scalar=dsc[:, CD_B, hd:hd + 1],
                                      in1=bank[:, :].rearrange("p (a e) -> p a e", a=2), op0=ALU.mult, op1=ALU.add))
                          act.op([Sst], [nxt], lambda h, nxt=nxt: h.activation(out=nxt[:], in_=Sst[:], func=AF.Copy))
                          sub(34)
              S.barrier()

              chk()
              with ExitStack() as ph:
                  wco = sb(ph, "wco", [128, 8, D], BF16)
                  load_weight_bf16(wco, w_conv_out[l], 8, D, 0)
                  praw = sb(ph, "praw", [34, D], F32)
                  sp.dma([], [praw], praw[0:31, :], conv_w[l])
                  sp.dma([], [praw], praw[31:32, :], conv_b[l:l + 1, :])
                  sp.dma([], [praw], praw[32:33, :], conv_ln_g[l:l + 1, :])
                  sp.dma([], [praw], praw[33:34, :], conv_ln_b[l:l + 1, :])
                  cpar = sb(ph, "cpar", [128, 8, 34], F32)
                  for cc in range(8):
                      bank = PB[cc]
                      transposes([bank], [praw, cst], [(bank[:, 0:34], praw[:, cc * 128:(cc + 1) * 128], cst[0:34, 0:34])])
                      dve.op([bank], [cpar], lambda h, cc=cc, bank=bank: h.tensor_copy(out=cpar[:, cc, :], in_=bank[:, 0:34]))
                  diag = sb(ph, "diag", [128, 8, 31, 128], BF16)
                  for cc in range(8):
                      for k in range(31):
                          e = dve if (k % 2 == 0) else pool
                          e.op([cst, cpar], [diag], lambda h, cc=cc, k=k: h.tensor_scalar(
                              out=diag[:, cc, k, :], in0=cst[:, 0:128], scalar1=cpar[:, cc, k:k + 1], scalar2=None,
                              op0=ALU.mult))
                  uh = [sb(ph, f"uh{i}", [128, 8, 286], BF16) for i in range(2)]
                  uhl = [Buf(f"uhl{i}") for i in range(2)]
                  uhr = [Buf(f"uhr{i}") for i in range(2)]
                  cv_ = [sb(ph, f"cv{i}", [128, 8, 256], F32) for i in range(2)]
                  sq2_ = [sb(ph, f"sq2{i}", [128, 8, 256], F32) for i in range(2)]
                  mean_ = [sb(ph, f"mean{i}", [128, 256], F32) for i in range(2)]
                  m2_ = [sb(ph, f"m2{i}", [128, 256], F32) for i in range(2)]
                  rstd_ = [sb(ph, f"rstdc{i}", [128, 256], F32) for i in range(2)]
                  uln = [sb(ph, f"uln{i}", [128, 8, 256], BF16) for i in range(2)]
                  ycb = [sb(ph, f"ycb{i}", [128, 2, D], BF16) for i in range(2)]
                  onesm = cst[:, 800:928]

                  def p2c_load(u):
                      a = u % 2
                      u_lo, u_hi, _ = units[u]
                      UTu = lambda uu: UT[uu].rearrange("p (a t) -> p a t", a=8)
                      sp.dma([DB("UT", u)], [uh[a]], uh[a][:, :, 15:271], UTu(u))
                      if u > u_lo:
                          sp.dma([DB("UT", u - 1)], [uhl[a]], uh[a][:, :, 0:15], UTu(u - 1)[:, :, 241:256])
                      else:
                          pool.op([], [uhl[a]], lambda h: h.memset(uh[a][:, :, 0:15], 0.0))
                      if u + 1 < u_hi:
                          sp.dma([DB("UT", u + 1)], [uhr[a]], uh[a][:, :, 271:286], UTu(u + 1)[:, :, 0:15])
                      else:
                          pool.op([], [uhr[a]], lambda h: h.memset(uh[a][:, :, 271:286], 0.0))

                  p2c_load(0)
                  for u in range(NU):
                      a = u % 2
                      if u + 1 < NU:
                          p2c_load(u + 1)
                      cv, sq2, mean, m2, rstd = cv_[a], sq2_[a], mean_[a], m2_[a], rstd_[a]
                      for cc in range(8):
                          bank = PB[cc // 2]
                          half = cc % 2
                          mms = [(bank[:, half * 256:(half + 1) * 256], diag[:, cc, k, :], uh[a][:, cc, k:k + 256])
                                 for k in range(31)]
                          mm_group(bank, [diag, uh[a], uhl[a], uhr[a]], mms)
                          act.op([bank, cpar], [cv], lambda h, cc=cc, bank=bank, half=half: h.activation(
                              out=cv[:, cc, :], in_=bank[:, half * 256:(half + 1) * 256], func=AF.Identity,
                              bias=cpar[:, cc, 31:32]))
                          act.op([cv], [sq2], lambda h, cc=cc: h.activation(out=sq2[:, cc, :], in_=cv[:, cc, :], func=AF.Square))
                      bm = PB[4 + a]
                      mm_group(bm, [cst, cv], [(bm[:, 0:256], onesm, cv[:, cc, :]) for cc in range(8)])
                      mm_group(bm, [cst, sq2], [(bm[:, 256:512], onesm, sq2[:, cc, :]) for cc in range(8)])
                      act.op([bm], [mean], lambda h: h.activation(out=mean[:], in_=bm[:, 0:256], func=AF.Copy))
                      act.op([bm], [m2], lambda h: h.activation(out=m2[:], in_=bm[:, 0:256], func=AF.Square))
                      dve.op([bm, m2], [m2], lambda h: h.tensor_tensor(out=m2[:], in0=bm[:, 256:512], in1=m2[:], op=ALU.subtract))
                      dve.op([m2], [m2], lambda h: h.tensor_scalar(out=m2[:], in0=m2[:], scalar1=EPS, scalar2=None, op0=ALU.add))
                      act.op([m2], [m2], lambda h: h.activation(out=m2[:], in_=m2[:], func=AF.Sqrt))
                      dve.op([m2], [rstd], lambda h: h.reciprocal(out=rstd[:], in_=m2[:]))
                      bc = lambda t: t[:].unsqueeze(1).to_broadcast([128, 8, 256])
                      pool.op([cv, mean], [cv], lambda h: h.tensor_tensor(out=cv[:], in0=cv[:], in1=bc(mean), op=ALU.subtract))
                      dve.op([cv, rstd], [cv], lambda h: h.tensor_tensor(out=cv[:], in0=cv[:], in1=bc(rstd), op=ALU.mult))
                      for cc in range(8):
                          act.op([cv, cpar], [uln[a]], lambda h, cc=cc: h.activation(
                              out=uln[a][:, cc, :], in_=cv[:, cc, :], func=AF.Silu,
                              scale=cpar[:, cc, 32:33], bias=cpar[:, cc, 33:34]))
                      for s in range(2):
                          for blk in range(2):
                              bank = PB[6 + (s * 2 + blk) % 2]
                              mms = [(bank[:, :], uln[a][:, cc, s * 128:(s + 1) * 128], wco[:, cc, blk * 512:(blk + 1) * 512])
                                     for cc in range(8)]
                              mm_group(bank, [uln[a], wco], mms)
                              dve.op([bank], [ycb[a]], lambda h, s=s, blk=blk, bank=bank: h.tensor_copy(
                                  out=ycb[a][:, s, blk * 512:(blk + 1) * 512], in_=bank[:, :]))
                          sw.dma([ycb[a]], [DB("YC", 2 * u + s)], YC[2 * u + s], ycb[a][:, s, :])
              S.barrier()

              chk()
              with ExitStack() as ph:
                  decT = sb(ph, "decT", [128, 4, 128], F32)
                  dtmp = sb(ph, "dtmp", [128, 2, 128], F32)
                  for hd in range(4):
                      dve.op([cst, lgfb], [dtmp], lambda h, hd=hd: h.tensor_scalar(
                          out=dtmp[:, 0, :], in0=cst[:, 256:384], scalar1=lgfb[:, hd:hd + 1], scalar2=None, op0=ALU.mult))
                      dve.op([cst, lgbb], [dtmp], lambda h, hd=hd: h.tensor_scalar(
                          out=dtmp[:, 1, :], in0=cst[:, 384:512], scalar1=lgbb[:, hd:hd + 1], scalar2=None, op0=ALU.mult))
                      act.op([dtmp], [dtmp], lambda h: h.activation(out=dtmp[:], in_=dtmp[:], func=AF.Exp))
                      dve.op([dtmp, cst], [dtmp], lambda h: h.tensor_tensor(
                          out=dtmp[:, 0, :], in0=dtmp[:, 0, :], in1=cst[:, 512:640], op=ALU.mult))
                      dve.op([dtmp, cst], [dtmp], lambda h: h.tensor_tensor(
                          out=dtmp[:, 1, :], in0=dtmp[:, 1, :], in1=cst[:, 640:768], op=ALU.mult))
                      dve.op([dtmp], [decT], lambda h, hd=hd: h.tensor_tensor(
                          out=decT[:, hd, :], in0=dtmp[:, 0, :], in1=dtmp[:, 1, :], op=ALU.add))
                  wro = sb(ph, "wro", [128, 8, D], BF16)
                  load_weight_bf16(wro, w_ret_out[l], 8, D, 0)
                  wo = sb(ph, "wo", [128, 8, D], BF16)
                  load_weight_bf16(wo, w_o[l], 8, D, 0)
                  xin = [sb(ph, f"xin{i}", [128, 2, D], F32) for i in range(2)]
                  qTl = [sb(ph, f"qTl{i}", [128, 8, 256], BF16) for i in range(2)]
                  kTl = [sb(ph, f"kTl{i}", [128, 8, 256], BF16) for i in range(2)]
                  kk = [sb(ph, f"kk{i}", [128, 2, D], BF16) for i in range(2)]
                  vv = [sb(ph, f"vv{i}", [128, 2, D], BF16) for i in range(2)]
                  sgl = [sb(ph, f"sgl{i}", [128, 2, D], BF16) for i in range(2)]
                  gtl = [sb(ph, f"gtl{i}", [128, 2, 2048], BF16) for i in range(2)]
                  ycl = [sb(ph, f"ycl{i}", [128, 2, D], BF16) for i in range(2)]
                  sbw = [sb(ph, f"sbw{i}", [128, 2, 8 * 256], BF16) for i in range(2)]
                  Sst = sb(ph, "Sstf", [128, 8, 256], F32)
                  Sbf = sb(ph, "Sbff", [128, 8, 256], BF16)
                  PT = [sb(ph, f"PT{i}", [128, 128], BF16) for i in range(2)]
                  tf = [sb(ph, f"tf{i}", [128, 256], F32) for i in range(2)]
                  o_sb = sb(ph, "o_sb", [128, D], F32)
                  vdf = sb(ph, "vdf", [128, D], BF16)
                  gn = [sb(ph, f"gn{i}", [128, 4], F32) for i in range(4)]
                  junk = sb(ph, "junk", [128, 256], BF16)
                  om = sb(ph, "om", [128, D], BF16)
                  omT = sb(ph, "omT", [128, 8, 256], BF16)
                  m1 = sb(ph, "m1", [128, D], F32)
                  mg = sb(ph, "mg", [128, D], BF16)
                  mgT = sb(ph, "mgT", [128, 8, 256], BF16)
                  xo = [sb(ph, f"xo{i}", [128, 2, D], F32) for i in range(2)]

                  def p3a_load(u):
                      a = u % 2
                      sp.dma([DB("X", u)], [xin[a]], xin[a][:], xrows(x_src, u))
                      sp.dma([DB("QT", u)], [qTl[a]], qTl[a][:].rearrange("p a t -> p (a t)"), QT[u])
                      sp.dma([DB("KT", u)], [kTl[a]], kTl[a][:].rearrange("p a t -> p (a t)"), KT[u])
                      for s in range(2):
                          c = 2 * u + s
                          sp.dma([DB("KK", c)], [kk[a]], kk[a][:, s, :], KK[c])
                          sp.dma([DB("VV", c)], [vv[a]], vv[a][:, s, :], VV[c])
                          sp.dma([DB("SG", c)], [sgl[a]], sgl[a][:, s, :], SG[c])
                          sp.dma([DB("GT", c)], [gtl[a]], gtl[a][:, s, :], GT[c])
                          sp.dma([DB("YC", c)], [ycl[a]], ycl[a][:, s, :], YC[c])
                          sp.dma([DB("SB", c)], [sbw[a]], sbw[a][:, s, :], SB[c])

                  p3a_load(0)
                  hi = 0
                  for u in range(NU):
                      a = u % 2
                      if u + 1 < NU:
                          p3a_load(u + 1)
                      if u == units[u][0]:
                          pool.op([], [Sst], lambda h: h.memset(Sst[:], 0.0))
                          pool.op([], [Sbf], lambda h: h.memset(Sbf[:], 0.0))
                      for s in range(2):
                          ts = slice(s * 128, (s + 1) * 128)
                          sbv = sbw[a][:, s, :].rearrange("p (a e) -> p a e", a=8)
                          for hd in range(4):
                              bA = PB[hi % 2]
                              bO = PB[2 + hi % 2]
                              bC = PB[4 + hi % 2]
                              pt = PT[hi % 2]
                              t_f = tf[hi % 2]
                              hi += 1
                              mm_group(bA, [kTl[a], qTl[a]], [(bA[:, 0:128], kTl[a][:, 2 * hd + dc, ts], qTl[a][:, 2 * hd + dc, ts])
                                                             for dc in range(2)])
                              dve.op([bA, decT], [pt], lambda h, hd=hd, bA=bA, pt=pt: h.tensor_tensor(
                                  out=pt[:], in0=bA[:, 0:128], in1=decT[:, hd, :], op=ALU.mult))
                              mm_group(bO, [pt, vv[a]], [(bO[:, 0:256], pt[:], vv[a][:, s, hd * 256:(hd + 1) * 256])])
                              mm_group(bO, [qTl[a], Sbf], [(bO[:, 256:512], qTl[a][:, 2 * hd + dc, ts], Sbf[:, 2 * hd + dc, :])
                                                          for dc in range(2)])
                              mm_group(bC, [qTl[a], sbw[a]], [(bC[:, 0:256], qTl[a][:, 2 * hd + dc, ts], sbv[:, 2 * hd + dc, :])
                                                             for dc in range(2)])
                              act.op([bO, dsc], [t_f], lambda h, hd=hd, bO=bO, t_f=t_f: h.activation(
                                  out=t_f[:], in_=bO[:, 256:512], func=AF.Copy, scale=dsc[:, A_F, hd:hd + 1]))
                              dve.op([bC, dsc, t_f], [t_f], lambda h, hd=hd, bC=bC, t_f=t_f: h.scalar_tensor_tensor(
                                  out=t_f[:], in0=bC[:, 0:256], scalar=dsc[:, A_B, hd:hd + 1], in1=t_f[:],
                                  op0=ALU.mult, op1=ALU.add))
                              dve.op([bO, t_f], [o_sb], lambda h, hd=hd, bO=bO, t_f=t_f: h.tensor_tensor(
                                  out=o_sb[:, hd * 256:(hd + 1) * 256], in0=bO[:, 0:256], in1=t_f[:], op=ALU.add))
                          for hd in range(4):
                              act.op([vv[a], dsc], [vdf], lambda h, hd=hd, s=s: h.activation(
                                  out=vdf[:, hd * 256:(hd + 1) * 256], in_=vv[a][:, s, hd * 256:(hd + 1) * 256],
                                  func=AF.Copy, scale=dsc[:, VD_F, hd:hd + 1]))
                          for hd in range(4):
                              bank = PB[6 + hd % 2]
                              for dc in range(2):
                                  mm_group(bank, [kk[a], vdf], [(bank[:, dc * 256:(dc + 1) * 256],
                                           kk[a][:, s, (2 * hd + dc) * 128:(2 * hd + dc + 1) * 128],
                                           vdf[:, hd * 256:(hd + 1) * 256])])
                              dve.op([bank, dsc, Sst], [Sst], lambda h, hd=hd, bank=bank: h.scalar_tensor_tensor(
                                  out=Sst[:, 2 * hd:2 * hd + 2, :], in0=Sst[:, 2 * hd:2 * hd + 2, :],
                                  scalar=dsc[:, CD_F, hd:hd + 1],
                                  in1=bank[:, :].rearrange("p (a e) -> p a e", a=2), op0=ALU.mult, op1=ALU.add))
                          act.op([Sst], [Sbf], lambda h: h.activation(out=Sbf[:], in_=Sst[:], func=AF.Copy))
                          for hd in range(4):
                              act.op([o_sb], [junk, gn[0]], lambda h, hd=hd: h.activation(
                                  out=junk[:], in_=o_sb[:, hd * 256:(hd + 1) * 256], func=AF.Square,
                                  accum_out=gn[0][:, hd:hd + 1]))
                          dve.op([gn[0]], [gn[1]], lambda h: h.tensor_scalar(
                              out=gn[1][:], in0=gn[0][:], scalar1=1.0 / 256.0, scalar2=EPS, op0=ALU.mult, op1=ALU.add))
                          act.op([gn[1]], [gn[2]], lambda h: h.activation(out=gn[2][:], in_=gn[1][:], func=AF.Sqrt))
                          dve.op([gn[2]], [gn[3]], lambda h: h.reciprocal(out=gn[3][:], in_=gn[2][:]))
                          for hd in range(4):
                              dve.op([o_sb, gn[3], sgl[a]], [om], lambda h, hd=hd, s=s: h.scalar_tensor_tensor(
                                  out=om[:, hd * 256:(hd + 1) * 256], in0=o_sb[:, hd * 256:(hd + 1) * 256],
                                  scalar=gn[3][:, hd:hd + 1], in1=sgl[a][:, s, hd * 256:(hd + 1) * 256],
                                  op0=ALU.mult, op1=ALU.mult))
                          bT = PB[6 + s]
                          v = pbf(bT)
                          transposes([bT], [om, identb], [(v[:, cc * 128:(cc + 1) * 128], om[:, cc * 128:(cc + 1) * 128], identb[:])
                                                          for cc in range(8)])
                          act.op([bT], [omT], lambda h, s=s, bT=bT: h.activation(
                              out=omT[:, :, s * 128:(s + 1) * 128], in_=pbf(bT).rearrange("p (a t) -> p a t", a=8), func=AF.Copy))
                          for blk in range(2):
                              bank = PB[blk]
                              mm_group(bank, [omT, wro], [(bank[:, :], omT[:, cc, ts], wro[:, cc, blk * 512:(blk + 1) * 512])
                                                         for cc in range(8)])
                              cs_ = slice(blk * 512, (blk + 1) * 512)
                              pool.op([ycl[a], gtl[a]], [m1], lambda h, s=s, cs_=cs_: h.tensor_tensor(
                                  out=m1[:, cs_], in0=ycl[a][:, s, cs_], in1=gtl[a][:, s, cs_], op=ALU.mult))
                              dve.op([bank, gtl[a], m1], [mg], lambda h, s=s, blk=blk, bank=bank, cs_=cs_: h.tensor_tensor(
                                  out=mg[:, cs_], in0=bank[:, :], in1=gtl[a][:, s, 1024 + blk * 512:1024 + (blk + 1) * 512],
                                  op=ALU.mult))
                              pool.op([m1, mg], [mg], lambda h, cs_=cs_: h.tensor_tensor(
                                  out=mg[:, cs_], in0=mg[:, cs_], in1=m1[:, cs_], op=ALU.add))
                          bT2 = PB[4 + s]
                          v2 = pbf(bT2)
                          transposes([bT2], [mg, identb], [(v2[:, cc * 128:(cc + 1) * 128], mg[:, cc * 128:(cc + 1) * 128], identb[:])
                                                           for cc in range(8)])
                          act.op([bT2], [mgT], lambda h, s=s, bT2=bT2: h.activation(
                              out=mgT[:, :, s * 128:(s + 1) * 128], in_=pbf(bT2).rearrange("p (a t) -> p a t", a=8), func=AF.Copy))
                          for blk in range(2):
                              bank = PB[2 + blk]
                              mm_group(bank, [mgT, wo], [(bank[:, :], mgT[:, cc, ts], wo[:, cc, blk * 512:(blk + 1) * 512])
                                                        for cc in range(8)])
                              dve.op([bank, xin[a]], [xo[a]], lambda h, s=s, blk=blk, bank=bank: h.tensor_tensor(
                                  out=xo[a][:, s, blk * 512:(blk + 1) * 512], in0=bank[:, :],
                                  in1=xin[a][:, s, blk * 512:(blk + 1) * 512], op=ALU.add))
                      sw.dma([xo[a]], [DB("XM", u)], xrows(XM, u), xo[a][:])
              S.barrier()

              chk()
              with ExitStack() as ph:
                  wq = sb(ph, "wq", [128, 8, 2048], BF16)
                  load_weight_bf16(wq, peer_wq[l], 8, 2048, 0)
                  gtile = sb(ph, "g3b", [128, D], F32)
                  bcast_row(gtile, norm_ffn_g[l:l + 1, :])
                  if last and do_final:
                      gfin = sb(ph, "gfin", [128, D], F32)
                      bcast_row(gfin, final_g[0:1, :])
                  skT = sb(ph, "skT", [128, 16, 128], BF16)
                  with ExitStack() as ph2:
                      skraw = sb(ph2, "skraw", [128, 16, 128], F32)
                      sp.dma([], [skraw], skraw[:], peer_sk[l].rearrange("h p k d -> k (h p) d"))
                      for cq in range(16):
                          bank = PB[cq % 8]
                          transposes([bank], [skraw, cst], [(bank[:, 0:128], skraw[:, cq, :], ident_f)])
                          dve.op([bank], [skT], lambda h, cq=cq, bank=bank: h.tensor_copy(out=skT[:, cq, :], in_=bank[:, 0:128]))
                      S.barrier()
                  xw = sb(ph, "xw", [128, 2, D], F32)
                  xnT = [sb(ph, f"xnT{i}", [128, 8, 256], BF16) for i in range(2)]
                  small = [sb(ph, f"sm{i}", [128, 2], F32) for i in range(4)]
                  qTp = sb(ph, "qTp", [128, 16, 256], BF16)
                  sc_ = [sb(ph, f"sc{i}", [128, 16, 128], F32) for i in range(2)]
                  sc2 = sb(ph, "sc2", [128, 16, 128], F32)
                  xn = TT(sc2.t[:].rearrange("p a k -> p (a k)").bitcast(BF16)[:, 0:2 * D].rearrange("p (s d) -> p s d", s=2),
                          "xn_alias")
                  xn.b = sc2.b
                  junk = (xn[:, 0, :], xn)
                  sv = sb(ph, "sv", [128, 16, 16], F32)
                  si = sb(ph, "si", [128, 16, 16], U32)
                  sif = sb(ph, "sif", [128, 16, 16], F32)
                  cand_ = []
                  for i_ in range(2):
                      c_ = TT(sc_[i_].t[:].rearrange("p (h two) k -> p h (two k)", two=2), f"cand_alias{i_}")
                      c_.b = sc_[i_].b
                      cand_.append(c_)
                  cand2 = TT(sc2.t[:].rearrange("p (h two) k -> p h (two k)", two=2), "cand2_alias")
                  cand2.b = sc2.b
                  s16_ = [sb(ph, f"s16_{i}", [128, 8, 16], F32) for i in range(2)]
                  ci = sb(ph, "ci", [128, 8, 16], U32)
                  cia = sb(ph, "cia", [128, 8, 16], U32)
                  cib = sb(ph, "cib", [128, 8, 16], U32)
                  caf = sb(ph, "caf", [128, 8, 16], F32)
                  cbf = sb(ph, "cbf", [128, 8, 16], F32)
                  eq = TT(sc2.t[:].rearrange("p (h two) (a b) -> p h (two a) b", two=2, b=16), "eq_alias")
                  eq.b = sc2.b
                  rowi_ = [[sb(ph, f"rowi{j}_{i}", [128, 128], F32) for i in range(3)] for j in range(2)]
                  gsum = sb(ph, "gsum", [128, 8], F32)
                  colT = [sb(ph, f"colT{i}", [128, 256], F32) for i in range(3)]
                  TBK = 4
                  P1 = [sb(ph, f"P1_{i}", [128, TBK, 128], BF16) for i in range(2)]
                  P2 = [sb(ph, f"P2_{i}", [128, TBK, 128], BF16) for i in range(2)]
                  GTs = sb(ph, "GTs", [128, 128, 256], BF16)
                  hT = [sb(ph, f"hT{i}", [128, 256], BF16) for i in range(3)]
                  gh = [sb(ph, f"gh{i}", [128, 256], BF16) for i in range(4)]
                  P2ENG = pool if CFG.get("P2POOL") else dve
                  iota3 = sb(ph, "iota3", [128, TBK, 128], BF16)
                  for tt_ in range(TBK):
                      dve.op([iotab], [iota3], lambda h, tt_=tt_: h.tensor_copy(out=iota3[:, tt_, :], in_=iotab[:]))
                  NSB = CFG["NSB"]
                  LOOK = CFG["LOOK"]
                  ub = [sb(ph, f"ub{i}", [128, 8, 256], BF16) for i in range(NSB)]
                  vbuf = [sb(ph, f"vbuf{i}", [128, 2, D], BF16) for i in range(NSB)]

                  def route_A(u):
                      a = u % 2
                      sp.dma([DB("XM", u)], [xw], xw[:], xrows(XM, u))
                      norm_T(xw, gtile, xn, xnT[a], junk, small, PB[6], PB[7], (act, dve))
                      for cq in range(16):
                          bank = PB[4 + cq % 2]
                          mm_group(bank, [wq, xnT[a]], [(bank[:, 0:256], wq[:, kc, cq * 128:(cq + 1) * 128], xnT[a][:, kc, :])
                                                       for kc in range(8)])
                          act.op([bank], [qTp], lambda h, cq=cq, bank=bank: h.activation(
                              out=qTp[:, cq, :], in_=bank[:, 0:256], func=AF.Copy))
                      for s in range(2):
                          ts = slice(s * 128, (s + 1) * 128)
                          sc = sc_[s]
                          for q4 in range(4):
                              bank = PB[4 + q4 % 2]
                              for j in range(4):
                                  cq = q4 * 4 + j
                                  mm_group(bank, [qTp, skT], [(bank[:, j * 128:(j + 1) * 128], qTp[:, cq, ts], skT[:, cq, :])])
                              act.op([bank], [sc], lambda h, q4=q4, bank=bank, sc=sc: h.activation(
                                  out=sc[:, q4 * 4:(q4 + 1) * 4, :], in_=bank[:, :].rearrange("p (a k) -> p a k", a=4),
                                  func=AF.Copy))
                      for s in range(2):
                          if "A2" in SKIP:
                              break
                          ts = slice(s * 128, (s + 1) * 128)
                          sc = sc_[s]
                          cand = cand_[s]
                          s16 = s16_[s]
                          rowi = rowi_[s]
                          for cq in range(16):
                              dve.op([sc], [sv], lambda h, cq=cq: h.max(out=sv[:, cq, 0:8], in_=sc[:, cq, :]))
                              dve.op([sc, sv], [si], lambda h, cq=cq: h.max_index(out=si[:, cq, 0:8], in_max=sv[:, cq, 0:8], in_values=sc[:, cq, :]))
                              dve.op([sc, sv], [sc2], lambda h, cq=cq: h.match_replace(
                                  out=sc2[:, cq, :], in_to_replace=sv[:, cq, 0:8], in_values=sc[:, cq, :], imm_value=NEG))
                              dve.op([sc2], [sv], lambda h, cq=cq: h.max(out=sv[:, cq, 8:16], in_=sc2[:, cq, :]))
                              dve.op([sc2, sv], [si], lambda h, cq=cq: h.max_index(out=si[:, cq, 8:16], in_max=sv[:, cq, 8:16], in_values=sc2[:, cq, :]))
                          dve.op([si], [sif], lambda h: h.tensor_copy(out=sif[:], in_=si[:]))
                          sv4 = sv[:].rearrange("p (h two) k -> p h two k", two=2)
                          sif4 = sif[:].rearrange("p (h two) k -> p h two k", two=2)
                          cand4 = cand[:].rearrange("p h (a b) -> p h a b", a=16)
                          dve.op([sv], [cand], lambda h: h.tensor_tensor(
                              out=cand4, in0=sv4[:, :, 0, :].unsqueeze(3).to_broadcast([128, 8, 16, 16]),
                              in1=sv4[:, :, 1, :].unsqueeze(2).to_broadcast([128, 8, 16, 16]), op=ALU.add))
                          for hd in range(8):
                              dve.op([cand], [s16], lambda h, hd=hd: h.max(out=s16[:, hd, 0:8], in_=cand[:, hd, :]))
                              dve.op([cand, s16], [ci], lambda h, hd=hd: h.max_index(out=ci[:, hd, 0:8], in_max=s16[:, hd, 0:8], in_values=cand[:, hd, :]))
                              dve.op([cand, s16], [cand2], lambda h, hd=hd: h.match_replace(
                                  out=cand2[:, hd, :], in_to_replace=s16[:, hd, 0:8], in_values=cand[:, hd, :], imm_value=NEG))
                              dve.op([cand2], [s16], lambda h, hd=hd: h.max(out=s16[:, hd, 8:16], in_=cand2[:, hd, :]))
                              dve.op([cand2, s16], [ci], lambda h, hd=hd: h.max_index(out=ci[:, hd, 8:16], in_max=s16[:, hd, 8:16], in_values=cand2[:, hd, :]))
                          dve.op([ci], [cia], lambda h: h.tensor_single_scalar(out=cia[:], in_=ci[:], scalar=4, op=ALU.logical_shift_right))
                          dve.op([ci], [cib], lambda h: h.tensor_single_scalar(out=cib[:], in_=ci[:], scalar=15, op=ALU.bitwise_and))
                          dve.op([cia], [caf], lambda h: h.tensor_copy(out=caf[:], in_=cia[:]))
                          dve.op([cib], [cbf], lambda h: h.tensor_copy(out=cbf[:], in_=cib[:]))
                          io4 = iota16.unsqueeze(1).unsqueeze(1).to_broadcast([128, 8, 16, 16])
                          for which, (cf, half) in enumerate(((caf, 0), (cbf, 1))):
                              dst3 = rowi[which][:].rearrange("p (h k) -> p h k", h=8)
                              dve.op([cf, cst], [eq], lambda h, cf=cf: h.tensor_tensor(
                                  out=eq[:], in0=cf[:].unsqueeze(3).to_broadcast([128, 8, 16, 16]), in1=io4, op=ALU.is_equal))
                              dve.op([eq, sif], [eq], lambda h, half=half: h.tensor_tensor(
                                  out=eq[:], in0=eq[:], in1=sif4[:, :, half, :].unsqueeze(2).to_broadcast([128, 8, 16, 16]),
                                  op=ALU.mult))
                              dve.op([eq], [rowi[which]], lambda h, dst3=dst3: h.tensor_reduce(
                                  out=dst3, in_=eq[:], axis=AX.X, op=ALU.add))

                  def route_B(u):
                      for s in range(2):
                          ts = slice(s * 128, (s + 1) * 128)
                          s16 = s16_[s]
                          rowi = rowi_[s]
                          g3 = rowi[2][:].rearrange("p (h k) -> p h k", h=8)
                          dve.op([s16], [rowi[2]], lambda h: h.tensor_tensor(
                              out=g3, in0=s16[:], in1=s16[:, :, 0:1].to_broadcast([128, 8, 16]), op=ALU.subtract))
                          act.op([rowi[2]], [rowi[2]], lambda h: h.activation(out=rowi[2][:], in_=rowi[2][:], func=AF.Exp))
                          dve.op([rowi[2]], [gsum], lambda h: h.tensor_reduce(out=gsum[:], in_=g3, axis=AX.X, op=ALU.add))
                          dve.op([gsum], [gsum], lambda h: h.reciprocal(out=gsum[:], in_=gsum[:]))
                          dve.op([rowi[2], gsum], [rowi[2]], lambda h: h.tensor_tensor(
                              out=g3, in0=g3, in1=gsum[:].unsqueeze(2).to_broadcast([128, 8, 16]), op=ALU.mult))
                          for w3 in range(3):
                              bank = PB[6 + w3 % 2]
                              transposes([bank], [rowi[w3], cst], [(bank[:, 0:128], rowi[w3][:], ident_f)])
                              act.op([bank], [colT[w3]], lambda h, w3=w3, bank=bank, ts=ts: h.activation(
                                  out=colT[w3][:, ts], in_=bank[:, 0:128], func=AF.Copy))
                      bi = 0
                      for t0 in range(0, 256, TBK):
                          p1 = P1[(t0 // TBK) % 2]
                          p2 = P2[(t0 // TBK) % 2]
                          for tt in range(TBK):
                              t = t0 + tt
                              dve.op([iotab, colT[0], colT[2]], [p1], lambda h, t=t, tt=tt, p1=p1: h.tensor_scalar(
                                  out=p1[:, tt, :], in0=iotab[:], scalar1=colT[0][:, t:t + 1], scalar2=colT[2][:, t:t + 1],
                                  op0=ALU.is_equal, op1=ALU.mult))
                              if not CFG.get("P2BATCH"):
                                  P2ENG.op([iotab, colT[1]], [p2], lambda h, t=t, tt=tt, p2=p2: h.tensor_scalar(
                                      out=p2[:, tt, :], in0=iotab[:], scalar1=colT[1][:, t:t + 1], scalar2=None,
                                      op0=ALU.is_equal))
                          if CFG.get("P2BATCH"):
                              dve.op([iota3, colT[1]], [p2], lambda h, t0=t0, p2=p2: h.tensor_tensor(
                                  out=p2[:], in0=iota3[:],
                                  in1=colT[1][:, t0:t0 + TBK].unsqueeze(2).to_broadcast([128, TBK, 128]), op=ALU.is_equal))
                          for q in range(TBK // 4):
                              bank = PB[4 + bi % 4]
                              bi += 1
                              for j in range(4):
                                  tt = q * 4 + j
                                  mm_group(bank, [p1, p2], [(bank[:, j * 128:(j + 1) * 128], p2[:, tt, :], p1[:, tt, :])])
                              tq = t0 + q * 4
                              act.op([bank], [GTs], lambda h, bank=bank, tq=tq: h.activation(
                                  out=GTs[:, :, tq:tq + 4].transpose([0, 2, 1]),
                                  in_=bank[:, :].rearrange("p (t i) -> p t i", t=4), func=AF.Copy))

                  def dense(u):
                      a = u % 2
                      steps = []
                      for g in range(64):
                          for c in range(2):
                              steps.append((g, c))

                      def emit_load(g):
                          if "dload" in SKIP and g >= NSB:
                              return
                          sp.dma([DB("UTS", (l, g))], [ub[g % NSB]], ub[g % NSB][:].rearrange("p a e -> p (a e)"), UTS[l, g])
                          sp.dma([DB("VS", (l, g))], [vbuf[g % NSB]], vbuf[g % NSB][:].rearrange("p a e -> p (a e)"), VS[l, g])

                      def emit_H(i):
                          g, c = steps[i]
                          bank = PB[4 + i % (LOOK + 1)]
                          mm_group(bank, [ub[g % NSB], xnT[a]], [(bank[:, 0:256], ub[g % NSB][:, kc, c * 128:(c + 1) * 128], xnT[a][:, kc, :])
                                                                for kc in range(8)])
                          h_ = hT[i % 3]
                          g_ = gh[i % 4]
                          act.op([bank], [h_], lambda h, bank=bank, h_=h_: h.activation(out=h_[:], in_=bank[:, 0:256], func=AF.Gelu))
                          (dve if CFG.get("GHENG") else pool).op([h_, GTs], [g_], lambda h, h_=h_, g_=g_, i1=2 * g + c: h.tensor_tensor(
                              out=g_[:], in0=h_[:], in1=GTs[:, i1, :], op=ALU.mult))

                      def emit_V(i):
                          g, c = steps[i]
                          g_ = gh[i % 4]
                          for s in range(2):
                              for blk in range(2):
                                  bank = PB[s * 2 + blk]
                                  pe.op([g_, vbuf[g % NSB]], [bank], lambda h, s=s, blk=blk, bank=bank, g_=g_, g=g, c=c, i=i: h.matmul(
                                      bank[:, :], lhsT=g_[:, s * 128:(s + 1) * 128], rhs=vbuf[g % NSB][:, c, blk * 512:(blk + 1) * 512],
                                      start=(i == 0), stop=(i == len(steps) - 1)))

                      for g0 in range(NSB - 1):
                          emit_load(g0)
                      for i0 in range(LOOK):
                          emit_H(i0)
                      for i in range(len(steps)):
                          g, c = steps[i]
                          if c == 0 and g + NSB - 1 < 64:
                              emit_load(g + NSB - 1)
                          if i + LOOK < len(steps):
                              emit_H(i + LOOK)
                          emit_V(i)

                  def finalize(u):
                      xo = xw
                      yo = xw
                      sp.dma([DB("XM", u)], [xw], xw[:], xrows(XM, u))
                      for s in range(2):
                          for blk in range(2):
                              bank = PB[s * 2 + blk]
                              dve.op([bank, xw], [xw], lambda h, s=s, blk=blk, bank=bank: h.tensor_tensor(
                                  out=xw[:, s, blk * 512:(blk + 1) * 512], in0=bank[:, :],
                                  in1=xw[:, s, blk * 512:(blk + 1) * 512], op=ALU.add))
                      if last and do_final:
                          ss, ms, sq, rstd = small
                          for s in range(2):
                              act.op([xo], [xn, ss], lambda h, s=s: h.activation(
                                  out=xn[:, 0, :], in_=xo[:, s, :], func=AF.Square, accum_out=ss[:, s:s + 1]))
                          dve.op([ss], [ms], lambda h: h.tensor_scalar(
                              out=ms[:], in0=ss[:], scalar1=1.0 / D, scalar2=EPS, op0=ALU.mult, op1=ALU.add))
                          act.op([ms], [sq], lambda h: h.activation(out=sq[:], in_=ms[:], func=AF.Sqrt))
                          dve.op([sq], [rstd], lambda h: h.reciprocal(out=rstd[:], in_=sq[:]))
                          for s in range(2):
                              dve.op([xo, rstd, gfin], [yo], lambda h, s=s: h.scalar_tensor_tensor(
                                  out=yo[:, s, :], in0=xo[:, s, :], scalar=rstd[:, s:s + 1], in1=gfin[:],
                                  op0=ALU.mult, op1=ALU.mult))
                          sw.dma([yo], [DB("Y", u)], xrows(x_dst, u), yo[:])
                      else:
                          sw.dma([xo], [DB("X", u)], xrows(x_dst, u), xo[:])

                  route_A(0)
                  if "routeB" not in SKIP:
                      route_B(0)
                  for u in range(NU):
                      if u + 1 < NU:
                          route_A(u + 1)
                      if "dense" not in SKIP:
                          dense(u)
                      finalize(u)
                      if u + 1 < NU and "routeB" not in SKIP:
                          route_B(u + 1)
              S.barrier()
              lay.close()
        except _Stop:
            pass
        S.barrier()
    return nc, S.ninst


_WEIGHT_KEYS = ["norm_mix_g", "w_in", "w_gate", "conv_w", "conv_b", "conv_ln_g", "conv_ln_b", "w_conv_out",
                "log_gamma_fwd", "log_gamma_bwd", "w_ret_out", "w_o", "norm_ffn_g", "peer_wq", "peer_subkeys",
                "peer_u", "peer_v"]


def run_cores(x_per_core, weights, seqs, n_layers=2, do_final=True, stop=0):
    nc, ninst = build(seqs, n_layers=n_layers, do_final=do_final, stop=stop)
    consts, cosT, sinT = host_consts(max(seqs))
    base = {k: np.ascontiguousarray(np.asarray(weights[k], dtype=np.float32)) for k in _WEIGHT_KEYS}
    base["final_norm_g"] = np.ascontiguousarray(np.asarray(weights["final_norm_g"], np.float32).reshape(1, D))
    base["consts"] = consts
    base["cosT"] = cosT
    base["sinT"] = sinT
    in_maps = []
    for xc in x_per_core:
        m = dict(base)
        m["x"] = np.ascontiguousarray(xc, dtype=np.float32)
        in_maps.append(m)
    res = run_bass_kernel_spmd(nc, in_maps, core_ids=list(range(len(x_per_core))))
    return [r["y"] for r in res.results]


def kernel(x_prompt, x_sample, **weights):
    x_prompt = np.asarray(x_prompt, dtype=np.float32)
    x_sample = np.asarray(x_sample, dtype=np.float32)
    xs = []
    for c in range(N_CORES):
        xs.append(np.concatenate([x_prompt[c], x_sample[2 * c], x_sample[2 * c + 1]], axis=0))
    ys = run_cores(xs, weights, SEQS_FULL)
    y_prompt = np.stack([y[0:8192] for y in ys], axis=0)
    y_sample = np.stack([ys[c // 2][8192 + 2048 * (c % 2):8192 + 2048 * (c % 2 + 1)] for c in range(16)], axis=0)
    return (y_prompt.astype(np.float32), y_sample.astype(np.float32))
```

```python
from contextlib import ExitStack
import numpy as np
import concourse.bass as bass
import concourse.mybir as mybir
from concourse.bass_utils import run_bass_kernel_spmd

F32 = mybir.dt.float32
BF16 = mybir.dt.bfloat16
U32 = mybir.dt.uint32
AF = mybir.ActivationFunctionType
ALU = mybir.AluOpType
AX = mybir.AxisListType

D = 1024
EPS = 1e-6
NEG = -1.0e30
N_CORES = 8
SEQS_FULL = (8192, 2048, 2048)


class Buf:
    __slots__ = ("name", "lw", "rd")

    def __init__(self, name):
        self.name = name
        self.lw = None
        self.rd = {}


class TT:
    def __init__(self, t, name):
        self.t = t
        self.b = Buf(name)

    def __getitem__(self, k):
        return self.t[k]


def _b(x):
    return x.b if isinstance(x, TT) else x


class Eng:
    def __init__(self, S, name, h, nslots=0, own_skip=False):
        self.S = S
        self.name = name
        self.h = h
        self.waited = {}
        self.own_skip = own_skip
        self.nslots = nslots
        if nslots == 0:
            self.sem = S.new_sem(name)
            self.count = 0
        else:
            self.slots = [[S.new_sem(f"{name}{i}"), 0] for i in range(nslots)]
            self.rr = 0

    def _wait(self, tick):
        sem, val, key = tick
        if self.own_skip and self.nslots == 0 and key == self.sem_key():
            return
        if self.waited.get(key, 0) >= val:
            return
        self.h.wait_ge(sem, val)
        self.waited[key] = val

    def sem_key(self):
        return self.name

    def _deps(self, reads, writes):
        own = self.name if self.nslots == 0 else None
        for b in reads:
            b = _b(b)
            if b.lw is not None:
                self._wait(b.lw)
        for b in writes:
            b = _b(b)
            if b.lw is not None and b.lw[2] != own:
                self._wait(b.lw)
            for t in b.rd.values():
                if t[2] != own:
                    self._wait(t)

    def _mark(self, reads, writes, tick):
        for b in reads:
            _b(b).rd[tick[2]] = tick
        for b in writes:
            b = _b(b)
            b.lw = tick
            b.rd = {}

    def op(self, reads, writes, fn):
        if DEAD[0]:
            return
        self._deps(reads, writes)
        inst = fn(self.h)
        self.count += 1
        inst.then_inc(self.sem, 1)
        tick = (self.sem, self.count, self.name)
        self._mark(reads, writes, tick)
        self.S.ninst += 1

    def dma(self, reads, writes, out, in_, **kw):
        if DEAD[0]:
            return
        slot = self.slots[self.rr]
        key = f"{self.name}{self.rr}"
        self.rr = (self.rr + 1) % self.nslots
        if slot[1] > 0:
            self._wait((slot[0], slot[1], key))
        self._deps(reads, writes)
        inst = self.h.dma_start(out=out, in_=in_, **kw)
        slot[1] += 16
        inst.then_inc(slot[0], 16)
        tick = (slot[0], slot[1], key)
        self._mark(reads, writes, tick)
        self.S.ninst += 1

    def last_ticks(self):
        if self.nslots == 0:
            return [(self.sem, self.count, self.name)] if self.count else []
        return [(s[0], s[1], f"{self.name}{i}") for i, s in enumerate(self.slots) if s[1]]


class Sched:
    def __init__(self, nc, stack):
        self.nc = nc
        self.stack = stack
        self.ninst = 0
        self.pe = Eng(self, "pe", nc.tensor, own_skip=True)
        self.act = Eng(self, "act", nc.scalar)
        self.dve = Eng(self, "dve", nc.vector)
        self.pool = Eng(self, "pool", nc.gpsimd)
        self.sp = Eng(self, "sp", nc.sync, nslots=8)
        self.sw = Eng(self, "sw", nc.gpsimd, nslots=6)
        self.all = [self.pe, self.act, self.dve, self.pool, self.sp, self.sw]

    def new_sem(self, name):
        return self.stack.enter_context(self.nc.semaphore("s_" + name))

    def barrier(self):
        ticks = []
        for e in self.all:
            ticks += e.last_ticks()
        for e in (self.pe, self.act, self.dve, self.pool, self.sp):
            for t in ticks:
                if e.nslots == 0 and t[2] == e.name:
                    if e.own_skip:
                        continue
                e._wait(t)


def host_consts(lmax):
    c = np.zeros((128, 1024), np.float32)
    p = np.arange(128, dtype=np.float32)
    c[:, 0:128] = np.eye(128, dtype=np.float32)
    c[:, 128:256] = p[None, :]
    diff = p[None, :] - p[:, None]
    c[:, 256:384] = np.maximum(diff, 0.0)
    c[:, 384:512] = np.maximum(-diff, 0.0)
    c[:, 512:640] = (diff >= 0).astype(np.float32) / 16.0
    c[:, 640:768] = (diff < 0).astype(np.float32) / 16.0
    c[:, 768] = p + 1.0
    c[:, 769] = 128.0 - p
    c[:, 770] = 127.0 - p
    c[:, 771] = p
    c[:, 772] = 128.0
    c[:, 776:792] = np.arange(16, dtype=np.float32)[None, :]
    c[:, 800:928] = 1.0 / 1024.0
    half = 128
    freqs = (1.0 / (10000.0 ** (np.arange(half, dtype=np.float32) / np.float32(half)))).astype(np.float32)
    ang = np.arange(lmax, dtype=np.float32)[:, None] * freqs[None, :]
    cosT = np.ascontiguousarray(np.cos(ang).astype(np.float32).T)
    sinT = np.ascontiguousarray(np.sin(ang).astype(np.float32).T)
    return c, cosT, sinT


class _Stop(Exception):
    pass


SUBSTOP = [0]
SKIP = set()
CFG = {"NSB": 3, "LOOK": 2}


DEAD = [False]


def sub(n):
    if SUBSTOP[0] == n:
        DEAD[0] = True


def build(seqs, n_layers=2, do_final=True, stop=0):
    T_tok = sum(seqs)
    NU = T_tok // 256
    NCH = T_tok // 128
    lmax = max(seqs)
    nc = bass.Bass("TRN2", target_bir_lowering=False)
    DEAD[0] = False

    def din(name, shape, dt=F32):
        return nc.dram_tensor(name, list(shape), dt, kind="ExternalInput").ap()

    def dscr(name, shape, dt):
        return nc.dram_tensor(name, list(shape), dt, kind="Internal").ap()

    x_in = din("x", [T_tok, D])
    consts_d = din("consts", [128, 1024])
    cos_d = din("cosT", [128, lmax])
    sin_d = din("sinT", [128, lmax])
    norm_mix_g = din("norm_mix_g", [2, D])
    w_in = din("w_in", [2, D, 6144])
    w_gate = din("w_gate", [2, D, 2048])
    conv_w = din("conv_w", [2, 31, D])
    conv_b = din("conv_b", [2, D])
    conv_ln_g = din("conv_ln_g", [2, D])
    conv_ln_b = din("conv_ln_b", [2, D])
    w_conv_out = din("w_conv_out", [2, D, D])
    lg_f = din("log_gamma_fwd", [2, 4])
    lg_b = din("log_gamma_bwd", [2, 4])
    w_ret_out = din("w_ret_out", [2, D, D])
    w_o = din("w_o", [2, D, D])
    norm_ffn_g = din("norm_ffn_g", [2, D])
    peer_wq = din("peer_wq", [2, D, 2048])
    peer_sk = din("peer_subkeys", [2, 8, 2, 128, 128])
    peer_u = din("peer_u", [2, 16384, D])
    peer_v = din("peer_v", [2, 16384, D])
    final_g = din("final_norm_g", [1, D])
    y_out = nc.dram_tensor("y", [T_tok, D], F32, kind="ExternalOutput").ap()

    X1 = dscr("X1", [T_tok, D], F32)
    XM = dscr("XM", [T_tok, D], F32)
    UT = dscr("UT", [NU, 128, 8 * 256], BF16)
    QT = dscr("QT", [NU, 128, 8 * 256], BF16)
    KT = dscr("KT", [NU, 128, 8 * 256], BF16)
    KK = dscr("KK", [NCH, 128, D], BF16)
    VV = dscr("VV", [NCH, 128, D], BF16)
    SG = dscr("SG", [NCH, 128, D], BF16)
    GT = dscr("GT", [NCH, 128, 2048], BF16)
    YC = dscr("YC", [NCH, 128, D], BF16)
    SB = dscr("SB", [NCH, 128, 8 * 256], BF16)
    UTS = dscr("UTS", [2, 64, 128, 8 * 256], BF16)
    VS = dscr("VS", [2, 64, 128, 2 * 1024], BF16)

    units = []
    u0 = 0
    seq_ranges = []
    for L in seqs:
        nu = L // 256
        seq_ranges.append((u0, u0 + nu))
        for i in range(nu):
            units.append((u0, u0 + nu, i * 256))
        u0 += nu

    dbufs = {}

    def DB(name, idx):
        k = (name, idx)
        if k not in dbufs:
            dbufs[k] = Buf(f"{name}{idx}")
        return dbufs[k]

    with ExitStack() as top:
        S = Sched(nc, top)
        pe, act, dve, pool, sp, sw = S.pe, S.act, S.dve, S.pool, S.sp, S.sw

        uniq = [0]

        def sb(stack, name, shape, dt):
            uniq[0] += 1
            name = f"{name}_{uniq[0]}"
            return TT(stack.enter_context(nc.sbuf_tensor(name, list(shape), dt)), name)

        PB = [TT(top.enter_context(nc.psum_tensor(f"pb{i}", [128, 512], F32)), f"pb{i}") for i in range(8)]

        def pbf(bank):
            return bank.t[:].bitcast(BF16)

        cst = sb(top, "cst", [128, 1024], F32)
        sp.dma([], [cst], cst[:], consts_d[:, :])
        ident_f = cst[:, 0:128]
        identb = sb(top, "identb", [128, 128], BF16)
        iotab = sb(top, "iotab", [128, 128], BF16)
        dve.op([cst], [identb], lambda h: h.tensor_copy(out=identb[:], in_=cst[:, 0:128]))
        dve.op([cst], [iotab], lambda h: h.tensor_copy(out=iotab[:], in_=cst[:, 128:256]))
        iota16 = cst[:, 776:792]

        def mm_group(out_bank, reads, mms, extra_writes=()):
            n = len(mms)

            def fn(h):
                inst = None
                for i, (o, l, r) in enumerate(mms):
                    inst = h.matmul(o, lhsT=l, rhs=r, start=(i == 0), stop=(i == n - 1))
                return inst
            pe.op(reads, [out_bank] + list(extra_writes), fn)

        def transposes(out_banks, reads, trs):
            def fn(h):
                inst = None
                for (o, i_, idn) in trs:
                    inst = h.transpose(o, i_, idn)
                return inst
            pe.op(reads, list(out_banks), fn)

        def norm_T(xin, gtile, xn, xnT, junk, small, bankA, bankB, evac_engs):
            ss, ms, sq, rstd = small
            if isinstance(junk, tuple):
                jap, jb = junk
            else:
                jap, jb = junk[:], junk
            for s in range(2):
                act.op([xin], [jb, ss], lambda h, s=s: h.activation(
                    out=jap, in_=xin[:, s, :], func=AF.Square, accum_out=ss[:, s:s + 1]))
            dve.op([ss], [ms], lambda h: h.tensor_scalar(
                out=ms[:], in0=ss[:], scalar1=1.0 / D, scalar2=EPS, op0=ALU.mult, op1=ALU.add))
            act.op([ms], [sq], lambda h: h.activation(out=sq[:], in_=ms[:], func=AF.Sqrt))
            dve.op([sq], [rstd], lambda h: h.reciprocal(out=rstd[:], in_=sq[:]))
            for s in range(2):
                dve.op([xin, rstd, gtile], [xn], lambda h, s=s: h.scalar_tensor_tensor(
                    out=xn[:, s, :], in0=xin[:, s, :], scalar=rstd[:, s:s + 1], in1=gtile[:],
                    op0=ALU.mult, op1=ALU.mult))
            trs = []
            for kc in range(8):
                bank = bankA if kc < 4 else bankB
                v = pbf(bank)
                for s in range(2):
                    o = v[:, (kc % 4) * 256 + s * 128:(kc % 4) * 256 + (s + 1) * 128]
                    trs.append((o, xn[:, s, kc * 128:(kc + 1) * 128], identb[:]))
            transposes([bankA, bankB], [xn, identb], trs)
            e0, e1 = evac_engs
            e0.op([bankA], [xnT], lambda h: _copy(h, xnT[:, 0:4, :], pbf(bankA).rearrange("p (a t) -> p a t", a=4)))
            e1.op([bankB], [xnT], lambda h: _copy(h, xnT[:, 4:8, :], pbf(bankB).rearrange("p (a t) -> p a t", a=4)))

        def _copy(h, out, in_):
            if h is nc.scalar:
                return h.activation(out=out, in_=in_, func=AF.Copy)
            return h.tensor_copy(out=out, in_=in_)

        def load_weight_bf16(dst, src3, nk, ncols, col0=0):
            for kc in range(nk):
                for c0 in range(0, ncols, 2048):
                    c1 = min(ncols, c0 + 2048)
                    sw.dma([], [dst], dst[:, kc, c0:c1], src3[kc * 128:(kc + 1) * 128, col0 + c0:col0 + c1])

        def bcast_row(dst, row_ap):
            sp.dma([], [dst], dst[:], row_ap.partition_broadcast(128))

        phase_ctr = [0]

        def chk():
            phase_ctr[0] += 1
            if stop and phase_ctr[0] > stop:
                DEAD[0] = True

        with ExitStack() as ph:
            uf = [sb(ph, f"uf{i}", [128, 4, D], F32) for i in range(2)]
            uts = [sb(ph, f"uts{i}", [128, 8, 512], BF16) for i in range(2)]
            vb = [sb(ph, f"vb{i}", [128, 4, D], BF16) for i in range(2)]
            it = 0
            for l in range(n_layers):
                for g in range(32):
                    a = it % 2
                    it += 1
                    rows = peer_u[l, g * 512:(g + 1) * 512, :].rearrange("(c p) d -> p c d", p=128)
                    sp.dma([], [uf[a]], uf[a][:], rows)
                    for kc in range(8):
                        bank = PB[kc]
                        trs = [(bank[:, c * 128:(c + 1) * 128], uf[a][:, c, kc * 128:(kc + 1) * 128], ident_f)
                               for c in range(4)]
                        transposes([bank], [uf[a], cst], trs)
                        e = act if kc % 2 == 0 else dve
                        e.op([bank], [uts[a]], lambda h, kc=kc, bank=bank, a=a: _copy(h, uts[a][:, kc, :], bank[:, :]))
                    for hf in range(2):
                        sp.dma([uts[a]], [DB("UTS", (l, 2 * g + hf))], UTS[l, 2 * g + hf].rearrange("p (a e) -> p a e", a=8),
                               uts[a][:, :, hf * 256:(hf + 1) * 256])
                    vrows = peer_v[l, g * 512:(g + 1) * 512, :].rearrange("(c p) d -> p c d", p=128)
                    sw.dma([], [vb[a]], vb[a][:], vrows)
                    for hf in range(2):
                        sp.dma([vb[a]], [DB("VS", (l, 2 * g + hf))], VS[l, 2 * g + hf].rearrange("p (a e) -> p a e", a=2),
                               vb[a][:, 2 * hf:2 * hf + 2, :])
        S.barrier()

        try:
          for l in range(n_layers):
              chk()
              x_src = x_in if l == 0 else X1
              last = (l == n_layers - 1)
              x_dst = y_out if last else X1

              def xrows(src, u):
                  return src[u * 256:(u + 1) * 256, :].rearrange("(s p) d -> p s d", p=128)

              lay = top.enter_context(ExitStack())
              lgfb = sb(lay, "lgfb", [128, 4], F32)
              lgbb = sb(lay, "lgbb", [128, 4], F32)
              bcast_row(lgfb, lg_f[l:l + 1, :])
              bcast_row(lgbb, lg_b[l:l + 1, :])
              dsc = sb(lay, "dsc", [128, 6, 4], F32)
              specs = [(lgfb, 768, 0.0), (lgbb, 769, 0.0), (lgfb, 770, -np.log(16.0)), (lgbb, 771, -np.log(16.0)),
                       (lgfb, 772, 0.0), (lgbb, 772, 0.0)]
              for i, (lgt, col, bias) in enumerate(specs):
                  dve.op([cst, lgt], [dsc], lambda h, i=i, lgt=lgt, col=col: h.tensor_scalar(
                      out=dsc[:, i, :], in0=lgt[:], scalar1=cst[:, col:col + 1], scalar2=None, op0=ALU.mult))
              dve.op([dsc], [dsc], lambda h: h.tensor_scalar(
                  out=dsc[:, 2:4, :], in0=dsc[:, 2:4, :], scalar1=float(-np.log(16.0)), scalar2=None, op0=ALU.add))
              act.op([dsc], [dsc], lambda h: h.activation(out=dsc[:], in_=dsc[:], func=AF.Exp))
              A_F, A_B, VD_F, VD_B, CD_F, CD_B = range(6)
              if SUBSTOP[0] == 50:
                  sp.dma([dsc], [DB("Y", -1)], y_out[0:128, 0:24], dsc[:].rearrange("p a b -> p (a b)"))
                  sp.dma([decT], [DB("Y", -2)], y_out[0:128, 32:32 + 512], decT[:].rearrange("p a b -> p (a b)"))
                  sp.dma([lgfb], [DB("Y", -3)], y_out[0:128, 600:604], lgfb[:])
                  sp.dma([lgbb], [DB("Y", -4)], y_out[0:128, 608:612], lgbb[:])
                  sub(50)

              chk()
              with ExitStack() as ph:
                  wsb = sb(ph, "w1a", [128, 8, 4096], BF16)
                  load_weight_bf16(wsb, w_in[l], 8, 4096, 0)
                  gtile = sb(ph, "g1a", [128, D], F32)
                  bcast_row(gtile, norm_mix_g[l:l + 1, :])
                  xin = [sb(ph, f"xin{i}", [128, 2, D], F32) for i in range(2)]
                  cs = [sb(ph, f"cs{i}", [128, 2, 256], F32) for i in range(2)]
                  xn = sb(ph, "xn", [128, 2, D], BF16)
                  xnT = sb(ph, "xnT", [128, 8, 256], BF16)
                  junk = sb(ph, "junk", [128, D], BF16)
                  small = [sb(ph, f"sm{i}", [128, 2], F32) for i in range(4)]
                  uT = [sb(ph, f"uT{i}", [128, 8, 256], BF16) for i in range(2)]
                  qT = [sb(ph, f"qT{i}", [128, 8, 256], BF16) for i in range(2)]
                  kT = [sb(ph, f"kT{i}", [128, 8, 256], BF16) for i in range(2)]
                  ktok = [sb(ph, f"ktok{i}", [128, 2, D], BF16) for i in range(2)]
                  sig = [sb(ph, f"sig{i}", [128, 256], F32) for i in range(2)]
                  rt = [sb(ph, f"rt{i}", [128, 4, 256], F32) for i in range(4)]
                  kraw = [sb(ph, f"kraw{i}", [128, 2, 256], F32) for i in range(2)]

                  def p1a_load_x(u):
                      a = u % 2
                      sp.dma([DB("X", u)], [xin[a]], xin[a][:], xrows(x_src, u))

                  def p1a_load_cs(u):
                      a = u % 2
                      pos = units[u][2]
                      sp.dma([], [cs[a]], cs[a][:, 0, :], cos_d[:, pos:pos + 256])
                      sp.dma([], [cs[a]], cs[a][:, 1, :], sin_d[:, pos:pos + 256])

                  xn2_ = [xn, sb(ph, "xnB", [128, 2, D], BF16)]
                  xnT2_ = [xnT, sb(ph, "xnTB", [128, 8, 256], BF16)]
                  p1a_load_x(0)
                  p1a_load_cs(0)
                  if NU > 1:
                      p1a_load_x(1)
                  norm_T(xin[0], gtile, xn2_[0], xnT2_[0], junk, small, PB[6], PB[7], (act, dve))
                  pair_i = 0
                  for u in range(NU):
                      a = u % 2
                      xnT = xnT2_[a]
                      if u + 1 < NU:
                          p1a_load_cs(u + 1)
                      if u + 2 < NU:
                          p1a_load_x(u + 2)
                      if u + 1 < NU:
                          norm_T(xin[1 - a], gtile, xn2_[1 - a], xnT2_[1 - a], junk, small, PB[6], PB[7], (act, dve))
                      cosA = cs[a][:, 0, :]
                      sinA = cs[a][:, 1, :]
                      for pi in range(16):
                          bank = PB[pair_i % 6]
                          pair_i += 1
                          if pi < 8:
                              c1, c2 = pi, 8 + pi
                          elif pi < 12:
                              c1, c2 = 16 + 2 * (pi - 8), 17 + 2 * (pi - 8)
                          else:
                              c1, c2 = 24 + 2 * (pi - 12), 25 + 2 * (pi - 12)
                          for half, cc in enumerate((c1, c2)):
                              mms = [(bank[:, half * 256:(half + 1) * 256], wsb[:, kc, cc * 128:(cc + 1) * 128],
                                      xnT[:, kc, :]) for kc in range(8)]
                              mm_group(bank, [wsb, xnT], mms)
                          p1 = bank[:, 0:256]
                          p2 = bank[:, 256:512]
                          if pi < 8:
                              sg_ = sig[pi % 2]
                              act.op([bank], [sg_], lambda h, sg_=sg_, p2=p2: h.activation(
                                  out=sg_[:], in_=p2, func=AF.Sigmoid))
                              dve.op([bank, sg_], [uT[a]], lambda h, sg_=sg_, p1=p1, pi=pi: h.tensor_tensor(
                                  out=uT[a][:, pi, :], in0=p1, in1=sg_[:], op=ALU.mult))
                          elif pi < 12:
                              hd = pi - 8
                              r = rt[pi % 4]
                              dst = qT[a]
                              dve.op([bank, cs[a]], [r], lambda h, r=r: h.tensor_tensor(out=r[:, 0, :], in0=p1, in1=cosA, op=ALU.mult))
                              dve.op([bank, cs[a]], [r], lambda h, r=r: h.tensor_tensor(out=r[:, 1, :], in0=p2, in1=sinA, op=ALU.mult))
                              dve.op([bank, cs[a]], [r], lambda h, r=r: h.tensor_tensor(out=r[:, 2, :], in0=p1, in1=sinA, op=ALU.mult))
                              dve.op([bank, cs[a]], [r], lambda h, r=r: h.tensor_tensor(out=r[:, 3, :], in0=p2, in1=cosA, op=ALU.mult))
                              pool.op([r], [dst], lambda h, r=r, dst=dst, hd=hd: h.tensor_tensor(
                                  out=dst[:, 2 * hd, :], in0=r[:, 0, :], in1=r[:, 1, :], op=ALU.subtract))
                              pool.op([r], [dst], lambda h, r=r, dst=dst, hd=hd: h.tensor_tensor(
                                  out=dst[:, 2 * hd + 1, :], in0=r[:, 2, :], in1=r[:, 3, :], op=ALU.add))
                          else:
                              hd = pi - 12
                              r = rt[pi % 4]
                              dst = kT[a]
                              dve.op([bank, cs[a]], [r], lambda h, r=r: h.tensor_tensor(out=r[:, 0, :], in0=p1, in1=cosA, op=ALU.mult))
                              dve.op([bank, cs[a]], [r], lambda h, r=r: h.tensor_tensor(out=r[:, 1, :], in0=p2, in1=sinA, op=ALU.mult))
                              dve.op([bank, cs[a]], [r], lambda h, r=r: h.tensor_tensor(out=r[:, 2, :], in0=p1, in1=sinA, op=ALU.mult))
                              dve.op([bank, cs[a]], [r], lambda h, r=r: h.tensor_tensor(out=r[:, 3, :], in0=p2, in1=cosA, op=ALU.mult))
                              pool.op([r], [dst], lambda h, r=r, dst=dst, hd=hd: h.tensor_tensor(
                                  out=dst[:, 2 * hd, :], in0=r[:, 0, :], in1=r[:, 1, :], op=ALU.subtract))
                              pool.op([r], [dst], lambda h, r=r, dst=dst, hd=hd: h.tensor_tensor(
                                  out=dst[:, 2 * hd + 1, :], in0=r[:, 2, :], in1=r[:, 3, :], op=ALU.add))
                      sub(4)
                      for s in range(2):
                          bank = PB[6 + s]
                          v = pbf(bank)
                          trs = [(v[:, cc * 128:(cc + 1) * 128], kT[a][:, cc, s * 128:(s + 1) * 128], identb[:])
                                 for cc in range(8)]
                          transposes([bank], [kT[a], identb], trs)
                          e = act if s == 0 else dve
                          e.op([bank], [ktok[a]], lambda h, s=s, bank=bank: _copy(h, ktok[a][:, s, :], pbf(bank)))
                      sub(5)
                      flat = lambda t: t[:].rearrange("p a t -> p (a t)")
                      sw.dma([uT[a]], [DB("UT", u)], UT[u], flat(uT[a]))
                      sw.dma([qT[a]], [DB("QT", u)], QT[u], flat(qT[a]))
                      sw.dma([kT[a]], [DB("KT", u)], KT[u], flat(kT[a]))
                      sub(6)
                      for s in range(2):
                          sw.dma([ktok[a]], [DB("KK", 2 * u + s)], KK[2 * u + s], ktok[a][:, s, :])
                      sub(7)
                      sub(10 + u)
              S.barrier()

              chk()
              with ExitStack() as ph:
                  wv = sb(ph, "w1b", [128, 8, 2048], BF16)
                  load_weight_bf16(wv, w_in[l], 8, 2048, 4096)
                  wg = sb(ph, "wg", [128, 8, 2048], BF16)
                  load_weight_bf16(wg, w_gate[l], 8, 2048, 0)
                  gtile = sb(ph, "g1b", [128, D], F32)
                  bcast_row(gtile, norm_mix_g[l:l + 1, :])
                  xin = [sb(ph, f"xin{i}", [128, 2, D], F32) for i in range(2)]
                  kk = [sb(ph, f"kk{i}", [128, 2, D], BF16) for i in range(2)]
                  xn = sb(ph, "xn", [128, 2, D], BF16)
                  xnT = sb(ph, "xnT", [128, 8, 256], BF16)
                  junk = sb(ph, "junk", [128, D], BF16)
                  small = [sb(ph, f"sm{i}", [128, 2], F32) for i in range(4)]
                  vsb = [sb(ph, f"vsb{i}", [128, 2, D], BF16) for i in range(2)]
                  vdb = [sb(ph, f"vdb{i}", [128, 2, D], BF16) for i in range(2)]
                  sgb = [sb(ph, f"sgb{i}", [128, 2, D], BF16) for i in range(2)]
                  gtb = [sb(ph, f"gtb{i}", [128, 2, 2048], BF16) for i in range(2)]
                  Sst = sb(ph, "Sst", [128, 8, 256], F32)
                  Sbf = [sb(ph, f"Sbf{i}", [128, 8, 256], BF16) for i in range(2)]
                  order = []
                  for (a0, a1) in seq_ranges:
                      order += list(range(a1 - 1, a0 - 1, -1))

                  def p1b_load_x(i):
                      u = order[i]
                      a = i % 2
                      sp.dma([DB("X", u)], [xin[a]], xin[a][:], xrows(x_src, u))

                  def p1b_load_kk(i):
                      u = order[i]
                      a = i % 2
                      for s in range(2):
                          sp.dma([DB("KK", 2 * u + s)], [kk[a]], kk[a][:, s, :], KK[2 * u + s])

                  xn2_ = [xn, sb(ph, "xnB", [128, 2, D], BF16)]
                  xnT2_ = [xnT, sb(ph, "xnTB", [128, 8, 256], BF16)]
                  p1b_load_x(0)
                  p1b_load_kk(0)
                  if NU > 1:
                      p1b_load_x(1)
                  norm_T(xin[0], gtile, xn2_[0], xnT2_[0], junk, small, PB[6], PB[7], (act, dve))
                  bi = 0
                  sbi = 0
                  for i, u in enumerate(order):
                      a = i % 2
                      xnT = xnT2_[a]
                      if i + 1 < NU:
                          p1b_load_kk(i + 1)
                      if i + 2 < NU:
                          p1b_load_x(i + 2)
                      if i + 1 < NU:
                          norm_T(xin[1 - a], gtile, xn2_[1 - a], xnT2_[1 - a], junk, small, PB[6], PB[7], (act, dve))
                      for s in range(2):
                          for blk in range(4):
                              bank = PB[bi % 4]
                              bi += 1
                              mms = [(bank[:, :], xnT[:, kc, s * 128:(s + 1) * 128], wv[:, kc, blk * 512:(blk + 1) * 512])
                                     for kc in range(8)]
                              mm_group(bank, [xnT, wv], mms)
                              sub(40)
                              if blk < 2:
                                  act.op([bank], [vsb[a]], lambda h, s=s, blk=blk, bank=bank: h.activation(
                                      out=vsb[a][:, s, blk * 512:(blk + 1) * 512], in_=bank[:, :], func=AF.Copy))
                                  sub(41)
                                  for hh in range(2):
                                      hd = blk * 2 + hh
                                      act.op([bank, dsc], [vdb[a]], lambda h, s=s, hd=hd, hh=hh, bank=bank: h.activation(
                                          out=vdb[a][:, s, hd * 256:(hd + 1) * 256], in_=bank[:, hh * 256:(hh + 1) * 256],
                                          func=AF.Copy, scale=dsc[:, VD_B, hd:hd + 1]))
                                      sub(42)
                              else:
                                  act.op([bank], [sgb[a]], lambda h, s=s, blk=blk, bank=bank: h.activation(
                                      out=sgb[a][:, s, (blk - 2) * 512:(blk - 1) * 512], in_=bank[:, :], func=AF.Silu))
                          for blk in range(4):
                              bank = PB[bi % 4]
                              bi += 1
                              mms = [(bank[:, :], xnT[:, kc, s * 128:(s + 1) * 128], wg[:, kc, blk * 512:(blk + 1) * 512])
                                     for kc in range(8)]
                              mm_group(bank, [xnT, wg], mms)
                              act.op([bank], [gtb[a]], lambda h, s=s, blk=blk, bank=bank: h.activation(
                                  out=gtb[a][:, s, blk * 512:(blk + 1) * 512], in_=bank[:, :], func=AF.Sigmoid))
                      sub(32)
                      for s in range(2):
                          c = 2 * u + s
                          sw.dma([vsb[a]], [DB("VV", c)], VV[c], vsb[a][:, s, :])
                          sw.dma([sgb[a]], [DB("SG", c)], SG[c], sgb[a][:, s, :])
                          sw.dma([gtb[a]], [DB("GT", c)], GT[c], gtb[a][:, s, :])
                      sub(33)
                      if u == units[u][1] - 1:
                          pool.op([], [Sst], lambda h: h.memset(Sst[:], 0.0))
                          pool.op([], [Sbf[sbi % 2]], lambda h, t=Sbf[sbi % 2]: h.memset(t[:], 0.0))
                      for s in (1, 0):
                          c = 2 * u + s
                          cur = Sbf[sbi % 2]
                          nxt = Sbf[(sbi + 1) % 2]
                          sbi += 1
                          sw.dma([cur], [DB("SB", c)], SB[c], cur[:].rearrange("p a e -> p (a e)"))
                          for hp in range(2):
                              for hh in range(2):
                                  hd = hp * 2 + hh
                                  bank = PB[4 + hh]
                                  for dc in range(2):
                                      mm_group(bank, [kk[a], vdb[a]], [(bank[:, dc * 256:(dc + 1) * 256],
                                               kk[a][:, s, (2 * hd + dc) * 128:(2 * hd + dc + 1) * 128],
                                               vdb[a][:, s, hd * 256:(hd + 1) * 256])])
                                  dve.op([bank, dsc, Sst], [Sst], lambda h, hd=hd, bank=bank: h.scalar_tensor_tensor(
                                      out=Sst[:, 2 * hd:2 * hd + 2, :], in0=Sst[:, 2 * hd:2 * hd + 2, :],
                                      scalar=dsc[:, CD_B, hd:hd + 1],
                                      in1=bank[:, :].rearrange("p (a e) -> p a e", a=2), op0=ALU.mult, op1=ALU.add))
                          act.op([Sst], [nxt], lambda h, nxt=nxt: h.activation(out=nxt[:], in_=Sst[:], func=AF.Copy))
                          sub(34)
              S.barrier()

              chk()
              with ExitStack() as ph:
                  wco = sb(ph, "wco", [128, 8, D], BF16)
                  load_weight_bf16(wco, w_conv_out[l], 8, D, 0)
                  praw = sb(ph, "praw", [34, D], F32)
                  sp.dma([], [praw], praw[0:31, :], conv_w[l])
                  sp.dma([], [praw], praw[31:32, :], conv_b[l:l + 1, :])
                  sp.dma([], [praw], praw[32:33, :], conv_ln_g[l:l + 1, :])
                  sp.dma([], [praw], praw[33:34, :], conv_ln_b[l:l + 1, :])
                  cpar = sb(ph, "cpar", [128, 8, 34], F32)
                  for cc in range(8):
                      bank = PB[cc]
                      transposes([bank], [praw, cst], [(bank[:, 0:34], praw[:, cc * 128:(cc + 1) * 128], cst[0:34, 0:34])])
                      dve.op([bank], [cpar], lambda h, cc=cc, bank=bank: h.tensor_copy(out=cpar[:, cc, :], in_=bank[:, 0:34]))
                  diag = sb(ph, "diag", [128, 8, 31, 128], BF16)
                  for cc in range(8):
                      for k in range(31):
                          e = dve
                          e.op([cst, cpar], [diag], lambda h, cc=cc, k=k: h.tensor_scalar(
                              out=diag[:, cc, k, :], in0=cst[:, 0:128], scalar1=cpar[:, cc, k:k + 1], scalar2=None,
                              op0=ALU.mult))
                  uh = [sb(ph, f"uh{i}", [128, 8, 286], BF16) for i in range(2)]
                  uhl = [Buf(f"uhl{i}") for i in range(2)]
                  uhr = [Buf(f"uhr{i}") for i in range(2)]
                  cv_ = [sb(ph, f"cv{i}", [128, 8, 256], F32) for i in range(2)]
                  sq2_ = [sb(ph, f"sq2{i}", [128, 8, 256], F32) for i in range(2)]
                  mean_ = [sb(ph, f"mean{i}", [128, 256], F32) for i in range(2)]
                  m2_ = [sb(ph, f"m2{i}", [128, 256], F32) for i in range(2)]
                  rstd_ = [sb(ph, f"rstdc{i}", [128, 256], F32) for i in range(2)]
                  uln = [sb(ph, f"uln{i}", [128, 8, 256], BF16) for i in range(2)]
                  ycb = [sb(ph, f"ycb{i}", [128, 2, D], BF16) for i in range(2)]
                  onesm = cst[:, 800:928]

                  def p2c_load(u):
                      a = u % 2
                      u_lo, u_hi, _ = units[u]
                      UTu = lambda uu: UT[uu].rearrange("p (a t) -> p a t", a=8)
                      sp.dma([DB("UT", u)], [uh[a]], uh[a][:, :, 15:271], UTu(u))
                      if u > u_lo:
                          sp.dma([DB("UT", u - 1)], [uhl[a]], uh[a][:, :, 0:15], UTu(u - 1)[:, :, 241:256])
                      else:
                          pool.op([], [uhl[a]], lambda h: h.memset(uh[a][:, :, 0:15], 0.0))
                      if u + 1 < u_hi:
                          sp.dma([DB("UT", u + 1)], [uhr[a]], uh[a][:, :, 271:286], UTu(u + 1)[:, :, 0:15])
                      else:
                          pool.op([], [uhr[a]], lambda h: h.memset(uh[a][:, :, 271:286], 0.0))

                  p2c_load(0)
                  for u in range(NU):
                      a = u % 2
                      if u + 1 < NU:
                          p2c_load(u + 1)
                      cv, sq2, mean, m2, rstd = cv_[a], sq2_[a], mean_[a], m2_[a], rstd_[a]
                      for cc in range(8):
                          bank = PB[cc // 2]
                          half = cc % 2
                          mms = [(bank[:, half * 256:(half + 1) * 256], diag[:, cc, k, :], uh[a][:, cc, k:k + 256])
                                 for k in range(31)]
                          mm_group(bank, [diag, uh[a], uhl[a], uhr[a]], mms)
                          act.op([bank, cpar], [cv], lambda h, cc=cc, bank=bank, half=half: h.activation(
                              out=cv[:, cc, :], in_=bank[:, half * 256:(half + 1) * 256], func=AF.Identity,
                              bias=cpar[:, cc, 31:32]))
                          act.op([cv], [sq2], lambda h, cc=cc: h.activation(out=sq2[:, cc, :], in_=cv[:, cc, :], func=AF.Square))
                      bm = PB[4 + a]
                      mm_group(bm, [cst, cv], [(bm[:, 0:256], onesm, cv[:, cc, :]) for cc in range(8)])
                      mm_group(bm, [cst, sq2], [(bm[:, 256:512], onesm, sq2[:, cc, :]) for cc in range(8)])
                      act.op([bm], [mean], lambda h: h.activation(out=mean[:], in_=bm[:, 0:256], func=AF.Copy))
                      act.op([bm], [m2], lambda h: h.activation(out=m2[:], in_=bm[:, 0:256], func=AF.Square))
                      dve.op([bm, m2], [m2], lambda h: h.tensor_tensor(out=m2[:], in0=bm[:, 256:512], in1=m2[:], op=ALU.subtract))
                      dve.op([m2], [m2], lambda h: h.tensor_scalar(out=m2[:], in0=m2[:], scalar1=EPS, scalar2=None, op0=ALU.add))
                      act.op([m2], [m2], lambda h: h.activation(out=m2[:], in_=m2[:], func=AF.Sqrt))
                      dve.op([m2], [rstd], lambda h: h.reciprocal(out=rstd[:], in_=m2[:]))
                      bc = lambda t: t[:].unsqueeze(1).to_broadcast([128, 8, 256])
                      pool.op([cv, mean], [cv], lambda h: h.tensor_tensor(out=cv[:], in0=cv[:], in1=bc(mean), op=ALU.subtract))
                      dve.op([cv, rstd], [cv], lambda h: h.tensor_tensor(out=cv[:], in0=cv[:], in1=bc(rstd), op=ALU.mult))
                      for cc in range(8):
                          act.op([cv, cpar], [uln[a]], lambda h, cc=cc: h.activation(
                              out=uln[a][:, cc, :], in_=cv[:, cc, :], func=AF.Silu,
                              scale=cpar[:, cc, 32:33], bias=cpar[:, cc, 33:34]))
                      for s in range(2):
                          for blk in range(2):
                              bank = PB[6 + (s * 2 + blk) % 2]
                              mms = [(bank[:, :], uln[a][:, cc, s * 128:(s + 1) * 128], wco[:, cc, blk * 512:(blk + 1) * 512])
                                     for cc in range(8)]
                              mm_group(bank, [uln[a], wco], mms)
                              dve.op([bank], [ycb[a]], lambda h, s=s, blk=blk, bank=bank: h.tensor_copy(
                                  out=ycb[a][:, s, blk * 512:(blk + 1) * 512], in_=bank[:, :]))
                          sw.dma([ycb[a]], [DB("YC", 2 * u + s)], YC[2 * u + s], ycb[a][:, s, :])
              S.barrier()

              chk()
              with ExitStack() as ph:
                  decT = sb(ph, "decT", [128, 4, 128], F32)
                  dtmp = sb(ph, "dtmp", [128, 2, 128], F32)
                  for hd in range(4):
                      dve.op([cst, lgfb], [dtmp], lambda h, hd=hd: h.tensor_scalar(
                          out=dtmp[:, 0, :], in0=cst[:, 256:384], scalar1=lgfb[:, hd:hd + 1], scalar2=None, op0=ALU.mult))
                      dve.op([cst, lgbb], [dtmp], lambda h, hd=hd: h.tensor_scalar(
                          out=dtmp[:, 1, :], in0=cst[:, 384:512], scalar1=lgbb[:, hd:hd + 1], scalar2=None, op0=ALU.mult))
                      act.op([dtmp], [dtmp], lambda h: h.activation(out=dtmp[:], in_=dtmp[:], func=AF.Exp))
                      dve.op([dtmp, cst], [dtmp], lambda h: h.tensor_tensor(
                          out=dtmp[:, 0, :], in0=dtmp[:, 0, :], in1=cst[:, 512:640], op=ALU.mult))
                      dve.op([dtmp, cst], [dtmp], lambda h: h.tensor_tensor(
                          out=dtmp[:, 1, :], in0=dtmp[:, 1, :], in1=cst[:, 640:768], op=ALU.mult))
                      dve.op([dtmp], [decT], lambda h, hd=hd: h.tensor_tensor(
                          out=decT[:, hd, :], in0=dtmp[:, 0, :], in1=dtmp[:, 1, :], op=ALU.add))
                  wro = sb(ph, "wro", [128, 8, D], BF16)
                  load_weight_bf16(wro, w_ret_out[l], 8, D, 0)
                  wo = sb(ph, "wo", [128, 8, D], BF16)
                  load_weight_bf16(wo, w_o[l], 8, D, 0)
                  xin = [sb(ph, f"xin{i}", [128, 2, D], F32) for i in range(2)]
                  qTl = [sb(ph, f"qTl{i}", [128, 8, 256], BF16) for i in range(2)]
                  kTl = [sb(ph, f"kTl{i}", [128, 8, 256], BF16) for i in range(2)]
                  kk = [sb(ph, f"kk{i}", [128, 2, D], BF16) for i in range(2)]
                  vv = [sb(ph, f"vv{i}", [128, 2, D], BF16) for i in range(2)]
                  sgl = [sb(ph, f"sgl{i}", [128, 2, D], BF16) for i in range(2)]
                  gtl = [sb(ph, f"gtl{i}", [128, 2, 2048], BF16) for i in range(2)]
                  ycl = [sb(ph, f"ycl{i}", [128, 2, D], BF16) for i in range(2)]
                  sbw = [sb(ph, f"sbw{i}", [128, 2, 8 * 256], BF16) for i in range(2)]
                  Sst = sb(ph, "Sstf", [128, 8, 256], F32)
                  Sbf = sb(ph, "Sbff", [128, 8, 256], BF16)
                  PT = [sb(ph, f"PT{i}", [128, 128], BF16) for i in range(2)]
                  tf = [sb(ph, f"tf{i}", [128, 256], F32) for i in range(2)]
                  o_sb = sb(ph, "o_sb", [128, D], F32)
                  vdf = sb(ph, "vdf", [128, D], BF16)
                  gn = [sb(ph, f"gn{i}", [128, 4], F32) for i in range(4)]
                  junk = sb(ph, "junk", [128, 256], BF16)
                  om = sb(ph, "om", [128, D], BF16)
                  omT = sb(ph, "omT", [128, 8, 256], BF16)
                  m1 = sb(ph, "m1", [128, D], F32)
                  mg = sb(ph, "mg", [128, D], BF16)
                  mgT = sb(ph, "mgT", [128, 8, 256], BF16)
                  xo = [sb(ph, f"xo{i}", [128, 2, D], F32) for i in range(2)]

                  def p3a_load(u):
                      a = u % 2
                      sp.dma([DB("X", u)], [xin[a]], xin[a][:], xrows(x_src, u))
                      sp.dma([DB("QT", u)], [qTl[a]], qTl[a][:].rearrange("p a t -> p (a t)"), QT[u])
                      sp.dma([DB("KT", u)], [kTl[a]], kTl[a][:].rearrange("p a t -> p (a t)"), KT[u])
                      for s in range(2):
                          c = 2 * u + s
                          sp.dma([DB("KK", c)], [kk[a]], kk[a][:, s, :], KK[c])
                          sp.dma([DB("VV", c)], [vv[a]], vv[a][:, s, :], VV[c])
                          sp.dma([DB("SG", c)], [sgl[a]], sgl[a][:, s, :], SG[c])
                          sp.dma([DB("GT", c)], [gtl[a]], gtl[a][:, s, :], GT[c])
                          sp.dma([DB("YC", c)], [ycl[a]], ycl[a][:, s, :], YC[c])
                          sp.dma([DB("SB", c)], [sbw[a]], sbw[a][:, s, :], SB[c])

                  p3a_load(0)
                  hi = 0
                  for u in range(NU):
                      a = u % 2
                      if u + 1 < NU:
                          p3a_load(u + 1)
                      if u == units[u][0]:
                          pool.op([], [Sst], lambda h: h.memset(Sst[:], 0.0))
                          pool.op([], [Sbf], lambda h: h.memset(Sbf[:], 0.0))
                      for s in range(2):
                          ts = slice(s * 128, (s + 1) * 128)
                          sbv = sbw[a][:, s, :].rearrange("p (a e) -> p a e", a=8)
                          for hd in range(4):
                              bA = PB[hi % 2]
                              bO = PB[2 + hi % 2]
                              bC = PB[4 + hi % 2]
                              pt = PT[hi % 2]
                              t_f = tf[hi % 2]
                              hi += 1
                              mm_group(bA, [kTl[a], qTl[a]], [(bA[:, 0:128], kTl[a][:, 2 * hd + dc, ts], qTl[a][:, 2 * hd + dc, ts])
                                                             for dc in range(2)])
                              dve.op([bA, decT], [pt], lambda h, hd=hd, bA=bA, pt=pt: h.tensor_tensor(
                                  out=pt[:], in0=bA[:, 0:128], in1=decT[:, hd, :], op=ALU.mult))
                              mm_group(bO, [pt, vv[a]], [(bO[:, 0:256], pt[:], vv[a][:, s, hd * 256:(hd + 1) * 256])])
                              mm_group(bO, [qTl[a], Sbf], [(bO[:, 256:512], qTl[a][:, 2 * hd + dc, ts], Sbf[:, 2 * hd + dc, :])
                                                          for dc in range(2)])
                              mm_group(bC, [qTl[a], sbw[a]], [(bC[:, 0:256], qTl[a][:, 2 * hd + dc, ts], sbv[:, 2 * hd + dc, :])
                                                             for dc in range(2)])
                              act.op([bO, dsc], [t_f], lambda h, hd=hd, bO=bO, t_f=t_f: h.activation(
                                  out=t_f[:], in_=bO[:, 256:512], func=AF.Copy, scale=dsc[:, A_F, hd:hd + 1]))
                              dve.op([bC, dsc, t_f], [t_f], lambda h, hd=hd, bC=bC, t_f=t_f: h.scalar_tensor_tensor(
                                  out=t_f[:], in0=bC[:, 0:256], scalar=dsc[:, A_B, hd:hd + 1], in1=t_f[:],
                                  op0=ALU.mult, op1=ALU.add))
                              dve.op([bO, t_f], [o_sb], lambda h, hd=hd, bO=bO, t_f=t_f: h.tensor_tensor(
                                  out=o_sb[:, hd * 256:(hd + 1) * 256], in0=bO[:, 0:256], in1=t_f[:], op=ALU.add))
                          for hd in range(4):
                              act.op([vv[a], dsc], [vdf], lambda h, hd=hd, s=s: h.activation(
                                  out=vdf[:, hd * 256:(hd + 1) * 256], in_=vv[a][:, s, hd * 256:(hd + 1) * 256],
                                  func=AF.Copy, scale=dsc[:, VD_F, hd:hd + 1]))
                          for hd in range(4):
                              bank = PB[6 + hd % 2]
                              for dc in range(2):
                                  mm_group(bank, [kk[a], vdf], [(bank[:, dc * 256:(dc + 1) * 256],
                                           kk[a][:, s, (2 * hd + dc) * 128:(2 * hd + dc + 1) * 128],
                                           vdf[:, hd * 256:(hd + 1) * 256])])
                              dve.op([bank, dsc, Sst], [Sst], lambda h, hd=hd, bank=bank: h.scalar_tensor_tensor(
                                  out=Sst[:, 2 * hd:2 * hd + 2, :], in0=Sst[:, 2 * hd:2 * hd + 2, :],
                                  scalar=dsc[:, CD_F, hd:hd + 1],
                                  in1=bank[:, :].rearrange("p (a e) -> p a e", a=2), op0=ALU.mult, op1=ALU.add))
                          act.op([Sst], [Sbf], lambda h: h.activation(out=Sbf[:], in_=Sst[:], func=AF.Copy))
                          for hd in range(4):
                              act.op([o_sb], [junk, gn[0]], lambda h, hd=hd: h.activation(
                                  out=junk[:], in_=o_sb[:, hd * 256:(hd + 1) * 256], func=AF.Square,
                                  accum_out=gn[0][:, hd:hd + 1]))
                          dve.op([gn[0]], [gn[1]], lambda h: h.tensor_scalar(
                              out=gn[1][:], in0=gn[0][:], scalar1=1.0 / 256.0, scalar2=EPS, op0=ALU.mult, op1=ALU.add))
                          act.op([gn[1]], [gn[2]], lambda h: h.activation(out=gn[2][:], in_=gn[1][:], func=AF.Sqrt))
                          dve.op([gn[2]], [gn[3]], lambda h: h.reciprocal(out=gn[3][:], in_=gn[2][:]))
                          for hd in range(4):
                              dve.op([o_sb, gn[3], sgl[a]], [om], lambda h, hd=hd, s=s: h.scalar_tensor_tensor(
                                  out=om[:, hd * 256:(hd + 1) * 256], in0=o_sb[:, hd * 256:(hd + 1) * 256],
                                  scalar=gn[3][:, hd:hd + 1], in1=sgl[a][:, s, hd * 256:(hd + 1) * 256],
                                  op0=ALU.mult, op1=ALU.mult))
                          bT = PB[6 + s]
                          v = pbf(bT)
                          transposes([bT], [om, identb], [(v[:, cc * 128:(cc + 1) * 128], om[:, cc * 128:(cc + 1) * 128], identb[:])
                                                          for cc in range(8)])
                          act.op([bT], [omT], lambda h, s=s, bT=bT: h.activation(
                              out=omT[:, :, s * 128:(s + 1) * 128], in_=pbf(bT).rearrange("p (a t) -> p a t", a=8), func=AF.Copy))
                          for blk in range(2):
                              bank = PB[blk]
                              mm_group(bank, [omT, wro], [(bank[:, :], omT[:, cc, ts], wro[:, cc, blk * 512:(blk + 1) * 512])
                                                         for cc in range(8)])
                              cs_ = slice(blk * 512, (blk + 1) * 512)
                              pool.op([ycl[a], gtl[a]], [m1], lambda h, s=s, cs_=cs_: h.tensor_tensor(
                                  out=m1[:, cs_], in0=ycl[a][:, s, cs_], in1=gtl[a][:, s, cs_], op=ALU.mult))
                              dve.op([bank, gtl[a], m1], [mg], lambda h, s=s, blk=blk, bank=bank, cs_=cs_: h.tensor_tensor(
                                  out=mg[:, cs_], in0=bank[:, :], in1=gtl[a][:, s, 1024 + blk * 512:1024 + (blk + 1) * 512],
                                  op=ALU.mult))
                              pool.op([m1, mg], [mg], lambda h, cs_=cs_: h.tensor_tensor(
                                  out=mg[:, cs_], in0=mg[:, cs_], in1=m1[:, cs_], op=ALU.add))
                          bT2 = PB[4 + s]
                          v2 = pbf(bT2)
                          transposes([bT2], [mg, identb], [(v2[:, cc * 128:(cc + 1) * 128], mg[:, cc * 128:(cc + 1) * 128], identb[:])
                                                           for cc in range(8)])
                          act.op([bT2], [mgT], lambda h, s=s, bT2=bT2: h.activation(
                              out=mgT[:, :, s * 128:(s + 1) * 128], in_=pbf(bT2).rearrange("p (a t) -> p a t", a=8), func=AF.Copy))
                          for blk in range(2):
                              bank = PB[2 + blk]
                              mm_group(bank, [mgT, wo], [(bank[:, :], mgT[:, cc, ts], wo[:, cc, blk * 512:(blk + 1) * 512])
                                                        for cc in range(8)])
                              dve.op([bank, xin[a]], [xo[a]], lambda h, s=s, blk=blk, bank=bank: h.tensor_tensor(
                                  out=xo[a][:, s, blk * 512:(blk + 1) * 512], in0=bank[:, :],
                                  in1=xin[a][:, s, blk * 512:(blk + 1) * 512], op=ALU.add))
                      sw.dma([xo[a]], [DB("XM", u)], xrows(XM, u), xo[a][:])
              S.barrier()

              chk()
              with ExitStack() as ph:
                  wq = sb(ph, "wq", [128, 8, 2048], BF16)
                  load_weight_bf16(wq, peer_wq[l], 8, 2048, 0)
                  gtile = sb(ph, "g3b", [128, D], F32)
                  bcast_row(gtile, norm_ffn_g[l:l + 1, :])
                  if last and do_final:
                      gfin = sb(ph, "gfin", [128, D], F32)
                      bcast_row(gfin, final_g[0:1, :])
                  skT = sb(ph, "skT", [128, 16, 128], BF16)
                  with ExitStack() as ph2:
                      skraw = sb(ph2, "skraw", [128, 16, 128], F32)
                      sp.dma([], [skraw], skraw[:], peer_sk[l].rearrange("h p k d -> k (h p) d"))
                      for cq in range(16):
                          bank = PB[cq % 8]
                          transposes([bank], [skraw, cst], [(bank[:, 0:128], skraw[:, cq, :], ident_f)])
                          dve.op([bank], [skT], lambda h, cq=cq, bank=bank: h.tensor_copy(out=skT[:, cq, :], in_=bank[:, 0:128]))
                      S.barrier()
                  xw = sb(ph, "xw", [128, 2, D], F32)
                  xnT = [sb(ph, f"xnT{i}", [128, 8, 256], BF16) for i in range(2)]
                  small = [sb(ph, f"sm{i}", [128, 2], F32) for i in range(4)]
                  qTp = sb(ph, "qTp", [128, 16, 256], BF16)
                  sc_ = [sb(ph, f"sc{i}", [128, 16, 128], F32) for i in range(2)]
                  sc2 = sb(ph, "sc2", [128, 16, 128], F32)
                  xn = TT(sc2.t[:].rearrange("p a k -> p (a k)").bitcast(BF16)[:, 0:2 * D].rearrange("p (s d) -> p s d", s=2),
                          "xn_alias")
                  xn.b = sc2.b
                  junk = (xn[:, 0, :], xn)
                  sv = sb(ph, "sv", [128, 16, 16], F32)
                  si = sb(ph, "si", [128, 16, 16], U32)
                  sif = sb(ph, "sif", [128, 16, 16], F32)
                  cand_ = []
                  for i_ in range(2):
                      c_ = TT(sc_[i_].t[:].rearrange("p (h two) k -> p h (two k)", two=2), f"cand_alias{i_}")
                      c_.b = sc_[i_].b
                      cand_.append(c_)
                  cand2 = TT(sc2.t[:].rearrange("p (h two) k -> p h (two k)", two=2), "cand2_alias")
                  cand2.b = sc2.b
                  s16_ = [sb(ph, f"s16_{i}", [128, 8, 16], F32) for i in range(2)]
                  ci = sb(ph, "ci", [128, 8, 16], U32)
                  cia = sb(ph, "cia", [128, 8, 16], U32)
                  cib = sb(ph, "cib", [128, 8, 16], U32)
                  caf = sb(ph, "caf", [128, 8, 16], F32)
                  cbf = sb(ph, "cbf", [128, 8, 16], F32)
                  eq = TT(sc2.t[:].rearrange("p (h two) (a b) -> p h (two a) b", two=2, b=16), "eq_alias")
                  eq.b = sc2.b
                  rowi_ = [[sb(ph, f"rowi{j}_{i}", [128, 128], F32) for i in range(3)] for j in range(2)]
                  gsum = sb(ph, "gsum", [128, 8], F32)
                  colT = [sb(ph, f"colT{i}", [128, 256], F32) for i in range(3)]
                  TBK = 4
                  P1 = [sb(ph, f"P1_{i}", [128, TBK, 128], BF16) for i in range(2)]
                  P2 = [sb(ph, f"P2_{i}", [128, TBK, 128], BF16) for i in range(2)]
                  GTs = sb(ph, "GTs", [128, 128, 256], BF16)
                  hT = [sb(ph, f"hT{i}", [128, 256], BF16) for i in range(3)]
                  gh = [sb(ph, f"gh{i}", [128, 256], BF16) for i in range(4)]
                  P2ENG = pool if CFG.get("P2POOL") else dve
                  nidx2 = sb(ph, "nidx2", [128, 256], F32)
                  atmp = [sb(ph, f"atmp{i}", [128, 128], BF16) for i in range(2)]
                  iota3 = sb(ph, "iota3", [128, TBK, 128], BF16)
                  for tt_ in range(TBK):
                      dve.op([iotab], [iota3], lambda h, tt_=tt_: h.tensor_copy(out=iota3[:, tt_, :], in_=iotab[:]))
                  NSB = CFG["NSB"]
                  LOOK = CFG["LOOK"]
                  ub = [sb(ph, f"ub{i}", [128, 8, 256], BF16) for i in range(NSB)]
                  vbuf = [sb(ph, f"vbuf{i}", [128, 2, D], BF16) for i in range(NSB)]

                  def route_A(u):
                      a = u % 2
                      sp.dma([DB("XM", u)], [xw], xw[:], xrows(XM, u))
                      norm_T(xw, gtile, xn, xnT[a], junk, small, PB[6], PB[7], (act, dve))
                      for cq in range(16):
                          bank = PB[4 + cq % 2]
                          mm_group(bank, [wq, xnT[a]], [(bank[:, 0:256], wq[:, kc, cq * 128:(cq + 1) * 128], xnT[a][:, kc, :])
                                                       for kc in range(8)])
                          act.op([bank], [qTp], lambda h, cq=cq, bank=bank: h.activation(
                              out=qTp[:, cq, :], in_=bank[:, 0:256], func=AF.Copy))
                      for s in range(2):
                          ts = slice(s * 128, (s + 1) * 128)
                          sc = sc_[s]
                          for q4 in range(4):
                              bank = PB[4 + q4 % 2]
                              for j in range(4):
                                  cq = q4 * 4 + j
                                  mm_group(bank, [qTp, skT], [(bank[:, j * 128:(j + 1) * 128], qTp[:, cq, ts], skT[:, cq, :])])
                              act.op([bank], [sc], lambda h, q4=q4, bank=bank, sc=sc: h.activation(
                                  out=sc[:, q4 * 4:(q4 + 1) * 4, :], in_=bank[:, :].rearrange("p (a k) -> p a k", a=4),
                                  func=AF.Copy))
                      for s in range(2):
                          if "A2" in SKIP:
                              break
                          ts = slice(s * 128, (s + 1) * 128)
                          sc = sc_[s]
                          cand = cand_[s]
                          s16 = s16_[s]
                          rowi = rowi_[s]
                          for cq in range(16):
                              dve.op([sc], [sv], lambda h, cq=cq: h.max(out=sv[:, cq, 0:8], in_=sc[:, cq, :]))
                              dve.op([sc, sv], [si], lambda h, cq=cq: h.max_index(out=si[:, cq, 0:8], in_max=sv[:, cq, 0:8], in_values=sc[:, cq, :]))
                              dve.op([sc, sv], [sc2], lambda h, cq=cq: h.match_replace(
                                  out=sc2[:, cq, :], in_to_replace=sv[:, cq, 0:8], in_values=sc[:, cq, :], imm_value=NEG))
                              dve.op([sc2], [sv], lambda h, cq=cq: h.max(out=sv[:, cq, 8:16], in_=sc2[:, cq, :]))
                              dve.op([sc2, sv], [si], lambda h, cq=cq: h.max_index(out=si[:, cq, 8:16], in_max=sv[:, cq, 8:16], in_values=sc2[:, cq, :]))
                          dve.op([si], [sif], lambda h: h.tensor_copy(out=sif[:], in_=si[:]))
                          sv4 = sv[:].rearrange("p (h two) k -> p h two k", two=2)
                          sif4 = sif[:].rearrange("p (h two) k -> p h two k", two=2)
                          cand4 = cand[:].rearrange("p h (a b) -> p h a b", a=16)
                          dve.op([sv], [cand], lambda h: h.tensor_tensor(
                              out=cand4, in0=sv4[:, :, 0, :].unsqueeze(3).to_broadcast([128, 8, 16, 16]),
                              in1=sv4[:, :, 1, :].unsqueeze(2).to_broadcast([128, 8, 16, 16]), op=ALU.add))
                          for hd in range(8):
                              dve.op([cand], [s16], lambda h, hd=hd: h.max(out=s16[:, hd, 0:8], in_=cand[:, hd, :]))
                              dve.op([cand, s16], [ci], lambda h, hd=hd: h.max_index(out=ci[:, hd, 0:8], in_max=s16[:, hd, 0:8], in_values=cand[:, hd, :]))
                              dve.op([cand, s16], [cand2], lambda h, hd=hd: h.match_replace(
                                  out=cand2[:, hd, :], in_to_replace=s16[:, hd, 0:8], in_values=cand[:, hd, :], imm_value=NEG))
                              dve.op([cand2], [s16], lambda h, hd=hd: h.max(out=s16[:, hd, 8:16], in_=cand2[:, hd, :]))
                              dve.op([cand2, s16], [ci], lambda h, hd=hd: h.max_index(out=ci[:, hd, 8:16], in_max=s16[:, hd, 8:16], in_values=cand2[:, hd, :]))
                          dve.op([ci], [cia], lambda h: h.tensor_single_scalar(out=cia[:], in_=ci[:], scalar=4, op=ALU.logical_shift_right))
                          dve.op([ci], [cib], lambda h: h.tensor_single_scalar(out=cib[:], in_=ci[:], scalar=15, op=ALU.bitwise_and))
                          dve.op([cia], [caf], lambda h: h.tensor_copy(out=caf[:], in_=cia[:]))
                          dve.op([cib], [cbf], lambda h: h.tensor_copy(out=cbf[:], in_=cib[:]))
                          io4 = iota16.unsqueeze(1).unsqueeze(1).to_broadcast([128, 8, 16, 16])
                          for which, (cf, half) in enumerate(((caf, 0), (cbf, 1))):
                              dst3 = rowi[which][:].rearrange("p (h k) -> p h k", h=8)
                              dve.op([cf, cst], [eq], lambda h, cf=cf: h.tensor_tensor(
                                  out=eq[:], in0=cf[:].unsqueeze(3).to_broadcast([128, 8, 16, 16]), in1=io4, op=ALU.is_equal))
                              dve.op([eq, sif], [eq], lambda h, half=half: h.tensor_tensor(
                                  out=eq[:], in0=eq[:], in1=sif4[:, :, half, :].unsqueeze(2).to_broadcast([128, 8, 16, 16]),
                                  op=ALU.mult))
                              dve.op([eq], [rowi[which]], lambda h, dst3=dst3: h.tensor_reduce(
                                  out=dst3, in_=eq[:], axis=AX.X, op=ALU.add))

                  def route_B(u):
                      for s in range(2):
                          ts = slice(s * 128, (s + 1) * 128)
                          s16 = s16_[s]
                          rowi = rowi_[s]
                          g3 = rowi[2][:].rearrange("p (h k) -> p h k", h=8)
                          dve.op([s16], [rowi[2]], lambda h: h.tensor_tensor(
                              out=g3, in0=s16[:], in1=s16[:, :, 0:1].to_broadcast([128, 8, 16]), op=ALU.subtract))
                          act.op([rowi[2]], [rowi[2]], lambda h: h.activation(out=rowi[2][:], in_=rowi[2][:], func=AF.Exp))
                          dve.op([rowi[2]], [gsum], lambda h: h.tensor_reduce(out=gsum[:], in_=g3, axis=AX.X, op=ALU.add))
                          dve.op([gsum], [gsum], lambda h: h.reciprocal(out=gsum[:], in_=gsum[:]))
                          dve.op([rowi[2], gsum], [rowi[2]], lambda h: h.tensor_tensor(
                              out=g3, in0=g3, in1=gsum[:].unsqueeze(2).to_broadcast([128, 8, 16]), op=ALU.mult))
                          for w3 in range(3):
                              bank = PB[6 + w3 % 2]
                              transposes([bank], [rowi[w3], cst], [(bank[:, 0:128], rowi[w3][:], ident_f)])
                              act.op([bank], [colT[w3]], lambda h, w3=w3, bank=bank, ts=ts: h.activation(
                                  out=colT[w3][:, ts], in_=bank[:, 0:128], func=AF.Copy))
                      bi = 0
                      dve.op([colT[1]], [nidx2], lambda h: h.tensor_scalar(
                          out=nidx2[:], in0=colT[1][:], scalar1=-1.0, scalar2=None, op0=ALU.mult))
                      for t0 in range(0, 256, TBK):
                          p1 = P1[(t0 // TBK) % 2]
                          p2 = P2[(t0 // TBK) % 2]
                          for tt in range(TBK):
                              t = t0 + tt
                              dve.op([iotab, colT[0], colT[2]], [p1], lambda h, t=t, tt=tt, p1=p1: h.tensor_scalar(
                                  out=p1[:, tt, :], in0=iotab[:], scalar1=colT[0][:, t:t + 1], scalar2=colT[2][:, t:t + 1],
                                  op0=ALU.is_equal, op1=ALU.mult))
                              if tt < CFG.get("ACTP2", 0):
                                  at_ = atmp[tt % 2]
                                  act.op([iotab, nidx2], [at_], lambda h, t=t, at_=at_: h.activation(
                                      out=at_[:], in_=iotab[:], func=AF.Abs, bias=nidx2[:, t:t + 1]))
                                  act.op([at_], [p2], lambda h, tt=tt, p2=p2, at_=at_: h.activation(
                                      out=p2[:, tt, :], in_=at_[:], func=AF.Relu, scale=-1.0, bias=1.0))
                              elif not CFG.get("P2BATCH"):
                                  P2ENG.op([iotab, colT[1]], [p2], lambda h, t=t, tt=tt, p2=p2: h.tensor_scalar(
                                      out=p2[:, tt, :], in0=iotab[:], scalar1=colT[1][:, t:t + 1], scalar2=None,
                                      op0=ALU.is_equal))
                          if CFG.get("P2BATCH"):
                              dve.op([iota3, colT[1]], [p2], lambda h, t0=t0, p2=p2: h.tensor_tensor(
                                  out=p2[:], in0=iota3[:],
                                  in1=colT[1][:, t0:t0 + TBK].unsqueeze(2).to_broadcast([128, TBK, 128]), op=ALU.is_equal))
                          for q in range(TBK // 4):
                              bank = PB[4 + bi % 4]
                              bi += 1
                              for j in range(4):
                                  tt = q * 4 + j
                                  mm_group(bank, [p1, p2], [(bank[:, j * 128:(j + 1) * 128], p2[:, tt, :], p1[:, tt, :])])
                              tq = t0 + q * 4
                              act.op([bank], [GTs], lambda h, bank=bank, tq=tq: h.activation(
                                  out=GTs[:, :, tq:tq + 4].transpose([0, 2, 1]),
                                  in_=bank[:, :].rearrange("p (t i) -> p t i", t=4), func=AF.Copy))

                  def dense(u):
                      a = u % 2
                      steps = []
                      for g in range(64):
                          for c in range(2):
                              steps.append((g, c))

                      def emit_load(g):
                          if "dload" in SKIP and g >= NSB:
                              return
                          sp.dma([DB("UTS", (l, g))], [ub[g % NSB]], ub[g % NSB][:].rearrange("p a e -> p (a e)"), UTS[l, g])
                          sp.dma([DB("VS", (l, g))], [vbuf[g % NSB]], vbuf[g % NSB][:].rearrange("p a e -> p (a e)"), VS[l, g])

                      def emit_H(i):
                          g, c = steps[i]
                          bank = PB[4 + i % (LOOK + 1)]
                          mm_group(bank, [ub[g % NSB], xnT[a]], [(bank[:, 0:256], ub[g % NSB][:, kc, c * 128:(c + 1) * 128], xnT[a][:, kc, :])
                                                                for kc in range(8)])
                          h_ = hT[i % 3]
                          g_ = gh[i % 4]
                          act.op([bank], [h_], lambda h, bank=bank, h_=h_: h.activation(out=h_[:], in_=bank[:, 0:256], func=AF.Gelu))
                          (dve if CFG.get("GHENG") else pool).op([h_, GTs], [g_], lambda h, h_=h_, g_=g_, i1=2 * g + c: h.tensor_tensor(
                              out=g_[:], in0=h_[:], in1=GTs[:, i1, :], op=ALU.mult))

                      def emit_V(i):
                          g, c = steps[i]
                          g_ = gh[i % 4]
                          for s in range(2):
                              for blk in range(2):
                                  bank = PB[s * 2 + blk]
                                  pe.op([g_, vbuf[g % NSB]], [bank], lambda h, s=s, blk=blk, bank=bank, g_=g_, g=g, c=c, i=i: h.matmul(
                                      bank[:, :], lhsT=g_[:, s * 128:(s + 1) * 128], rhs=vbuf[g % NSB][:, c, blk * 512:(blk + 1) * 512],
                                      start=(i == 0), stop=(i == len(steps) - 1)))

                      for g0 in range(NSB - 1):
                          emit_load(g0)
                      for i0 in range(LOOK):
                          emit_H(i0)
                      for i in range(len(steps)):
                          g, c = steps[i]
                          if c == 0 and g + NSB - 1 < 64:
                              emit_load(g + NSB - 1)
                          if i + LOOK < len(steps):
                              emit_H(i + LOOK)
                          emit_V(i)

                  def finalize(u):
                      xo = xw
                      yo = xw
                      sp.dma([DB("XM", u)], [xw], xw[:], xrows(XM, u))
                      for s in range(2):
                          for blk in range(2):
                              bank = PB[s * 2 + blk]
                              dve.op([bank, xw], [xw], lambda h, s=s, blk=blk, bank=bank: h.tensor_tensor(
                                  out=xw[:, s, blk * 512:(blk + 1) * 512], in0=bank[:, :],
                                  in1=xw[:, s, blk * 512:(blk + 1) * 512], op=ALU.add))
                      if last and do_final:
                          ss, ms, sq, rstd = small
                          for s in range(2):
                              act.op([xo], [xn, ss], lambda h, s=s: h.activation(
                                  out=xn[:, 0, :], in_=xo[:, s, :], func=AF.Square, accum_out=ss[:, s:s + 1]))
                          dve.op([ss], [ms], lambda h: h.tensor_scalar(
                              out=ms[:], in0=ss[:], scalar1=1.0 / D, scalar2=EPS, op0=ALU.mult, op1=ALU.add))
                          act.op([ms], [sq], lambda h: h.activation(out=sq[:], in_=ms[:], func=AF.Sqrt))
                          dve.op([sq], [rstd], lambda h: h.reciprocal(out=rstd[:], in_=sq[:]))
                          for s in range(2):
                              dve.op([xo, rstd, gfin], [yo], lambda h, s=s: h.scalar_tensor_tensor(
                                  out=yo[:, s, :], in0=xo[:, s, :], scalar=rstd[:, s:s + 1], in1=gfin[:],
                                  op0=ALU.mult, op1=ALU.mult))
                          sw.dma([yo], [DB("Y", u)], xrows(x_dst, u), yo[:])
                      else:
                          sw.dma([xo], [DB("X", u)], xrows(x_dst, u), xo[:])

                  route_A(0)
                  if "routeB" not in SKIP:
                      route_B(0)
                  for u in range(NU):
                      if u + 1 < NU:
                          route_A(u + 1)
                      if "dense" not in SKIP:
                          dense(u)
                      finalize(u)
                      if u + 1 < NU and "routeB" not in SKIP:
                          route_B(u + 1)
              S.barrier()
              lay.close()
        except _Stop:
            pass
        S.barrier()
    return nc, S.ninst


_WEIGHT_KEYS = ["norm_mix_g", "w_in", "w_gate", "conv_w", "conv_b", "conv_ln_g", "conv_ln_b", "w_conv_out",
                "log_gamma_fwd", "log_gamma_bwd", "w_ret_out", "w_o", "norm_ffn_g", "peer_wq", "peer_subkeys",
                "peer_u", "peer_v"]


def run_cores(x_per_core, weights, seqs, n_layers=2, do_final=True, stop=0):
    nc, ninst = build(seqs, n_layers=n_layers, do_final=do_final, stop=stop)
    consts, cosT, sinT = host_consts(max(seqs))
    base = {k: np.ascontiguousarray(np.asarray(weights[k], dtype=np.float32)) for k in _WEIGHT_KEYS}
    base["final_norm_g"] = np.ascontiguousarray(np.asarray(weights["final_norm_g"], np.float32).reshape(1, D))
    base["consts"] = consts
    base["cosT"] = cosT
    base["sinT"] = sinT
    in_maps = []
    for xc in x_per_core:
        m = dict(base)
        m["x"] = np.ascontiguousarray(xc, dtype=np.float32)
        in_maps.append(m)
    res = run_bass_kernel_spmd(nc, in_maps, core_ids=list(range(len(x_per_core))))
    return [r["y"] for r in res.results]


def kernel(x_prompt, x_sample, **weights):
    x_prompt = np.asarray(x_prompt, dtype=np.float32)
    x_sample = np.asarray(x_sample, dtype=np.float32)
    xs = []
    for c in range(N_CORES):
        xs.append(np.concatenate([x_prompt[c], x_sample[2 * c], x_sample[2 * c + 1]], axis=0))
    ys = run_cores(xs, weights, SEQS_FULL)
    y_prompt = np.stack([y[0:8192] for y in ys], axis=0)
    y_sample = np.stack([ys[c // 2][8192 + 2048 * (c % 2):8192 + 2048 * (c % 2 + 1)] for c in range(16)], axis=0)
    return (y_prompt.astype(np.float32), y_sample.astype(np.float32))
```

```python
from contextlib import ExitStack
import numpy as np
import concourse.bass as bass
import concourse.mybir as mybir
from concourse.bass_utils import run_bass_kernel_spmd

F32 = mybir.dt.float32
BF16 = mybir.dt.bfloat16
U32 = mybir.dt.uint32
AF = mybir.ActivationFunctionType
ALU = mybir.AluOpType
AX = mybir.AxisListType

D = 1024
EPS = 1e-6
NEG = -1.0e30
N_CORES = 8
SEQS_FULL = (8192, 2048, 2048)


class Buf:
    __slots__ = ("name", "lw", "rd")

    def __init__(self, name):
        self.name = name
        self.lw = None
        self.rd = {}


class TT:
    def __init__(self, t, name):
        self.t = t
        self.b = Buf(name)

    def __getitem__(self, k):
        return self.t[k]


def _b(x):
    return x.b if isinstance(x, TT) else x


class Eng:
    def __init__(self, S, name, h, nslots=0, own_skip=False):
        self.S = S
        self.name = name
        self.h = h
        self.waited = {}
        self.snap = {}
        self.dirty = False
        self.own_skip = own_skip
        self.nslots = nslots
        if nslots == 0:
            self.sem = S.new_sem(name)
            self.count = 0
        else:
            self.slots = [[S.new_sem(f"{name}{i}"), 0] for i in range(nslots)]
            self.rr = 0

    def _wait(self, tick):
        sem, val, key = tick[:3]
        if self.own_skip and self.nslots == 0 and key == self.sem_key():
            return
        if self.waited.get(key, 0) >= val:
            return
        self.h.wait_ge(sem, val)
        self.waited[key] = val
        self.dirty = True
        if len(tick) > 3:
            w = self.waited
            for k, v in tick[3].items():
                if w.get(k, 0) < v:
                    w[k] = v

    def _snapshot(self):
        if self.dirty:
            self.snap = dict(self.waited)
            self.dirty = False
        return self.snap

    def sem_key(self):
        return self.name

    def _deps(self, reads, writes):
        own = self.name if self.nslots == 0 else None
        for b in reads:
            b = _b(b)
            if b.lw is not None:
                self._wait(b.lw)
        for b in writes:
            b = _b(b)
            if b.lw is not None and b.lw[2] != own:
                self._wait(b.lw)
            for t in b.rd.values():
                if t[2] != own:
                    self._wait(t)

    def _mark(self, reads, writes, tick):
        for b in reads:
            _b(b).rd[tick[2]] = tick
        for b in writes:
            b = _b(b)
            b.lw = tick
            b.rd = {}

    def op(self, reads, writes, fn):
        if DEAD[0]:
            return
        self._deps(reads, writes)
        inst = fn(self.h)
        self.count += 1
        inst.then_inc(self.sem, 1)
        tick = (self.sem, self.count, self.name, self._snapshot())
        self._mark(reads, writes, tick)
        self.S.ninst += 1

    def dma(self, reads, writes, out, in_, **kw):
        if DEAD[0]:
            return
        slot = self.slots[self.rr]
        key = f"{self.name}{self.rr}"
        self.rr = (self.rr + 1) % self.nslots
        if slot[1] > 0:
            self._wait((slot[0], slot[1], key))
        self._deps(reads, writes)
        inst = self.h.dma_start(out=out, in_=in_, **kw)
        slot[1] += 16
        inst.then_inc(slot[0], 16)
        tick = (slot[0], slot[1], key, self._snapshot())
        self._mark(reads, writes, tick)
        self.S.ninst += 1

    def last_ticks(self):
        if self.nslots == 0:
            return [(self.sem, self.count, self.name)] if self.count else []
        return [(s[0], s[1], f"{self.name}{i}") for i, s in enumerate(self.slots) if s[1]]


class Sched:
    def __init__(self, nc, stack):
        self.nc = nc
        self.stack = stack
        self.ninst = 0
        self.pe = Eng(self, "pe", nc.tensor, own_skip=True)
        self.act = Eng(self, "act", nc.scalar)
        self.dve = Eng(self, "dve", nc.vector)
        self.pool = Eng(self, "pool", nc.gpsimd)
        self.sp = Eng(self, "sp", nc.sync, nslots=8)
        self.sw = Eng(self, "sw", nc.gpsimd, nslots=6)
        self.all = [self.pe, self.act, self.dve, self.pool, self.sp, self.sw]

    def new_sem(self, name):
        return self.stack.enter_context(self.nc.semaphore("s_" + name))

    def barrier(self):
        ticks = []
        for e in self.all:
            ticks += e.last_ticks()
        for e in (self.pe, self.act, self.dve, self.pool, self.sp):
            for t in ticks:
                if e.nslots == 0 and t[2] == e.name:
                    if e.own_skip:
                        continue
                e._wait(t)


def host_consts(lmax):
    c = np.zeros((128, 1024), np.float32)
    p = np.arange(128, dtype=np.float32)
    c[:, 0:128] = np.eye(128, dtype=np.float32)
    c[:, 128:256] = p[None, :]
    diff = p[None, :] - p[:, None]
    c[:, 256:384] = np.maximum(diff, 0.0)
    c[:, 384:512] = np.maximum(-diff, 0.0)
    c[:, 512:640] = (diff >= 0).astype(np.float32) / 16.0
    c[:, 640:768] = (diff < 0).astype(np.float32) / 16.0
    c[:, 768] = p + 1.0
    c[:, 769] = 128.0 - p
    c[:, 770] = 127.0 - p
    c[:, 771] = p
    c[:, 772] = 128.0
    c[:, 776:792] = np.arange(16, dtype=np.float32)[None, :]
    c[:, 800:928] = 1.0 / 1024.0
    half = 128
    freqs = (1.0 / (10000.0 ** (np.arange(half, dtype=np.float32) / np.float32(half)))).astype(np.float32)
    ang = np.arange(lmax, dtype=np.float32)[:, None] * freqs[None, :]
    cosT = np.ascontiguousarray(np.cos(ang).astype(np.float32).T)
    sinT = np.ascontiguousarray(np.sin(ang).astype(np.float32).T)
    return c, cosT, sinT


class _Stop(Exception):
    pass


SUBSTOP = [0]
SKIP = set()
CFG = {"NSB": 3, "LOOK": 2}


DEAD = [False]


def sub(n):
    if SUBSTOP[0] == n:
        DEAD[0] = True


def build(seqs, n_layers=2, do_final=True, stop=0):
    T_tok = sum(seqs)
    NU = T_tok // 256
    NCH = T_tok // 128
    lmax = max(seqs)
    nc = bass.Bass("TRN2", target_bir_lowering=False)
    DEAD[0] = False

    def din(name, shape, dt=F32):
        return nc.dram_tensor(name, list(shape), dt, kind="ExternalInput").ap()

    def dscr(name, shape, dt):
        return nc.dram_tensor(name, list(shape), dt, kind="Internal").ap()

    x_in = din("x", [T_tok, D])
    consts_d = din("consts", [128, 1024])
    cos_d = din("cosT", [128, lmax])
    sin_d = din("sinT", [128, lmax])
    norm_mix_g = din("norm_mix_g", [2, D])
    w_in = din("w_in", [2, D, 6144])
    w_gate = din("w_gate", [2, D, 2048])
    conv_w = din("conv_w", [2, 31, D])
    conv_b = din("conv_b", [2, D])
    conv_ln_g = din("conv_ln_g", [2, D])
    conv_ln_b = din("conv_ln_b", [2, D])
    w_conv_out = din("w_conv_out", [2, D, D])
    lg_f = din("log_gamma_fwd", [2, 4])
    lg_b = din("log_gamma_bwd", [2, 4])
    w_ret_out = din("w_ret_out", [2, D, D])
    w_o = din("w_o", [2, D, D])
    norm_ffn_g = din("norm_ffn_g", [2, D])
    peer_wq = din("peer_wq", [2, D, 2048])
    peer_sk = din("peer_subkeys", [2, 8, 2, 128, 128])
    peer_u = din("peer_u", [2, 16384, D])
    peer_v = din("peer_v", [2, 16384, D])
    final_g = din("final_norm_g", [1, D])
    y_out = nc.dram_tensor("y", [T_tok, D], F32, kind="ExternalOutput").ap()

    X1 = dscr("X1", [T_tok, D], F32)
    XM = dscr("XM", [T_tok, D], F32)
    UT = dscr("UT", [NU, 128, 8 * 256], BF16)
    QT = dscr("QT", [NU, 128, 8 * 256], BF16)
    KT = dscr("KT", [NU, 128, 8 * 256], BF16)
    KK = dscr("KK", [NCH, 128, D], BF16)
    VV = dscr("VV", [NCH, 128, D], BF16)
    SG = dscr("SG", [NCH, 128, D], BF16)
    GT = dscr("GT", [NCH, 128, 2048], BF16)
    YC = dscr("YC", [NCH, 128, D], BF16)
    SB = dscr("SB", [NCH, 128, 8 * 256], BF16)
    UTS = dscr("UTS", [2, 64, 128, 8 * 256], BF16)
    VS = dscr("VS", [2, 64, 128, 2 * 1024], BF16)

    units = []
    u0 = 0
    seq_ranges = []
    for L in seqs:
        nu = L // 256
        seq_ranges.append((u0, u0 + nu))
        for i in range(nu):
            units.append((u0, u0 + nu, i * 256))
        u0 += nu

    dbufs = {}

    def DB(name, idx):
        k = (name, idx)
        if k not in dbufs:
            dbufs[k] = Buf(f"{name}{idx}")
        return dbufs[k]

    with ExitStack() as top:
        S = Sched(nc, top)
        pe, act, dve, pool, sp, sw = S.pe, S.act, S.dve, S.pool, S.sp, S.sw

        uniq = [0]

        def sb(stack, name, shape, dt):
            uniq[0] += 1
            name = f"{name}_{uniq[0]}"
            return TT(stack.enter_context(nc.sbuf_tensor(name, list(shape), dt)), name)

        PB = [TT(top.enter_context(nc.psum_tensor(f"pb{i}", [128, 512], F32)), f"pb{i}") for i in range(8)]

        def pbf(bank):
            return bank.t[:].bitcast(BF16)

        cst = sb(top, "cst", [128, 1024], F32)
        sp.dma([], [cst], cst[:], consts_d[:, :])
        ident_f = cst[:, 0:128]
        identb = sb(top, "identb", [128, 128], BF16)
        iotab = sb(top, "iotab", [128, 128], BF16)
        dve.op([cst], [identb], lambda h: h.tensor_copy(out=identb[:], in_=cst[:, 0:128]))
        dve.op([cst], [iotab], lambda h: h.tensor_copy(out=iotab[:], in_=cst[:, 128:256]))
        iota16 = cst[:, 776:792]

        def mm_group(out_bank, reads, mms, extra_writes=()):
            n = len(mms)

            def fn(h):
                inst = None
                for i, (o, l, r) in enumerate(mms):
                    inst = h.matmul(o, lhsT=l, rhs=r, start=(i == 0), stop=(i == n - 1))
                return inst
            pe.op(reads, [out_bank] + list(extra_writes), fn)

        def transposes(out_banks, reads, trs):
            def fn(h):
                inst = None
                for (o, i_, idn) in trs:
                    inst = h.transpose(o, i_, idn)
                return inst
            pe.op(reads, list(out_banks), fn)

        def norm_T(xin, gtile, xn, xnT, junk, small, bankA, bankB, evac_engs):
            ss, ms, sq, rstd = small
            if isinstance(junk, tuple):
                jap, jb = junk
            else:
                jap, jb = junk[:], junk
            for s in range(2):
                act.op([xin], [jb, ss], lambda h, s=s: h.activation(
                    out=jap, in_=xin[:, s, :], func=AF.Square, accum_out=ss[:, s:s + 1]))
            dve.op([ss], [ms], lambda h: h.tensor_scalar(
                out=ms[:], in0=ss[:], scalar1=1.0 / D, scalar2=EPS, op0=ALU.mult, op1=ALU.add))
            act.op([ms], [sq], lambda h: h.activation(out=sq[:], in_=ms[:], func=AF.Sqrt))
            dve.op([sq], [rstd], lambda h: h.reciprocal(out=rstd[:], in_=sq[:]))
            for s in range(2):
                dve.op([xin, rstd, gtile], [xn], lambda h, s=s: h.scalar_tensor_tensor(
                    out=xn[:, s, :], in0=xin[:, s, :], scalar=rstd[:, s:s + 1], in1=gtile[:],
                    op0=ALU.mult, op1=ALU.mult))
            trs = []
            for kc in range(8):
                bank = bankA if kc < 4 else bankB
                v = pbf(bank)
                for s in range(2):
                    o = v[:, (kc % 4) * 256 + s * 128:(kc % 4) * 256 + (s + 1) * 128]
                    trs.append((o, xn[:, s, kc * 128:(kc + 1) * 128], identb[:]))
            transposes([bankA, bankB], [xn, identb], trs)
            e0, e1 = evac_engs
            e0.op([bankA], [xnT], lambda h: _copy(h, xnT[:, 0:4, :], pbf(bankA).rearrange("p (a t) -> p a t", a=4)))
            e1.op([bankB], [xnT], lambda h: _copy(h, xnT[:, 4:8, :], pbf(bankB).rearrange("p (a t) -> p a t", a=4)))

        def _copy(h, out, in_):
            if h is nc.scalar:
                return h.activation(out=out, in_=in_, func=AF.Copy)
            return h.tensor_copy(out=out, in_=in_)

        def load_weight_bf16(dst, src3, nk, ncols, col0=0):
            for kc in range(nk):
                for c0 in range(0, ncols, 2048):
                    c1 = min(ncols, c0 + 2048)
                    sw.dma([], [dst], dst[:, kc, c0:c1], src3[kc * 128:(kc + 1) * 128, col0 + c0:col0 + c1])

        def bcast_row(dst, row_ap):
            sp.dma([], [dst], dst[:], row_ap.partition_broadcast(128))

        phase_ctr = [0]

        def chk():
            phase_ctr[0] += 1
            if stop and phase_ctr[0] > stop:
                DEAD[0] = True

        p0_list = [(l_, g_) for l_ in range(n_layers) for g_ in range(32)]
        p0_pos = [0]

        def p0_emit(bufs, n, banks):
            uf, uts, vb = bufs
            for _ in range(n):
                if p0_pos[0] >= len(p0_list):
                    return
                l_, g = p0_list[p0_pos[0]]
                p0_pos[0] += 1
                rows = peer_u[l_, g * 512:(g + 1) * 512, :].rearrange("(c p) d -> p c d", p=128)
                sp.dma([], [uf], uf[:], rows)
                for kc in range(8):
                    bank = banks[kc % len(banks)]
                    trs = [(bank[:, c * 128:(c + 1) * 128], uf[:, c, kc * 128:(kc + 1) * 128], ident_f)
                           for c in range(4)]
                    transposes([bank], [uf, cst], trs)
                    e = act if kc % 2 == 0 else dve
                    e.op([bank], [uts], lambda h, kc=kc, bank=bank: _copy(h, uts[:, kc, :], bank[:, :]))
                for hf in range(2):
                    sp.dma([uts], [DB("UTS", (l_, 2 * g + hf))], UTS[l_, 2 * g + hf].rearrange("p (a e) -> p a e", a=8),
                           uts[:, :, hf * 256:(hf + 1) * 256])
                vrows = peer_v[l_, g * 512:(g + 1) * 512, :].rearrange("(c p) d -> p c d", p=128)
                sw.dma([], [vb], vb[:], vrows)
                for hf in range(2):
                    sp.dma([vb], [DB("VS", (l_, 2 * g + hf))], VS[l_, 2 * g + hf].rearrange("p (a e) -> p a e", a=2),
                           vb[:, 2 * hf:2 * hf + 2, :])

        try:
          for l in range(n_layers):
              chk()
              x_src = x_in if l == 0 else X1
              last = (l == n_layers - 1)
              x_dst = y_out if last else X1

              def xrows(src, u):
                  return src[u * 256:(u + 1) * 256, :].rearrange("(s p) d -> p s d", p=128)

              lay = top.enter_context(ExitStack())
              lgfb = sb(lay, "lgfb", [128, 4], F32)
              lgbb = sb(lay, "lgbb", [128, 4], F32)
              bcast_row(lgfb, lg_f[l:l + 1, :])
              bcast_row(lgbb, lg_b[l:l + 1, :])
              dsc = sb(lay, "dsc", [128, 6, 4], F32)
              specs = [(lgfb, 768, 0.0), (lgbb, 769, 0.0), (lgfb, 770, -np.log(16.0)), (lgbb, 771, -np.log(16.0)),
                       (lgfb, 772, 0.0), (lgbb, 772, 0.0)]
              for i, (lgt, col, bias) in enumerate(specs):
                  dve.op([cst, lgt], [dsc], lambda h, i=i, lgt=lgt, col=col: h.tensor_scalar(
                      out=dsc[:, i, :], in0=lgt[:], scalar1=cst[:, col:col + 1], scalar2=None, op0=ALU.mult))
              dve.op([dsc], [dsc], lambda h: h.tensor_scalar(
                  out=dsc[:, 2:4, :], in0=dsc[:, 2:4, :], scalar1=float(-np.log(16.0)), scalar2=None, op0=ALU.add))
              act.op([dsc], [dsc], lambda h: h.activation(out=dsc[:], in_=dsc[:], func=AF.Exp))
              A_F, A_B, VD_F, VD_B, CD_F, CD_B = range(6)
              if SUBSTOP[0] == 50:
                  sp.dma([dsc], [DB("Y", -1)], y_out[0:128, 0:24], dsc[:].rearrange("p a b -> p (a b)"))
                  sp.dma([decT], [DB("Y", -2)], y_out[0:128, 32:32 + 512], decT[:].rearrange("p a b -> p (a b)"))
                  sp.dma([lgfb], [DB("Y", -3)], y_out[0:128, 600:604], lgfb[:])
                  sp.dma([lgbb], [DB("Y", -4)], y_out[0:128, 608:612], lgbb[:])
                  sub(50)

              chk()
              with ExitStack() as ph:
                  wsb = sb(ph, "w1a", [128, 8, 4096], BF16)
                  load_weight_bf16(wsb, w_in[l], 8, 4096, 0)
                  gtile = sb(ph, "g1a", [128, D], F32)
                  bcast_row(gtile, norm_mix_g[l:l + 1, :])
                  xin = [sb(ph, f"xin{i}", [128, 2, D], F32) for i in range(2)]
                  cs = [sb(ph, f"cs{i}", [128, 2, 256], F32) for i in range(2)]
                  xn = sb(ph, "xn", [128, 2, D], BF16)
                  xnT = sb(ph, "xnT", [128, 8, 256], BF16)
                  junk = sb(ph, "junk", [128, D], BF16)
                  small = [sb(ph, f"sm{i}", [128, 2], F32) for i in range(4)]
                  uT = [sb(ph, f"uT{i}", [128, 8, 256], BF16) for i in range(2)]
                  qT = [sb(ph, f"qT{i}", [128, 8, 256], BF16) for i in range(2)]
                  kT = [sb(ph, f"kT{i}", [128, 8, 256], BF16) for i in range(2)]
                  ktok = [sb(ph, f"ktok{i}", [128, 2, D], BF16) for i in range(2)]
                  sig = [sb(ph, f"sig{i}", [128, 256], F32) for i in range(2)]
                  rt = [sb(ph, f"rt{i}", [128, 4, 256], F32) for i in range(4)]
                  kraw = [sb(ph, f"kraw{i}", [128, 2, 256], F32) for i in range(2)]

                  def p1a_load_x(u):
                      a = u % 2
                      sp.dma([DB("X", u)], [xin[a]], xin[a][:], xrows(x_src, u))

                  def p1a_load_cs(u):
                      a = u % 2
                      pos = units[u][2]
                      sp.dma([], [cs[a]], cs[a][:, 0, :], cos_d[:, pos:pos + 256])
                      sp.dma([], [cs[a]], cs[a][:, 1, :], sin_d[:, pos:pos + 256])

                  xn2_ = [xn, sb(ph, "xnB", [128, 2, D], BF16)]
                  xnT2_ = [xnT, sb(ph, "xnTB", [128, 8, 256], BF16)]
                  p0bufs = None
                  if l == 0:
                      p0bufs = (sb(ph, "p0uf", [128, 4, D], F32), sb(ph, "p0uts", [128, 8, 512], BF16),
                                sb(ph, "p0vb", [128, 4, D], BF16))
                  p1a_load_x(0)
                  p1a_load_cs(0)
                  if NU > 1:
                      p1a_load_x(1)
                  norm_T(xin[0], gtile, xn2_[0], xnT2_[0], junk, small, PB[6], PB[7], (act, dve))
                  pair_i = 0
                  for u in range(NU):
                      a = u % 2
                      xnT = xnT2_[a]
                      if u + 1 < NU:
                          p1a_load_cs(u + 1)
                      if u + 2 < NU:
                          p1a_load_x(u + 2)
                      if u + 1 < NU:
                          norm_T(xin[1 - a], gtile, xn2_[1 - a], xnT2_[1 - a], junk, small, PB[6], PB[7], (act, dve))
                      cosA = cs[a][:, 0, :]
                      sinA = cs[a][:, 1, :]
                      for pi in range(16):
                          bank = PB[pair_i % 6]
                          pair_i += 1
                          if pi < 8:
                              c1, c2 = pi, 8 + pi
                          elif pi < 12:
                              c1, c2 = 16 + 2 * (pi - 8), 17 + 2 * (pi - 8)
                          else:
                              c1, c2 = 24 + 2 * (pi - 12), 25 + 2 * (pi - 12)
                          for half, cc in enumerate((c1, c2)):
                              mms = [(bank[:, half * 256:(half + 1) * 256], wsb[:, kc, cc * 128:(cc + 1) * 128],
                                      xnT[:, kc, :]) for kc in range(8)]
                              mm_group(bank, [wsb, xnT], mms)
                          p1 = bank[:, 0:256]
                          p2 = bank[:, 256:512]
                          if pi < 8:
                              sg_ = sig[pi % 2]
                              act.op([bank], [sg_], lambda h, sg_=sg_, p2=p2: h.activation(
                                  out=sg_[:], in_=p2, func=AF.Sigmoid))
                              dve.op([bank, sg_], [uT[a]], lambda h, sg_=sg_, p1=p1, pi=pi: h.tensor_tensor(
                                  out=uT[a][:, pi, :], in0=p1, in1=sg_[:], op=ALU.mult))
                          elif pi < 12:
                              hd = pi - 8
                              r = rt[pi % 4]
                              dst = qT[a]
                              dve.op([bank, cs[a]], [r], lambda h, r=r: h.tensor_tensor(out=r[:, 0, :], in0=p1, in1=cosA, op=ALU.mult))
                              dve.op([bank, cs[a]], [r], lambda h, r=r: h.tensor_tensor(out=r[:, 1, :], in0=p2, in1=sinA, op=ALU.mult))
                              dve.op([bank, cs[a]], [r], lambda h, r=r: h.tensor_tensor(out=r[:, 2, :], in0=p1, in1=sinA, op=ALU.mult))
                              dve.op([bank, cs[a]], [r], lambda h, r=r: h.tensor_tensor(out=r[:, 3, :], in0=p2, in1=cosA, op=ALU.mult))
                              pool.op([r], [dst], lambda h, r=r, dst=dst, hd=hd: h.tensor_tensor(
                                  out=dst[:, 2 * hd, :], in0=r[:, 0, :], in1=r[:, 1, :], op=ALU.subtract))
                              pool.op([r], [dst], lambda h, r=r, dst=dst, hd=hd: h.tensor_tensor(
                                  out=dst[:, 2 * hd + 1, :], in0=r[:, 2, :], in1=r[:, 3, :], op=ALU.add))
                          else:
                              hd = pi - 12
                              r = rt[pi % 4]
                              dst = kT[a]
                              dve.op([bank, cs[a]], [r], lambda h, r=r: h.tensor_tensor(out=r[:, 0, :], in0=p1, in1=cosA, op=ALU.mult))
                              dve.op([bank, cs[a]], [r], lambda h, r=r: h.tensor_tensor(out=r[:, 1, :], in0=p2, in1=sinA, op=ALU.mult))
                              dve.op([bank, cs[a]], [r], lambda h, r=r: h.tensor_tensor(out=r[:, 2, :], in0=p1, in1=sinA, op=ALU.mult))
                              dve.op([bank, cs[a]], [r], lambda h, r=r: h.tensor_tensor(out=r[:, 3, :], in0=p2, in1=cosA, op=ALU.mult))
                              pool.op([r], [dst], lambda h, r=r, dst=dst, hd=hd: h.tensor_tensor(
                                  out=dst[:, 2 * hd, :], in0=r[:, 0, :], in1=r[:, 1, :], op=ALU.subtract))
                              pool.op([r], [dst], lambda h, r=r, dst=dst, hd=hd: h.tensor_tensor(
                                  out=dst[:, 2 * hd + 1, :], in0=r[:, 2, :], in1=r[:, 3, :], op=ALU.add))
                      sub(4)
                      for s in range(2):
                          bank = PB[6 + s]
                          v = pbf(bank)
                          trs = [(v[:, cc * 128:(cc + 1) * 128], kT[a][:, cc, s * 128:(s + 1) * 128], identb[:])
                                 for cc in range(8)]
                          transposes([bank], [kT[a], identb], trs)
                          e = act if s == 0 else dve
                          e.op([bank], [ktok[a]], lambda h, s=s, bank=bank: _copy(h, ktok[a][:, s, :], pbf(bank)))
                      sub(5)
                      flat = lambda t: t[:].rearrange("p a t -> p (a t)")
                      sw.dma([uT[a]], [DB("UT", u)], UT[u], flat(uT[a]))
                      sw.dma([qT[a]], [DB("QT", u)], QT[u], flat(qT[a]))
                      sw.dma([kT[a]], [DB("KT", u)], KT[u], flat(kT[a]))
                      sub(6)
                      for s in range(2):
                          sw.dma([ktok[a]], [DB("KK", 2 * u + s)], KK[2 * u + s], ktok[a][:, s, :])
                      if p0bufs is not None:
                          p0_emit(p0bufs, 2, [PB[6], PB[7]])
                  if p0bufs is not None:
                      while p0_pos[0] < len(p0_list):
                          p0_emit(p0bufs, 1, PB)
              S.barrier()

              chk()
              with ExitStack() as ph:
                  wv = sb(ph, "w1b", [128, 8, 2048], BF16)
                  load_weight_bf16(wv, w_in[l], 8, 2048, 4096)
                  wg = sb(ph, "wg", [128, 8, 2048], BF16)
                  load_weight_bf16(wg, w_gate[l], 8, 2048, 0)
                  gtile = sb(ph, "g1b", [128, D], F32)
                  bcast_row(gtile, norm_mix_g[l:l + 1, :])
                  xin = [sb(ph, f"xin{i}", [128, 2, D], F32) for i in range(2)]
                  kk = [sb(ph, f"kk{i}", [128, 2, D], BF16) for i in range(2)]
                  xn = sb(ph, "xn", [128, 2, D], BF16)
                  xnT = sb(ph, "xnT", [128, 8, 256], BF16)
                  junk = sb(ph, "junk", [128, D], BF16)
                  small = [sb(ph, f"sm{i}", [128, 2], F32) for i in range(4)]
                  vsb = [sb(ph, f"vsb{i}", [128, 2, D], BF16) for i in range(2)]
                  vdb = [sb(ph, f"vdb{i}", [128, 2, D], BF16) for i in range(2)]
                  sgb = [sb(ph, f"sgb{i}", [128, 2, D], BF16) for i in range(2)]
                  gtb = [sb(ph, f"gtb{i}", [128, 2, 2048], BF16) for i in range(2)]
                  Sst = sb(ph, "Sst", [128, 8, 256], F32)
                  Sbf = [sb(ph, f"Sbf{i}", [128, 8, 256], BF16) for i in range(2)]
                  order = []
                  for (a0, a1) in seq_ranges:
                      order += list(range(a1 - 1, a0 - 1, -1))

                  def p1b_load_x(i):
                      u = order[i]
                      a = i % 2
                      sp.dma([DB("X", u)], [xin[a]], xin[a][:], xrows(x_src, u))

                  def p1b_load_kk(i):
                      u = order[i]
                      a = i % 2
                      for s in range(2):
                          sp.dma([DB("KK", 2 * u + s)], [kk[a]], kk[a][:, s, :], KK[2 * u + s])

                  xn2_ = [xn, sb(ph, "xnB", [128, 2, D], BF16)]
                  xnT2_ = [xnT, sb(ph, "xnTB", [128, 8, 256], BF16)]
                  p1b_load_x(0)
                  p1b_load_kk(0)
                  if NU > 1:
                      p1b_load_x(1)
                  norm_T(xin[0], gtile, xn2_[0], xnT2_[0], junk, small, PB[6], PB[7], (act, dve))
                  bi = 0
                  sbi = 0
                  for i, u in enumerate(order):
                      a = i % 2
                      xnT = xnT2_[a]
                      if i + 1 < NU:
                          p1b_load_kk(i + 1)
                      if i + 2 < NU:
                          p1b_load_x(i + 2)
                      if i + 1 < NU:
                          norm_T(xin[1 - a], gtile, xn2_[1 - a], xnT2_[1 - a], junk, small, PB[6], PB[7], (act, dve))
                      for s in range(2):
                          for blk in range(4):
                              bank = PB[bi % 4]
                              bi += 1
                              mms = [(bank[:, :], xnT[:, kc, s * 128:(s + 1) * 128], wv[:, kc, blk * 512:(blk + 1) * 512])
                                     for kc in range(8)]
                              mm_group(bank, [xnT, wv], mms)
                              sub(40)
                              if blk < 2:
                                  act.op([bank], [vsb[a]], lambda h, s=s, blk=blk, bank=bank: h.activation(
                                      out=vsb[a][:, s, blk * 512:(blk + 1) * 512], in_=bank[:, :], func=AF.Copy))
                                  sub(41)
                                  for hh in range(2):
                                      hd = blk * 2 + hh
                                      act.op([bank, dsc], [vdb[a]], lambda h, s=s, hd=hd, hh=hh, bank=bank: h.activation(
                                          out=vdb[a][:, s, hd * 256:(hd + 1) * 256], in_=bank[:, hh * 256:(hh + 1) * 256],
                                          func=AF.Copy, scale=dsc[:, VD_B, hd:hd + 1]))
                                      sub(42)
                              else:
                                  act.op([bank], [sgb[a]], lambda h, s=s, blk=blk, bank=bank: h.activation(
                                      out=sgb[a][:, s, (blk - 2) * 512:(blk - 1) * 512], in_=bank[:, :], func=AF.Silu))
                          for blk in range(4):
                              bank = PB[bi % 4]
                              bi += 1
                              mms = [(bank[:, :], xnT[:, kc, s * 128:(s + 1) * 128], wg[:, kc, blk * 512:(blk + 1) * 512])
                                     for kc in range(8)]
                              mm_group(bank, [xnT, wg], mms)
                              act.op([bank], [gtb[a]], lambda h, s=s, blk=blk, bank=bank: h.activation(
                                  out=gtb[a][:, s, blk * 512:(blk + 1) * 512], in_=bank[:, :], func=AF.Sigmoid))
                      sub(32)
                      for s in range(2):
                          c = 2 * u + s
                          sw.dma([vsb[a]], [DB("VV", c)], VV[c], vsb[a][:, s, :])
                          sw.dma([sgb[a]], [DB("SG", c)], SG[c], sgb[a][:, s, :])
                          sw.dma([gtb[a]], [DB("GT", c)], GT[c], gtb[a][:, s, :])
                      sub(33)
                      if u == units[u][1] - 1:
                          pool.op([], [Sst], lambda h: h.memset(Sst[:], 0.0))
                          pool.op([], [Sbf[sbi % 2]], lambda h, t=Sbf[sbi % 2]: h.memset(t[:], 0.0))
                      for s in (1, 0):
                          c = 2 * u + s
                          cur = Sbf[sbi % 2]
                          nxt = Sbf[(sbi + 1) % 2]
                          sbi += 1
                          sw.dma([cur], [DB("SB", c)], SB[c], cur[:].rearrange("p a e -> p (a e)"))
                          for hp in range(2):
                              for hh in range(2):
                                  hd = hp * 2 + hh
                                  bank = PB[4 + hh]
                                  for dc in range(2):
                                      mm_group(bank, [kk[a], vdb[a]], [(bank[:, dc * 256:(dc + 1) * 256],
                                               kk[a][:, s, (2 * hd + dc) * 128:(2 * hd + dc + 1) * 128],
                                               vdb[a][:, s, hd * 256:(hd + 1) * 256])])
                                  dve.op([bank, dsc, Sst], [Sst], lambda h, hd=hd, bank=bank: h.scalar_tensor_tensor(
                                      out=Sst[:, 2 * hd:2 * hd + 2, :], in0=Sst[:, 2 * hd:2 * hd + 2, :],
                                      scalar=dsc[:, CD_B, hd:hd + 1],
                                      in1=bank[:, :].rearrange("p (a e) -> p a e", a=2), op0=ALU.mult, op1=ALU.add))
                          act.op([Sst], [nxt], lambda h, nxt=nxt: h.activation(out=nxt[:], in_=Sst[:], func=AF.Copy))
                          sub(34)
              S.barrier()

              chk()
              with ExitStack() as ph:
                  wco = sb(ph, "wco", [128, 8, D], BF16)
                  load_weight_bf16(wco, w_conv_out[l], 8, D, 0)
                  praw = sb(ph, "praw", [34, D], F32)
                  sp.dma([], [praw], praw[0:31, :], conv_w[l])
                  sp.dma([], [praw], praw[31:32, :], conv_b[l:l + 1, :])
                  sp.dma([], [praw], praw[32:33, :], conv_ln_g[l:l + 1, :])
                  sp.dma([], [praw], praw[33:34, :], conv_ln_b[l:l + 1, :])
                  cpar = sb(ph, "cpar", [128, 8, 34], F32)
                  for cc in range(8):
                      bank = PB[cc]
                      transposes([bank], [praw, cst], [(bank[:, 0:34], praw[:, cc * 128:(cc + 1) * 128], cst[0:34, 0:34])])
                      dve.op([bank], [cpar], lambda h, cc=cc, bank=bank: h.tensor_copy(out=cpar[:, cc, :], in_=bank[:, 0:34]))
                  diag = sb(ph, "diag", [128, 8, 31, 128], BF16)
                  for cc in range(8):
                      for k in range(31):
                          e = dve
                          e.op([cst, cpar], [diag], lambda h, cc=cc, k=k: h.tensor_scalar(
                              out=diag[:, cc, k, :], in0=cst[:, 0:128], scalar1=cpar[:, cc, k:k + 1], scalar2=None,
                              op0=ALU.mult))
                  uh = [sb(ph, f"uh{i}", [128, 8, 286], BF16) for i in range(2)]
                  uhl = [Buf(f"uhl{i}") for i in range(2)]
                  uhr = [Buf(f"uhr{i}") for i in range(2)]
                  cv_ = [sb(ph, f"cv{i}", [128, 8, 256], F32) for i in range(2)]
                  sq2_ = [sb(ph, f"sq2{i}", [128, 8, 256], F32) for i in range(2)]
                  mean_ = [sb(ph, f"mean{i}", [128, 256], F32) for i in range(2)]
                  m2_ = [sb(ph, f"m2{i}", [128, 256], F32) for i in range(2)]
                  rstd_ = [sb(ph, f"rstdc{i}", [128, 256], F32) for i in range(2)]
                  uln = [sb(ph, f"uln{i}", [128, 8, 256], BF16) for i in range(2)]
                  ycb = [sb(ph, f"ycb{i}", [128, 2, D], BF16) for i in range(2)]
                  onesm = cst[:, 800:928]

                  def p2c_load(u):
                      a = u % 2
                      u_lo, u_hi, _ = units[u]
                      UTu = lambda uu: UT[uu].rearrange("p (a t) -> p a t", a=8)
                      sp.dma([DB("UT", u)], [uh[a]], uh[a][:, :, 15:271], UTu(u))
                      if u > u_lo:
                          sp.dma([DB("UT", u - 1)], [uhl[a]], uh[a][:, :, 0:15], UTu(u - 1)[:, :, 241:256])
                      else:
                          pool.op([], [uhl[a]], lambda h: h.memset(uh[a][:, :, 0:15], 0.0))
                      if u + 1 < u_hi:
                          sp.dma([DB("UT", u + 1)], [uhr[a]], uh[a][:, :, 271:286], UTu(u + 1)[:, :, 0:15])
                      else:
                          pool.op([], [uhr[a]], lambda h: h.memset(uh[a][:, :, 271:286], 0.0))

                  p2c_load(0)
                  for u in range(NU):
                      a = u % 2
                      if u + 1 < NU:
                          p2c_load(u + 1)
                      cv, sq2, mean, m2, rstd = cv_[a], sq2_[a], mean_[a], m2_[a], rstd_[a]
                      for cc in range(8):
                          bank = PB[cc // 2]
                          half = cc % 2
                          mms = [(bank[:, half * 256:(half + 1) * 256], diag[:, cc, k, :], uh[a][:, cc, k:k + 256])
                                 for k in range(31)]
                          mm_group(bank, [diag, uh[a], uhl[a], uhr[a]], mms)
                          act.op([bank, cpar], [cv], lambda h, cc=cc, bank=bank, half=half: h.activation(
                              out=cv[:, cc, :], in_=bank[:, half * 256:(half + 1) * 256], func=AF.Identity,
                              bias=cpar[:, cc, 31:32]))
                          act.op([cv], [sq2], lambda h, cc=cc: h.activation(out=sq2[:, cc, :], in_=cv[:, cc, :], func=AF.Square))
                      bm = PB[4 + a]
                      mm_group(bm, [cst, cv], [(bm[:, 0:256], onesm, cv[:, cc, :]) for cc in range(8)])
                      mm_group(bm, [cst, sq2], [(bm[:, 256:512], onesm, sq2[:, cc, :]) for cc in range(8)])
                      act.op([bm], [mean], lambda h: h.activation(out=mean[:], in_=bm[:, 0:256], func=AF.Copy))
                      act.op([bm], [m2], lambda h: h.activation(out=m2[:], in_=bm[:, 0:256], func=AF.Square))
                      dve.op([bm, m2], [m2], lambda h: h.tensor_tensor(out=m2[:], in0=bm[:, 256:512], in1=m2[:], op=ALU.subtract))
                      dve.op([m2], [m2], lambda h: h.tensor_scalar(out=m2[:], in0=m2[:], scalar1=EPS, scalar2=None, op0=ALU.add))
                      act.op([m2], [m2], lambda h: h.activation(out=m2[:], in_=m2[:], func=AF.Sqrt))
                      dve.op([m2], [rstd], lambda h: h.reciprocal(out=rstd[:], in_=m2[:]))
                      bc = lambda t: t[:].unsqueeze(1).to_broadcast([128, 8, 256])
                      pool.op([cv, mean], [cv], lambda h: h.tensor_tensor(out=cv[:], in0=cv[:], in1=bc(mean), op=ALU.subtract))
                      dve.op([cv, rstd], [cv], lambda h: h.tensor_tensor(out=cv[:], in0=cv[:], in1=bc(rstd), op=ALU.mult))
                      for cc in range(8):
                          act.op([cv, cpar], [uln[a]], lambda h, cc=cc: h.activation(
                              out=uln[a][:, cc, :], in_=cv[:, cc, :], func=AF.Silu,
                              scale=cpar[:, cc, 32:33], bias=cpar[:, cc, 33:34]))
                      for s in range(2):
                          for blk in range(2):
                              bank = PB[6 + (s * 2 + blk) % 2]
                              mms = [(bank[:, :], uln[a][:, cc, s * 128:(s + 1) * 128], wco[:, cc, blk * 512:(blk + 1) * 512])
                                     for cc in range(8)]
                              mm_group(bank, [uln[a], wco], mms)
                              dve.op([bank], [ycb[a]], lambda h, s=s, blk=blk, bank=bank: h.tensor_copy(
                                  out=ycb[a][:, s, blk * 512:(blk + 1) * 512], in_=bank[:, :]))
                          sw.dma([ycb[a]], [DB("YC", 2 * u + s)], YC[2 * u + s], ycb[a][:, s, :])
              S.barrier()

              chk()
              with ExitStack() as ph:
                  decT = sb(ph, "decT", [128, 4, 128], F32)
                  dtmp = sb(ph, "dtmp", [128, 2, 128], F32)
                  for hd in range(4):
                      dve.op([cst, lgfb], [dtmp], lambda h, hd=hd: h.tensor_scalar(
                          out=dtmp[:, 0, :], in0=cst[:, 256:384], scalar1=lgfb[:, hd:hd + 1], scalar2=None, op0=ALU.mult))
                      dve.op([cst, lgbb], [dtmp], lambda h, hd=hd: h.tensor_scalar(
                          out=dtmp[:, 1, :], in0=cst[:, 384:512], scalar1=lgbb[:, hd:hd + 1], scalar2=None, op0=ALU.mult))
                      act.op([dtmp], [dtmp], lambda h: h.activation(out=dtmp[:], in_=dtmp[:], func=AF.Exp))
                      dve.op([dtmp, cst], [dtmp], lambda h: h.tensor_tensor(
                          out=dtmp[:, 0, :], in0=dtmp[:, 0, :], in1=cst[:, 512:640], op=ALU.mult))
                      dve.op([dtmp, cst], [dtmp], lambda h: h.tensor_tensor(
                          out=dtmp[:, 1, :], in0=dtmp[:, 1, :], in1=cst[:, 640:768], op=ALU.mult))
                      dve.op([dtmp], [decT], lambda h, hd=hd: h.tensor_tensor(
                          out=decT[:, hd, :], in0=dtmp[:, 0, :], in1=dtmp[:, 1, :], op=ALU.add))
                  wro = sb(ph, "wro", [128, 8, D], BF16)
                  load_weight_bf16(wro, w_ret_out[l], 8, D, 0)
                  wo = sb(ph, "wo", [128, 8, D], BF16)
                  load_weight_bf16(wo, w_o[l], 8, D, 0)
                  xin = [sb(ph, f"xin{i}", [128, 2, D], F32) for i in range(2)]
                  qTl = [sb(ph, f"qTl{i}", [128, 8, 256], BF16) for i in range(2)]
                  kTl = [sb(ph, f"kTl{i}", [128, 8, 256], BF16) for i in range(2)]
                  kk = [sb(ph, f"kk{i}", [128, 2, D], BF16) for i in range(2)]
                  vv = [sb(ph, f"vv{i}", [128, 2, D], BF16) for i in range(2)]
                  sgl = [sb(ph, f"sgl{i}", [128, 2, D], BF16) for i in range(2)]
                  gtl = [sb(ph, f"gtl{i}", [128, 2, 2048], BF16) for i in range(2)]
                  ycl = [sb(ph, f"ycl{i}", [128, 2, D], BF16) for i in range(2)]
                  sbw = [sb(ph, f"sbw{i}", [128, 2, 8 * 256], BF16) for i in range(2)]
                  Sst = sb(ph, "Sstf", [128, 8, 256], F32)
                  Sbf = sb(ph, "Sbff", [128, 8, 256], BF16)
                  PT = [sb(ph, f"PT{i}", [128, 128], BF16) for i in range(2)]
                  tf = [sb(ph, f"tf{i}", [128, 256], F32) for i in range(2)]
                  o_sb = sb(ph, "o_sb", [128, D], F32)
                  vdf = sb(ph, "vdf", [128, D], BF16)
                  gn = [sb(ph, f"gn{i}", [128, 4], F32) for i in range(4)]
                  junk = sb(ph, "junk", [128, 256], BF16)
                  om = sb(ph, "om", [128, D], BF16)
                  omT = sb(ph, "omT", [128, 8, 256], BF16)
                  m1 = sb(ph, "m1", [128, D], F32)
                  mg = sb(ph, "mg", [128, D], BF16)
                  mgT = sb(ph, "mgT", [128, 8, 256], BF16)
                  xo = [sb(ph, f"xo{i}", [128, 2, D], F32) for i in range(2)]

                  def p3a_load(u):
                      a = u % 2
                      sp.dma([DB("X", u)], [xin[a]], xin[a][:], xrows(x_src, u))
                      sp.dma([DB("QT", u)], [qTl[a]], qTl[a][:].rearrange("p a t -> p (a t)"), QT[u])
                      sp.dma([DB("KT", u)], [kTl[a]], kTl[a][:].rearrange("p a t -> p (a t)"), KT[u])
                      for s in range(2):
                          c = 2 * u + s
                          sp.dma([DB("KK", c)], [kk[a]], kk[a][:, s, :], KK[c])
                          sp.dma([DB("VV", c)], [vv[a]], vv[a][:, s, :], VV[c])
                          sp.dma([DB("SG", c)], [sgl[a]], sgl[a][:, s, :], SG[c])
                          sp.dma([DB("GT", c)], [gtl[a]], gtl[a][:, s, :], GT[c])
                          sp.dma([DB("YC", c)], [ycl[a]], ycl[a][:, s, :], YC[c])
                          sp.dma([DB("SB", c)], [sbw[a]], sbw[a][:, s, :], SB[c])

                  p3a_load(0)
                  hi = 0
                  for u in range(NU):
                      a = u % 2
                      if u + 1 < NU:
                          p3a_load(u + 1)
                      if u == units[u][0]:
                          pool.op([], [Sst], lambda h: h.memset(Sst[:], 0.0))
                          pool.op([], [Sbf], lambda h: h.memset(Sbf[:], 0.0))
                      for s in range(2):
                          ts = slice(s * 128, (s + 1) * 128)
                          sbv = sbw[a][:, s, :].rearrange("p (a e) -> p a e", a=8)
                          for hd in range(4):
                              bA = PB[hi % 2]
                              bO = PB[2 + hi % 2]
                              bC = PB[4 + hi % 2]
                              pt = PT[hi % 2]
                              t_f = tf[hi % 2]
                              hi += 1
                              mm_group(bA, [kTl[a], qTl[a]], [(bA[:, 0:128], kTl[a][:, 2 * hd + dc, ts], qTl[a][:, 2 * hd + dc, ts])
                                                             for dc in range(2)])
                              dve.op([bA, decT], [pt], lambda h, hd=hd, bA=bA, pt=pt: h.tensor_tensor(
                                  out=pt[:], in0=bA[:, 0:128], in1=decT[:, hd, :], op=ALU.mult))
                              mm_group(bO, [pt, vv[a]], [(bO[:, 0:256], pt[:], vv[a][:, s, hd * 256:(hd + 1) * 256])])
                              mm_group(bO, [qTl[a], Sbf], [(bO[:, 256:512], qTl[a][:, 2 * hd + dc, ts], Sbf[:, 2 * hd + dc, :])
                                                          for dc in range(2)])
                              mm_group(bC, [qTl[a], sbw[a]], [(bC[:, 0:256], qTl[a][:, 2 * hd + dc, ts], sbv[:, 2 * hd + dc, :])
                                                             for dc in range(2)])
                              act.op([bO, dsc], [t_f], lambda h, hd=hd, bO=bO, t_f=t_f: h.activation(
                                  out=t_f[:], in_=bO[:, 256:512], func=AF.Copy, scale=dsc[:, A_F, hd:hd + 1]))
                              dve.op([bC, dsc, t_f], [t_f], lambda h, hd=hd, bC=bC, t_f=t_f: h.scalar_tensor_tensor(
                                  out=t_f[:], in0=bC[:, 0:256], scalar=dsc[:, A_B, hd:hd + 1], in1=t_f[:],
                                  op0=ALU.mult, op1=ALU.add))
                              dve.op([bO, t_f], [o_sb], lambda h, hd=hd, bO=bO, t_f=t_f: h.tensor_tensor(
                                  out=o_sb[:, hd * 256:(hd + 1) * 256], in0=bO[:, 0:256], in1=t_f[:], op=ALU.add))
                          for hd in range(4):
                              act.op([vv[a], dsc], [vdf], lambda h, hd=hd, s=s: h.activation(
                                  out=vdf[:, hd * 256:(hd + 1) * 256], in_=vv[a][:, s, hd * 256:(hd + 1) * 256],
                                  func=AF.Copy, scale=dsc[:, VD_F, hd:hd + 1]))
                          for hd in range(4):
                              bank = PB[6 + hd % 2]
                              for dc in range(2):
                                  mm_group(bank, [kk[a], vdf], [(bank[:, dc * 256:(dc + 1) * 256],
                                           kk[a][:, s, (2 * hd + dc) * 128:(2 * hd + dc + 1) * 128],
                                           vdf[:, hd * 256:(hd + 1) * 256])])
                              dve.op([bank, dsc, Sst], [Sst], lambda h, hd=hd, bank=bank: h.scalar_tensor_tensor(
                                  out=Sst[:, 2 * hd:2 * hd + 2, :], in0=Sst[:, 2 * hd:2 * hd + 2, :],
                                  scalar=dsc[:, CD_F, hd:hd + 1],
                                  in1=bank[:, :].rearrange("p (a e) -> p a e", a=2), op0=ALU.mult, op1=ALU.add))
                          act.op([Sst], [Sbf], lambda h: h.activation(out=Sbf[:], in_=Sst[:], func=AF.Copy))
                          for hd in range(4):
                              act.op([o_sb], [junk, gn[0]], lambda h, hd=hd: h.activation(
                                  out=junk[:], in_=o_sb[:, hd * 256:(hd + 1) * 256], func=AF.Square,
                                  accum_out=gn[0][:, hd:hd + 1]))
                          dve.op([gn[0]], [gn[1]], lambda h: h.tensor_scalar(
                              out=gn[1][:], in0=gn[0][:], scalar1=1.0 / 256.0, scalar2=EPS, op0=ALU.mult, op1=ALU.add))
                          act.op([gn[1]], [gn[2]], lambda h: h.activation(out=gn[2][:], in_=gn[1][:], func=AF.Sqrt))
                          dve.op([gn[2]], [gn[3]], lambda h: h.reciprocal(out=gn[3][:], in_=gn[2][:]))
                          for hd in range(4):
                              dve.op([o_sb, gn[3], sgl[a]], [om], lambda h, hd=hd, s=s: h.scalar_tensor_tensor(
                                  out=om[:, hd * 256:(hd + 1) * 256], in0=o_sb[:, hd * 256:(hd + 1) * 256],
                                  scalar=gn[3][:, hd:hd + 1], in1=sgl[a][:, s, hd * 256:(hd + 1) * 256],
                                  op0=ALU.mult, op1=ALU.mult))
                          bT = PB[6 + s]
                          v = pbf(bT)
                          transposes([bT], [om, identb], [(v[:, cc * 128:(cc + 1) * 128], om[:, cc * 128:(cc + 1) * 128], identb[:])
                                                          for cc in range(8)])
                          act.op([bT], [omT], lambda h, s=s, bT=bT: h.activation(
                              out=omT[:, :, s * 128:(s + 1) * 128], in_=pbf(bT).rearrange("p (a t) -> p a t", a=8), func=AF.Copy))
                          for blk in range(2):
                              bank = PB[blk]
                              mm_group(bank, [omT, wro], [(bank[:, :], omT[:, cc, ts], wro[:, cc, blk * 512:(blk + 1) * 512])
                                                         for cc in range(8)])
                              cs_ = slice(blk * 512, (blk + 1) * 512)
                              pool.op([ycl[a], gtl[a]], [m1], lambda h, s=s, cs_=cs_: h.tensor_tensor(
                                  out=m1[:, cs_], in0=ycl[a][:, s, cs_], in1=gtl[a][:, s, cs_], op=ALU.mult))
                              dve.op([bank, gtl[a], m1], [mg], lambda h, s=s, blk=blk, bank=bank, cs_=cs_: h.tensor_tensor(
                                  out=mg[:, cs_], in0=bank[:, :], in1=gtl[a][:, s, 1024 + blk * 512:1024 + (blk + 1) * 512],
                                  op=ALU.mult))
                              pool.op([m1, mg], [mg], lambda h, cs_=cs_: h.tensor_tensor(
                                  out=mg[:, cs_], in0=mg[:, cs_], in1=m1[:, cs_], op=ALU.add))
                          bT2 = PB[4 + s]
                          v2 = pbf(bT2)
                          transposes([bT2], [mg, identb], [(v2[:, cc * 128:(cc + 1) * 128], mg[:, cc * 128:(cc + 1) * 128], identb[:])
                                                           for cc in range(8)])
                          act.op([bT2], [mgT], lambda h, s=s, bT2=bT2: h.activation(
                              out=mgT[:, :, s * 128:(s + 1) * 128], in_=pbf(bT2).rearrange("p (a t) -> p a t", a=8), func=AF.Copy))
                          for blk in range(2):
                              bank = PB[2 + blk]
                              mm_group(bank, [mgT, wo], [(bank[:, :], mgT[:, cc, ts], wo[:, cc, blk * 512:(blk + 1) * 512])
                                                        for cc in range(8)])
                              dve.op([bank, xin[a]], [xo[a]], lambda h, s=s, blk=blk, bank=bank: h.tensor_tensor(
                                  out=xo[a][:, s, blk * 512:(blk + 1) * 512], in0=bank[:, :],
                                  in1=xin[a][:, s, blk * 512:(blk + 1) * 512], op=ALU.add))
                      sw.dma([xo[a]], [DB("XM", u)], xrows(XM, u), xo[a][:])
              S.barrier()

              chk()
              with ExitStack() as ph:
                  wq = sb(ph, "wq", [128, 8, 2048], BF16)
                  load_weight_bf16(wq, peer_wq[l], 8, 2048, 0)
                  gtile = sb(ph, "g3b", [128, D], F32)
                  bcast_row(gtile, norm_ffn_g[l:l + 1, :])
                  if last and do_final:
                      gfin = sb(ph, "gfin", [128, D], F32)
                      bcast_row(gfin, final_g[0:1, :])
                  skT = sb(ph, "skT", [128, 16, 128], BF16)
                  with ExitStack() as ph2:
                      skraw = sb(ph2, "skraw", [128, 16, 128], F32)
                      sp.dma([], [skraw], skraw[:], peer_sk[l].rearrange("h p k d -> k (h p) d"))
                      for cq in range(16):
                          bank = PB[cq % 8]
                          transposes([bank], [skraw, cst], [(bank[:, 0:128], skraw[:, cq, :], ident_f)])
                          dve.op([bank], [skT], lambda h, cq=cq, bank=bank: h.tensor_copy(out=skT[:, cq, :], in_=bank[:, 0:128]))
                      S.barrier()
                  xw = sb(ph, "xw", [128, 2, D], F32)
                  xnT = [sb(ph, f"xnT{i}", [128, 8, 256], BF16) for i in range(2)]
                  small = [sb(ph, f"sm{i}", [128, 2], F32) for i in range(4)]
                  qTp = sb(ph, "qTp", [128, 16, 256], BF16)
                  sc_ = [sb(ph, f"sc{i}", [128, 16, 128], F32) for i in range(2)]
                  sc2 = sb(ph, "sc2", [128, 16, 128], F32)
                  xn = TT(sc2.t[:].rearrange("p a k -> p (a k)").bitcast(BF16)[:, 0:2 * D].rearrange("p (s d) -> p s d", s=2),
                          "xn_alias")
                  xn.b = sc2.b
                  junk = (xn[:, 0, :], xn)
                  sv = sb(ph, "sv", [128, 16, 16], F32)
                  si = sb(ph, "si", [128, 16, 16], U32)
                  sif = sb(ph, "sif", [128, 16, 16], F32)
                  cand_ = []
                  for i_ in range(2):
                      c_ = TT(sc_[i_].t[:].rearrange("p (h two) k -> p h (two k)", two=2), f"cand_alias{i_}")
                      c_.b = sc_[i_].b
                      cand_.append(c_)
                  cand2 = TT(sc2.t[:].rearrange("p (h two) k -> p h (two k)", two=2), "cand2_alias")
                  cand2.b = sc2.b
                  s16_ = [sb(ph, f"s16_{i}", [128, 8, 16], F32) for i in range(2)]
                  ci = sb(ph, "ci", [128, 8, 16], U32)
                  cia = sb(ph, "cia", [128, 8, 16], U32)
                  cib = sb(ph, "cib", [128, 8, 16], U32)
                  caf = sb(ph, "caf", [128, 8, 16], F32)
                  cbf = sb(ph, "cbf", [128, 8, 16], F32)
                  eq = TT(sc2.t[:].rearrange("p (h two) (a b) -> p h (two a) b", two=2, b=16), "eq_alias")
                  eq.b = sc2.b
                  rowi_ = [[sb(ph, f"rowi{j}_{i}", [128, 128], F32) for i in range(3)] for j in range(2)]
                  gsum = sb(ph, "gsum", [128, 8], F32)
                  colT = [sb(ph, f"colT{i}", [128, 256], F32) for i in range(3)]
                  TBK = 4
                  P1 = [sb(ph, f"P1_{i}", [128, TBK, 128], BF16) for i in range(2)]
                  P2 = [sb(ph, f"P2_{i}", [128, TBK, 128], BF16) for i in range(2)]
                  GTs = sb(ph, "GTs", [128, 128, 256], BF16)
                  hT = [sb(ph, f"hT{i}", [128, 256], BF16) for i in range(3)]
                  gh = [sb(ph, f"gh{i}", [128, 256], BF16) for i in range(4)]
                  P2ENG = pool if CFG.get("P2POOL") else dve
                  nidx2 = sb(ph, "nidx2", [128, 256], F32)
                  atmp = [sb(ph, f"atmp{i}", [128, 128], BF16) for i in range(2)]
                  iota3 = sb(ph, "iota3", [128, TBK, 128], BF16)
                  for tt_ in range(TBK):
                      dve.op([iotab], [iota3], lambda h, tt_=tt_: h.tensor_copy(out=iota3[:, tt_, :], in_=iotab[:]))
                  NSB = CFG["NSB"]
                  LOOK = CFG["LOOK"]
                  ub = [sb(ph, f"ub{i}", [128, 8, 256], BF16) for i in range(NSB)]
                  vbuf = [sb(ph, f"vbuf{i}", [128, 2, D], BF16) for i in range(NSB)]

                  def route_A(u):
                      a = u % 2
                      sp.dma([DB("XM", u)], [xw], xw[:], xrows(XM, u))
                      norm_T(xw, gtile, xn, xnT[a], junk, small, PB[6], PB[7], (act, dve))
                      for cq in range(16):
                          bank = PB[4 + cq % 2]
                          mm_group(bank, [wq, xnT[a]], [(bank[:, 0:256], wq[:, kc, cq * 128:(cq + 1) * 128], xnT[a][:, kc, :])
                                                       for kc in range(8)])
                          act.op([bank], [qTp], lambda h, cq=cq, bank=bank: h.activation(
                              out=qTp[:, cq, :], in_=bank[:, 0:256], func=AF.Copy))
                      for s in range(2):
                          ts = slice(s * 128, (s + 1) * 128)
                          sc = sc_[s]
                          for q4 in range(4):
                              bank = PB[4 + q4 % 2]
                              for j in range(4):
                                  cq = q4 * 4 + j
                                  mm_group(bank, [qTp, skT], [(bank[:, j * 128:(j + 1) * 128], qTp[:, cq, ts], skT[:, cq, :])])
                              act.op([bank], [sc], lambda h, q4=q4, bank=bank, sc=sc: h.activation(
                                  out=sc[:, q4 * 4:(q4 + 1) * 4, :], in_=bank[:, :].rearrange("p (a k) -> p a k", a=4),
                                  func=AF.Copy))
                      for s in range(2):
                          if "A2" in SKIP:
                              break
                          ts = slice(s * 128, (s + 1) * 128)
                          sc = sc_[s]
                          cand = cand_[s]
                          s16 = s16_[s]
                          rowi = rowi_[s]
                          for cq in range(16):
                              dve.op([sc], [sv], lambda h, cq=cq: h.max(out=sv[:, cq, 0:8], in_=sc[:, cq, :]))
                              dve.op([sc, sv], [si], lambda h, cq=cq: h.max_index(out=si[:, cq, 0:8], in_max=sv[:, cq, 0:8], in_values=sc[:, cq, :]))
                              dve.op([sc, sv], [sc2], lambda h, cq=cq: h.match_replace(
                                  out=sc2[:, cq, :], in_to_replace=sv[:, cq, 0:8], in_values=sc[:, cq, :], imm_value=NEG))
                              dve.op([sc2], [sv], lambda h, cq=cq: h.max(out=sv[:, cq, 8:16], in_=sc2[:, cq, :]))
                              dve.op([sc2, sv], [si], lambda h, cq=cq: h.max_index(out=si[:, cq, 8:16], in_max=sv[:, cq, 8:16], in_values=sc2[:, cq, :]))
                          dve.op([si], [sif], lambda h: h.tensor_copy(out=sif[:], in_=si[:]))
                          sv4 = sv[:].rearrange("p (h two) k -> p h two k", two=2)
                          sif4 = sif[:].rearrange("p (h two) k -> p h two k", two=2)
                          cand4 = cand[:].rearrange("p h (a b) -> p h a b", a=16)
                          dve.op([sv], [cand], lambda h: h.tensor_tensor(
                              out=cand4, in0=sv4[:, :, 0, :].unsqueeze(3).to_broadcast([128, 8, 16, 16]),
                              in1=sv4[:, :, 1, :].unsqueeze(2).to_broadcast([128, 8, 16, 16]), op=ALU.add))
                          for hd in range(8):
                              dve.op([cand], [s16], lambda h, hd=hd: h.max(out=s16[:, hd, 0:8], in_=cand[:, hd, :]))
                              dve.op([cand, s16], [ci], lambda h, hd=hd: h.max_index(out=ci[:, hd, 0:8], in_max=s16[:, hd, 0:8], in_values=cand[:, hd, :]))
                              dve.op([cand, s16], [cand2], lambda h, hd=hd: h.match_replace(
                                  out=cand2[:, hd, :], in_to_replace=s16[:, hd, 0:8], in_values=cand[:, hd, :], imm_value=NEG))
                              dve.op([cand2], [s16], lambda h, hd=hd: h.max(out=s16[:, hd, 8:16], in_=cand2[:, hd, :]))
                              dve.op([cand2, s16], [ci], lambda h, hd=hd: h.max_index(out=ci[:, hd, 8:16], in_max=s16[:, hd, 8:16], in_values=cand2[:, hd, :]))
                          dve.op([ci], [cia], lambda h: h.tensor_single_scalar(out=cia[:], in_=ci[:], scalar=4, op=ALU.logical_shift_right))
                          dve.op([ci], [cib], lambda h: h.tensor_single_scalar(out=cib[:], in_=ci[:], scalar=15, op=ALU.bitwise_and))
                          dve.op([cia], [caf], lambda h: h.tensor_copy(out=caf[:], in_=cia[:]))
                          dve.op([cib], [cbf], lambda h: h.tensor_copy(out=cbf[:], in_=cib[:]))
                          io4 = iota16.unsqueeze(1).unsqueeze(1).to_broadcast([128, 8, 16, 16])
                          for which, (cf, half) in enumerate(((caf, 0), (cbf, 1))):
                              dst3 = rowi[which][:].rearrange("p (h k) -> p h k", h=8)
                              dve.op([cf, cst], [eq], lambda h, cf=cf: h.tensor_tensor(
                                  out=eq[:], in0=cf[:].unsqueeze(3).to_broadcast([128, 8, 16, 16]), in1=io4, op=ALU.is_equal))
                              dve.op([eq, sif], [eq], lambda h, half=half: h.tensor_tensor(
                                  out=eq[:], in0=eq[:], in1=sif4[:, :, half, :].unsqueeze(2).to_broadcast([128, 8, 16, 16]),
                                  op=ALU.mult))
                              dve.op([eq], [rowi[which]], lambda h, dst3=dst3: h.tensor_reduce(
                                  out=dst3, in_=eq[:], axis=AX.X, op=ALU.add))

                  def route_B(u):
                      for s in range(2):
                          ts = slice(s * 128, (s + 1) * 128)
                          s16 = s16_[s]
                          rowi = rowi_[s]
                          g3 = rowi[2][:].rearrange("p (h k) -> p h k", h=8)
                          dve.op([s16], [rowi[2]], lambda h: h.tensor_tensor(
                              out=g3, in0=s16[:], in1=s16[:, :, 0:1].to_broadcast([128, 8, 16]), op=ALU.subtract))
                          act.op([rowi[2]], [rowi[2]], lambda h: h.activation(out=rowi[2][:], in_=rowi[2][:], func=AF.Exp))
                          dve.op([rowi[2]], [gsum], lambda h: h.tensor_reduce(out=gsum[:], in_=g3, axis=AX.X, op=ALU.add))
                          dve.op([gsum], [gsum], lambda h: h.reciprocal(out=gsum[:], in_=gsum[:]))
                          dve.op([rowi[2], gsum], [rowi[2]], lambda h: h.tensor_tensor(
                              out=g3, in0=g3, in1=gsum[:].unsqueeze(2).to_broadcast([128, 8, 16]), op=ALU.mult))
                          for w3 in range(3):
                              bank = PB[6 + w3 % 2]
                              transposes([bank], [rowi[w3], cst], [(bank[:, 0:128], rowi[w3][:], ident_f)])
                              act.op([bank], [colT[w3]], lambda h, w3=w3, bank=bank, ts=ts: h.activation(
                                  out=colT[w3][:, ts], in_=bank[:, 0:128], func=AF.Copy))
                      bi = 0
                      dve.op([colT[1]], [nidx2], lambda h: h.tensor_scalar(
                          out=nidx2[:], in0=colT[1][:], scalar1=-1.0, scalar2=None, op0=ALU.mult))
                      for t0 in range(0, 256, TBK):
                          p1 = P1[(t0 // TBK) % 2]
                          p2 = P2[(t0 // TBK) % 2]
                          for tt in range(TBK):
                              t = t0 + tt
                              dve.op([iotab, colT[0], colT[2]], [p1], lambda h, t=t, tt=tt, p1=p1: h.tensor_scalar(
                                  out=p1[:, tt, :], in0=iotab[:], scalar1=colT[0][:, t:t + 1], scalar2=colT[2][:, t:t + 1],
                                  op0=ALU.is_equal, op1=ALU.mult))
                              if tt < CFG.get("ACTP2", 0):
                                  at_ = atmp[tt % 2]
                                  act.op([iotab, nidx2], [at_], lambda h, t=t, at_=at_: h.activation(
                                      out=at_[:], in_=iotab[:], func=AF.Abs, bias=nidx2[:, t:t + 1]))
                                  act.op([at_], [p2], lambda h, tt=tt, p2=p2, at_=at_: h.activation(
                                      out=p2[:, tt, :], in_=at_[:], func=AF.Relu, scale=-1.0, bias=1.0))
                              elif not CFG.get("P2BATCH"):
                                  P2ENG.op([iotab, colT[1]], [p2], lambda h, t=t, tt=tt, p2=p2: h.tensor_scalar(
                                      out=p2[:, tt, :], in0=iotab[:], scalar1=colT[1][:, t:t + 1], scalar2=None,
                                      op0=ALU.is_equal))
                          if CFG.get("P2BATCH"):
                              dve.op([iota3, colT[1]], [p2], lambda h, t0=t0, p2=p2: h.tensor_tensor(
                                  out=p2[:], in0=iota3[:],
                                  in1=colT[1][:, t0:t0 + TBK].unsqueeze(2).to_broadcast([128, TBK, 128]), op=ALU.is_equal))
                          for q in range(TBK // 4):
                              bank = PB[4 + bi % 4]
                              bi += 1
                              for j in range(4):
                                  tt = q * 4 + j
                                  mm_group(bank, [p1, p2], [(bank[:, j * 128:(j + 1) * 128], p2[:, tt, :], p1[:, tt, :])])
                              tq = t0 + q * 4
                              act.op([bank], [GTs], lambda h, bank=bank, tq=tq: h.activation(
                                  out=GTs[:, :, tq:tq + 4].transpose([0, 2, 1]),
                                  in_=bank[:, :].rearrange("p (t i) -> p t i", t=4), func=AF.Copy))

                  def dense(u):
                      a = u % 2
                      steps = []
                      for g in range(64):
                          for c in range(2):
                              steps.append((g, c))

                      def emit_load(g):
                          if "dload" in SKIP and g >= NSB:
                              return
                          sp.dma([DB("UTS", (l, g))], [ub[g % NSB]], ub[g % NSB][:].rearrange("p a e -> p (a e)"), UTS[l, g])
                          sp.dma([DB("VS", (l, g))], [vbuf[g % NSB]], vbuf[g % NSB][:].rearrange("p a e -> p (a e)"), VS[l, g])

                      def emit_H(i):
                          g, c = steps[i]
                          bank = PB[4 + i % (LOOK + 1)]
                          mm_group(bank, [ub[g % NSB], xnT[a]], [(bank[:, 0:256], ub[g % NSB][:, kc, c * 128:(c + 1) * 128], xnT[a][:, kc, :])
                                                                for kc in range(8)])
                          h_ = hT[i % 3]
                          g_ = gh[i % 4]
                          act.op([bank], [h_], lambda h, bank=bank, h_=h_: h.activation(out=h_[:], in_=bank[:, 0:256], func=AF.Gelu))
                          (dve if CFG.get("GHENG") else pool).op([h_, GTs], [g_], lambda h, h_=h_, g_=g_, i1=2 * g + c: h.tensor_tensor(
                              out=g_[:], in0=h_[:], in1=GTs[:, i1, :], op=ALU.mult))

                      def emit_V(i):
                          g, c = steps[i]
                          g_ = gh[i % 4]
                          def fnv(h, g_=g_, g=g, c=c, i=i):
                              inst = None
                              for s in range(2):
                                  for blk in range(2):
                                      bank = PB[s * 2 + blk]
                                      inst = h.matmul(bank[:, :], lhsT=g_[:, s * 128:(s + 1) * 128],
                                                      rhs=vbuf[g % NSB][:, c, blk * 512:(blk + 1) * 512],
                                                      start=(i == 0), stop=(i == len(steps) - 1))
                              return inst
                          pe.op([g_, vbuf[g % NSB]], [PB[0], PB[1], PB[2], PB[3]], fnv)

                      for g0 in range(NSB - 1):
                          emit_load(g0)
                      for i0 in range(LOOK):
                          emit_H(i0)
                      for i in range(len(steps)):
                          g, c = steps[i]
                          if c == 0 and g + NSB - 1 < 64:
                              emit_load(g + NSB - 1)
                          if i + LOOK < len(steps):
                              emit_H(i + LOOK)
                          emit_V(i)

                  def finalize(u):
                      xo = xw
                      yo = xw
                      sp.dma([DB("XM", u)], [xw], xw[:], xrows(XM, u))
                      for s in range(2):
                          for blk in range(2):
                              bank = PB[s * 2 + blk]
                              dve.op([bank, xw], [xw], lambda h, s=s, blk=blk, bank=bank: h.tensor_tensor(
                                  out=xw[:, s, blk * 512:(blk + 1) * 512], in0=bank[:, :],
                                  in1=xw[:, s, blk * 512:(blk + 1) * 512], op=ALU.add))
                      if last and do_final:
                          ss, ms, sq, rstd = small
                          for s in range(2):
                              act.op([xo], [xn, ss], lambda h, s=s: h.activation(
                                  out=xn[:, 0, :], in_=xo[:, s, :], func=AF.Square, accum_out=ss[:, s:s + 1]))
                          dve.op([ss], [ms], lambda h: h.tensor_scalar(
                              out=ms[:], in0=ss[:], scalar1=1.0 / D, scalar2=EPS, op0=ALU.mult, op1=ALU.add))
                          act.op([ms], [sq], lambda h: h.activation(out=sq[:], in_=ms[:], func=AF.Sqrt))
                          dve.op([sq], [rstd], lambda h: h.reciprocal(out=rstd[:], in_=sq[:]))
                          for s in range(2):
                              dve.op([xo, rstd, gfin], [yo], lambda h, s=s: h.scalar_tensor_tensor(
                                  out=yo[:, s, :], in0=xo[:, s, :], scalar=rstd[:, s:s + 1], in1=gfin[:],
                                  op0=ALU.mult, op1=ALU.mult))
                          sw.dma([yo], [DB("Y", u)], xrows(x_dst, u), yo[:])
                      else:
                          sw.dma([xo], [DB("X", u)], xrows(x_dst, u), xo[:])

                  route_A(0)
                  if "routeB" not in SKIP:
                      route_B(0)
                  for u in range(NU):
                      if u + 1 < NU:
                          route_A(u + 1)
                      if "dense" not in SKIP:
                          dense(u)
                      finalize(u)
                      if u + 1 < NU and "routeB" not in SKIP:
                          route_B(u + 1)
              S.barrier()
              lay.close()
        except _Stop:
            pass
        S.barrier()
    return nc, S.ninst


_WEIGHT_KEYS = ["norm_mix_g", "w_in", "w_gate", "conv_w", "conv_b", "conv_ln_g", "conv_ln_b", "w_conv_out",
                "log_gamma_fwd", "log_gamma_bwd", "w_ret_out", "w_o", "norm_ffn_g", "peer_wq", "peer_subkeys",
                "peer_u", "peer_v"]


def run_cores(x_per_core, weights, seqs, n_layers=2, do_final=True, stop=0):
    nc, ninst = build(seqs, n_layers=n_layers, do_final=do_final, stop=stop)
    consts, cosT, sinT = host_consts(max(seqs))
    base = {k: np.ascontiguousarray(np.asarray(weights[k], dtype=np.float32)) for k in _WEIGHT_KEYS}
    base["final_norm_g"] = np.ascontiguousarray(np.asarray(weights["final_norm_g"], np.float32).reshape(1, D))
    base["consts"] = consts
    base["cosT"] = cosT
    base["sinT"] = sinT
    in_maps = []
    for xc in x_per_core:
        m = dict(base)
        m["x"] = np.ascontiguousarray(xc, dtype=np.float32)
        in_maps.append(m)
    res = run_bass_kernel_spmd(nc, in_maps, core_ids=list(range(len(x_per_core))))
    return [r["y"] for r in res.results]


def kernel(x_prompt, x_sample, **weights):
    x_prompt = np.asarray(x_prompt, dtype=np.float32)
    x_sample = np.asarray(x_sample, dtype=np.float32)
    xs = []
    for c in range(N_CORES):
        xs.append(np.concatenate([x_prompt[c], x_sample[2 * c], x_sample[2 * c + 1]], axis=0))
    ys = run_cores(xs, weights, SEQS_FULL)
    y_prompt = np.stack([y[0:8192] for y in ys], axis=0)
    y_sample = np.stack([ys[c // 2][8192 + 2048 * (c % 2):8192 + 2048 * (c % 2 + 1)] for c in range(16)], axis=0)
    return (y_prompt.astype(np.float32), y_sample.astype(np.float32))
```

```python
from contextlib import ExitStack
import numpy as np
import concourse.bass as bass
import concourse.mybir as mybir
from concourse.bass_utils import run_bass_kernel_spmd

F32 = mybir.dt.float32
BF16 = mybir.dt.bfloat16
U32 = mybir.dt.uint32
AF = mybir.ActivationFunctionType
ALU = mybir.AluOpType
AX = mybir.AxisListType

D = 1024
EPS = 1e-6
NEG = -1.0e30
N_CORES = 8
SEQS_FULL = (8192, 2048, 2048)


class Buf:
    __slots__ = ("name", "lw", "rd")

    def __init__(self, name):
        self.name = name
        self.lw = None
        self.rd = {}


class TT:
    def __init__(self, t, name):
        self.t = t
        self.b = Buf(name)

    def __getitem__(self, k):
        return self.t[k]


def _b(x):
    return x.b if isinstance(x, TT) else x


class Eng:
    def __init__(self, S, name, h, nslots=0, own_skip=False):
        self.S = S
        self.name = name
        self.h = h
        self.waited = {}
        self.snap = {}
        self.dirty = False
        self.own_skip = own_skip
        self.nslots = nslots
        if nslots == 0:
            self.sem = S.new_sem(name)
            self.count = 0
        else:
            self.slots = [[S.new_sem(f"{name}{i}"), 0] for i in range(nslots)]
            self.rr = 0

    def _wait(self, tick):
        sem, val, key = tick[:3]
        if self.own_skip and self.nslots == 0 and key == self.sem_key():
            return
        if self.waited.get(key, 0) >= val:
            return
        self.h.wait_ge(sem, val)
        self.waited[key] = val
        self.dirty = True
        if len(tick) > 3:
            w = self.waited
            for k, v in tick[3].items():
                if w.get(k, 0) < v:
                    w[k] = v

    def _snapshot(self):
        if self.dirty:
            self.snap = dict(self.waited)
            self.dirty = False
        return self.snap

    def sem_key(self):
        return self.name

    def _deps(self, reads, writes):
        own = self.name if self.nslots == 0 else None
        for b in reads:
            b = _b(b)
            if b.lw is not None:
                self._wait(b.lw)
        for b in writes:
            b = _b(b)
            if b.lw is not None and b.lw[2] != own:
                self._wait(b.lw)
            for t in b.rd.values():
                if t[2] != own:
                    self._wait(t)

    def _mark(self, reads, writes, tick):
        for b in reads:
            _b(b).rd[tick[2]] = tick
        for b in writes:
            b = _b(b)
            b.lw = tick
            b.rd = {}

    def op(self, reads, writes, fn):
        if DEAD[0]:
            return
        self._deps(reads, writes)
        inst = fn(self.h)
        self.count += 1
        inst.then_inc(self.sem, 1)
        tick = (self.sem, self.count, self.name, self._snapshot())
        self._mark(reads, writes, tick)
        self.S.ninst += 1

    def dma(self, reads, writes, out, in_, **kw):
        if DEAD[0]:
            return
        slot = self.slots[self.rr]
        key = f"{self.name}{self.rr}"
        self.rr = (self.rr + 1) % self.nslots
        if slot[1] > 0:
            self._wait((slot[0], slot[1], key))
        self._deps(reads, writes)
        inst = self.h.dma_start(out=out, in_=in_, **kw)
        slot[1] += 16
        inst.then_inc(slot[0], 16)
        tick = (slot[0], slot[1], key, self._snapshot())
        self._mark(reads, writes, tick)
        self.S.ninst += 1

    def last_ticks(self):
        if self.nslots == 0:
            return [(self.sem, self.count, self.name)] if self.count else []
        return [(s[0], s[1], f"{self.name}{i}") for i, s in enumerate(self.slots) if s[1]]


class Sched:
    def __init__(self, nc, stack):
        self.nc = nc
        self.stack = stack
        self.ninst = 0
        self.pe = Eng(self, "pe", nc.tensor, own_skip=True)
        self.act = Eng(self, "act", nc.scalar)
        self.dve = Eng(self, "dve", nc.vector)
        self.pool = Eng(self, "pool", nc.gpsimd)
        self.sp = Eng(self, "sp", nc.sync, nslots=8)
        self.sw = Eng(self, "sw", nc.gpsimd, nslots=6)
        self.all = [self.pe, self.act, self.dve, self.pool, self.sp, self.sw]

    def new_sem(self, name):
        return self.stack.enter_context(self.nc.semaphore("s_" + name))

    def barrier(self):
        ticks = []
        for e in self.all:
            ticks += e.last_ticks()
        for e in (self.pe, self.act, self.dve, self.pool, self.sp):
            for t in ticks:
                if e.nslots == 0 and t[2] == e.name:
                    if e.own_skip:
                        continue
                e._wait(t)


def host_consts(lmax):
    c = np.zeros((128, 1024), np.float32)
    p = np.arange(128, dtype=np.float32)
    c[:, 0:128] = np.eye(128, dtype=np.float32)
    c[:, 128:256] = p[None, :]
    diff = p[None, :] - p[:, None]
    c[:, 256:384] = np.maximum(diff, 0.0)
    c[:, 384:512] = np.maximum(-diff, 0.0)
    c[:, 512:640] = (diff >= 0).astype(np.float32) / 16.0
    c[:, 640:768] = (diff < 0).astype(np.float32) / 16.0
    c[:, 768] = p + 1.0
    c[:, 769] = 128.0 - p
    c[:, 770] = 127.0 - p
    c[:, 771] = p
    c[:, 772] = 128.0
    c[:, 776:792] = np.arange(16, dtype=np.float32)[None, :]
    c[:, 800:928] = 1.0 / 1024.0
    half = 128
    freqs = (1.0 / (10000.0 ** (np.arange(half, dtype=np.float32) / np.float32(half)))).astype(np.float32)
    ang = np.arange(lmax, dtype=np.float32)[:, None] * freqs[None, :]
    cosT = np.ascontiguousarray(np.cos(ang).astype(np.float32).T)
    sinT = np.ascontiguousarray(np.sin(ang).astype(np.float32).T)
    return c, cosT, sinT


class _Stop(Exception):
    pass


SUBSTOP = [0]
SKIP = set()
CFG = {"NSB": 3, "LOOK": 2}


DEAD = [False]


def sub(n):
    if SUBSTOP[0] == n:
        DEAD[0] = True


def build(seqs, n_layers=2, do_final=True, stop=0):
    T_tok = sum(seqs)
    NU = T_tok // 256
    NCH = T_tok // 128
    lmax = max(seqs)
    nc = bass.Bass("TRN2", target_bir_lowering=False)
    DEAD[0] = False

    def din(name, shape, dt=F32):
        return nc.dram_tensor(name, list(shape), dt, kind="ExternalInput").ap()

    def dscr(name, shape, dt):
        return nc.dram_tensor(name, list(shape), dt, kind="Internal").ap()

    x_in = din("x", [T_tok, D])
    consts_d = din("consts", [128, 1024])
    cos_d = din("cosT", [128, lmax])
    sin_d = din("sinT", [128, lmax])
    norm_mix_g = din("norm_mix_g", [2, D])
    w_in = din("w_in", [2, D, 6144])
    w_gate = din("w_gate", [2, D, 2048])
    conv_w = din("conv_w", [2, 31, D])
    conv_b = din("conv_b", [2, D])
    conv_ln_g = din("conv_ln_g", [2, D])
    conv_ln_b = din("conv_ln_b", [2, D])
    w_conv_out = din("w_conv_out", [2, D, D])
    lg_f = din("log_gamma_fwd", [2, 4])
    lg_b = din("log_gamma_bwd", [2, 4])
    w_ret_out = din("w_ret_out", [2, D, D])
    w_o = din("w_o", [2, D, D])
    norm_ffn_g = din("norm_ffn_g", [2, D])
    peer_wq = din("peer_wq", [2, D, 2048])
    peer_sk = din("peer_subkeys", [2, 8, 2, 128, 128])
    peer_u = din("peer_u", [2, 16384, D])
    peer_v = din("peer_v", [2, 16384, D])
    final_g = din("final_norm_g", [1, D])
    y_out = nc.dram_tensor("y", [T_tok, D], F32, kind="ExternalOutput").ap()

    X1 = dscr("X1", [T_tok, D], F32)
    XM = dscr("XM", [T_tok, D], F32)
    UT = dscr("UT", [NU, 128, 8 * 256], BF16)
    QT = dscr("QT", [NU, 128, 8 * 256], BF16)
    KT = dscr("KT", [NU, 128, 8 * 256], BF16)
    KK = dscr("KK", [NCH, 128, D], BF16)
    VV = dscr("VV", [NCH, 128, D], BF16)
    SG = dscr("SG", [NCH, 128, D], BF16)
    GT = dscr("GT", [NCH, 128, 2048], BF16)
    YC = dscr("YC", [NCH, 128, D], BF16)
    SB = dscr("SB", [NCH, 128, 8 * 256], BF16)
    UTS = dscr("UTS", [2, 64, 128, 8 * 256], BF16)
    VS = dscr("VS", [2, 64, 128, 2 * 1024], BF16)

    units = []
    u0 = 0
    seq_ranges = []
    for L in seqs:
        nu = L // 256
        seq_ranges.append((u0, u0 + nu))
        for i in range(nu):
            units.append((u0, u0 + nu, i * 256))
        u0 += nu

    dbufs = {}

    def DB(name, idx):
        k = (name, idx)
        if k not in dbufs:
            dbufs[k] = Buf(f"{name}{idx}")
        return dbufs[k]

    with ExitStack() as top:
        S = Sched(nc, top)
        pe, act, dve, pool, sp, sw = S.pe, S.act, S.dve, S.pool, S.sp, S.sw

        uniq = [0]

        def sb(stack, name, shape, dt):
            uniq[0] += 1
            name = f"{name}_{uniq[0]}"
            return TT(stack.enter_context(nc.sbuf_tensor(name, list(shape), dt)), name)

        PB = [TT(top.enter_context(nc.psum_tensor(f"pb{i}", [128, 512], F32)), f"pb{i}") for i in range(8)]

        def pbf(bank):
            return bank.t[:].bitcast(BF16)

        cst = sb(top, "cst", [128, 1024], F32)
        sp.dma([], [cst], cst[:], consts_d[:, :])
        ident_f = cst[:, 0:128]
        identb = sb(top, "identb", [128, 128], BF16)
        iotab = sb(top, "iotab", [128, 128], BF16)
        dve.op([cst], [identb], lambda h: h.tensor_copy(out=identb[:], in_=cst[:, 0:128]))
        dve.op([cst], [iotab], lambda h: h.tensor_copy(out=iotab[:], in_=cst[:, 128:256]))
        iota16 = cst[:, 776:792]

        def mm_group(out_bank, reads, mms, extra_writes=()):
            n = len(mms)

            def fn(h):
                inst = None
                for i, (o, l, r) in enumerate(mms):
                    inst = h.matmul(o, lhsT=l, rhs=r, start=(i == 0), stop=(i == n - 1))
                return inst
            pe.op(reads, [out_bank] + list(extra_writes), fn)

        def transposes(out_banks, reads, trs):
            def fn(h):
                inst = None
                for (o, i_, idn) in trs:
                    inst = h.transpose(o, i_, idn)
                return inst
            pe.op(reads, list(out_banks), fn)

        def norm_T(xin, gtile, xn, xnT, junk, small, bankA, bankB, evac_engs):
            ss, ms, sq, rstd = small
            if isinstance(junk, tuple):
                jap, jb = junk
            else:
                jap, jb = junk[:], junk
            for s in range(2):
                act.op([xin], [jb, ss], lambda h, s=s: h.activation(
                    out=jap, in_=xin[:, s, :], func=AF.Square, accum_out=ss[:, s:s + 1]))
            dve.op([ss], [ms], lambda h: h.tensor_scalar(
                out=ms[:], in0=ss[:], scalar1=1.0 / D, scalar2=EPS, op0=ALU.mult, op1=ALU.add))
            act.op([ms], [sq], lambda h: h.activation(out=sq[:], in_=ms[:], func=AF.Sqrt))
            dve.op([sq], [rstd], lambda h: h.reciprocal(out=rstd[:], in_=sq[:]))
            for s in range(2):
                dve.op([xin, rstd, gtile], [xn], lambda h, s=s: h.scalar_tensor_tensor(
                    out=xn[:, s, :], in0=xin[:, s, :], scalar=rstd[:, s:s + 1], in1=gtile[:],
                    op0=ALU.mult, op1=ALU.mult))
            trs = []
            for kc in range(8):
                bank = bankA if kc < 4 else bankB
                v = pbf(bank)
                for s in range(2):
                    o = v[:, (kc % 4) * 256 + s * 128:(kc % 4) * 256 + (s + 1) * 128]
                    trs.append((o, xn[:, s, kc * 128:(kc + 1) * 128], identb[:]))
            transposes([bankA, bankB], [xn, identb], trs)
            e0, e1 = evac_engs
            e0.op([bankA], [xnT], lambda h: _copy(h, xnT[:, 0:4, :], pbf(bankA).rearrange("p (a t) -> p a t", a=4)))
            e1.op([bankB], [xnT], lambda h: _copy(h, xnT[:, 4:8, :], pbf(bankB).rearrange("p (a t) -> p a t", a=4)))

        def _copy(h, out, in_):
            if h is nc.scalar:
                return h.activation(out=out, in_=in_, func=AF.Copy)
            return h.tensor_copy(out=out, in_=in_)

        def load_weight_bf16(dst, src3, nk, ncols, col0=0):
            for kc in range(nk):
                for c0 in range(0, ncols, 2048):
                    c1 = min(ncols, c0 + 2048)
                    sw.dma([], [dst], dst[:, kc, c0:c1], src3[kc * 128:(kc + 1) * 128, col0 + c0:col0 + c1])

        def bcast_row(dst, row_ap):
            sp.dma([], [dst], dst[:], row_ap.partition_broadcast(128))

        phase_ctr = [0]

        def chk():
            phase_ctr[0] += 1
            if stop and phase_ctr[0] > stop:
                DEAD[0] = True

        p0_list = [(l_, g_) for l_ in range(n_layers) for g_ in range(32)]
        p0_pos = [0]

        def p0_emit(bufs, n, banks):
            uf, uts, vb = bufs
            for _ in range(n):
                if p0_pos[0] >= len(p0_list):
                    return
                l_, g = p0_list[p0_pos[0]]
                p0_pos[0] += 1
                rows = peer_u[l_, g * 512:(g + 1) * 512, :].rearrange("(c p) d -> p c d", p=128)
                sp.dma([], [uf], uf[:], rows)
                for kc in range(8):
                    bank = banks[kc % len(banks)]
                    trs = [(bank[:, c * 128:(c + 1) * 128], uf[:, c, kc * 128:(kc + 1) * 128], ident_f)
                           for c in range(4)]
                    transposes([bank], [uf, cst], trs)
                    e = act if kc % 2 == 0 else dve
                    e.op([bank], [uts], lambda h, kc=kc, bank=bank: _copy(h, uts[:, kc, :], bank[:, :]))
                for hf in range(2):
                    sp.dma([uts], [DB("UTS", (l_, 2 * g + hf))], UTS[l_, 2 * g + hf].rearrange("p (a e) -> p a e", a=8),
                           uts[:, :, hf * 256:(hf + 1) * 256])
                vrows = peer_v[l_, g * 512:(g + 1) * 512, :].rearrange("(c p) d -> p c d", p=128)
                sw.dma([], [vb], vb[:], vrows)
                for hf in range(2):
                    sp.dma([vb], [DB("VS", (l_, 2 * g + hf))], VS[l_, 2 * g + hf].rearrange("p (a e) -> p a e", a=2),
                           vb[:, 2 * hf:2 * hf + 2, :])

        try:
          for l in range(n_layers):
              chk()
              x_src = x_in if l == 0 else X1
              last = (l == n_layers - 1)
              x_dst = y_out if last else X1

              def xrows(src, u):
                  return src[u * 256:(u + 1) * 256, :].rearrange("(s p) d -> p s d", p=128)

              lay = top.enter_context(ExitStack())
              lgfb = sb(lay, "lgfb", [128, 4], F32)
              lgbb = sb(lay, "lgbb", [128, 4], F32)
              bcast_row(lgfb, lg_f[l:l + 1, :])
              bcast_row(lgbb, lg_b[l:l + 1, :])
              dsc = sb(lay, "dsc", [128, 6, 4], F32)
              specs = [(lgfb, 768, 0.0), (lgbb, 769, 0.0), (lgfb, 770, -np.log(16.0)), (lgbb, 771, -np.log(16.0)),
                       (lgfb, 772, 0.0), (lgbb, 772, 0.0)]
              for i, (lgt, col, bias) in enumerate(specs):
                  dve.op([cst, lgt], [dsc], lambda h, i=i, lgt=lgt, col=col: h.tensor_scalar(
                      out=dsc[:, i, :], in0=lgt[:], scalar1=cst[:, col:col + 1], scalar2=None, op0=ALU.mult))
              dve.op([dsc], [dsc], lambda h: h.tensor_scalar(
                  out=dsc[:, 2:4, :], in0=dsc[:, 2:4, :], scalar1=float(-np.log(16.0)), scalar2=None, op0=ALU.add))
              act.op([dsc], [dsc], lambda h: h.activation(out=dsc[:], in_=dsc[:], func=AF.Exp))
              A_F, A_B, VD_F, VD_B, CD_F, CD_B = range(6)
              if SUBSTOP[0] == 50:
                  sp.dma([dsc], [DB("Y", -1)], y_out[0:128, 0:24], dsc[:].rearrange("p a b -> p (a b)"))
                  sp.dma([decT], [DB("Y", -2)], y_out[0:128, 32:32 + 512], decT[:].rearrange("p a b -> p (a b)"))
                  sp.dma([lgfb], [DB("Y", -3)], y_out[0:128, 600:604], lgfb[:])
                  sp.dma([lgbb], [DB("Y", -4)], y_out[0:128, 608:612], lgbb[:])
                  sub(50)

              chk()
              with ExitStack() as ph:
                  wsb = sb(ph, "w1a", [128, 8, 4096], BF16)
                  load_weight_bf16(wsb, w_in[l], 8, 4096, 0)
                  gtile = sb(ph, "g1a", [128, D], F32)
                  bcast_row(gtile, norm_mix_g[l:l + 1, :])
                  xin = [sb(ph, f"xin{i}", [128, 2, D], F32) for i in range(2)]
                  cs = [sb(ph, f"cs{i}", [128, 2, 256], F32) for i in range(2)]
                  xn = sb(ph, "xn", [128, 2, D], BF16)
                  xnT = sb(ph, "xnT", [128, 8, 256], BF16)
                  junk = sb(ph, "junk", [128, D], BF16)
                  small = [sb(ph, f"sm{i}", [128, 2], F32) for i in range(4)]
                  uT = [sb(ph, f"uT{i}", [128, 8, 256], BF16) for i in range(2)]
                  qT = [sb(ph, f"qT{i}", [128, 8, 256], BF16) for i in range(2)]
                  kT = [sb(ph, f"kT{i}", [128, 8, 256], BF16) for i in range(2)]
                  ktok = [sb(ph, f"ktok{i}", [128, 2, D], BF16) for i in range(2)]
                  sig = [sb(ph, f"sig{i}", [128, 256], F32) for i in range(2)]
                  rt = [sb(ph, f"rt{i}", [128, 4, 256], F32) for i in range(4)]
                  kraw = [sb(ph, f"kraw{i}", [128, 2, 256], F32) for i in range(2)]

                  def p1a_load_x(u):
                      a = u % 2
                      sp.dma([DB("X", u)], [xin[a]], xin[a][:], xrows(x_src, u))

                  def p1a_load_cs(u):
                      a = u % 2
                      pos = units[u][2]
                      sp.dma([], [cs[a]], cs[a][:, 0, :], cos_d[:, pos:pos + 256])
                      sp.dma([], [cs[a]], cs[a][:, 1, :], sin_d[:, pos:pos + 256])

                  xn2_ = [xn, sb(ph, "xnB", [128, 2, D], BF16)]
                  xnT2_ = [xnT, sb(ph, "xnTB", [128, 8, 256], BF16)]
                  p0bufs = None
                  if l == 0:
                      p0bufs = (sb(ph, "p0uf", [128, 4, D], F32), sb(ph, "p0uts", [128, 8, 512], BF16),
                                sb(ph, "p0vb", [128, 4, D], BF16))
                  p1a_load_x(0)
                  p1a_load_cs(0)
                  if NU > 1:
                      p1a_load_x(1)
                  norm_T(xin[0], gtile, xn2_[0], xnT2_[0], junk, small, PB[6], PB[7], (act, dve))
                  pair_i = 0
                  for u in range(NU):
                      a = u % 2
                      xnT = xnT2_[a]
                      if u + 1 < NU:
                          p1a_load_cs(u + 1)
                      if u + 2 < NU:
                          p1a_load_x(u + 2)
                      if u + 1 < NU:
                          norm_T(xin[1 - a], gtile, xn2_[1 - a], xnT2_[1 - a], junk, small, PB[6], PB[7], (act, dve))
                      cosA = cs[a][:, 0, :]
                      sinA = cs[a][:, 1, :]
                      for pi in range(16):
                          bank = PB[pair_i % 6]
                          pair_i += 1
                          if pi < 8:
                              c1, c2 = pi, 8 + pi
                          elif pi < 12:
                              c1, c2 = 16 + 2 * (pi - 8), 17 + 2 * (pi - 8)
                          else:
                              c1, c2 = 24 + 2 * (pi - 12), 25 + 2 * (pi - 12)
                          for half, cc in enumerate((c1, c2)):
                              mms = [(bank[:, half * 256:(half + 1) * 256], wsb[:, kc, cc * 128:(cc + 1) * 128],
                                      xnT[:, kc, :]) for kc in range(8)]
                              mm_group(bank, [wsb, xnT], mms)
                          p1 = bank[:, 0:256]
                          p2 = bank[:, 256:512]
                          if pi < 8:
                              sg_ = sig[pi % 2]
                              act.op([bank], [sg_], lambda h, sg_=sg_, p2=p2: h.activation(
                                  out=sg_[:], in_=p2, func=AF.Sigmoid))
                              dve.op([bank, sg_], [uT[a]], lambda h, sg_=sg_, p1=p1, pi=pi: h.tensor_tensor(
                                  out=uT[a][:, pi, :], in0=p1, in1=sg_[:], op=ALU.mult))
                          elif pi < 12:
                              hd = pi - 8
                              r = rt[pi % 4]
                              dst = qT[a]
                              dve.op([bank, cs[a]], [r], lambda h, r=r: h.tensor_tensor(out=r[:, 0, :], in0=p1, in1=cosA, op=ALU.mult))
                              dve.op([bank, cs[a]], [r], lambda h, r=r: h.tensor_tensor(out=r[:, 1, :], in0=p2, in1=sinA, op=ALU.mult))
                              dve.op([bank, cs[a]], [r], lambda h, r=r: h.tensor_tensor(out=r[:, 2, :], in0=p1, in1=sinA, op=ALU.mult))
                              dve.op([bank, cs[a]], [r], lambda h, r=r: h.tensor_tensor(out=r[:, 3, :], in0=p2, in1=cosA, op=ALU.mult))
                              pool.op([r], [dst], lambda h, r=r, dst=dst, hd=hd: h.tensor_tensor(
                                  out=dst[:, 2 * hd, :], in0=r[:, 0, :], in1=r[:, 1, :], op=ALU.subtract))
                              pool.op([r], [dst], lambda h, r=r, dst=dst, hd=hd: h.tensor_tensor(
                                  out=dst[:, 2 * hd + 1, :], in0=r[:, 2, :], in1=r[:, 3, :], op=ALU.add))
                          else:
                              hd = pi - 12
                              r = rt[pi % 4]
                              dst = kT[a]
                              dve.op([bank, cs[a]], [r], lambda h, r=r: h.tensor_tensor(out=r[:, 0, :], in0=p1, in1=cosA, op=ALU.mult))
                              dve.op([bank, cs[a]], [r], lambda h, r=r: h.tensor_tensor(out=r[:, 1, :], in0=p2, in1=sinA, op=ALU.mult))
                              dve.op([bank, cs[a]], [r], lambda h, r=r: h.tensor_tensor(out=r[:, 2, :], in0=p1, in1=sinA, op=ALU.mult))
                              dve.op([bank, cs[a]], [r], lambda h, r=r: h.tensor_tensor(out=r[:, 3, :], in0=p2, in1=cosA, op=ALU.mult))
                              pool.op([r], [dst], lambda h, r=r, dst=dst, hd=hd: h.tensor_tensor(
                                  out=dst[:, 2 * hd, :], in0=r[:, 0, :], in1=r[:, 1, :], op=ALU.subtract))
                              pool.op([r], [dst], lambda h, r=r, dst=dst, hd=hd: h.tensor_tensor(
                                  out=dst[:, 2 * hd + 1, :], in0=r[:, 2, :], in1=r[:, 3, :], op=ALU.add))
                      sub(4)
                      for s in range(2):
                          bank = PB[6 + s]
                          v = pbf(bank)
                          trs = [(v[:, cc * 128:(cc + 1) * 128], kT[a][:, cc, s * 128:(s + 1) * 128], identb[:])
                                 for cc in range(8)]
                          transposes([bank], [kT[a], identb], trs)
                          e = act if s == 0 else dve
                          e.op([bank], [ktok[a]], lambda h, s=s, bank=bank: _copy(h, ktok[a][:, s, :], pbf(bank)))
                      sub(5)
                      flat = lambda t: t[:].rearrange("p a t -> p (a t)")
                      sw.dma([uT[a]], [DB("UT", u)], UT[u], flat(uT[a]))
                      sw.dma([qT[a]], [DB("QT", u)], QT[u], flat(qT[a]))
                      sw.dma([kT[a]], [DB("KT", u)], KT[u], flat(kT[a]))
                      sub(6)
                      for s in range(2):
                          sw.dma([ktok[a]], [DB("KK", 2 * u + s)], KK[2 * u + s], ktok[a][:, s, :])
                      if p0bufs is not None:
                          p0_emit(p0bufs, 2, [PB[6], PB[7]])
                  if p0bufs is not None:
                      while p0_pos[0] < len(p0_list):
                          p0_emit(p0bufs, 1, PB)
              S.barrier()

              chk()
              with ExitStack() as ph:
                  wv = sb(ph, "w1b", [128, 8, 2048], BF16)
                  load_weight_bf16(wv, w_in[l], 8, 2048, 4096)
                  wg = sb(ph, "wg", [128, 8, 2048], BF16)
                  load_weight_bf16(wg, w_gate[l], 8, 2048, 0)
                  gtile = sb(ph, "g1b", [128, D], F32)
                  bcast_row(gtile, norm_mix_g[l:l + 1, :])
                  xin = [sb(ph, f"xin{i}", [128, 2, D], F32) for i in range(2)]
                  kk = [sb(ph, f"kk{i}", [128, 2, D], BF16) for i in range(2)]
                  xn = sb(ph, "xn", [128, 2, D], BF16)
                  xnT = sb(ph, "xnT", [128, 8, 256], BF16)
                  junk = sb(ph, "junk", [128, D], BF16)
                  small = [sb(ph, f"sm{i}", [128, 2], F32) for i in range(4)]
                  vsb = [sb(ph, f"vsb{i}", [128, 2, D], BF16) for i in range(2)]
                  vdb = [sb(ph, f"vdb{i}", [128, 2, D], BF16) for i in range(2)]
                  sgb = [sb(ph, f"sgb{i}", [128, 2, D], BF16) for i in range(2)]
                  gtb = [sb(ph, f"gtb{i}", [128, 2, 2048], BF16) for i in range(2)]
                  Sst = sb(ph, "Sst", [128, 8, 256], F32)
                  Sbf = [sb(ph, f"Sbf{i}", [128, 8, 256], BF16) for i in range(2)]
                  order = []
                  for (a0, a1) in seq_ranges:
                      order += list(range(a1 - 1, a0 - 1, -1))

                  def p1b_load_x(i):
                      u = order[i]
                      a = i % 2
                      sp.dma([DB("X", u)], [xin[a]], xin[a][:], xrows(x_src, u))

                  def p1b_load_kk(i):
                      u = order[i]
                      a = i % 2
                      for s in range(2):
                          sp.dma([DB("KK", 2 * u + s)], [kk[a]], kk[a][:, s, :], KK[2 * u + s])

                  xn2_ = [xn, sb(ph, "xnB", [128, 2, D], BF16)]
                  xnT2_ = [xnT, sb(ph, "xnTB", [128, 8, 256], BF16)]
                  p1b_load_x(0)
                  p1b_load_kk(0)
                  if NU > 1:
                      p1b_load_x(1)
                  norm_T(xin[0], gtile, xn2_[0], xnT2_[0], junk, small, PB[6], PB[7], (act, dve))
                  bi = 0
                  sbi = 0
                  for i, u in enumerate(order):
                      a = i % 2
                      xnT = xnT2_[a]
                      if i + 1 < NU:
                          p1b_load_kk(i + 1)
                      if i + 2 < NU:
                          p1b_load_x(i + 2)
                      if i + 1 < NU:
                          norm_T(xin[1 - a], gtile, xn2_[1 - a], xnT2_[1 - a], junk, small, PB[6], PB[7], (act, dve))
                      for s in range(2):
                          for blk in range(4):
                              bank = PB[bi % 4]
                              bi += 1
                              mms = [(bank[:, :], xnT[:, kc, s * 128:(s + 1) * 128], wv[:, kc, blk * 512:(blk + 1) * 512])
                                     for kc in range(8)]
                              mm_group(bank, [xnT, wv], mms)
                              sub(40)
                              if blk < 2:
                                  act.op([bank], [vsb[a]], lambda h, s=s, blk=blk, bank=bank: h.activation(
                                      out=vsb[a][:, s, blk * 512:(blk + 1) * 512], in_=bank[:, :], func=AF.Copy))
                                  sub(41)
                                  for hh in range(2):
                                      hd = blk * 2 + hh
                                      act.op([bank, dsc], [vdb[a]], lambda h, s=s, hd=hd, hh=hh, bank=bank: h.activation(
                                          out=vdb[a][:, s, hd * 256:(hd + 1) * 256], in_=bank[:, hh * 256:(hh + 1) * 256],
                                          func=AF.Copy, scale=dsc[:, VD_B, hd:hd + 1]))
                                      sub(42)
                              else:
                                  act.op([bank], [sgb[a]], lambda h, s=s, blk=blk, bank=bank: h.activation(
                                      out=sgb[a][:, s, (blk - 2) * 512:(blk - 1) * 512], in_=bank[:, :], func=AF.Silu))
                          for blk in range(4):
                              bank = PB[bi % 4]
                              bi += 1
                              mms = [(bank[:, :], xnT[:, kc, s * 128:(s + 1) * 128], wg[:, kc, blk * 512:(blk + 1) * 512])
                                     for kc in range(8)]
                              mm_group(bank, [xnT, wg], mms)
                              act.op([bank], [gtb[a]], lambda h, s=s, blk=blk, bank=bank: h.activation(
                                  out=gtb[a][:, s, blk * 512:(blk + 1) * 512], in_=bank[:, :], func=AF.Sigmoid))
                      sub(32)
                      for s in range(2):
                          c = 2 * u + s
                          sw.dma([vsb[a]], [DB("VV", c)], VV[c], vsb[a][:, s, :])
                          sw.dma([sgb[a]], [DB("SG", c)], SG[c], sgb[a][:, s, :])
                          sw.dma([gtb[a]], [DB("GT", c)], GT[c], gtb[a][:, s, :])
                      sub(33)
                      if u == units[u][1] - 1:
                          pool.op([], [Sst], lambda h: h.memset(Sst[:], 0.0))
                          pool.op([], [Sbf[sbi % 2]], lambda h, t=Sbf[sbi % 2]: h.memset(t[:], 0.0))
                      for s in (1, 0):
                          c = 2 * u + s
                          cur = Sbf[sbi % 2]
                          nxt = Sbf[(sbi + 1) % 2]
                          sbi += 1
                          sw.dma([cur], [DB("SB", c)], SB[c], cur[:].rearrange("p a e -> p (a e)"))
                          for hp in range(2):
                              for hh in range(2):
                                  hd = hp * 2 + hh
                                  bank = PB[4 + hh]
                                  for dc in range(2):
                                      mm_group(bank, [kk[a], vdb[a]], [(bank[:, dc * 256:(dc + 1) * 256],
                                               kk[a][:, s, (2 * hd + dc) * 128:(2 * hd + dc + 1) * 128],
                                               vdb[a][:, s, hd * 256:(hd + 1) * 256])])
                                  dve.op([bank, dsc, Sst], [Sst], lambda h, hd=hd, bank=bank: h.scalar_tensor_tensor(
                                      out=Sst[:, 2 * hd:2 * hd + 2, :], in0=Sst[:, 2 * hd:2 * hd + 2, :],
                                      scalar=dsc[:, CD_B, hd:hd + 1],
                                      in1=bank[:, :].rearrange("p (a e) -> p a e", a=2), op0=ALU.mult, op1=ALU.add))
                          act.op([Sst], [nxt], lambda h, nxt=nxt: h.activation(out=nxt[:], in_=Sst[:], func=AF.Copy))
                          sub(34)
              S.barrier()

              chk()
              with ExitStack() as ph:
                  wco = sb(ph, "wco", [128, 8, D], BF16)
                  load_weight_bf16(wco, w_conv_out[l], 8, D, 0)
                  praw = sb(ph, "praw", [34, D], F32)
                  sp.dma([], [praw], praw[0:31, :], conv_w[l])
                  sp.dma([], [praw], praw[31:32, :], conv_b[l:l + 1, :])
                  sp.dma([], [praw], praw[32:33, :], conv_ln_g[l:l + 1, :])
                  sp.dma([], [praw], praw[33:34, :], conv_ln_b[l:l + 1, :])
                  cpar = sb(ph, "cpar", [128, 8, 34], F32)
                  for cc in range(8):
                      bank = PB[cc]
                      transposes([bank], [praw, cst], [(bank[:, 0:34], praw[:, cc * 128:(cc + 1) * 128], cst[0:34, 0:34])])
                      dve.op([bank], [cpar], lambda h, cc=cc, bank=bank: h.tensor_copy(out=cpar[:, cc, :], in_=bank[:, 0:34]))
                  diag = sb(ph, "diag", [128, 8, 31, 128], BF16)
                  for cc in range(8):
                      for k in range(31):
                          e = dve
                          e.op([cst, cpar], [diag], lambda h, cc=cc, k=k: h.tensor_scalar(
                              out=diag[:, cc, k, :], in0=cst[:, 0:128], scalar1=cpar[:, cc, k:k + 1], scalar2=None,
                              op0=ALU.mult))
                  uh = [sb(ph, f"uh{i}", [128, 8, 286], BF16) for i in range(2)]
                  uhl = [Buf(f"uhl{i}") for i in range(2)]
                  uhr = [Buf(f"uhr{i}") for i in range(2)]
                  cv_ = [sb(ph, f"cv{i}", [128, 8, 256], F32) for i in range(2)]
                  sq2_ = [sb(ph, f"sq2{i}", [128, 8, 256], F32) for i in range(2)]
                  mean_ = [sb(ph, f"mean{i}", [128, 256], F32) for i in range(2)]
                  m2_ = [sb(ph, f"m2{i}", [128, 256], F32) for i in range(2)]
                  rstd_ = [sb(ph, f"rstdc{i}", [128, 256], F32) for i in range(2)]
                  uln = [sb(ph, f"uln{i}", [128, 8, 256], BF16) for i in range(2)]
                  ycb = [sb(ph, f"ycb{i}", [128, 2, D], BF16) for i in range(2)]
                  onesm = cst[:, 800:928]

                  def p2c_load(u):
                      a = u % 2
                      u_lo, u_hi, _ = units[u]
                      UTu = lambda uu: UT[uu].rearrange("p (a t) -> p a t", a=8)
                      sp.dma([DB("UT", u)], [uh[a]], uh[a][:, :, 15:271], UTu(u))
                      if u > u_lo:
                          sp.dma([DB("UT", u - 1)], [uhl[a]], uh[a][:, :, 0:15], UTu(u - 1)[:, :, 241:256])
                      else:
                          pool.op([], [uhl[a]], lambda h: h.memset(uh[a][:, :, 0:15], 0.0))
                      if u + 1 < u_hi:
                          sp.dma([DB("UT", u + 1)], [uhr[a]], uh[a][:, :, 271:286], UTu(u + 1)[:, :, 0:15])
                      else:
                          pool.op([], [uhr[a]], lambda h: h.memset(uh[a][:, :, 271:286], 0.0))

                  p2c_load(0)
                  for u in range(NU):
                      a = u % 2
                      if u + 1 < NU:
                          p2c_load(u + 1)
                      cv, sq2, mean, m2, rstd = cv_[a], sq2_[a], mean_[a], m2_[a], rstd_[a]
                      for cc in range(8):
                          bank = PB[cc // 2]
                          half = cc % 2
                          mms = [(bank[:, half * 256:(half + 1) * 256], diag[:, cc, k, :], uh[a][:, cc, k:k + 256])
                                 for k in range(31)]
                          mm_group(bank, [diag, uh[a], uhl[a], uhr[a]], mms)
                          act.op([bank, cpar], [cv], lambda h, cc=cc, bank=bank, half=half: h.activation(
                              out=cv[:, cc, :], in_=bank[:, half * 256:(half + 1) * 256], func=AF.Identity,
                              bias=cpar[:, cc, 31:32]))
                          act.op([cv], [sq2], lambda h, cc=cc: h.activation(out=sq2[:, cc, :], in_=cv[:, cc, :], func=AF.Square))
                      bm = PB[4 + a]
                      mm_group(bm, [cst, cv], [(bm[:, 0:256], onesm, cv[:, cc, :]) for cc in range(8)])
                      mm_group(bm, [cst, sq2], [(bm[:, 256:512], onesm, sq2[:, cc, :]) for cc in range(8)])
                      act.op([bm], [mean], lambda h: h.activation(out=mean[:], in_=bm[:, 0:256], func=AF.Copy))
                      act.op([bm], [m2], lambda h: h.activation(out=m2[:], in_=bm[:, 0:256], func=AF.Square))
                      dve.op([bm, m2], [m2], lambda h: h.tensor_tensor(out=m2[:], in0=bm[:, 256:512], in1=m2[:], op=ALU.subtract))
                      dve.op([m2], [m2], lambda h: h.tensor_scalar(out=m2[:], in0=m2[:], scalar1=EPS, scalar2=None, op0=ALU.add))
                      act.op([m2], [m2], lambda h: h.activation(out=m2[:], in_=m2[:], func=AF.Sqrt))
                      dve.op([m2], [rstd], lambda h: h.reciprocal(out=rstd[:], in_=m2[:]))
                      bc = lambda t: t[:].unsqueeze(1).to_broadcast([128, 8, 256])
                      pool.op([cv, mean], [cv], lambda h: h.tensor_tensor(out=cv[:], in0=cv[:], in1=bc(mean), op=ALU.subtract))
                      dve.op([cv, rstd], [cv], lambda h: h.tensor_tensor(out=cv[:], in0=cv[:], in1=bc(rstd), op=ALU.mult))
                      for cc in range(8):
                          act.op([cv, cpar], [uln[a]], lambda h, cc=cc: h.activation(
                              out=uln[a][:, cc, :], in_=cv[:, cc, :], func=AF.Silu,
                              scale=cpar[:, cc, 32:33], bias=cpar[:, cc, 33:34]))
                      for s in range(2):
                          for blk in range(2):
                              bank = PB[6 + (s * 2 + blk) % 2]
                              mms = [(bank[:, :], uln[a][:, cc, s * 128:(s + 1) * 128], wco[:, cc, blk * 512:(blk + 1) * 512])
                                     for cc in range(8)]
                              mm_group(bank, [uln[a], wco], mms)
                              dve.op([bank], [ycb[a]], lambda h, s=s, blk=blk, bank=bank: h.tensor_copy(
                                  out=ycb[a][:, s, blk * 512:(blk + 1) * 512], in_=bank[:, :]))
                          sw.dma([ycb[a]], [DB("YC", 2 * u + s)], YC[2 * u + s], ycb[a][:, s, :])
              S.barrier()

              chk()
              with ExitStack() as ph:
                  decT = sb(ph, "decT", [128, 4, 128], F32)
                  dtmp = sb(ph, "dtmp", [128, 2, 128], F32)
                  for hd in range(4):
                      dve.op([cst, lgfb], [dtmp], lambda h, hd=hd: h.tensor_scalar(
                          out=dtmp[:, 0, :], in0=cst[:, 256:384], scalar1=lgfb[:, hd:hd + 1], scalar2=None, op0=ALU.mult))
                      dve.op([cst, lgbb], [dtmp], lambda h, hd=hd: h.tensor_scalar(
                          out=dtmp[:, 1, :], in0=cst[:, 384:512], scalar1=lgbb[:, hd:hd + 1], scalar2=None, op0=ALU.mult))
                      act.op([dtmp], [dtmp], lambda h: h.activation(out=dtmp[:], in_=dtmp[:], func=AF.Exp))
                      dve.op([dtmp, cst], [dtmp], lambda h: h.tensor_tensor(
                          out=dtmp[:, 0, :], in0=dtmp[:, 0, :], in1=cst[:, 512:640], op=ALU.mult))
                      dve.op([dtmp, cst], [dtmp], lambda h: h.tensor_tensor(
                          out=dtmp[:, 1, :], in0=dtmp[:, 1, :], in1=cst[:, 640:768], op=ALU.mult))
                      dve.op([dtmp], [decT], lambda h, hd=hd: h.tensor_tensor(
                          out=decT[:, hd, :], in0=dtmp[:, 0, :], in1=dtmp[:, 1, :], op=ALU.add))
                  wro = sb(ph, "wro", [128, 8, D], BF16)
                  load_weight_bf16(wro, w_ret_out[l], 8, D, 0)
                  wo = sb(ph, "wo", [128, 8, D], BF16)
                  load_weight_bf16(wo, w_o[l], 8, D, 0)
                  xin = [sb(ph, f"xin{i}", [128, 2, D], F32) for i in range(2)]
                  qTl = [sb(ph, f"qTl{i}", [128, 8, 256], BF16) for i in range(2)]
                  kTl = [sb(ph, f"kTl{i}", [128, 8, 256], BF16) for i in range(2)]
                  kk = [sb(ph, f"kk{i}", [128, 2, D], BF16) for i in range(2)]
                  vv = [sb(ph, f"vv{i}", [128, 2, D], BF16) for i in range(2)]
                  sgl = [sb(ph, f"sgl{i}", [128, 2, D], BF16) for i in range(2)]
                  gtl = [sb(ph, f"gtl{i}", [128, 2, 2048], BF16) for i in range(2)]
                  ycl = [sb(ph, f"ycl{i}", [128, 2, D], BF16) for i in range(2)]
                  sbw = [sb(ph, f"sbw{i}", [128, 2, 8 * 256], BF16) for i in range(2)]
                  Sst = sb(ph, "Sstf", [128, 8, 256], F32)
                  Sbf = sb(ph, "Sbff", [128, 8, 256], BF16)
                  PT = [sb(ph, f"PT{i}", [128, 128], BF16) for i in range(2)]
                  tf = [sb(ph, f"tf{i}", [128, 256], F32) for i in range(2)]
                  o_sb = sb(ph, "o_sb", [128, D], F32)
                  vdf = sb(ph, "vdf", [128, D], BF16)
                  gn = [sb(ph, f"gn{i}", [128, 4], F32) for i in range(4)]
                  junk = sb(ph, "junk", [128, 256], BF16)
                  om = sb(ph, "om", [128, D], BF16)
                  omT = sb(ph, "omT", [128, 8, 256], BF16)
                  m1 = sb(ph, "m1", [128, D], F32)
                  mg = sb(ph, "mg", [128, D], BF16)
                  mgT = sb(ph, "mgT", [128, 8, 256], BF16)
                  xo = [sb(ph, f"xo{i}", [128, 2, D], F32) for i in range(2)]

                  def p3a_load(u):
                      a = u % 2
                      sp.dma([DB("X", u)], [xin[a]], xin[a][:], xrows(x_src, u))
                      sp.dma([DB("QT", u)], [qTl[a]], qTl[a][:].rearrange("p a t -> p (a t)"), QT[u])
                      sp.dma([DB("KT", u)], [kTl[a]], kTl[a][:].rearrange("p a t -> p (a t)"), KT[u])
                      for s in range(2):
                          c = 2 * u + s
                          sp.dma([DB("KK", c)], [kk[a]], kk[a][:, s, :], KK[c])
                          sp.dma([DB("VV", c)], [vv[a]], vv[a][:, s, :], VV[c])
                          sp.dma([DB("SG", c)], [sgl[a]], sgl[a][:, s, :], SG[c])
                          sp.dma([DB("GT", c)], [gtl[a]], gtl[a][:, s, :], GT[c])
                          sp.dma([DB("YC", c)], [ycl[a]], ycl[a][:, s, :], YC[c])
                          sp.dma([DB("SB", c)], [sbw[a]], sbw[a][:, s, :], SB[c])

                  om_ = [om, sb(ph, "omB", [128, D], BF16)]
                  hi_box = [0]

                  def R_chunk(u, s):
                      a = u % 2
                      hi = hi_box[0]
                      if s == 0 and u == units[u][0]:
                          pool.op([], [Sst], lambda h: h.memset(Sst[:], 0.0))
                          pool.op([], [Sbf], lambda h: h.memset(Sbf[:], 0.0))
                      ts = slice(s * 128, (s + 1) * 128)
                      sbv = sbw[a][:, s, :].rearrange("p (a e) -> p a e", a=8)
                      for hd in range(4):
                          bA = PB[hi % 2]
                          bO = PB[2 + hi % 2]
                          bC = PB[4 + hi % 2]
                          pt = PT[hi % 2]
                          t_f = tf[hi % 2]
                          hi += 1
                          mm_group(bA, [kTl[a], qTl[a]], [(bA[:, 0:128], kTl[a][:, 2 * hd + dc, ts], qTl[a][:, 2 * hd + dc, ts])
                                                         for dc in range(2)])
                          dve.op([bA, decT], [pt], lambda h, hd=hd, bA=bA, pt=pt: h.tensor_tensor(
                              out=pt[:], in0=bA[:, 0:128], in1=decT[:, hd, :], op=ALU.mult))
                          mm_group(bO, [pt, vv[a]], [(bO[:, 0:256], pt[:], vv[a][:, s, hd * 256:(hd + 1) * 256])])
                          mm_group(bO, [qTl[a], Sbf], [(bO[:, 256:512], qTl[a][:, 2 * hd + dc, ts], Sbf[:, 2 * hd + dc, :])
                                                      for dc in range(2)])
                          mm_group(bC, [qTl[a], sbw[a]], [(bC[:, 0:256], qTl[a][:, 2 * hd + dc, ts], sbv[:, 2 * hd + dc, :])
                                                         for dc in range(2)])
                          act.op([bO, dsc], [t_f], lambda h, hd=hd, bO=bO, t_f=t_f: h.activation(
                              out=t_f[:], in_=bO[:, 256:512], func=AF.Copy, scale=dsc[:, A_F, hd:hd + 1]))
                          dve.op([bC, dsc, t_f], [t_f], lambda h, hd=hd, bC=bC, t_f=t_f: h.scalar_tensor_tensor(
                              out=t_f[:], in0=bC[:, 0:256], scalar=dsc[:, A_B, hd:hd + 1], in1=t_f[:],
                              op0=ALU.mult, op1=ALU.add))
                          dve.op([bO, t_f], [o_sb], lambda h, hd=hd, bO=bO, t_f=t_f: h.tensor_tensor(
                              out=o_sb[:, hd * 256:(hd + 1) * 256], in0=bO[:, 0:256], in1=t_f[:], op=ALU.add))
                      for hd in range(4):
                          act.op([vv[a], dsc], [vdf], lambda h, hd=hd, s=s: h.activation(
                              out=vdf[:, hd * 256:(hd + 1) * 256], in_=vv[a][:, s, hd * 256:(hd + 1) * 256],
                              func=AF.Copy, scale=dsc[:, VD_F, hd:hd + 1]))
                      for hd in range(4):
                          bank = PB[6 + hd % 2]
                          for dc in range(2):
                              mm_group(bank, [kk[a], vdf], [(bank[:, dc * 256:(dc + 1) * 256],
                                       kk[a][:, s, (2 * hd + dc) * 128:(2 * hd + dc + 1) * 128],
                                       vdf[:, hd * 256:(hd + 1) * 256])])
                          dve.op([bank, dsc, Sst], [Sst], lambda h, hd=hd, bank=bank: h.scalar_tensor_tensor(
                              out=Sst[:, 2 * hd:2 * hd + 2, :], in0=Sst[:, 2 * hd:2 * hd + 2, :],
                              scalar=dsc[:, CD_F, hd:hd + 1],
                              in1=bank[:, :].rearrange("p (a e) -> p a e", a=2), op0=ALU.mult, op1=ALU.add))
                      act.op([Sst], [Sbf], lambda h: h.activation(out=Sbf[:], in_=Sst[:], func=AF.Copy))
                      for hd in range(4):
                          act.op([o_sb], [junk, gn[0]], lambda h, hd=hd: h.activation(
                              out=junk[:], in_=o_sb[:, hd * 256:(hd + 1) * 256], func=AF.Square,
                              accum_out=gn[0][:, hd:hd + 1]))
                      dve.op([gn[0]], [gn[1]], lambda h: h.tensor_scalar(
                          out=gn[1][:], in0=gn[0][:], scalar1=1.0 / 256.0, scalar2=EPS, op0=ALU.mult, op1=ALU.add))
                      act.op([gn[1]], [gn[2]], lambda h: h.activation(out=gn[2][:], in_=gn[1][:], func=AF.Sqrt))
                      dve.op([gn[2]], [gn[3]], lambda h: h.reciprocal(out=gn[3][:], in_=gn[2][:]))
                      for hd in range(4):
                          dve.op([o_sb, gn[3], sgl[a]], [om_[(2 * u + s) % 2]], lambda h, hd=hd, s=s: h.scalar_tensor_tensor(
                              out=om_[(2 * u + s) % 2][:, hd * 256:(hd + 1) * 256], in0=o_sb[:, hd * 256:(hd + 1) * 256],
                              scalar=gn[3][:, hd:hd + 1], in1=sgl[a][:, s, hd * 256:(hd + 1) * 256],
                              op0=ALU.mult, op1=ALU.mult))
                      hi_box[0] = hi

                  def M_chunk(u, s):
                      a = u % 2
                      ts = slice(s * 128, (s + 1) * 128)
                      bT = PB[6 + s]
                      v = pbf(bT)
                      transposes([bT], [om_[(2 * u + s) % 2], identb], [(v[:, cc * 128:(cc + 1) * 128], om_[(2 * u + s) % 2][:, cc * 128:(cc + 1) * 128], identb[:])
                                                      for cc in range(8)])
                      act.op([bT], [omT], lambda h, s=s, bT=bT: h.activation(
                          out=omT[:, :, s * 128:(s + 1) * 128], in_=pbf(bT).rearrange("p (a t) -> p a t", a=8), func=AF.Copy))
                      for blk in range(2):
                          bank = PB[blk]
                          mm_group(bank, [omT, wro], [(bank[:, :], omT[:, cc, ts], wro[:, cc, blk * 512:(blk + 1) * 512])
                                                     for cc in range(8)])
                          cs_ = slice(blk * 512, (blk + 1) * 512)
                          pool.op([ycl[a], gtl[a]], [m1], lambda h, s=s, cs_=cs_: h.tensor_tensor(
                              out=m1[:, cs_], in0=ycl[a][:, s, cs_], in1=gtl[a][:, s, cs_], op=ALU.mult))
                          dve.op([bank, gtl[a], m1], [mg], lambda h, s=s, blk=blk, bank=bank, cs_=cs_: h.tensor_tensor(
                              out=mg[:, cs_], in0=bank[:, :], in1=gtl[a][:, s, 1024 + blk * 512:1024 + (blk + 1) * 512],
                              op=ALU.mult))
                          pool.op([m1, mg], [mg], lambda h, cs_=cs_: h.tensor_tensor(
                              out=mg[:, cs_], in0=mg[:, cs_], in1=m1[:, cs_], op=ALU.add))
                      bT2 = PB[4 + s]
                      v2 = pbf(bT2)
                      transposes([bT2], [mg, identb], [(v2[:, cc * 128:(cc + 1) * 128], mg[:, cc * 128:(cc + 1) * 128], identb[:])
                                                       for cc in range(8)])
                      act.op([bT2], [mgT], lambda h, s=s, bT2=bT2: h.activation(
                          out=mgT[:, :, s * 128:(s + 1) * 128], in_=pbf(bT2).rearrange("p (a t) -> p a t", a=8), func=AF.Copy))
                      for blk in range(2):
                          bank = PB[2 + blk]
                          mm_group(bank, [mgT, wo], [(bank[:, :], mgT[:, cc, ts], wo[:, cc, blk * 512:(blk + 1) * 512])
                                                    for cc in range(8)])
                          dve.op([bank, xin[a]], [xo[a]], lambda h, s=s, blk=blk, bank=bank: h.tensor_tensor(
                              out=xo[a][:, s, blk * 512:(blk + 1) * 512], in0=bank[:, :],
                              in1=xin[a][:, s, blk * 512:(blk + 1) * 512], op=ALU.add))

                  p3a_load(0)
                  if NU > 1:
                      p3a_load(1)
                  chunks = [(u_, s_) for u_ in range(NU) for s_ in range(2)]
                  R_chunk(0, 0)
                  for ci_, (u, s) in enumerate(chunks):
                      if ci_ + 1 < len(chunks):
                          R_chunk(*chunks[ci_ + 1])
                      M_chunk(u, s)
                      if s == 1:
                          a = u % 2
                          sw.dma([xo[a]], [DB("XM", u)], xrows(XM, u), xo[a][:])
                          if u + 2 < NU:
                              p3a_load(u + 2)
              S.barrier()

              chk()
              with ExitStack() as ph:
                  wq = sb(ph, "wq", [128, 8, 2048], BF16)
                  load_weight_bf16(wq, peer_wq[l], 8, 2048, 0)
                  gtile = sb(ph, "g3b", [128, D], F32)
                  bcast_row(gtile, norm_ffn_g[l:l + 1, :])
                  if last and do_final:
                      gfin = sb(ph, "gfin", [128, D], F32)
                      bcast_row(gfin, final_g[0:1, :])
                  skT = sb(ph, "skT", [128, 16, 128], BF16)
                  with ExitStack() as ph2:
                      skraw = sb(ph2, "skraw", [128, 16, 128], F32)
                      sp.dma([], [skraw], skraw[:], peer_sk[l].rearrange("h p k d -> k (h p) d"))
                      for cq in range(16):
                          bank = PB[cq % 8]
                          transposes([bank], [skraw, cst], [(bank[:, 0:128], skraw[:, cq, :], ident_f)])
                          dve.op([bank], [skT], lambda h, cq=cq, bank=bank: h.tensor_copy(out=skT[:, cq, :], in_=bank[:, 0:128]))
                      S.barrier()
                  xw = sb(ph, "xw", [128, 2, D], F32)
                  xnT = [sb(ph, f"xnT{i}", [128, 8, 256], BF16) for i in range(2)]
                  small = [sb(ph, f"sm{i}", [128, 2], F32) for i in range(4)]
                  qTp = sb(ph, "qTp", [128, 16, 256], BF16)
                  sc_ = [sb(ph, f"sc{i}", [128, 16, 128], F32) for i in range(2)]
                  sc2 = sb(ph, "sc2", [128, 16, 128], F32)
                  xn = TT(sc2.t[:].rearrange("p a k -> p (a k)").bitcast(BF16)[:, 0:2 * D].rearrange("p (s d) -> p s d", s=2),
                          "xn_alias")
                  xn.b = sc2.b
                  junk = (xn[:, 0, :], xn)
                  sv = sb(ph, "sv", [128, 16, 16], F32)
                  si = sb(ph, "si", [128, 16, 16], U32)
                  sif = sb(ph, "sif", [128, 16, 16], F32)
                  cand_ = []
                  for i_ in range(2):
                      c_ = TT(sc_[i_].t[:].rearrange("p (h two) k -> p h (two k)", two=2), f"cand_alias{i_}")
                      c_.b = sc_[i_].b
                      cand_.append(c_)
                  cand2 = TT(sc2.t[:].rearrange("p (h two) k -> p h (two k)", two=2), "cand2_alias")
                  cand2.b = sc2.b
                  s16_ = [sb(ph, f"s16_{i}", [128, 8, 16], F32) for i in range(2)]
                  ci = sb(ph, "ci", [128, 8, 16], U32)
                  cia = sb(ph, "cia", [128, 8, 16], U32)
                  cib = sb(ph, "cib", [128, 8, 16], U32)
                  caf = sb(ph, "caf", [128, 8, 16], F32)
                  cbf = sb(ph, "cbf", [128, 8, 16], F32)
                  eq = TT(sc2.t[:].rearrange("p (h two) (a b) -> p h (two a) b", two=2, b=16), "eq_alias")
                  eq.b = sc2.b
                  rowi_ = [[sb(ph, f"rowi{j}_{i}", [128, 128], F32) for i in range(3)] for j in range(2)]
                  gsum = sb(ph, "gsum", [128, 8], F32)
                  colT = [sb(ph, f"colT{i}", [128, 256], F32) for i in range(3)]
                  TBK = 4
                  P1 = [sb(ph, f"P1_{i}", [128, TBK, 128], BF16) for i in range(2)]
                  P2 = [sb(ph, f"P2_{i}", [128, TBK, 128], BF16) for i in range(2)]
                  GTs = sb(ph, "GTs", [128, 128, 256], BF16)
                  hT = [sb(ph, f"hT{i}", [128, 256], BF16) for i in range(3)]
                  gh = [sb(ph, f"gh{i}", [128, 256], BF16) for i in range(4)]
                  P2ENG = pool if CFG.get("P2POOL") else dve
                  nidx2 = sb(ph, "nidx2", [128, 256], F32)
                  atmp = [sb(ph, f"atmp{i}", [128, 128], BF16) for i in range(2)]
                  iota3 = sb(ph, "iota3", [128, TBK, 128], BF16)
                  for tt_ in range(TBK):
                      dve.op([iotab], [iota3], lambda h, tt_=tt_: h.tensor_copy(out=iota3[:, tt_, :], in_=iotab[:]))
                  NSB = CFG["NSB"]
                  LOOK = CFG["LOOK"]
                  ub = [sb(ph, f"ub{i}", [128, 8, 256], BF16) for i in range(NSB)]
                  vbuf = [sb(ph, f"vbuf{i}", [128, 2, D], BF16) for i in range(NSB)]

                  def route_A(u):
                      a = u % 2
                      sp.dma([DB("XM", u)], [xw], xw[:], xrows(XM, u))
                      norm_T(xw, gtile, xn, xnT[a], junk, small, PB[6], PB[7], (act, dve))
                      for cq in range(16):
                          bank = PB[4 + cq % 2]
                          mm_group(bank, [wq, xnT[a]], [(bank[:, 0:256], wq[:, kc, cq * 128:(cq + 1) * 128], xnT[a][:, kc, :])
                                                       for kc in range(8)])
                          act.op([bank], [qTp], lambda h, cq=cq, bank=bank: h.activation(
                              out=qTp[:, cq, :], in_=bank[:, 0:256], func=AF.Copy))
                      for s in range(2):
                          ts = slice(s * 128, (s + 1) * 128)
                          sc = sc_[s]
                          for q4 in range(4):
                              bank = PB[4 + q4 % 2]
                              for j in range(4):
                                  cq = q4 * 4 + j
                                  mm_group(bank, [qTp, skT], [(bank[:, j * 128:(j + 1) * 128], qTp[:, cq, ts], skT[:, cq, :])])
                              act.op([bank], [sc], lambda h, q4=q4, bank=bank, sc=sc: h.activation(
                                  out=sc[:, q4 * 4:(q4 + 1) * 4, :], in_=bank[:, :].rearrange("p (a k) -> p a k", a=4),
                                  func=AF.Copy))
                      for s in range(2):
                          if "A2" in SKIP:
                              break
                          ts = slice(s * 128, (s + 1) * 128)
                          sc = sc_[s]
                          cand = cand_[s]
                          s16 = s16_[s]
                          rowi = rowi_[s]
                          for cq in range(16):
                              dve.op([sc], [sv], lambda h, cq=cq: h.max(out=sv[:, cq, 0:8], in_=sc[:, cq, :]))
                              dve.op([sc, sv], [si], lambda h, cq=cq: h.max_index(out=si[:, cq, 0:8], in_max=sv[:, cq, 0:8], in_values=sc[:, cq, :]))
                              dve.op([sc, sv], [sc2], lambda h, cq=cq: h.match_replace(
                                  out=sc2[:, cq, :], in_to_replace=sv[:, cq, 0:8], in_values=sc[:, cq, :], imm_value=NEG))
                              dve.op([sc2], [sv], lambda h, cq=cq: h.max(out=sv[:, cq, 8:16], in_=sc2[:, cq, :]))
                              dve.op([sc2, sv], [si], lambda h, cq=cq: h.max_index(out=si[:, cq, 8:16], in_max=sv[:, cq, 8:16], in_values=sc2[:, cq, :]))
                          dve.op([si], [sif], lambda h: h.tensor_copy(out=sif[:], in_=si[:]))
                          sv4 = sv[:].rearrange("p (h two) k -> p h two k", two=2)
                          sif4 = sif[:].rearrange("p (h two) k -> p h two k", two=2)
                          cand4 = cand[:].rearrange("p h (a b) -> p h a b", a=16)
                          dve.op([sv], [cand], lambda h: h.tensor_tensor(
                              out=cand4, in0=sv4[:, :, 0, :].unsqueeze(3).to_broadcast([128, 8, 16, 16]),
                              in1=sv4[:, :, 1, :].unsqueeze(2).to_broadcast([128, 8, 16, 16]), op=ALU.add))
                          for hd in range(8):
                              dve.op([cand], [s16], lambda h, hd=hd: h.max(out=s16[:, hd, 0:8], in_=cand[:, hd, :]))
                              dve.op([cand, s16], [ci], lambda h, hd=hd: h.max_index(out=ci[:, hd, 0:8], in_max=s16[:, hd, 0:8], in_values=cand[:, hd, :]))
                              dve.op([cand, s16], [cand2], lambda h, hd=hd: h.match_replace(
                                  out=cand2[:, hd, :], in_to_replace=s16[:, hd, 0:8], in_values=cand[:, hd, :], imm_value=NEG))
                              dve.op([cand2], [s16], lambda h, hd=hd: h.max(out=s16[:, hd, 8:16], in_=cand2[:, hd, :]))
                              dve.op([cand2, s16], [ci], lambda h, hd=hd: h.max_index(out=ci[:, hd, 8:16], in_max=s16[:, hd, 8:16], in_values=cand2[:, hd, :]))
                          dve.op([ci], [cia], lambda h: h.tensor_single_scalar(out=cia[:], in_=ci[:], scalar=4, op=ALU.logical_shift_right))
                          dve.op([ci], [cib], lambda h: h.tensor_single_scalar(out=cib[:], in_=ci[:], scalar=15, op=ALU.bitwise_and))
                          dve.op([cia], [caf], lambda h: h.tensor_copy(out=caf[:], in_=cia[:]))
                          dve.op([cib], [cbf], lambda h: h.tensor_copy(out=cbf[:], in_=cib[:]))
                          io4 = iota16.unsqueeze(1).unsqueeze(1).to_broadcast([128, 8, 16, 16])
                          for which, (cf, half) in enumerate(((caf, 0), (cbf, 1))):
                              dst3 = rowi[which][:].rearrange("p (h k) -> p h k", h=8)
                              dve.op([cf, cst], [eq], lambda h, cf=cf: h.tensor_tensor(
                                  out=eq[:], in0=cf[:].unsqueeze(3).to_broadcast([128, 8, 16, 16]), in1=io4, op=ALU.is_equal))
                              dve.op([eq, sif], [eq], lambda h, half=half: h.tensor_tensor(
                                  out=eq[:], in0=eq[:], in1=sif4[:, :, half, :].unsqueeze(2).to_broadcast([128, 8, 16, 16]),
                                  op=ALU.mult))
                              dve.op([eq], [rowi[which]], lambda h, dst3=dst3: h.tensor_reduce(
                                  out=dst3, in_=eq[:], axis=AX.X, op=ALU.add))

                  def route_B(u):
                      for s in range(2):
                          ts = slice(s * 128, (s + 1) * 128)
                          s16 = s16_[s]
                          rowi = rowi_[s]
                          g3 = rowi[2][:].rearrange("p (h k) -> p h k", h=8)
                          dve.op([s16], [rowi[2]], lambda h: h.tensor_tensor(
                              out=g3, in0=s16[:], in1=s16[:, :, 0:1].to_broadcast([128, 8, 16]), op=ALU.subtract))
                          act.op([rowi[2]], [rowi[2]], lambda h: h.activation(out=rowi[2][:], in_=rowi[2][:], func=AF.Exp))
                          dve.op([rowi[2]], [gsum], lambda h: h.tensor_reduce(out=gsum[:], in_=g3, axis=AX.X, op=ALU.add))
                          dve.op([gsum], [gsum], lambda h: h.reciprocal(out=gsum[:], in_=gsum[:]))
                          dve.op([rowi[2], gsum], [rowi[2]], lambda h: h.tensor_tensor(
                              out=g3, in0=g3, in1=gsum[:].unsqueeze(2).to_broadcast([128, 8, 16]), op=ALU.mult))
                          for w3 in range(3):
                              bank = PB[6 + w3 % 2]
                              transposes([bank], [rowi[w3], cst], [(bank[:, 0:128], rowi[w3][:], ident_f)])
                              act.op([bank], [colT[w3]], lambda h, w3=w3, bank=bank, ts=ts: h.activation(
                                  out=colT[w3][:, ts], in_=bank[:, 0:128], func=AF.Copy))
                      bi = 0
                      dve.op([colT[1]], [nidx2], lambda h: h.tensor_scalar(
                          out=nidx2[:], in0=colT[1][:], scalar1=-1.0, scalar2=None, op0=ALU.mult))
                      for t0 in range(0, 256, TBK):
                          p1 = P1[(t0 // TBK) % 2]
                          p2 = P2[(t0 // TBK) % 2]
                          for tt in range(TBK):
                              t = t0 + tt
                              dve.op([iotab, colT[0], colT[2]], [p1], lambda h, t=t, tt=tt, p1=p1: h.tensor_scalar(
                                  out=p1[:, tt, :], in0=iotab[:], scalar1=colT[0][:, t:t + 1], scalar2=colT[2][:, t:t + 1],
                                  op0=ALU.is_equal, op1=ALU.mult))
                              if tt < CFG.get("ACTP2", 0):
                                  at_ = atmp[tt % 2]
                                  act.op([iotab, nidx2], [at_], lambda h, t=t, at_=at_: h.activation(
                                      out=at_[:], in_=iotab[:], func=AF.Abs, bias=nidx2[:, t:t + 1]))
                                  act.op([at_], [p2], lambda h, tt=tt, p2=p2, at_=at_: h.activation(
                                      out=p2[:, tt, :], in_=at_[:], func=AF.Relu, scale=-1.0, bias=1.0))
                              elif not CFG.get("P2BATCH"):
                                  P2ENG.op([iotab, colT[1]], [p2], lambda h, t=t, tt=tt, p2=p2: h.tensor_scalar(
                                      out=p2[:, tt, :], in0=iotab[:], scalar1=colT[1][:, t:t + 1], scalar2=None,
                                      op0=ALU.is_equal))
                          if CFG.get("P2BATCH"):
                              dve.op([iota3, colT[1]], [p2], lambda h, t0=t0, p2=p2: h.tensor_tensor(
                                  out=p2[:], in0=iota3[:],
                                  in1=colT[1][:, t0:t0 + TBK].unsqueeze(2).to_broadcast([128, TBK, 128]), op=ALU.is_equal))
                          for q in range(TBK // 4):
                              bank = PB[4 + bi % 4]
                              bi += 1
                              for j in range(4):
                                  tt = q * 4 + j
                                  mm_group(bank, [p1, p2], [(bank[:, j * 128:(j + 1) * 128], p2[:, tt, :], p1[:, tt, :])])
                              tq = t0 + q * 4
                              act.op([bank], [GTs], lambda h, bank=bank, tq=tq: h.activation(
                                  out=GTs[:, :, tq:tq + 4].transpose([0, 2, 1]),
                                  in_=bank[:, :].rearrange("p (t i) -> p t i", t=4), func=AF.Copy))

                  def dense(u):
                      a = u % 2
                      steps = []
                      for g in range(64):
                          for c in range(2):
                              steps.append((g, c))

                      def emit_load(g):
                          if "dload" in SKIP and g >= NSB:
                              return
                          sp.dma([DB("UTS", (l, g))], [ub[g % NSB]], ub[g % NSB][:].rearrange("p a e -> p (a e)"), UTS[l, g])
                          sp.dma([DB("VS", (l, g))], [vbuf[g % NSB]], vbuf[g % NSB][:].rearrange("p a e -> p (a e)"), VS[l, g])

                      def emit_H(i):
                          g, c = steps[i]
                          bank = PB[4 + i % (LOOK + 1)]
                          mm_group(bank, [ub[g % NSB], xnT[a]], [(bank[:, 0:256], ub[g % NSB][:, kc, c * 128:(c + 1) * 128], xnT[a][:, kc, :])
                                                                for kc in range(8)])
                          h_ = hT[i % 3]
                          g_ = gh[i % 4]
                          act.op([bank], [h_], lambda h, bank=bank, h_=h_: h.activation(out=h_[:], in_=bank[:, 0:256], func=AF.Gelu))
                          (dve if CFG.get("GHENG") else pool).op([h_, GTs], [g_], lambda h, h_=h_, g_=g_, i1=2 * g + c: h.tensor_tensor(
                              out=g_[:], in0=h_[:], in1=GTs[:, i1, :], op=ALU.mult))

                      def emit_V(i):
                          g, c = steps[i]
                          g_ = gh[i % 4]
                          def fnv(h, g_=g_, g=g, c=c, i=i):
                              inst = None
                              for s in range(2):
                                  for blk in range(2):
                                      bank = PB[s * 2 + blk]
                                      inst = h.matmul(bank[:, :], lhsT=g_[:, s * 128:(s + 1) * 128],
                                                      rhs=vbuf[g % NSB][:, c, blk * 512:(blk + 1) * 512],
                                                      start=(i == 0), stop=(i == len(steps) - 1))
                              return inst
                          pe.op([g_, vbuf[g % NSB]], [PB[0], PB[1], PB[2], PB[3]], fnv)

                      for g0 in range(NSB - 1):
                          emit_load(g0)
                      for i0 in range(LOOK):
                          emit_H(i0)
                      for i in range(len(steps)):
                          g, c = steps[i]
                          if c == 0 and g + NSB - 1 < 64:
                              emit_load(g + NSB - 1)
                          if i + LOOK < len(steps):
                              emit_H(i + LOOK)
                          emit_V(i)

                  def finalize(u):
                      xo = xw
                      yo = xw
                      sp.dma([DB("XM", u)], [xw], xw[:], xrows(XM, u))
                      for s in range(2):
                          for blk in range(2):
                              bank = PB[s * 2 + blk]
                              dve.op([bank, xw], [xw], lambda h, s=s, blk=blk, bank=bank: h.tensor_tensor(
                                  out=xw[:, s, blk * 512:(blk + 1) * 512], in0=bank[:, :],
                                  in1=xw[:, s, blk * 512:(blk + 1) * 512], op=ALU.add))
                      if last and do_final:
                          ss, ms, sq, rstd = small
                          for s in range(2):
                              act.op([xo], [xn, ss], lambda h, s=s: h.activation(
                                  out=xn[:, 0, :], in_=xo[:, s, :], func=AF.Square, accum_out=ss[:, s:s + 1]))
                          dve.op([ss], [ms], lambda h: h.tensor_scalar(
                              out=ms[:], in0=ss[:], scalar1=1.0 / D, scalar2=EPS, op0=ALU.mult, op1=ALU.add))
                          act.op([ms], [sq], lambda h: h.activation(out=sq[:], in_=ms[:], func=AF.Sqrt))
                          dve.op([sq], [rstd], lambda h: h.reciprocal(out=rstd[:], in_=sq[:]))
                          for s in range(2):
                              dve.op([xo, rstd, gfin], [yo], lambda h, s=s: h.scalar_tensor_tensor(
                                  out=yo[:, s, :], in0=xo[:, s, :], scalar=rstd[:, s:s + 1], in1=gfin[:],
                                  op0=ALU.mult, op1=ALU.mult))
                          sw.dma([yo], [DB("Y", u)], xrows(x_dst, u), yo[:])
                      else:
                          sw.dma([xo], [DB("X", u)], xrows(x_dst, u), xo[:])

                  route_A(0)
                  if "routeB" not in SKIP:
                      route_B(0)
                  for u in range(NU):
                      if u + 1 < NU:
                          route_A(u + 1)
                      if "dense" not in SKIP:
                          dense(u)
                      finalize(u)
                      if u + 1 < NU and "routeB" not in SKIP:
                          route_B(u + 1)
              S.barrier()
              lay.close()
        except _Stop:
            pass
        S.barrier()
    return nc, S.ninst


_WEIGHT_KEYS = ["norm_mix_g", "w_in", "w_gate", "conv_w", "conv_b", "conv_ln_g", "conv_ln_b", "w_conv_out",
                "log_gamma_fwd", "log_gamma_bwd", "w_ret_out", "w_o", "norm_ffn_g", "peer_wq", "peer_subkeys",
                "peer_u", "peer_v"]


def run_cores(x_per_core, weights, seqs, n_layers=2, do_final=True, stop=0):
    nc, ninst = build(seqs, n_layers=n_layers, do_final=do_final, stop=stop)
    consts, cosT, sinT = host_consts(max(seqs))
    base = {k: np.ascontiguousarray(np.asarray(weights[k], dtype=np.float32)) for k in _WEIGHT_KEYS}
    base["final_norm_g"] = np.ascontiguousarray(np.asarray(weights["final_norm_g"], np.float32).reshape(1, D))
    base["consts"] = consts
    base["cosT"] = cosT
    base["sinT"] = sinT
    in_maps = []
    for xc in x_per_core:
        m = dict(base)
        m["x"] = np.ascontiguousarray(xc, dtype=np.float32)
        in_maps.append(m)
    res = run_bass_kernel_spmd(nc, in_maps, core_ids=list(range(len(x_per_core))))
    return [r["y"] for r in res.results]


def kernel(x_prompt, x_sample, **weights):
    x_prompt = np.asarray(x_prompt, dtype=np.float32)
    x_sample = np.asarray(x_sample, dtype=np.float32)
    xs = []
    for c in range(N_CORES):
        xs.append(np.concatenate([x_prompt[c], x_sample[2 * c], x_sample[2 * c + 1]], axis=0))
    ys = run_cores(xs, weights, SEQS_FULL)
    y_prompt = np.stack([y[0:8192] for y in ys], axis=0)
    y_sample = np.stack([ys[c // 2][8192 + 2048 * (c % 2):8192 + 2048 * (c % 2 + 1)] for c in range(16)], axis=0)
    return (y_prompt.astype(np.float32), y_sample.astype(np.float32))
```
